# Optimizing a Trainium2 kernel written in Bass

```python
import math
import jax, jax.numpy as jnp
from jax import lax
import numpy as np

D_MODEL = 1024
BATCH = 4
SEQ = 4096
DEPTH = 2

CHUNK = 64
N_A_LAYERS = DEPTH // 2
N_B_LAYERS = DEPTH - N_A_LAYERS
N_DENSE_FFN = (DEPTH + 1) // 2
N_MOE_FFN = DEPTH // 2
DEEPNORM_ALPHA = (2.0 * DEPTH) ** 0.25
DEEPNORM_BETA = (8.0 * DEPTH) ** -0.25
LN_EPS = 1e-5

RET_HEADS = 4
RET_QK_DIM = D_MODEL // RET_HEADS
RET_V_WIDTH = 2 * D_MODEL
RET_V_DIM = RET_V_WIDTH // RET_HEADS
RET_IN_WIDTH = 2 * D_MODEL + 2 * RET_V_WIDTH
ROPE_BASE = 10000.0

SB_HEADS = 16
SB_HEAD_DIM = D_MODEL // SB_HEADS
SB_QBLOCK = 128

D_FF = 2816
N_EXPERTS = 8
TOP_K = 2
D_EXPERT = 3584

kernel_name = "retention_stickbreaking_yoco_moe_deepnorm"


def layer_norm(x, g, b):
    xf = x.astype(jnp.float32)
    mu = jnp.mean(xf, axis=-1, keepdims=True)
    var = jnp.mean(jnp.square(xf - mu), axis=-1, keepdims=True)
    y = (xf - mu) * lax.rsqrt(var + LN_EPS) * g.astype(jnp.float32) + b.astype(jnp.float32)
    return y.astype(x.dtype)


def rotary(x):
    s, d = x.shape[-2], x.shape[-1]
    inv_freq = 1.0 / (ROPE_BASE ** (jnp.arange(0, d, 2, dtype=jnp.float32) / d))
    ang = jnp.arange(s, dtype=jnp.float32)[:, None] * inv_freq[None, :]
    cos, sin = jnp.cos(ang).astype(x.dtype), jnp.sin(ang).astype(x.dtype)
    x1, x2 = x[..., : d // 2], x[..., d // 2:]
    return jnp.concatenate([x1 * cos - x2 * sin, x1 * sin + x2 * cos], axis=-1)


def retention_mixer(x, w_in, w_out):
    b, s, _ = x.shape
    nc = s // CHUNK
    dt = x.dtype
    proj = x @ w_in
    q, k, v, g = jnp.split(proj, [D_MODEL, 2 * D_MODEL, 2 * D_MODEL + RET_V_WIDTH], axis=-1)
    q = q.reshape(b, s, RET_HEADS, RET_QK_DIM).transpose(0, 2, 1, 3)
    k = k.reshape(b, s, RET_HEADS, RET_QK_DIM).transpose(0, 2, 1, 3) * (RET_QK_DIM ** -0.5)
    v = v.reshape(b, s, RET_HEADS, RET_V_DIM).transpose(0, 2, 1, 3)
    q, k = rotary(q), rotary(k)
    log_g = jnp.log(1.0 - 2.0 ** (-5.0 - jnp.arange(RET_HEADS, dtype=jnp.float32)))
    idx = jnp.arange(CHUNK, dtype=jnp.float32)
    intra_decay = jnp.exp(log_g[:, None, None] * jnp.abs(idx[:, None] - idx[None, :])).astype(dt)
    q_decay = jnp.exp(log_g[:, None] * (idx + 1.0))[..., None].astype(dt)
    k_decay = jnp.exp(log_g[:, None] * (CHUNK - 1.0 - idx))[..., None].astype(dt)
    chunk_decay = jnp.exp(log_g * CHUNK)[:, None, None].astype(dt)

    def to_chunks(t):
        return t.reshape(b, RET_HEADS, nc, CHUNK, t.shape[-1]).transpose(2, 0, 1, 3, 4)

    def step(state, inp):
        qi, ki, vi = inp
        scores = jnp.einsum('bhnd,bhmd->bhnm', qi, ki) * intra_decay
        out = (jnp.einsum('bhnm,bhme->bhne', scores, vi)
               + jnp.einsum('bhnd,bhde->bhne', qi * q_decay, state))
        state = state * chunk_decay + jnp.einsum('bhmd,bhme->bhde', ki * k_decay, vi)
        return state, out

    state0 = jnp.zeros((b, RET_HEADS, RET_QK_DIM, RET_V_DIM), dt)
    _, oc = lax.scan(step, state0, (to_chunks(q), to_chunks(k), to_chunks(v)))
    o = oc.transpose(1, 0, 3, 2, 4).reshape(b, s, RET_HEADS, RET_V_DIM)
    of = o.astype(jnp.float32)
    mu = jnp.mean(of, axis=-1, keepdims=True)
    var = jnp.mean(jnp.square(of - mu), axis=-1, keepdims=True)
    o = ((of - mu) * lax.rsqrt(var + LN_EPS)).astype(dt).reshape(b, s, RET_V_WIDTH)
    return (jax.nn.silu(g) * o) @ w_out


def shared_kv(x, w_kv):
    b, s, _ = x.shape
    kv = x @ w_kv
    k, v = jnp.split(kv, 2, axis=-1)
    k = k.reshape(b, s, SB_HEADS, SB_HEAD_DIM).transpose(0, 2, 1, 3)
    v = v.reshape(b, s, SB_HEADS, SB_HEAD_DIM).transpose(0, 2, 1, 3)
    return k, v


def stick_breaking_mixer(x, k, v, w_q, w_out):
    b, s, _ = x.shape
    scale = SB_HEAD_DIM ** -0.5
    q = (x @ w_q).reshape(b, s, SB_HEADS, SB_HEAD_DIM).transpose(0, 2, 1, 3)
    outs = []
    for start in range(0, s, SB_QBLOCK):
        end = start + SB_QBLOCK
        qb = q[:, :, start:end]
        kb, vb = k[:, :, :end], v[:, :, :end]
        z = jnp.einsum('bhtd,bhsd->bhts', qb, kb).astype(jnp.float32) * scale
        causal = jnp.arange(end)[None, :] < jnp.arange(start, end)[:, None]
        log_keep = jnp.where(causal, jax.nn.log_sigmoid(-z), 0.0)
        between = lax.cumsum(log_keep, axis=3, reverse=True) - log_keep
        w = jnp.where(causal, jnp.exp(jax.nn.log_sigmoid(z) + between), 0.0)
        outs.append(jnp.einsum('bhts,bhsd->bhtd', w.astype(vb.dtype), vb))
    o = jnp.concatenate(outs, axis=2).transpose(0, 2, 1, 3).reshape(b, s, D_MODEL)
    return o @ w_out


def swiglu(x, w_in, w_out):
    gate, up = jnp.split(x @ w_in, 2, axis=-1)
    return (jax.nn.silu(gate) * up) @ w_out


def moe_swiglu(x, w_router, w_in, w_out):
    b, s, d = x.shape
    xt = x.reshape(b * s, d)
    logits = (xt @ w_router).astype(jnp.float32)
    top_val, top_idx = lax.top_k(logits, TOP_K)
    top_w = jax.nn.softmax(top_val, axis=-1)
    gates = jnp.sum(jax.nn.one_hot(top_idx, N_EXPERTS, dtype=jnp.float32) * top_w[..., None], axis=1)
    gates = gates.astype(x.dtype)
    y = jnp.zeros_like(xt)
    for e in range(N_EXPERTS):
        y = y + gates[:, e:e + 1] * swiglu(xt, w_in[e], w_out[e])
    return y.reshape(b, s, d)


def setup_inputs(seed: int = 0) -> dict:
    key = jax.random.key(seed)
    ks = jax.random.split(key, 20)
    f32 = jnp.float32
    beta = DEEPNORM_BETA

    def nrm(k, shape, fan_in, gain=1.0):
        return jax.random.normal(k, shape, f32) * (fan_in ** -0.5) * gain

    x = jax.random.normal(ks[0], (BATCH, SEQ, D_MODEL), f32)
    ret_w_in = jnp.concatenate([
        nrm(ks[1], (N_A_LAYERS, D_MODEL, 2 * D_MODEL), D_MODEL),
        nrm(ks[2], (N_A_LAYERS, D_MODEL, RET_V_WIDTH), D_MODEL, beta),
        nrm(ks[3], (N_A_LAYERS, D_MODEL, RET_V_WIDTH), D_MODEL),
    ], axis=-1)
    ret_w_out = nrm(ks[4], (N_A_LAYERS, RET_V_WIDTH, D_MODEL), RET_V_WIDTH, beta)
    sb_w_kv = jnp.concatenate([
        nrm(ks[5], (D_MODEL, D_MODEL), D_MODEL),
        nrm(ks[6], (D_MODEL, D_MODEL), D_MODEL, beta),
    ], axis=-1)
    sb_w_q = nrm(ks[7], (N_B_LAYERS, D_MODEL, D_MODEL), D_MODEL)
    sb_w_out = nrm(ks[8], (N_B_LAYERS, D_MODEL, D_MODEL), D_MODEL, beta)
    ln_mix_g = 1.0 + 0.02 * jax.random.normal(ks[9], (DEPTH, D_MODEL), f32)
    ln_mix_b = 0.02 * jax.random.normal(ks[10], (DEPTH, D_MODEL), f32)
    ln_ffn_g = 1.0 + 0.02 * jax.random.normal(ks[11], (DEPTH, D_MODEL), f32)
    ln_ffn_b = 0.02 * jax.random.normal(ks[12], (DEPTH, D_MODEL), f32)
    ffn_w_in = nrm(ks[13], (N_DENSE_FFN, D_MODEL, 2 * D_FF), D_MODEL, beta)
    ffn_w_out = nrm(ks[14], (N_DENSE_FFN, D_FF, D_MODEL), D_FF, beta)
    moe_w_router = nrm(ks[15], (N_MOE_FFN, D_MODEL, N_EXPERTS), D_MODEL)
    moe_w_in = nrm(ks[16], (N_MOE_FFN, N_EXPERTS, D_MODEL, 2 * D_EXPERT), D_MODEL, beta)
    moe_w_out = nrm(ks[17], (N_MOE_FFN, N_EXPERTS, D_EXPERT, D_MODEL), D_EXPERT, beta)
    return {"x": x, "ret_w_in": ret_w_in, "ret_w_out": ret_w_out, "sb_w_kv": sb_w_kv,
            "sb_w_q": sb_w_q, "sb_w_out": sb_w_out, "ln_mix_g": ln_mix_g, "ln_mix_b": ln_mix_b,
            "ln_ffn_g": ln_ffn_g, "ln_ffn_b": ln_ffn_b, "ffn_w_in": ffn_w_in, "ffn_w_out": ffn_w_out,
            "moe_w_router": moe_w_router, "moe_w_in": moe_w_in, "moe_w_out": moe_w_out}


def reference(x, ret_w_in, ret_w_out, sb_w_kv, sb_w_q, sb_w_out, ln_mix_g, ln_mix_b,
              ln_ffn_g, ln_ffn_b, ffn_w_in, ffn_w_out, moe_w_router, moe_w_in, moe_w_out):
    shared_k = None
    shared_v = None
    for layer in range(DEPTH):
        if layer < N_A_LAYERS:
            mix = retention_mixer(x, ret_w_in[layer], ret_w_out[layer])
        else:
            j = layer - N_A_LAYERS
            mix = stick_breaking_mixer(x, shared_k, shared_v, sb_w_q[j], sb_w_out[j])
        x = layer_norm(DEEPNORM_ALPHA * x + mix, ln_mix_g[layer], ln_mix_b[layer])
        if layer % 2 == 0:
            f = swiglu(x, ffn_w_in[layer // 2], ffn_w_out[layer // 2])
        else:
            f = moe_swiglu(x, moe_w_router[layer // 2], moe_w_in[layer // 2], moe_w_out[layer // 2])
        x = layer_norm(DEEPNORM_ALPHA * x + f, ln_ffn_g[layer], ln_ffn_b[layer])
        if layer == N_A_LAYERS - 1:
            shared_k, shared_v = shared_kv(x, sb_w_kv)
    return x
```

```python
import contextlib
import numpy as np
import ml_dtypes
import concourse.bass as bass
import concourse.mybir as mybir
from concourse.bass_utils import run_bass_kernel_spmd

F32 = mybir.dt.float32
BF16 = mybir.dt.bfloat16
AF = mybir.ActivationFunctionType
ALU = mybir.AluOpType

ENGS = ("pe", "act", "dve", "pool", "sp")

D = 1024
TOK = 2048
GRP = 1024
GT = GRP // 128
ALPHA = float((2.0 * 2) ** 0.25)
EPS = 1e-5
RH = 4
D_FF = 2816
N_EXP = 8
D_EXP = 3584


class Buf:
    __slots__ = ("name", "w", "r", "key")

    def __init__(self, name="", key=None):
        self.name = name
        self.w = None
        self.r = []
        self.key = key


class Op:
    __slots__ = ("eng", "fn", "deps", "kind", "owner", "dval", "ms", "ndma")

    def __init__(self, eng, fn, deps, kind, owner=None, dval=0, ndma=0):
        self.eng = eng
        self.fn = fn
        self.deps = deps
        self.kind = kind
        self.owner = owner
        self.dval = dval
        self.ms = None
        self.ndma = ndma


class Prog:
    def __init__(self):
        self.ops = []
        self.by_eng = {e: [] for e in ENGS}
        self.ndsem = 0
        self.fence = set()
        self.dstate = {}
        self.regions = []
        self.cur_region = None
        self.regs = {}

    def bufs(self, n, name=""):
        return [Buf(f"{name}{i}") for i in range(n)]

    def _deps(self, eng, reads, writes, oid):
        deps = set()
        for b in reads:
            if b.w is not None:
                deps.add(b.w)
        for b in writes:
            if b.w is not None:
                deps.add(b.w)
            deps.update(b.r)
        for b in reads:
            b.r.append(oid)
        for b in writes:
            b.w = oid
            b.r = []
        deps |= self.fence
        deps.discard(oid)
        if eng == "pe":
            deps = {d for d in deps if not (self.ops[d].eng == "pe" and self.ops[d].kind == "c")}
        return deps

    def add(self, eng, fn, reads=(), writes=()):
        oid = len(self.ops)
        deps = self._deps(eng, list(reads), list(writes), oid)
        self.ops.append(Op(eng, fn, deps, "c"))
        self.by_eng[eng].append(oid)
        return oid

    def dma(self, eng, fn, owner, reads=(), writes=(), n=1):
        assert self.cur_region is None, "no DMA inside dynamic regions"
        oid = len(self.ops)
        deps = self._deps(eng, list(reads), list(writes), oid)
        key = owner.key if owner.key is not None else id(owner)
        stt = self.dstate.get(key)
        if stt is None:
            stt = [self.ndsem, 0, None]
            self.ndsem += 1
            self.dstate[key] = stt
            self._keep = getattr(self, "_keep", [])
            self._keep.append(owner)
        if stt[2] is not None:
            deps.add(stt[2])
        stt[1] += n
        stt[2] = oid
        self.ops.append(Op(eng, fn, deps, "d", owner=stt[0], dval=16 * stt[1], ndma=n))
        self.by_eng[eng].append(oid)
        return oid

    def region_begin(self, reg_key, thr):
        rid = len(self.regions)
        self.regions.append((reg_key, thr))
        for e in ENGS:
            oid = len(self.ops)
            self.ops.append(Op(e, rid, set(), "rb"))
            self.by_eng[e].append(oid)
        self.cur_region = rid

    def region_end(self):
        for e in ENGS:
            oid = len(self.ops)
            self.ops.append(Op(e, self.cur_region, set(), "re"))
            self.by_eng[e].append(oid)
        self.cur_region = None

    def reg_load(self, eng, reg_key, ap, reads):
        def fn(h, eng=eng):
            r = self.regs.setdefault((eng, reg_key), None)
            if r is None:
                r = h.alloc_register(f"r_{reg_key}_{eng}")
                self.regs[(eng, reg_key)] = r
            return h.reg_load(r, ap)
        return self.add(eng, fn, reads=reads)

    def barrier(self):
        f = set()
        for e in ENGS:
            for oid in reversed(self.by_eng[e]):
                if self.ops[oid].kind in ("c", "d"):
                    f.add(oid)
                    break
        f.update(v[2] for v in self.dstate.values() if v[2] is not None)
        self.fence = f

    def emit(self, nc):
        ops = self.ops
        needed = set()
        for op in ops:
            for d in op.deps:
                if ops[d].kind == "c":
                    needed.add(d)
        cnt = {e: 0 for e in ENGS}
        rinfo = {}
        for e in ENGS:
            cur = None
            for oid in self.by_eng[e]:
                op = ops[oid]
                if op.kind == "rb":
                    cur = [0, cnt[e], 0]
                    rinfo[(e, op.fn)] = cur
                elif op.kind == "re":
                    cur[2] = cnt[e] - cur[1]
                    cur = None
                else:
                    if cur is not None:
                        cur[0] += 1
                    if op.kind == "c" and oid in needed:
                        cnt[e] += 1
                        op.ms = cnt[e]
        with contextlib.ExitStack() as st:
            esem = {e: st.enter_context(nc.semaphore(f"s_{e}")) for e in ENGS}
            dsem = [st.enter_context(nc.semaphore(f"d_{i}")) for i in range(self.ndsem)]
            block = st.enter_context(nc.Block())

            def token(d):
                o = ops[d]
                if o.kind == "c":
                    return ("e", o.eng), o.ms
                return ("d", o.owner), o.dval

            def run(e, h):
                seen = {}
                snap = None
                guard = None
                for oid in self.by_eng[e]:
                    op = ops[oid]
                    if op.kind == "rb":
                        info = rinfo[(e, op.fn)]
                        if info[0] == 0:
                            continue
                        reg_key, thr = self.regions[op.fn]
                        r = self.regs[(e, reg_key)]
                        g1 = h.If_lt(r, thr + 1)
                        g1.__enter__()
                        if info[2] > 0:
                            if info[1] > 0:
                                h.wait_ge(esem[e], info[1])
                            h.sem_inc(esem[e], info[2])
                        g1.__exit__(None, None, None)
                        guard = h.Else()
                        guard.__enter__()
                        snap = dict(seen)
                        continue
                    if op.kind == "re":
                        if rinfo[(e, op.fn)][0] == 0:
                            continue
                        guard.__exit__(None, None, None)
                        guard = None
                        seen = snap
                        snap = None
                        continue
                    want = {}
                    for d in op.deps:
                        k, v = token(d)
                        if v > want.get(k, 0):
                            want[k] = v
                    for k, v in want.items():
                        if seen.get(k, 0) >= v:
                            continue
                        seen[k] = v
                        s = esem[k[1]] if k[0] == "e" else dsem[k[1]]
                        h.wait_ge(s, v)
                    r = op.fn(h)
                    if op.kind == "c":
                        if op.ms is not None:
                            r.then_inc(esem[e], 1)
                    else:
                        if not isinstance(r, (list, tuple)):
                            r = [r]
                        assert len(r) == op.ndma, (len(r), op.ndma)
                        for ins in r:
                            ins.then_inc(dsem[op.owner], 16)

            block.tensor(lambda h: run("pe", h))
            block.scalar(lambda h: run("act", h))
            block.vector(lambda h: run("dve", h))
            block.gpsimd(lambda h: run("pool", h))
            block.sync(lambda h: run("sp", h))
        return cnt


class Rot:
    def __init__(self, items):
        self.items = items
        self.i = 0

    def next(self):
        it = self.items[self.i % len(self.items)]
        self.i += 1
        return it


def f_mm(out, pairs):
    def fn(h):
        n = len(pairs)
        ins = None
        for i, (l, r) in enumerate(pairs):
            ins = h.matmul(out, lhsT=l, rhs=r, start=(i == 0), stop=(i == n - 1))
        return ins
    return fn


def f_mm_multi(groups):
    def fn(h):
        ins = None
        for out, pairs in groups:
            n = len(pairs)
            for i, (l, r) in enumerate(pairs):
                ins = h.matmul(out, lhsT=l, rhs=r, start=(i == 0), stop=(i == n - 1))
        return ins
    return fn


def f_tr(items, ident):
    def fn(h):
        ins = None
        for o, i in items:
            ins = h.transpose(out=o, in_=i, identity=ident)
        return ins
    return fn


def f_act(out, in_, func, bias=None, scale=None):
    kw = {}
    if bias is not None:
        kw["bias"] = bias
    if scale is not None:
        kw["scale"] = scale
    return lambda h: h.activation(out=out, in_=in_, func=func, **kw)


def f_tt(out, in0, in1, op):
    return lambda h: h.tensor_tensor(out=out, in0=in0, in1=in1, op=op)


def f_ts(out, in0, s1, s2, op0, op1=None):
    if op1 is None:
        return lambda h: h.tensor_scalar(out=out, in0=in0, scalar1=s1, scalar2=None, op0=op0)
    return lambda h: h.tensor_scalar(out=out, in0=in0, scalar1=s1, scalar2=s2, op0=op0, op1=op1)


def f_stt(out, in0, scalar, in1, op0, op1):
    return lambda h: h.scalar_tensor_tensor(out=out, in0=in0, scalar=scalar, in1=in1, op0=op0, op1=op1)


def f_copy(out, in_):
    return lambda h: h.tensor_copy(out=out, in_=in_)


def f_dma(pairs):
    return lambda h: [h.dma_start(out=o, in_=i) for o, i in pairs]


def f_memset(ap, v):
    return lambda h: h.memset(ap, v)


class Ctx:
    def __init__(self, nc):
        self.nc = nc
        self.P = Prog()
        self.root = contextlib.ExitStack()
        self.stack = self.root
        self.uid = 0

    def sb(self, shape, dt, name=None, stack=None):
        self.uid += 1
        name = (name or "t") + f"_{self.uid}"
        return (stack or self.stack).enter_context(self.nc.sbuf_tensor(name, shape, dt))

    def ps(self, shape, dt, name=None):
        self.uid += 1
        name = (name or "p") + f"_{self.uid}"
        return self.root.enter_context(self.nc.psum_tensor(name, shape, dt))

    def dram(self, name, shape, dt, kind):
        return self.nc.dram_tensor(name, shape, dt, kind=kind).ap()

    def rot(self, n, shape, dt, name, stack=None):
        return Rot([(self.sb(shape, dt, name, stack), Buf(name, key=f"{name}{i}")) for i in range(n)])


def setup_common(c, ident_d):
    P = c.P
    c.pm = Rot([(c.ps([128, 512], F32, "pm"), Buf("pm")) for _ in range(6)])
    c.pt = Rot([(c.ps([128, 1024], BF16, "pt"), Buf("pt")) for _ in range(2)])
    c.idf = c.sb([128, 128], F32, "idf")
    c.idb = c.sb([128, 128], BF16, "idb")
    c.b_idf = Buf("idf")
    c.b_idb = Buf("idb")
    P.dma("sp", f_dma([(c.idf[:], ident_d)]), c.b_idf, writes=[c.b_idf])
    P.add("dve", f_copy(c.idb[:], c.idf[:]), reads=[c.b_idf], writes=[c.b_idb])
    c.stage = None
    c.xb = c.rot(2, [128, 1024], BF16, "xb")
    c.stat = c.rot(4, [128, 16], F32, "stat")
    c.eps_t = c.sb([128, 2], F32, "eps")
    c.b_eps = Buf("eps")
    P.add("dve", f_memset(c.eps_t[:], EPS), writes=[c.b_eps])


def load_const(c, dram_ap, shape, dt=F32, name="const", eng="sp"):
    t = c.sb(shape, dt, name)
    b = Buf(name)
    full = t[:] if len(shape) == 2 else t[:]
    c.P.dma(eng, f_dma([(full, dram_ap)]), b, writes=[b])
    return t, b


def tile_to_T(c, src_ap, src_buf, dstT, dst_bufs, t, nchunk=8):
    P = c.P
    xb, b_xb = c.xb.next()
    P.add("act", f_act(xb[:, 0:nchunk * 128], src_ap, AF.Copy), reads=[src_buf], writes=[b_xb])
    pt, b_pt = c.pt.next()
    ptv = pt[:].rearrange("p (c n) -> p c n", n=128)
    P.add("pe", f_tr([(ptv[:, ch, :], xb[:, ch * 128:(ch + 1) * 128]) for ch in range(nchunk)], c.idb[:]),
          reads=[b_xb, c.b_idb], writes=[b_pt])
    P.add("dve", f_copy(dstT[:, 0:nchunk, t * 128:(t + 1) * 128], ptv[:, 0:nchunk, :]),
          reads=[b_pt], writes=[dst_bufs[t]])


def load_w(c, w_ap, slot_view, buf, nsplit=1):
    src = w_ap.rearrange("(c p) n -> p c n", p=128)
    kc = src.shape[1]
    if nsplit == 1 or kc < nsplit:
        c.P.dma("pool", f_dma([(slot_view, src)]), buf, writes=[buf])
    else:
        step = (kc + nsplit - 1) // nsplit
        pairs = [(slot_view[:, i:min(i + step, kc), :], src[:, i:min(i + step, kc), :]) for i in range(0, kc, step)]
        c.P.dma("pool", f_dma(pairs), buf, writes=[buf], n=len(pairs))


def ln_inplace(c, x_ap, x_buf, gam, bet, b_gb, tmp_rot):
    P = c.P
    st, b_st = c.stat.next()
    P.add("dve", lambda h: h.bn_stats(out=st[:, 0:6], in_=x_ap[:, 0:512]), reads=[x_buf], writes=[b_st])
    P.add("dve", lambda h: h.bn_stats(out=st[:, 6:12], in_=x_ap[:, 512:1024]), reads=[x_buf, b_st], writes=[b_st])
    mv, b_mv = c.stat.next()
    P.add("dve", lambda h: h.bn_aggr(out=mv[:, 0:2], in_=st[:, 0:12]), reads=[b_st], writes=[b_mv])
    P.add("act", f_act(mv[:, 2:3], mv[:, 1:2], AF.Sqrt, bias=c.eps_t[:, 0:1]), reads=[b_mv, c.b_eps], writes=[b_mv])
    P.add("dve", lambda h: h.reciprocal(out=mv[:, 2:3], in_=mv[:, 2:3]), reads=[b_mv], writes=[b_mv])
    P.add("dve", f_stt(mv[:, 3:4], mv[:, 0:1], -1.0, mv[:, 2:3], ALU.mult, ALU.mult), reads=[b_mv], writes=[b_mv])
    tmp, b_tmp = tmp_rot.next()
    P.add("act", f_act(tmp[:], x_ap, AF.Identity, bias=mv[:, 3:4], scale=mv[:, 2:3]),
          reads=[x_buf, b_mv], writes=[b_tmp])
    P.add("dve", f_tt(tmp[:], tmp[:], gam[:], ALU.mult), reads=[b_tmp, b_gb], writes=[b_tmp])
    P.add("dve", f_tt(x_ap, tmp[:], bet[:], ALU.add), reads=[b_tmp, b_gb], writes=[x_buf])


def ffn_unit(c, st, xT, xT_bufs, w1, dh, w2, yacc, yacc_bufs, subs, gate=None):
    P = c.P
    maxsub = max(n for _, n in subs)
    actT = c.sb([128, maxsub, GRP], BF16, "actT", st)
    act_bufs = [[Buf("act") for _ in range(GT)] for _ in range(maxsub)]
    w1s = c.rot(2, [128, 8, 2, 256], BF16, "w1s", st)
    w2s = c.rot(2, [128, maxsub, 512], BF16, "w2s", st)
    sg = c.rot(2, [128, 512], BF16, "sg", st)
    for (j0, nj) in subs:
        assert nj % 2 == 0
        for blk in range(nj // 2):
            jj = j0 + blk * 2
            ws, b_ws = w1s.next()
            srcg = w1[:, jj * 128: jj * 128 + 256].rearrange("(c p) n -> p c n", p=128)
            srcu = w1[:, dh + jj * 128: dh + jj * 128 + 256].rearrange("(c p) n -> p c n", p=128)
            P.dma("pool", f_dma([(ws[:, :, 0, :], srcg), (ws[:, :, 1, :], srcu)]), b_ws, writes=[b_ws], n=2)
            for ci in range(2):
                jl = blk * 2 + ci
                for tg in range(GRP // 512):
                    pg, b_pg = c.pm.next()
                    pu, b_pu = c.pm.next()
                    tb = xT_bufs[tg * 4:(tg + 1) * 4]
                    P.add("pe", f_mm_multi([
                        (pg[:], [(ws[:, k, 0, ci * 128:(ci + 1) * 128], xT[:, k, tg * 512:(tg + 1) * 512]) for k in range(8)]),
                        (pu[:], [(ws[:, k, 1, ci * 128:(ci + 1) * 128], xT[:, k, tg * 512:(tg + 1) * 512]) for k in range(8)]),
                    ]), reads=[b_ws] + tb, writes=[b_pg, b_pu])
                    s, b_s = sg.next()
                    P.add("act", f_act(s[:], pg[:], AF.Silu), reads=[b_pg], writes=[b_s])
                    P.add("dve", f_tt(actT[:, jl, tg * 512:(tg + 1) * 512], s[:], pu[:], ALU.mult),
                          reads=[b_s, b_pu], writes=act_bufs[jl][tg * 4:(tg + 1) * 4])
        for half in range(2):
            w2v, b_w2 = w2s.next()
            load_w(c, w2[j0 * 128:(j0 + nj) * 128, half * 512:(half + 1) * 512], w2v[:, 0:nj, :], b_w2, nsplit=2)
            for t in range(GT):
                po, b_po = c.pm.next()
                P.add("pe", f_mm(po[:], [(actT[:, j, t * 128:(t + 1) * 128], w2v[:, j, :]) for j in range(nj)]),
                      reads=[b_w2] + [act_bufs[j][t] for j in range(nj)], writes=[b_po])
                ya = yacc[:, t, half * 512:(half + 1) * 512]
                if gate is None:
                    P.add("dve", f_tt(ya, po[:], ya, ALU.add), reads=[b_po, yacc_bufs[t]], writes=[yacc_bufs[t]])
                else:
                    gfn, b_g = gate
                    P.add("dve", f_stt(ya, po[:], gfn(t), ya, ALU.mult, ALU.add),
                          reads=[b_po, yacc_bufs[t], b_g], writes=[yacc_bufs[t]])


def rope_tables(pos0, n):
    d = 256
    inv = (1.0 / (np.float32(10000.0) ** (np.arange(0, d, 2, dtype=np.float32) / np.float32(d)))).astype(np.float32)
    pos = np.arange(pos0, pos0 + n, dtype=np.float32)
    ang = (pos[:, None] * inv[None, :]).astype(np.float32)
    return np.ascontiguousarray(np.cos(ang).astype(np.float32).T), np.ascontiguousarray(np.sin(ang).astype(np.float32).T)


def retention_tables():
    gam = 1.0 - 2.0 ** (-5.0 - np.arange(RH, dtype=np.float64))
    n = np.arange(128)
    m = np.arange(128)
    dtab = np.zeros((128, RH, 128), np.float64)
    cn = np.zeros((128, RH), np.float64)
    kdec = np.zeros((128, RH), np.float64)
    for h in range(RH):
        g = gam[h]
        M, N = np.meshgrid(m, n, indexing="ij")
        same = (M // 64) == (N // 64)
        cross = (M < 64) & (N >= 64)
        dm = np.where(same, g ** np.abs(N - M), np.where(cross, g ** (N - M).clip(0), 0.0))
        cnh = g ** (n + 1.0)
        dtab[:, h, :] = dm / cnh[None, :] / 16.0
        cn[:, h] = cnh
        kdec[:, h] = g ** (127.0 - m) / 16.0
    cd = [float(g ** 128) for g in gam]
    return dtab.astype(np.float32), cn.astype(np.float32), kdec.astype(np.float32), cd


def load_ln(c, st, gd, bd, name):
    g = c.sb([128, 1024], F32, name + "g", st)
    b = c.sb([128, 1024], F32, name + "b", st)
    buf = Buf(name, key=name)
    c.P.dma("sp", f_dma([(g[:], gd), (b[:], bd)]), buf, writes=[buf], n=2)
    return g, b, buf


def store(c, io, dst_ap, src_ap, src_buf):
    ob = Buf("out")
    io["outs"].append(ob)
    c.P.dma("sp", f_dma([(dst_ap, src_ap)]), src_buf, reads=[src_buf], writes=[ob])


def ret_heads(c, rst, io, mode, g, xsrc, cosd, sind, S32, Sb, b_S32, b_Sb, consts, yT, b_yT, scratch=None):
    P = c.P
    own = mode == "own"
    dtab, b_dtab, cn, b_cn, kdec, b_kdec, cd = consts
    w_in = io["ret_w_in"]
    if scratch is not None:
        sb16 = scratch.bitcast(BF16)
        xT = sb16[:, 0:8192].rearrange("p (c n) -> p c n", c=8)
        qT = sb16[:, 8192:10240].rearrange("p (c n) -> p c n", c=2)
        kT = sb16[:, 10240:12288].rearrange("p (c n) -> p c n", c=2)
        cosT = scratch[:, 6144:7168]
        sinT = scratch[:, 7168:8192]
    else:
        xT = c.sb([128, 8, GRP], BF16, "xT", rst)
        cosT = c.sb([128, GRP], F32, "cosT", rst)
        sinT = c.sb([128, GRP], F32, "sinT", rst)
        qT = c.sb([128, 2, GRP], BF16, "qT", rst)
        kT = c.sb([128, 2, GRP], BF16, "kT", rst)
    b_xT = [Buf("xT") for _ in range(GT)]
    b_cs = Buf("cs", key="cs")
    P.dma("sp", f_dma([(cosT[:], cosd), (sinT[:], sind)]), b_cs, writes=[b_cs], n=2)
    for t in range(GT):
        stg, b_stg = c.stage.next()
        P.dma("sp", f_dma([(stg[:], xsrc[t * 128:(t + 1) * 128, :])]), b_stg, writes=[b_stg])
        tile_to_T(c, stg[:], b_stg, xT, b_xT, t)
    kd = c.sb([128, GT, 256], BF16, "kd", rst)
    Vh = c.sb([128, GT, 512], BF16, "Vh", rst)
    Gh = c.sb([128, GT, 512], BF16, "Gh", rst) if own else None
    wqk = c.rot(2, [128, 8, 256], BF16, "wqk", rst)
    wvg = c.rot(2, [128, 8, 512], BF16, "wvg", rst)
    rtmp = c.rot(4, [128, 512], F32, "rtmp", rst)
    PTr = c.rot(2, [128, 128], BF16, "PT", rst)
    osb = c.rot(3, [128, 512], F32, "osb", rst)
    ysb = c.rot(2, [128, 512], BF16, "ysb", rst)
    for h in range(RH):
        b_qT = [Buf("qT") for _ in range(GT)]
        b_kT = [Buf("kT") for _ in range(GT)]
        b_kd = [Buf("kd") for _ in range(GT)]
        b_V = [Buf("V") for _ in range(GT)]
        b_G = [Buf("G") for _ in range(GT)]
        for which in (("q", "k") if own else ("k",)):
            col0 = (0 if which == "q" else 1024) + h * 256
            ws, b_ws = wqk.next()
            load_w(c, w_in[:, col0:col0 + 256], ws[:], b_ws)
            dst = qT if which == "q" else kT
            b_dst = b_qT if which == "q" else b_kT
            for tg in range(GRP // 512):
                pa, b_pa = c.pm.next()
                pb, b_pb = c.pm.next()
                tb = b_xT[tg * 4:(tg + 1) * 4]
                tok = slice(tg * 512, (tg + 1) * 512)
                P.add("pe", f_mm_multi([
                    (pa[:], [(ws[:, k, 0:128], xT[:, k, tok]) for k in range(8)]),
                    (pb[:], [(ws[:, k, 128:256], xT[:, k, tok]) for k in range(8)]),
                ]), reads=[b_ws] + tb, writes=[b_pa, b_pb])
                t1, b_t1 = rtmp.next()
                t2, b_t2 = rtmp.next()
                P.add("dve", f_tt(t1[:], pa[:], cosT[:, tok], ALU.mult), reads=[b_pa, b_cs], writes=[b_t1])
                P.add("dve", f_tt(t2[:], pb[:], sinT[:, tok], ALU.mult), reads=[b_pb, b_cs], writes=[b_t2])
                P.add("dve", f_tt(dst[:, 0, tok], t1[:], t2[:], ALU.subtract), reads=[b_t1, b_t2],
                      writes=b_dst[tg * 4:(tg + 1) * 4])
                t3, b_t3 = rtmp.next()
                t4, b_t4 = rtmp.next()
                P.add("dve", f_tt(t3[:], pa[:], sinT[:, tok], ALU.mult), reads=[b_pa, b_cs], writes=[b_t3])
                P.add("dve", f_tt(t4[:], pb[:], cosT[:, tok], ALU.mult), reads=[b_pb, b_cs], writes=[b_t4])
                P.add("dve", f_tt(dst[:, 1, tok], t3[:], t4[:], ALU.add), reads=[b_t3, b_t4] + b_dst[tg * 4:(tg + 1) * 4],
                      writes=b_dst[tg * 4:(tg + 1) * 4])
        for t in range(GT):
            pt, b_pt = c.pt.next()
            ptv = pt[:].rearrange("p (c n) -> p c n", n=128)
            P.add("pe", f_tr([(ptv[:, ch, :], kT[:, ch, t * 128:(t + 1) * 128]) for ch in range(2)], c.idb[:]),
                  reads=[b_kT[t], c.b_idb], writes=[b_pt])
            P.add("act", f_act(kd[:, t, :], pt[:, 0:256], AF.Identity, scale=kdec[:, h:h + 1]),
                  reads=[b_pt, b_kdec], writes=[b_kd[t]])
        for which in (("v", "g") if own else ("v",)):
            col0 = (2048 if which == "v" else 4096) + h * 512
            ws, b_ws = wvg.next()
            load_w(c, w_in[:, col0:col0 + 512], ws[:], b_ws, nsplit=2)
            for t in range(GT):
                pv, b_pv = c.pm.next()
                P.add("pe", f_mm(pv[:], [(xT[:, k, t * 128:(t + 1) * 128], ws[:, k, :]) for k in range(8)]),
                      reads=[b_ws, b_xT[t]], writes=[b_pv])
                if which == "v":
                    P.add("act", f_act(Vh[:, t, :], pv[:], AF.Copy), reads=[b_pv], writes=[b_V[t]])
                else:
                    P.add("act", f_act(Gh[:, t, :], pv[:], AF.Silu), reads=[b_pv], writes=[b_G[t]])
        pend = None

        def epilogue(o, b_o, t):
            tk = slice(t * 128, (t + 1) * 128)
            st, b_st = c.stat.next()
            P.add("dve", lambda hh, st=st, o=o: hh.bn_stats(out=st[:, 0:6], in_=o[:]), reads=[b_o], writes=[b_st])
            P.add("dve", lambda hh, st=st: hh.bn_aggr(out=st[:, 8:10], in_=st[:, 0:6]), reads=[b_st], writes=[b_st])
            P.add("act", f_act(st[:, 10:11], st[:, 9:10], AF.Sqrt, bias=c.eps_t[:, 0:1]), reads=[b_st, c.b_eps], writes=[b_st])
            P.add("dve", lambda hh, st=st: hh.reciprocal(out=st[:, 10:11], in_=st[:, 10:11]), reads=[b_st], writes=[b_st])
            P.add("dve", f_stt(st[:, 11:12], st[:, 8:9], -1.0, st[:, 10:11], ALU.mult, ALU.mult), reads=[b_st], writes=[b_st])
            P.add("act", f_act(o[:], o[:], AF.Identity, bias=st[:, 11:12], scale=st[:, 10:11]),
                  reads=[b_o, b_st], writes=[b_o])
            y, b_y = ysb.next()
            P.add("dve", f_tt(y[:], o[:], Gh[:, t, :], ALU.mult), reads=[b_o, b_G[t]], writes=[b_y])
            pt, b_pt = c.pt.next()
            ptv = pt[:].rearrange("p (c n) -> p c n", n=128)
            P.add("pe", f_tr([(ptv[:, ch, :], y[:, ch * 128:(ch + 1) * 128]) for ch in range(4)], c.idb[:]),
                  reads=[b_y, c.b_idb], writes=[b_pt])
            P.add("act", f_act(yT[:, h * 4:(h + 1) * 4, tk], ptv[:, 0:4, :], AF.Copy), reads=[b_pt], writes=[b_yT[h][t]])

        for t in range(GT):
            tk = slice(t * 128, (t + 1) * 128)
            p0, b_p0 = c.pm.next()
            p1, b_p1 = c.pm.next()
            P.add("pe", f_mm_multi([(p0[:], [(kd[:, t, 0:128], Vh[:, t, :])]), (p1[:], [(kd[:, t, 128:256], Vh[:, t, :])])]),
                  reads=[b_kd[t], b_V[t]], writes=[b_p0, b_p1])
            if own:
                pS, b_pS = c.pm.next()
                P.add("pe", f_mm(pS[:, 0:128], [(kT[:, ch, tk], qT[:, ch, tk]) for ch in range(2)]),
                      reads=[b_kT[t], b_qT[t]], writes=[b_pS])
                PT, b_PT = PTr.next()
                P.add("dve", f_tt(PT[:], pS[:, 0:128], dtab[:, h, :], ALU.mult), reads=[b_pS, b_dtab], writes=[b_PT])
                po, b_po = c.pm.next()
                P.add("pe", f_mm(po[:], [(PT[:], Vh[:, t, :]), (qT[:, 0, tk], Sb[:, h, 0, :]), (qT[:, 1, tk], Sb[:, h, 1, :])]),
                      reads=[b_PT, b_V[t], b_qT[t], b_Sb[h]], writes=[b_po])
                o, b_o = osb.next()
                P.add("act", f_act(o[:], po[:], AF.Identity, scale=cn[:, h:h + 1]), reads=[b_po, b_cn], writes=[b_o])
            P.add("dve", f_stt(S32[:, h, 0, :], S32[:, h, 0, :], cd[h], p0[:], ALU.mult, ALU.add),
                  reads=[b_p0, b_S32[h]], writes=[b_S32[h]])
            P.add("dve", f_stt(S32[:, h, 1, :], S32[:, h, 1, :], cd[h], p1[:], ALU.mult, ALU.add),
                  reads=[b_p1, b_S32[h]], writes=[b_S32[h]])
            P.add("act", f_act(Sb[:, h, :, :], S32[:, h, :, :], AF.Copy), reads=[b_S32[h]], writes=[b_Sb[h]])
            if own:
                if pend is not None:
                    epilogue(*pend)
                pend = (o, b_o, t)
        if own and pend is not None:
            epilogue(*pend)


def build_layer0(c, io):
    P = c.P
    _, _, _, cd = retention_tables()
    dtab, b_dtab = load_const(c, io["dtab"], [128, RH, 128], F32, "dtab")
    cn, b_cn = load_const(c, io["cn"], [128, RH], F32, "cn")
    kdec, b_kdec = load_const(c, io["kdec"], [128, RH], F32, "kdec")
    consts = (dtab, b_dtab, cn, b_cn, kdec, b_kdec, cd)
    S32 = c.sb([128, RH, 2, 512], F32, "S32")
    Sb = c.sb([128, RH, 2, 512], BF16, "Sb")
    b_S32 = [Buf("S32") for _ in range(RH)]
    b_Sb = [Buf("Sb") for _ in range(RH)]
    for h in range(RH):
        P.add("dve", f_memset(S32[:, h, :, :], 0.0), writes=[b_S32[h]])
        P.add("dve", f_memset(Sb[:, h, :, :], 0.0), writes=[b_Sb[h]])
    w_out = io["ret_w_out"]

    passes = [("prev", 0), ("prev", 1), ("own", 0), ("own", 1)]
    for mode, g in passes:
        own = mode == "own"
        xsrc = (io["x_own"] if own else io["x_prev"])[g * GRP:(g + 1) * GRP, :]
        cosd = (io["cos_own"] if own else io["cos_prev"])[:, g * GRP:(g + 1) * GRP]
        sind = (io["sin_own"] if own else io["sin_prev"])[:, g * GRP:(g + 1) * GRP]
        P.barrier()
        if not own:
            with contextlib.ExitStack() as rst:
                ret_heads(c, rst, io, mode, g, xsrc, cosd, sind, S32, Sb, b_S32, b_Sb, consts, None, None)
            continue
        with contextlib.ExitStack() as gst:
            x1 = c.sb([128, GT, 1024], F32, "x1", gst)
            b_x1 = [Buf("x1") for _ in range(GT)]
            with contextlib.ExitStack() as yst:
                yT = c.sb([128, 16, GRP], BF16, "yT", yst)
                b_yT = [[Buf("yT") for _ in range(GT)] for _ in range(RH)]
                with contextlib.ExitStack() as rst:
                    ret_heads(c, rst, io, mode, g, xsrc, cosd, sind, S32, Sb, b_S32, b_Sb, consts, yT, b_yT)
                P.barrier()
                with contextlib.ExitStack() as mst:
                    gam, bet, b_gb = load_ln(c, mst, io["g_mix"], io["b_mix"], "lnmix")
                    tmpf = c.rot(2, [128, 1024], F32, "tmpf", mst)
                    wos = c.rot(2, [128, 16, 512], BF16, "wos", mst)
                    xst = c.rot(2, [128, 512], F32, "xst", mst)
                    for half in range(2):
                        wo, b_wo = wos.next()
                        load_w(c, w_out[:, half * 512:(half + 1) * 512], wo[:], b_wo, nsplit=2)
                        for t in range(GT):
                            pm_, b_pm = c.pm.next()
                            P.add("pe", f_mm(pm_[:], [(yT[:, ch, t * 128:(t + 1) * 128], wo[:, ch, :]) for ch in range(16)]),
                                  reads=[b_wo] + [b_yT[h][t] for h in range(RH)], writes=[b_pm])
                            xs, b_xs = xst.next()
                            P.dma("sp", f_dma([(xs[:], xsrc[t * 128:(t + 1) * 128, half * 512:(half + 1) * 512])]), b_xs, writes=[b_xs])
                            P.add("dve", f_stt(x1[:, t, half * 512:(half + 1) * 512], xs[:], ALPHA, pm_[:], ALU.mult, ALU.add),
                                  reads=[b_xs, b_pm], writes=[b_x1[t]])
                    for t in range(GT):
                        ln_inplace(c, x1[:, t, :], b_x1[t], gam, bet, b_gb, tmpf)
            P.barrier()
            with contextlib.ExitStack() as fst:
                gam, bet, b_gb = load_ln(c, fst, io["g_ffn"], io["b_ffn"], "lnffn")
                tmpf = c.rot(2, [128, 1024], F32, "tmpf", fst)
                x1T = c.sb([128, 8, GRP], BF16, "x1T", fst)
                b_x1T = [Buf("x1T") for _ in range(GT)]
                yacc = c.sb([128, GT, 1024], F32, "yacc", fst)
                b_ya = [Buf("yacc", key=f"ya{t}") for t in range(GT)]
                for t in range(GT):
                    tile_to_T(c, x1[:, t, :], b_x1[t], x1T, b_x1T, t)
                    P.add("act", f_act(yacc[:, t, :], x1[:, t, :], AF.Copy, scale=ALPHA), reads=[b_x1[t]], writes=[b_ya[t]])
                with contextlib.ExitStack() as ust:
                    ffn_unit(c, ust, x1T, b_x1T, io["ffn_w_in"], D_FF, io["ffn_w_out"], yacc, b_ya, [(0, 12), (12, 10)])
                for t in range(GT):
                    ln_inplace(c, yacc[:, t, :], b_ya[t], gam, bet, b_gb, tmpf)
                    r0 = g * GRP + t * 128
                    store(c, io, io["x2"][r0:r0 + 128, :], yacc[:, t, :], b_ya[t])
                P.barrier()
                with contextlib.ExitStack() as kst:
                    x2T = x1T
                    b_x2T = [Buf("x2T") for _ in range(GT)]
                    for t in range(GT):
                        tile_to_T(c, yacc[:, t, :], b_ya[t], x2T, b_x2T, t)
                    wkv = c.rot(2, [128, 8, 512], BF16, "wkv", kst)
                    ksb = c.rot(3, [128, 512], BF16, "ksb", kst)
                    for cb in range(2):
                        ws, b_ws = wkv.next()
                        load_w(c, io["sb_w_kv"][:, cb * 512:(cb + 1) * 512], ws[:], b_ws, nsplit=2)
                        for ch in range(4):
                            for tg in range(GRP // 512):
                                pk, b_pk = c.pm.next()
                                tok = slice(tg * 512, (tg + 1) * 512)
                                P.add("pe", f_mm(pk[:], [(ws[:, k, ch * 128:(ch + 1) * 128], x2T[:, k, tok]) for k in range(8)]),
                                      reads=[b_ws] + b_x2T[tg * 4:(tg + 1) * 4], writes=[b_pk])
                                ks, b_ks = ksb.next()
                                P.add("act", f_act(ks[:], pk[:], AF.Copy), reads=[b_pk], writes=[b_ks])
                                f0 = (cb * 4 + ch) * 128
                                t0 = g * GRP + tg * 512
                                store(c, io, io["KT"][f0:f0 + 128, t0:t0 + 512], ks[:], b_ks)
                    for cb in range(2):
                        ws, b_ws = wkv.next()
                        load_w(c, io["sb_w_kv"][:, 1024 + cb * 512:1024 + (cb + 1) * 512], ws[:], b_ws, nsplit=2)
                        for t in range(GT):
                            pv, b_pv = c.pm.next()
                            P.add("pe", f_mm(pv[:], [(x2T[:, k, t * 128:(t + 1) * 128], ws[:, k, :]) for k in range(8)]),
                                  reads=[b_ws, b_x2T[t]], writes=[b_pv])
                            ks, b_ks = ksb.next()
                            P.add("act", f_act(ks[:], pv[:], AF.Copy), reads=[b_pv], writes=[b_ks])
                            r0 = g * GRP + t * 128
                            store(c, io, io["V"][r0:r0 + 128, cb * 512:(cb + 1) * 512], ks[:], b_ks)


def make_nc_layer0():
    nc = bass.Bass("TRN2", target_bir_lowering=False)
    c = Ctx(nc)
    io = {"outs": []}

    def din(name, shape, dt=F32):
        io[name] = c.dram(name, shape, dt, "ExternalInput")
    din("x_own", [TOK, D]); din("x_prev", [TOK, D])
    din("cos_own", [128, TOK]); din("sin_own", [128, TOK]); din("cos_prev", [128, TOK]); din("sin_prev", [128, TOK])
    din("ret_w_in", [D, 6144]); din("ret_w_out", [2048, D])
    din("ffn_w_in", [D, 2 * D_FF]); din("ffn_w_out", [D_FF, D]); din("sb_w_kv", [D, 2048])
    din("g_mix", [128, D]); din("b_mix", [128, D]); din("g_ffn", [128, D]); din("b_ffn", [128, D])
    din("dtab", [128, RH, 128]); din("cn", [128, RH]); din("kdec", [128, RH]); din("ident", [128, 128])
    io["x2"] = c.dram("x2", [TOK, D], F32, "ExternalOutput")
    io["KT"] = c.dram("KT", [D, TOK], BF16, "ExternalOutput")
    io["V"] = c.dram("V", [TOK, D], BF16, "ExternalOutput")
    with c.root:
        setup_common(c, io["ident"])
        build_layer0(c, io)
        c.P.add("sp", lambda h: None, reads=io["outs"])
        cnt = c.P.emit(nc)
    return nc, cnt


def layer0_inputs(inputs, core):
    b, half = core // 2, core % 2
    x = inputs["x"]
    dtab, cn, kdec, _ = retention_tables()
    co, so = rope_tables(half * TOK, TOK)
    cp, sp_ = rope_tables(0, TOK)
    f = np.float32
    bc = lambda v: np.ascontiguousarray(np.broadcast_to(np.asarray(v, f)[None, :], (128, D)))
    return {
        "x_own": np.ascontiguousarray(x[b, half * TOK:(half + 1) * TOK]),
        "x_prev": np.ascontiguousarray(x[b, 0:TOK]) if half == 1 else np.zeros((TOK, D), f),
        "cos_own": co, "sin_own": so, "cos_prev": cp, "sin_prev": sp_,
        "ret_w_in": np.ascontiguousarray(inputs["ret_w_in"][0]),
        "ret_w_out": np.ascontiguousarray(inputs["ret_w_out"][0]),
        "ffn_w_in": np.ascontiguousarray(inputs["ffn_w_in"][0]),
        "ffn_w_out": np.ascontiguousarray(inputs["ffn_w_out"][0]),
        "sb_w_kv": np.ascontiguousarray(inputs["sb_w_kv"]),
        "g_mix": bc(inputs["ln_mix_g"][0]), "b_mix": bc(inputs["ln_mix_b"][0]),
        "g_ffn": bc(inputs["ln_ffn_g"][0]), "b_ffn": bc(inputs["ln_ffn_b"][0]),
        "dtab": dtab, "cn": cn, "kdec": kdec, "ident": np.eye(128, dtype=f),
    }


NEG = -30000.0


def attn_tables():
    s = np.arange(128)[:, None]
    t = np.arange(512)[None, :]
    m01 = np.zeros((128, 4, 512), np.float32)
    for di in range(4):
        m01[:, di, :] = (di * 128 + s < t).astype(np.float32)
    negm = (1.0 - m01) * NEG
    j = np.arange(128)[:, None]
    ss = np.arange(128)[None, :]
    m1 = -(j >= ss).astype(np.float32)
    negones = -np.ones((128, 128), np.float32)
    return m01, negm, m1, negones


def f_mm1(out, l, r, start, stop):
    return lambda h: h.matmul(out, lhsT=l, rhs=r, start=start, stop=stop)


def attention_group(c, ast, io, g, oT, b_oT, x2T, b_x2T, consts):
    P = c.P
    m01, b_m01, negm, b_negm, M1, b_M1, NO, b_NO = consts
    nslot_blocks = 16 + g * 8 + 8
    KTr = c.rot(2, [128, 4096], BF16, "KTh", ast)
    Vr = c.rot(2, [128, 32, 128], BF16, "Vhp", ast)
    QTr = c.rot(2, [128, GRP], BF16, "QTh", ast)
    wqr = c.rot(2, [128, 8, 128], BF16, "wq", ast)
    esb = c.rot(4, [128, 512], F32, "esb", ast)
    spb = c.rot(8, [128, 512], BF16, "spb", ast)
    wbf = c.rot(4, [128, 512], BF16, "wbf", ast)
    Rbf = [c.rot(2, [128, 512], BF16, f"Rbf{i}", ast) for i in range(2)]
    accs = c.pt.items
    for hp in range(8):
        KT, b_KT = KTr.next()
        V, b_V = Vr.next()
        ns = nslot_blocks * 128
        P.dma("sp", f_dma([(KT[:, 0:ns], io["KTs"][hp * 128:(hp + 1) * 128, 0:ns])]), b_KT, writes=[b_KT])
        P.dma("sp", f_dma([(V[:, 0:nslot_blocks, :],
                            io["Vs"][0:ns, hp * 128:(hp + 1) * 128].rearrange("(k p) n -> p k n", p=128))]),
              b_V, writes=[b_V])
        wq, b_wq = wqr.next()
        load_w(c, io["sb_w_q"][:, hp * 128:(hp + 1) * 128], wq[:], b_wq)
        QT, b_QT = QTr.next()
        for tg in range(GRP // 512):
            pq, b_pq = c.pm.next()
            tok = slice(tg * 512, (tg + 1) * 512)
            P.add("pe", f_mm(pq[:], [(wq[:, k, :], x2T[:, k, tok]) for k in range(8)]),
                  reads=[b_wq] + b_x2T[tg * 4:(tg + 1) * 4], writes=[b_pq])
            P.add("act", f_act(QT[:, tok], pq[:], AF.Copy, scale=0.125), reads=[b_pq], writes=[b_QT])
        for qt in range(GRP // 512):
            nkb = 16 + g * 8 + qt * 4 + 4
            order = list(reversed(range(nkb)))
            n = len(order)
            tok = slice(qt * 512, (qt + 1) * 512)
            st = {}

            def stage1(i, hd):
                kb = order[i]
                pb = 64 * hd
                di = kb - (nkb - 4)
                pz, b_pz = c.pm.next()
                P.add("pe", f_mm(pz[:], [(KT[pb:pb + 64, kb * 128:(kb + 1) * 128], QT[pb:pb + 64, tok])]),
                      reads=[b_KT, b_QT], writes=[b_pz])
                e, b_e = esb.next()
                P.add("act", f_act(e[:], pz[:], AF.Exp), reads=[b_pz], writes=[b_e])
                sp_, b_sp = spb.next()
                P.add("act", f_act(sp_[:], e[:], AF.Ln, bias=1.0), reads=[b_e], writes=[b_sp])
                if di >= 0:
                    P.add("dve", f_tt(sp_[:], sp_[:], m01[:, di, :], ALU.mult), reads=[b_sp, b_m01], writes=[b_sp])
                st[(i, hd)] = (sp_, b_sp)

            def stage2(i, hd):
                kb = order[i]
                pb = 64 * hd
                di = kb - (nkb - 4)
                sp_, b_sp = st[(i, hd)]
                pE, b_pE = c.pm.next()
                pairs = [(KT[pb:pb + 64, kb * 128:(kb + 1) * 128], QT[pb:pb + 64, tok]), (M1[:], sp_[:])]
                reads = [b_KT, b_QT, b_M1, b_sp]
                if i > 0:
                    rb, b_rb = st[("R", i, hd)]
                    pairs.append((NO[:], rb[:]))
                    reads += [b_NO, b_rb]
                if di >= 0:
                    pairs.append((c.idb[:], negm[:, di, :]))
                    reads += [c.b_idb, b_negm]
                P.add("pe", f_mm(pE[:], pairs), reads=reads, writes=[b_pE])
                w, b_w = wbf.next()
                P.add("act", f_act(w[:], pE[:], AF.Exp), reads=[b_pE], writes=[b_w])
                st[("w", i, hd)] = (w, b_w)
                if i < n - 1:
                    if i == 0:
                        st[("R", 1, hd)] = (sp_, b_sp)
                    else:
                        ro, b_ro = st[("R", i, hd)]
                        rb, b_rb = Rbf[hd].next()
                        P.add("dve", f_tt(rb[:], ro[:], sp_[:], ALU.add), reads=[b_sp, b_ro], writes=[b_rb])
                        st[("R", i + 1, hd)] = (rb, b_rb)

            def stage3(i, hd):
                kb = order[i]
                w, b_w = st.pop(("w", i, hd))
                acc_t, b_acc = accs[hd]
                acc = acc_t[:].bitcast(F32)
                P.add("pe", f_mm1(acc, V[:, kb, :], w[:], i == 0, i == n - 1), reads=[b_V, b_w], writes=[b_acc])

            for step in range(n + 2):
                for hd in range(2):
                    if step < n:
                        stage1(step, hd)
                for hd in range(2):
                    if 0 <= step - 1 < n:
                        stage2(step - 1, hd)
                for hd in range(2):
                    if 0 <= step - 2 < n:
                        stage3(step - 2, hd)
            for hd in range(2):
                pb = 64 * hd
                acc_t, b_acc = accs[hd]
                acc = acc_t[:].bitcast(F32)
                P.add("act", f_act(oT[pb:pb + 64, hp, tok], acc[pb:pb + 64, :], AF.Copy), reads=[b_acc],
                      writes=b_oT[qt * 4:(qt + 1) * 4])


def build_layer1(c, io):
    P = c.P
    c.one_t = c.sb([128, 2], F32, "one")
    P.add("dve", f_memset(c.one_t[:], 1.0), writes=[c.b_eps])
    m01, b_m01 = load_const(c, io["m01"], [128, 4, 512], BF16, "m01", eng="pool")
    negm, b_negm = load_const(c, io["negm"], [128, 4, 512], BF16, "negm", eng="pool")
    M1, b_M1 = load_const(c, io["M1"], [128, 128], BF16, "M1", eng="pool")
    NO, b_NO = load_const(c, io["NO"], [128, 128], BF16, "NO", eng="pool")
    aconsts = (m01, b_m01, negm, b_negm, M1, b_M1, NO, b_NO)
    wr, b_wr = load_const(c, io["moe_w_router"].rearrange("(c p) n -> p c n", p=128), [128, 8, N_EXP], F32, "wr")
    for g in range(TOK // GRP):
        xsrc = io["x2"][g * GRP:(g + 1) * GRP, :]
        P.barrier()
        with contextlib.ExitStack() as gst:
            x3 = c.sb([128, GT, 1024], F32, "x3", gst)
            b_x3 = [Buf("x3") for _ in range(GT)]
            with contextlib.ExitStack() as ast:
                x2T = c.sb([128, 8, GRP], BF16, "x2T", ast)
                b_x2T = [Buf("x2T") for _ in range(GT)]
                oT = c.sb([128, 8, GRP], BF16, "oT", ast)
                b_oT = [Buf("oT") for _ in range(GT)]
                for t in range(GT):
                    stg, b_stg = c.stage.next()
                    P.dma("sp", f_dma([(stg[:], xsrc[t * 128:(t + 1) * 128, :])]), b_stg, writes=[b_stg])
                    tile_to_T(c, stg[:], b_stg, x2T, b_x2T, t)
                with contextlib.ExitStack() as hst:
                    attention_group(c, hst, io, g, oT, b_oT, x2T, b_x2T, aconsts)
                P.barrier()
                with contextlib.ExitStack() as mst:
                    gam, bet, b_gb = load_ln(c, mst, io["g_mix"], io["b_mix"], "lnmix")
                    tmpf = c.rot(2, [128, 1024], F32, "tmpf", mst)
                    wos = c.rot(2, [128, 8, 512], BF16, "wos", mst)
                    xst = c.rot(2, [128, 512], F32, "xst", mst)
                    for half in range(2):
                        wo, b_wo = wos.next()
                        load_w(c, io["sb_w_out"][:, half * 512:(half + 1) * 512], wo[:], b_wo, nsplit=2)
                        for t in range(GT):
                            pm_, b_pm = c.pm.next()
                            P.add("pe", f_mm(pm_[:], [(oT[:, ch, t * 128:(t + 1) * 128], wo[:, ch, :]) for ch in range(8)]),
                                  reads=[b_wo, b_oT[t]], writes=[b_pm])
                            xs, b_xs = xst.next()
                            P.dma("sp", f_dma([(xs[:], xsrc[t * 128:(t + 1) * 128, half * 512:(half + 1) * 512])]), b_xs, writes=[b_xs])
                            P.add("dve", f_stt(x3[:, t, half * 512:(half + 1) * 512], xs[:], ALPHA, pm_[:], ALU.mult, ALU.add),
                                  reads=[b_xs, b_pm], writes=[b_x3[t]])
                    for t in range(GT):
                        ln_inplace(c, x3[:, t, :], b_x3[t], gam, bet, b_gb, tmpf)
            P.barrier()
            with contextlib.ExitStack() as fst:
                gam, bet, b_gb = load_ln(c, fst, io["g_ffn"], io["b_ffn"], "lnffn")
                tmpf = c.rot(2, [128, 1024], F32, "tmpf", fst)
                x3T = c.sb([128, 8, GRP], BF16, "x3T", fst)
                b_x3T = [Buf("x3T") for _ in range(GT)]
                yacc = c.sb([128, GT, 1024], F32, "yacc", fst)
                b_ya = [Buf("yacc", key=f"ya{t}") for t in range(GT)]
                gates = c.sb([128, GT, N_EXP], F32, "gates", fst)
                b_gt = Buf("gates")
                xTf = c.rot(2, [128, 8, 128], F32, "xTf", fst)
                lgr = c.rot(2, [128, 32], F32, "lg", fst)
                for t in range(GT):
                    tile_to_T(c, x3[:, t, :], b_x3[t], x3T, b_x3T, t)
                    P.add("act", f_act(yacc[:, t, :], x3[:, t, :], AF.Copy, scale=ALPHA), reads=[b_x3[t]], writes=[b_ya[t]])
                    xf, b_xf = xTf.next()
                    for hf in range(2):
                        pp, b_pp = c.pm.next()
                        ppv = pp[:].rearrange("p (c n) -> p c n", n=128)
                        P.add("pe", f_tr([(ppv[:, ch, :], x3[:, t, (hf * 4 + ch) * 128:(hf * 4 + ch + 1) * 128]) for ch in range(4)], c.idf[:]),
                              reads=[b_x3[t], c.b_idf], writes=[b_pp])
                        P.add("dve", f_copy(xf[:, hf * 4:(hf + 1) * 4, :], ppv[:, 0:4, :]), reads=[b_pp], writes=[b_xf])
                    pl, b_pl = c.pm.next()
                    P.add("pe", f_mm(pl[:, 0:N_EXP], [(xf[:, ch, :], wr[:, ch, :]) for ch in range(8)]),
                          reads=[b_xf, b_wr], writes=[b_pl])
                    lg, b_lg = lgr.next()
                    P.add("dve", f_copy(lg[:, 0:8], pl[:, 0:N_EXP]), reads=[b_pl], writes=[b_lg])
                    P.add("dve", lambda h, lg=lg: h.max(out=lg[:, 8:16], in_=lg[:, 0:8]), reads=[b_lg], writes=[b_lg])
                    P.add("dve", f_ts(lg[:, 16:17], lg[:, 8:9], -1.0, None, ALU.mult), reads=[b_lg], writes=[b_lg])
                    P.add("act", f_act(lg[:, 24:32], lg[:, 0:8], AF.Exp, bias=lg[:, 16:17]), reads=[b_lg], writes=[b_lg])
                    P.add("dve", f_ts(gates[:, t, :], lg[:, 0:8], lg[:, 9:10], None, ALU.is_ge), reads=[b_lg, b_gt], writes=[b_gt])
                    P.add("dve", f_tt(gates[:, t, :], gates[:, t, :], lg[:, 24:32], ALU.mult), reads=[b_lg, b_gt], writes=[b_gt])
                    P.add("dve", lambda h, lg=lg, t=t: h.reduce_sum(out=lg[:, 17:18], in_=gates[:, t, :], axis=mybir.AxisListType.X),
                          reads=[b_gt, b_lg], writes=[b_lg])
                    P.add("dve", lambda h, lg=lg: h.reciprocal(out=lg[:, 17:18], in_=lg[:, 17:18]), reads=[b_lg], writes=[b_lg])
                    P.add("dve", f_ts(gates[:, t, :], gates[:, t, :], lg[:, 17:18], None, ALU.mult), reads=[b_lg, b_gt], writes=[b_gt])
                for e in range(N_EXP):
                    with contextlib.ExitStack() as ust:
                        ffn_unit(c, ust, x3T, b_x3T, io["moe_w_in"][e * D:(e + 1) * D, :], D_EXP,
                                 io["moe_w_out"][e * D_EXP:(e + 1) * D_EXP, :], yacc, b_ya, [(0, 14), (14, 14)],
                                 gate=((lambda t, e=e: gates[:, t, e:e + 1]), b_gt))
                    P.barrier()
                for t in range(GT):
                    ln_inplace(c, yacc[:, t, :], b_ya[t], gam, bet, b_gb, tmpf)
                    r0 = g * GRP + t * 128
                    store(c, io, io["out"][r0:r0 + 128, :], yacc[:, t, :], b_ya[t])


def make_nc_layer1():
    nc = bass.Bass("TRN2", target_bir_lowering=False)
    c = Ctx(nc)
    io = {"outs": []}

    def din(name, shape, dt=F32):
        io[name] = c.dram(name, shape, dt, "ExternalInput")
    din("x2", [TOK, D]); din("KTs", [D, 2 * TOK], BF16); din("Vs", [2 * TOK, D], BF16)
    din("sb_w_q", [D, D]); din("sb_w_out", [D, D])
    din("moe_w_router", [D, N_EXP]); din("moe_w_in", [N_EXP * D, 2 * D_EXP]); din("moe_w_out", [N_EXP * D_EXP, D])
    din("g_mix", [128, D]); din("b_mix", [128, D]); din("g_ffn", [128, D]); din("b_ffn", [128, D])
    din("m01", [128, 4, 512]); din("negm", [128, 4, 512]); din("M1", [128, 128]); din("NO", [128, 128]); din("ident", [128, 128])
    io["out"] = c.dram("out", [TOK, D], F32, "ExternalOutput")
    with c.root:
        setup_common(c, io["ident"])
        build_layer1(c, io)
        c.P.add("sp", lambda h: None, reads=io["outs"])
        cnt = c.P.emit(nc)
    return nc, cnt


def layer1_inputs(inputs, core, x2, KTs, Vs):
    f = np.float32
    m01, negm, m1, negones = attn_tables()
    bc = lambda v: np.ascontiguousarray(np.broadcast_to(np.asarray(v, f)[None, :], (128, D)))
    return {
        "x2": x2, "KTs": KTs, "Vs": Vs,
        "sb_w_q": np.ascontiguousarray(inputs["sb_w_q"][0]), "sb_w_out": np.ascontiguousarray(inputs["sb_w_out"][0]),
        "moe_w_router": np.ascontiguousarray(inputs["moe_w_router"][0]),
        "moe_w_in": np.ascontiguousarray(inputs["moe_w_in"][0]).reshape(N_EXP * D, 2 * D_EXP),
        "moe_w_out": np.ascontiguousarray(inputs["moe_w_out"][0]).reshape(N_EXP * D_EXP, D),
        "g_mix": bc(inputs["ln_mix_g"][1]), "b_mix": bc(inputs["ln_mix_b"][1]),
        "g_ffn": bc(inputs["ln_ffn_g"][1]), "b_ffn": bc(inputs["ln_ffn_b"][1]),
        "m01": m01, "negm": negm, "M1": m1, "NO": negones, "ident": np.eye(128, dtype=f),
    }


def layer0_group(c, io, L0, g_tok, xsrc, cosd, sind, xs, b_xs, slot0, vscale):
    P = c.P
    S32, Sb, b_S32, b_Sb, consts = L0
    scratch = xs.rearrange("p t d -> p (t d)")
    P.barrier()
    with contextlib.ExitStack() as yst:
        yT = c.sb([128, 16, GRP], BF16, "yT", yst)
        b_yT = [[Buf("yT") for _ in range(GT)] for _ in range(RH)]
        with contextlib.ExitStack() as rst:
            ret_heads(c, rst, io, "own", 0, xsrc, cosd, sind, S32, Sb, b_S32, b_Sb, consts, yT, b_yT, scratch=scratch)
        P.barrier()
        with contextlib.ExitStack() as mst:
            gam, bet, b_gb = load_ln(c, mst, io["g_mix0"], io["b_mix0"], "lnmix")
            tmpf = c.rot(2, [128, 1024], F32, "tmpf", mst)
            wos = c.rot(2, [128, 16, 512], BF16, "wos", mst)
            xst = c.rot(2, [128, 512], F32, "xst", mst)
            for half in range(2):
                wo, b_wo = wos.next()
                load_w(c, io["ret_w_out"][:, half * 512:(half + 1) * 512], wo[:], b_wo, nsplit=2)
                for t in range(GT):
                    pm_, b_pm = c.pm.next()
                    P.add("pe", f_mm(pm_[:], [(yT[:, ch, t * 128:(t + 1) * 128], wo[:, ch, :]) for ch in range(16)]),
                          reads=[b_wo] + [b_yT[h][t] for h in range(RH)], writes=[b_pm])
                    xq, b_xq = xst.next()
                    P.dma("sp", f_dma([(xq[:], xsrc[t * 128:(t + 1) * 128, half * 512:(half + 1) * 512])]), b_xq, writes=[b_xq])
                    P.add("dve", f_stt(xs[:, t, half * 512:(half + 1) * 512], xq[:], ALPHA, pm_[:], ALU.mult, ALU.add),
                          reads=[b_xq, b_pm], writes=[b_xs[t]])
            for t in range(GT):
                ln_inplace(c, xs[:, t, :], b_xs[t], gam, bet, b_gb, tmpf)
    P.barrier()
    with contextlib.ExitStack() as fst:
        gam, bet, b_gb = load_ln(c, fst, io["g_ffn0"], io["b_ffn0"], "lnffn")
        tmpf = c.rot(2, [128, 1024], F32, "tmpf", fst)
        x1T = c.sb([128, 8, GRP], BF16, "x1T", fst)
        b_x1T = [Buf("x1T") for _ in range(GT)]
        for t in range(GT):
            tile_to_T(c, xs[:, t, :], b_xs[t], x1T, b_x1T, t)
            P.add("act", f_act(xs[:, t, :], xs[:, t, :], AF.Copy, scale=ALPHA), reads=[b_xs[t]], writes=[b_xs[t]])
        with contextlib.ExitStack() as ust:
            ffn_unit(c, ust, x1T, b_x1T, io["ffn_w_in"], D_FF, io["ffn_w_out"], xs, b_xs, [(0, 12), (12, 10)])
        for t in range(GT):
            ln_inplace(c, xs[:, t, :], b_xs[t], gam, bet, b_gb, tmpf)
        P.barrier()
        with contextlib.ExitStack() as kst:
            x2T = x1T
            b_x2T = [Buf("x2T") for _ in range(GT)]
            for t in range(GT):
                tile_to_T(c, xs[:, t, :], b_xs[t], x2T, b_x2T, t)
            wkv = c.rot(2, [128, 8, 512], BF16, "wkv", kst)
            ksb = c.rot(3, [128, 512], BF16, "ksb", kst)
            for cb in range(2):
                ws, b_ws = wkv.next()
                load_w(c, io["sb_w_kv"][:, cb * 512:(cb + 1) * 512], ws[:], b_ws, nsplit=2)
                for ch in range(4):
                    for tg in range(GRP // 512):
                        pk, b_pk = c.pm.next()
                        tok = slice(tg * 512, (tg + 1) * 512)
                        P.add("pe", f_mm(pk[:], [(ws[:, k, ch * 128:(ch + 1) * 128], x2T[:, k, tok]) for k in range(8)]),
                              reads=[b_ws] + b_x2T[tg * 4:(tg + 1) * 4], writes=[b_pk])
                        ks, b_ks = ksb.next()
                        P.add("act", f_act(ks[:], pk[:], AF.Copy), reads=[b_pk], writes=[b_ks])
                        f0 = (cb * 4 + ch) * 128
                        t0 = slot0 + tg * 512
                        store(c, io, io["KTs"][f0:f0 + 128, t0:t0 + 512], ks[:], b_ks)
            for cb in range(2):
                ws, b_ws = wkv.next()
                load_w(c, io["sb_w_kv"][:, 1024 + cb * 512:1024 + (cb + 1) * 512], ws[:], b_ws, nsplit=2)
                for t in range(GT):
                    pv, b_pv = c.pm.next()
                    P.add("pe", f_mm(pv[:], [(x2T[:, k, t * 128:(t + 1) * 128], ws[:, k, :]) for k in range(8)]),
                          reads=[b_ws, b_x2T[t]], writes=[b_pv])
                    ks, b_ks = ksb.next()
                    if vscale is None:
                        P.add("act", f_act(ks[:], pv[:], AF.Copy), reads=[b_pv], writes=[b_ks])
                    else:
                        P.add("act", f_act(ks[:], pv[:], AF.Identity, scale=vscale[0]), reads=[b_pv, vscale[1]], writes=[b_ks])
                    r0 = slot0 + t * 128
                    store(c, io, io["Vs"][r0:r0 + 128, cb * 512:(cb + 1) * 512], ks[:], b_ks)


def layer1_group(c, io, g, xs, b_xs, aconsts, wr, b_wr, attn_only=False):
    P = c.P
    P.barrier()
    with contextlib.ExitStack() as ast:
        x2T = c.sb([128, 8, GRP], BF16, "x2T", ast)
        b_x2T = [Buf("x2T") for _ in range(GT)]
        oT = c.sb([128, 8, GRP], BF16, "oT", ast)
        b_oT = [Buf("oT") for _ in range(GT)]
        for t in range(GT):
            tile_to_T(c, xs[:, t, :], b_xs[t], x2T, b_x2T, t)
        with contextlib.ExitStack() as hst:
            attention_group(c, hst, io, g, oT, b_oT, x2T, b_x2T, aconsts)
        P.barrier()
        with contextlib.ExitStack() as mst:
            gam, bet, b_gb = load_ln(c, mst, io["g_mix1"], io["b_mix1"], "lnmix")
            tmpf = c.rot(2, [128, 1024], F32, "tmpf", mst)
            wos = c.rot(2, [128, 8, 512], BF16, "wos", mst)
            for half in range(2):
                wo, b_wo = wos.next()
                load_w(c, io["sb_w_out"][:, half * 512:(half + 1) * 512], wo[:], b_wo, nsplit=2)
                for t in range(GT):
                    pm_, b_pm = c.pm.next()
                    P.add("pe", f_mm(pm_[:], [(oT[:, ch, t * 128:(t + 1) * 128], wo[:, ch, :]) for ch in range(8)]),
                          reads=[b_wo, b_oT[t]], writes=[b_pm])
                    xh = xs[:, t, half * 512:(half + 1) * 512]
                    P.add("dve", f_stt(xh, xh, ALPHA, pm_[:], ALU.mult, ALU.add), reads=[b_xs[t], b_pm], writes=[b_xs[t]])
            for t in range(GT):
                ln_inplace(c, xs[:, t, :], b_xs[t], gam, bet, b_gb, tmpf)
    if attn_only:
        return
    P.barrier()
    with contextlib.ExitStack() as fst:
        gam, bet, b_gb = load_ln(c, fst, io["g_ffn1"], io["b_ffn1"], "lnffn")
        tmpf = c.rot(2, [128, 1024], F32, "tmpf", fst)
        x3T = c.sb([128, 8, GRP], BF16, "x3T", fst)
        b_x3T = [Buf("x3T") for _ in range(GT)]
        gates = c.sb([128, GT, N_EXP], F32, "gates", fst)
        b_gt = Buf("gates")
        xTf = c.rot(2, [128, 8, 128], F32, "xTf", fst)
        lgr = c.rot(2, [128, 32], F32, "lg", fst)
        for t in range(GT):
            tile_to_T(c, xs[:, t, :], b_xs[t], x3T, b_x3T, t)
            xf, b_xf = xTf.next()
            for hf in range(2):
                pp, b_pp = c.pm.next()
                ppv = pp[:].rearrange("p (c n) -> p c n", n=128)
                P.add("pe", f_tr([(ppv[:, ch, :], xs[:, t, (hf * 4 + ch) * 128:(hf * 4 + ch + 1) * 128]) for ch in range(4)], c.idf[:]),
                      reads=[b_xs[t], c.b_idf], writes=[b_pp])
                P.add("dve", f_copy(xf[:, hf * 4:(hf + 1) * 4, :], ppv[:, 0:4, :]), reads=[b_pp], writes=[b_xf])
            pl, b_pl = c.pm.next()
            P.add("pe", f_mm(pl[:, 0:N_EXP], [(xf[:, ch, :], wr[:, ch, :]) for ch in range(8)]),
                  reads=[b_xf, b_wr], writes=[b_pl])
            P.add("act", f_act(xs[:, t, :], xs[:, t, :], AF.Copy, scale=ALPHA), reads=[b_xs[t]], writes=[b_xs[t]])
            lg, b_lg = lgr.next()
            P.add("dve", f_copy(lg[:, 0:8], pl[:, 0:N_EXP]), reads=[b_pl], writes=[b_lg])
            P.add("dve", lambda h, lg=lg: h.max(out=lg[:, 8:16], in_=lg[:, 0:8]), reads=[b_lg], writes=[b_lg])
            P.add("dve", f_ts(lg[:, 16:17], lg[:, 8:9], -1.0, None, ALU.mult), reads=[b_lg], writes=[b_lg])
            P.add("act", f_act(lg[:, 24:32], lg[:, 0:8], AF.Exp, bias=lg[:, 16:17]), reads=[b_lg], writes=[b_lg])
            P.add("dve", f_ts(gates[:, t, :], lg[:, 0:8], lg[:, 9:10], None, ALU.is_ge), reads=[b_lg, b_gt], writes=[b_gt])
            P.add("dve", f_tt(gates[:, t, :], gates[:, t, :], lg[:, 24:32], ALU.mult), reads=[b_lg, b_gt], writes=[b_gt])
            P.add("dve", lambda h, lg=lg, t=t: h.reduce_sum(out=lg[:, 17:18], in_=gates[:, t, :], axis=mybir.AxisListType.X),
                  reads=[b_gt, b_lg], writes=[b_lg])
            P.add("dve", lambda h, lg=lg: h.reciprocal(out=lg[:, 17:18], in_=lg[:, 17:18]), reads=[b_lg], writes=[b_lg])
            P.add("dve", f_ts(gates[:, t, :], gates[:, t, :], lg[:, 17:18], None, ALU.mult), reads=[b_lg, b_gt], writes=[b_gt])
        for e in range(N_EXP):
            with contextlib.ExitStack() as ust:
                ffn_unit(c, ust, x3T, b_x3T, io["moe_w_in"][e * D:(e + 1) * D, :], D_EXP,
                         io["moe_w_out"][e * D_EXP:(e + 1) * D_EXP, :], xs, b_xs, [(0, 14), (14, 14)],
                         gate=((lambda t, e=e: gates[:, t, e:e + 1]), b_gt))
            P.barrier()
        for t in range(GT):
            ln_inplace(c, xs[:, t, :], b_xs[t], gam, bet, b_gb, tmpf)
            r0 = g * GRP + t * 128
            store(c, io, io["out"][r0:r0 + 128, :], xs[:, t, :], b_xs[t])


ROUTED = True
I32 = mybir.dt.int32
REGS = [(None, [(0, 384)]), (384, [(384, 128), (512, 512)])]


def moe_tables():
    iota_f = np.broadcast_to(np.arange(128, dtype=np.float32)[None, :], (128, 128)).copy()
    iotap = (np.arange(128, dtype=np.float32)[:, None] + 128.0 * np.arange(8, dtype=np.float32)[None, :]).copy()
    k = np.arange(128)[:, None]
    m = np.arange(128)[None, :]
    ulow = (k < m).astype(np.float32)
    ones = np.ones((128, 128), np.float32)
    oh = np.zeros((8, 8, 128), np.float32)
    for r in range(8):
        oh[r, r, :] = 1.0
    return iota_f, iotap, ulow, ones, oh.reshape(8, 8 * 128)


def moe_routed_group(c, io, g, xs, b_xs, mc):
    P = c.P
    iota_f, b_iota, iotap, b_iotap, Ulow, b_Ul, ones_b, b_ones, OH, b_OH, wr, b_wr = mc
    P.barrier()
    with contextlib.ExitStack() as fst:
        rt_in = c.sb([128, GT, 16], F32, "rt_in", fst)
        b_rt = [Buf("rt_in") for _ in range(GT)]
        maskt = c.sb([128, GT, 8], F32, "maskt", fst)
        maskb = c.sb([128, GT, 8], BF16, "maskb", fst)
        b_mask = [Buf("mask") for _ in range(GT)]
        cnt_i = c.sb([128, 8], I32, "cnt_i", fst)
        b_cnt = Buf("cnt")
        RTp = c.sb([8, GRP], F32, "RTp", fst)
        RTg = c.sb([8, GRP], F32, "RTg", fst)
        b_RT = Buf("RT")
        x3b = c.sb([128, GT, 1024], BF16, "x3b", fst)
        b_x3b = [Buf("x3b") for _ in range(GT)]
        pst = contextlib.ExitStack()
        xTf = c.rot(2, [128, 8, 128], F32, "xTf", pst)
        lgr = c.rot(2, [128, 32], F32, "lg", pst)
        for t in range(GT):
            P.add("act", f_act(x3b[:, t, :], xs[:, t, :], AF.Copy), reads=[b_xs[t]], writes=[b_x3b[t]])
            xf, b_xf = xTf.next()
            for hf in range(2):
                pp, b_pp = c.pm.next()
                ppv = pp[:].rearrange("p (c n) -> p c n", n=128)
                P.add("pe", f_tr([(ppv[:, ch, :], xs[:, t, (hf * 4 + ch) * 128:(hf * 4 + ch + 1) * 128]) for ch in range(4)], c.idf[:]),
                      reads=[b_xs[t], c.b_idf], writes=[b_pp])
                P.add("dve", f_copy(xf[:, hf * 4:(hf + 1) * 4, :], ppv[:, 0:4, :]), reads=[b_pp], writes=[b_xf])
            pl, b_pl = c.pm.next()
            P.add("pe", f_mm(pl[:, 0:N_EXP], [(xf[:, ch, :], wr[:, ch, :]) for ch in range(8)]),
                  reads=[b_xf, b_wr], writes=[b_pl])
            P.add("act", f_act(xs[:, t, :], xs[:, t, :], AF.Copy, scale=ALPHA), reads=[b_xs[t]], writes=[b_xs[t]])
            lg, b_lg = lgr.next()
            gt = rt_in[:, t, 8:16]
            P.add("dve", f_copy(lg[:, 0:8], pl[:, 0:N_EXP]), reads=[b_pl], writes=[b_lg])
            P.add("dve", lambda h, lg=lg: h.max(out=lg[:, 8:16], in_=lg[:, 0:8]), reads=[b_lg], writes=[b_lg])
            P.add("dve", f_ts(lg[:, 16:17], lg[:, 8:9], -1.0, None, ALU.mult), reads=[b_lg], writes=[b_lg])
            P.add("act", f_act(lg[:, 24:32], lg[:, 0:8], AF.Exp, bias=lg[:, 16:17]), reads=[b_lg], writes=[b_lg])
            P.add("dve", f_ts(maskt[:, t, :], lg[:, 0:8], lg[:, 9:10], None, ALU.is_ge), reads=[b_lg], writes=[b_mask[t]])
            P.add("dve", f_copy(maskb[:, t, :], maskt[:, t, :]), reads=[b_mask[t]], writes=[b_mask[t]])
            P.add("dve", f_tt(gt, maskt[:, t, :], lg[:, 24:32], ALU.mult), reads=[b_lg, b_mask[t]], writes=[b_rt[t]])
            P.add("dve", lambda h, lg=lg, gt=gt: h.reduce_sum(out=lg[:, 17:18], in_=gt, axis=mybir.AxisListType.X),
                  reads=[b_rt[t], b_lg], writes=[b_lg])
            P.add("dve", lambda h, lg=lg: h.reciprocal(out=lg[:, 17:18], in_=lg[:, 17:18]), reads=[b_lg], writes=[b_lg])
            P.add("dve", f_ts(gt, gt, lg[:, 17:18], None, ALU.mult), reads=[b_lg, b_rt[t]], writes=[b_rt[t]])
        for t in range(GT):
            pp, b_pp = c.pm.next()
            pairs = [(Ulow[:], maskb[:, t, :])] + [(ones_b[:], maskb[:, t2, :]) for t2 in range(t)]
            P.add("pe", f_mm(pp[:, 0:8], pairs), reads=[b_Ul, b_ones] + b_mask[0:t + 1], writes=[b_pp])
            P.add("dve", f_stt(rt_in[:, t, 0:8], pp[:, 0:8], 1.0, maskt[:, t, :], ALU.add, ALU.mult),
                  reads=[b_pp, b_mask[t], b_rt[t]], writes=[b_rt[t]])
            P.add("dve", f_ts(rt_in[:, t, 0:8], rt_in[:, t, 0:8], -1.0, None, ALU.add), reads=[b_rt[t]], writes=[b_rt[t]])
        pc, b_pc = c.pm.next()
        P.add("pe", f_mm(pc[:, 0:8], [(ones_b[:], maskb[:, t, :]) for t in range(GT)]), reads=[b_ones] + b_mask, writes=[b_pc])
        P.add("dve", f_copy(cnt_i[:], pc[:, 0:8]), reads=[b_pc], writes=[b_cnt])
        for (dst, c0) in ((RTp, 0), (RTg, 8)):
            for hf in range(2):
                pp, b_pp = c.pm.next()
                P.add("pe", f_tr([(pp[0:8, j * 128:(j + 1) * 128], rt_in[:, hf * 4 + j, c0:c0 + 8]) for j in range(4)], c.idf[:]),
                      reads=b_rt[hf * 4:(hf + 1) * 4] + [c.b_idf], writes=[b_pp])
                P.add("act", f_act(dst[:, hf * 512:(hf + 1) * 512], pp[0:8, :], AF.Copy), reads=[b_pp], writes=[b_RT])

        pst.close()
        P.barrier()
        xgT = c.sb([128, 8, GRP], BF16, "xgT", fst)
        b_xg = [Buf("xg") for _ in range(GT)]
        P.add("dve", f_memset(xgT[:], 0.0), writes=b_xg)
        actT = c.sb([128, 14, GRP], BF16, "actT", fst)
        b_act = [[Buf("act") for _ in range(GT)] for _ in range(14)]
        oute = c.sb([128, GT, 512], BF16, "oute", fst)
        b_oute = [Buf("oute") for _ in range(GT)]
        pbc = c.sb([128, GRP], F32, "pbc", fst)
        gbc = c.sb([128, GRP], F32, "gbc", fst)
        b_bc = Buf("bc")
        selr = c.rot(16, [128, 128], BF16, "sel", fst)
        nst0 = REGS[0][1][0][1] // 128
        sTc = c.sb([128, nst0, GT, 128], BF16, "sTc", fst)
        b_sTc = [[Buf("sTc") for _ in range(GT)] for _ in range(nst0)]
        selTr = c.rot(4, [128, 128], BF16, "selT", fst)
        w1s = c.rot(2, [128, 8, 2, 256], BF16, "w1s", fst)
        w2s = c.rot(1, [128, 14, 512], BF16, "w2s", fst)
        sg = c.rot(2, [128, 512], BF16, "sg", fst)

        def regions():
            for thr, pieces in REGS:
                if thr is not None:
                    P.region_begin("cnt", thr)
                yield thr, pieces
                if thr is not None:
                    P.region_end()

        for e in range(N_EXP):
            w1 = io["moe_w_in"][e * D:(e + 1) * D, :]
            w2 = io["moe_w_out"][e * D_EXP:(e + 1) * D_EXP, :]
            for eng in ("pe", "act", "dve"):
                P.reg_load(eng, "cnt", cnt_i[0:1, e:e + 1], reads=[b_cnt])
            for (dst, src) in ((pbc, RTp), (gbc, RTg)):
                for hf in range(2):
                    pb, b_pb = c.pm.next()
                    P.add("pe", f_mm(pb[:], [(OH[:, e * 128:(e + 1) * 128], src[:, hf * 512:(hf + 1) * 512])]),
                          reads=[b_OH, b_RT], writes=[b_pb])
                    P.add("dve", f_copy(dst[:, hf * 512:(hf + 1) * 512], pb[:]), reads=[b_pb], writes=[b_bc])
            for thr, pieces in regions():
                for (s0, w) in pieces:
                    for st in range(w // 128):
                        slot0 = s0 + st * 128
                        sels = []
                        for t in range(GT):
                            sl, b_sl = selr.next()
                            P.add("dve", f_ts(sl[:], iota_f[:], rt_in[:, t, e:e + 1], -float(slot0), ALU.subtract, ALU.is_equal),
                                  reads=[b_iota, b_rt[t]], writes=[b_sl])
                            sels.append((sl, b_sl))
                        for hf in range(2):
                            pg, b_pg = c.pm.next()
                            groups = [(pg[:, j * 128:(j + 1) * 128],
                                       [(x3b[:, t, (hf * 4 + j) * 128:(hf * 4 + j + 1) * 128], sels[t][0][:]) for t in range(GT)])
                                      for j in range(4)]
                            P.add("pe", f_mm_multi(groups), reads=b_x3b + [b for _, b in sels], writes=[b_pg])
                            P.add("dve", f_copy(xgT[:, hf * 4:(hf + 1) * 4, slot0:slot0 + 128],
                                                pg[:].rearrange("p (c n) -> p c n", n=128)),
                                  reads=[b_pg], writes=[b_xg[slot0 // 128]])
            for sti in range(nst0):
                for t in range(GT):
                    tk = slice(t * 128, (t + 1) * 128)
                    P.add("dve", f_stt(sTc[:, sti, t, :], pbc[:, tk], iotap[:, sti:sti + 1], gbc[:, tk], ALU.is_equal, ALU.mult),
                          reads=[b_bc, b_iotap], writes=[b_sTc[sti][t]])
            for (j0, nj) in ((0, 14), (14, 14)):
                for blk in range(nj // 2):
                    jj = j0 + blk * 2
                    ws, b_ws = w1s.next()
                    srcg = w1[:, jj * 128: jj * 128 + 256].rearrange("(c p) n -> p c n", p=128)
                    srcu = w1[:, D_EXP + jj * 128: D_EXP + jj * 128 + 256].rearrange("(c p) n -> p c n", p=128)
                    P.dma("pool", f_dma([(ws[:, :, 0, :], srcg), (ws[:, :, 1, :], srcu)]), b_ws, writes=[b_ws], n=2)
                    for thr, pieces in regions():
                        for (s0, w) in pieces:
                            tiles = list(range(s0 // 128, (s0 + w) // 128))
                            for ci in range(2):
                                jl = blk * 2 + ci
                                pg, b_pg = c.pm.next()
                                pu, b_pu = c.pm.next()
                                P.add("pe", f_mm_multi([
                                    (pg[:, 0:w], [(ws[:, k, 0, ci * 128:(ci + 1) * 128], xgT[:, k, s0:s0 + w]) for k in range(8)]),
                                    (pu[:, 0:w], [(ws[:, k, 1, ci * 128:(ci + 1) * 128], xgT[:, k, s0:s0 + w]) for k in range(8)]),
                                ]), reads=[b_ws] + [b_xg[i] for i in tiles], writes=[b_pg, b_pu])
                                s_, b_s = sg.next()
                                P.add("act", f_act(s_[:, 0:w], pg[:, 0:w], AF.Silu), reads=[b_pg], writes=[b_s])
                                P.add("dve", f_tt(actT[:, jl, s0:s0 + w], s_[:, 0:w], pu[:, 0:w], ALU.mult),
                                      reads=[b_s, b_pu], writes=[b_act[jl][i] for i in tiles])
                for half in range(2):
                    w2v, b_w2 = w2s.next()
                    load_w(c, w2[j0 * 128:(j0 + nj) * 128, half * 512:(half + 1) * 512], w2v[:, 0:nj, :], b_w2, nsplit=2)
                    for thr, pieces in regions():
                        for (s0, w) in pieces:
                            for sti in range(s0 // 128, (s0 + w) // 128):
                                po, b_po = c.pm.next()
                                P.add("pe", f_mm(po[:], [(actT[:, j, sti * 128:(sti + 1) * 128], w2v[:, j, :]) for j in range(nj)]),
                                      reads=[b_w2] + [b_act[j][sti] for j in range(nj)], writes=[b_po])
                                P.add("dve", f_copy(oute[:, sti, :], po[:]), reads=[b_po], writes=[b_oute[sti]])
                    for thr, pieces in regions():
                        stis = [sti for (s0, w) in pieces for sti in range(s0 // 128, (s0 + w) // 128)]
                        for t in range(GT):
                            tk = slice(t * 128, (t + 1) * 128)
                            if thr is None:
                                sTs = [(sTc[:, sti, t, :], b_sTc[sti][t]) for sti in stis]
                                groups = [stis]
                            else:
                                sTs = None
                                groups = [stis[i:i + 4] for i in range(0, len(stis), 4)]
                            for grp in groups:
                                if thr is None:
                                    cur = sTs
                                else:
                                    cur = []
                                    for sti in grp:
                                        sT, b_sT = selTr.next()
                                        P.add("dve", f_stt(sT[:], pbc[:, tk], iotap[:, sti:sti + 1], gbc[:, tk], ALU.is_equal, ALU.mult),
                                              reads=[b_bc, b_iotap], writes=[b_sT])
                                        cur.append((sT[:], b_sT))
                                ps_, b_ps = c.pm.next()
                                P.add("pe", f_mm(ps_[:], [(cur[i][0], oute[:, sti, :]) for i, sti in enumerate(grp)]),
                                      reads=[b for _, b in cur] + [b_oute[sti] for sti in grp], writes=[b_ps])
                                xh = xs[:, t, half * 512:(half + 1) * 512]
                                P.add("dve", f_tt(xh, ps_[:], xh, ALU.add), reads=[b_ps, b_xs[t]], writes=[b_xs[t]])
    P.barrier()
    with contextlib.ExitStack() as lst:
        gam, bet, b_gb = load_ln(c, lst, io["g_ffn1"], io["b_ffn1"], "lnffn")
        tmpf = c.rot(2, [128, 1024], F32, "tmpf", lst)
        for t in range(GT):
            ln_inplace(c, xs[:, t, :], b_xs[t], gam, bet, b_gb, tmpf)
            r0 = g * GRP + t * 128
            store(c, io, io["out"][r0:r0 + 128, :], xs[:, t, :], b_xs[t])


def build_fused(c, io):
    P = c.P
    xres = c.sb([128, TOK // 128, 1024], F32, "xres")
    b_xres = [Buf("xres", key=f"xres{t}") for t in range(TOK // 128)]
    c.one_t = c.sb([128, 2], F32, "one")
    P.add("dve", f_memset(c.one_t[:], 1.0), writes=[c.b_eps])
    flag, b_flag = load_const(c, io["flag"], [128, 1], F32, "flag")
    with contextlib.ExitStack() as l0st:
        c.stack = l0st
        c.stage = c.rot(2, [128, 1024], F32, "stage")
        _, _, _, cd = retention_tables()
        dtab, b_dtab = load_const(c, io["dtab"], [128, RH, 128], F32, "dtab")
        cn, b_cn = load_const(c, io["cn"], [128, RH], F32, "cn")
        kdec, b_kdec = load_const(c, io["kdec"], [128, RH], F32, "kdec")
        consts = (dtab, b_dtab, cn, b_cn, kdec, b_kdec, cd)
        S32 = c.sb([128, RH, 2, 512], F32, "S32")
        Sb = c.sb([128, RH, 2, 512], BF16, "Sb")
        b_S32 = [Buf("S32") for _ in range(RH)]
        b_Sb = [Buf("Sb") for _ in range(RH)]
        for h in range(RH):
            P.add("dve", f_memset(S32[:, h, :, :], 0.0), writes=[b_S32[h]])
            P.add("dve", f_memset(Sb[:, h, :, :], 0.0), writes=[b_Sb[h]])
        L0 = (S32, Sb, b_S32, b_Sb, consts)
        for mode, g in (("prev", 0), ("prev", 1), ("own", 0), ("own", 1)):
            own = mode == "own"
            xsrc = (io["x_own"] if own else io["x_prev"])[g * GRP:(g + 1) * GRP, :]
            cosd = (io["cos_own"] if own else io["cos_prev"])[:, g * GRP:(g + 1) * GRP]
            sind = (io["sin_own"] if own else io["sin_prev"])[:, g * GRP:(g + 1) * GRP]
            gs = g if own else 1
            xs = xres[:, gs * GT:(gs + 1) * GT, :]
            bx = b_xres[gs * GT:(gs + 1) * GT]
            slot0 = (TOK if own else 0) + g * GRP
            layer0_group(c, io, L0, g, xsrc, cosd, sind, xs, bx, slot0, None if own else (flag[:, 0:1], b_flag))
        c.stack = c.root
    P.barrier()
    wr, b_wr = load_const(c, io["moe_w_router"].rearrange("(c p) n -> p c n", p=128), [128, 8, N_EXP], F32, "wr")
    with contextlib.ExitStack() as a1st:
        c.stack = a1st
        m01, b_m01 = load_const(c, io["m01"], [128, 4, 512], BF16, "m01", eng="pool")
        negm, b_negm = load_const(c, io["negm"], [128, 4, 512], BF16, "negm", eng="pool")
        M1, b_M1 = load_const(c, io["M1"], [128, 128], BF16, "M1", eng="pool")
        NO, b_NO = load_const(c, io["NO"], [128, 128], BF16, "NO", eng="pool")
        aconsts = (m01, b_m01, negm, b_negm, M1, b_M1, NO, b_NO)
        c.stack = c.root
        for g in range(TOK // GRP):
            layer1_group(c, io, g, xres[:, g * GT:(g + 1) * GT, :], b_xres[g * GT:(g + 1) * GT], aconsts, wr, b_wr,
                         attn_only=ROUTED)
    if ROUTED:
        P.barrier()
        iota_f, b_iota = load_const(c, io["iota_f"], [128, 128], F32, "iota_f")
        iotap, b_iotap = load_const(c, io["iotap"], [128, 8], F32, "iotap")
        Ulow, b_Ul = load_const(c, io["Ulow"], [128, 128], BF16, "Ulow", eng="pool")
        ones_b, b_ones = load_const(c, io["ones"], [128, 128], BF16, "ones_b", eng="pool")
        OH, b_OH = load_const(c, io["OH"], [8, 8 * 128], F32, "OH")
        mc = (iota_f, b_iota, iotap, b_iotap, Ulow, b_Ul, ones_b, b_ones, OH, b_OH, wr, b_wr)
        for g in range(TOK // GRP):
            moe_routed_group(c, io, g, xres[:, g * GT:(g + 1) * GT, :], b_xres[g * GT:(g + 1) * GT], mc)


def make_nc_fused():
    nc = bass.Bass("TRN2", target_bir_lowering=False)
    c = Ctx(nc)
    io = {"outs": []}

    def din(name, shape, dt=F32):
        io[name] = c.dram(name, shape, dt, "ExternalInput")
    din("x_own", [TOK, D]); din("x_prev", [TOK, D])
    din("cos_own", [128, TOK]); din("sin_own", [128, TOK]); din("cos_prev", [128, TOK]); din("sin_prev", [128, TOK])
    din("ret_w_in", [D, 6144]); din("ret_w_out", [2048, D])
    din("ffn_w_in", [D, 2 * D_FF]); din("ffn_w_out", [D_FF, D]); din("sb_w_kv", [D, 2048])
    din("sb_w_q", [D, D]); din("sb_w_out", [D, D])
    din("moe_w_router", [D, N_EXP]); din("moe_w_in", [N_EXP * D, 2 * D_EXP]); din("moe_w_out", [N_EXP * D_EXP, D])
    for l in (0, 1):
        for nm in ("g_mix", "b_mix", "g_ffn", "b_ffn"):
            din(f"{nm}{l}", [128, D])
    din("dtab", [128, RH, 128]); din("cn", [128, RH]); din("kdec", [128, RH]); din("ident", [128, 128]); din("flag", [128, 1])
    din("m01", [128, 4, 512]); din("negm", [128, 4, 512]); din("M1", [128, 128]); din("NO", [128, 128])
    din("iota_f", [128, 128]); din("iotap", [128, 8]); din("Ulow", [128, 128]); din("ones", [128, 128]); din("OH", [8, 8 * 128])
    io["KTs"] = nc.dram_tensor("KTs_i", [D, 2 * TOK], BF16).ap()
    io["Vs"] = nc.dram_tensor("Vs_i", [2 * TOK, D], BF16).ap()
    io["out"] = c.dram("out", [TOK, D], F32, "ExternalOutput")
    with c.root:
        setup_common(c, io["ident"])
        build_fused(c, io)
        c.P.add("sp", lambda h: None, reads=io["outs"])
        cnt = c.P.emit(nc)
    return nc, cnt


def fused_inputs(inputs, core, shared):
    b, half = core // 2, core % 2
    x = inputs["x"]
    f = np.float32
    co, so = rope_tables(half * TOK, TOK)
    d = dict(shared)
    d.update({
        "x_own": np.ascontiguousarray(x[b, half * TOK:(half + 1) * TOK]),
        "x_prev": np.ascontiguousarray(x[b, 0:TOK]) if half == 1 else np.zeros((TOK, D), f),
        "cos_own": co, "sin_own": so,
        "flag": np.full((128, 1), float(half), f),
    })
    return d


def shared_inputs(inputs):
    f = np.float32
    dtab, cn, kdec, _ = retention_tables()
    cp, sp_ = rope_tables(0, TOK)
    m01, negm, m1, negones = attn_tables()
    bc = lambda v: np.ascontiguousarray(np.broadcast_to(np.asarray(v, f)[None, :], (128, D)))
    d = {
        "cos_prev": cp, "sin_prev": sp_,
        "ret_w_in": np.ascontiguousarray(inputs["ret_w_in"][0]),
        "ret_w_out": np.ascontiguousarray(inputs["ret_w_out"][0]),
        "ffn_w_in": np.ascontiguousarray(inputs["ffn_w_in"][0]),
        "ffn_w_out": np.ascontiguousarray(inputs["ffn_w_out"][0]),
        "sb_w_kv": np.ascontiguousarray(inputs["sb_w_kv"]),
        "sb_w_q": np.ascontiguousarray(inputs["sb_w_q"][0]), "sb_w_out": np.ascontiguousarray(inputs["sb_w_out"][0]),
        "moe_w_router": np.ascontiguousarray(inputs["moe_w_router"][0]),
        "moe_w_in": np.ascontiguousarray(inputs["moe_w_in"][0]).reshape(N_EXP * D, 2 * D_EXP),
        "moe_w_out": np.ascontiguousarray(inputs["moe_w_out"][0]).reshape(N_EXP * D_EXP, D),
        "dtab": dtab, "cn": cn, "kdec": kdec, "ident": np.eye(128, dtype=f),
        "m01": m01, "negm": negm, "M1": m1, "NO": negones,
    }
    d["iota_f"], d["iotap"], d["Ulow"], d["ones"], d["OH"] = moe_tables()
    for l in (0, 1):
        d[f"g_mix{l}"] = bc(inputs["ln_mix_g"][l]); d[f"b_mix{l}"] = bc(inputs["ln_mix_b"][l])
        d[f"g_ffn{l}"] = bc(inputs["ln_ffn_g"][l]); d[f"b_ffn{l}"] = bc(inputs["ln_ffn_b"][l])
    return d


_NC_CACHE = {}


def kernel(**inputs):
    inputs = {k: np.asarray(v) for k, v in inputs.items()}
    cores = list(range(8))
    if "f" not in _NC_CACHE:
        _NC_CACHE["f"] = make_nc_fused()[0]
    shared = shared_inputs(inputs)
    in_maps = [fused_inputs(inputs, cr, shared) for cr in cores]
    res = run_bass_kernel_spmd(_NC_CACHE["f"], in_maps, core_ids=cores).results
    out = np.stack([np.concatenate([res[2 * b]["out"], res[2 * b + 1]["out"]], axis=0) for b in range(4)], axis=0)
    return out.astype(np.float32)
```

```python
import contextlib
import numpy as np
import ml_dtypes
import concourse.bass as bass
import concourse.mybir as mybir
from concourse.bass_utils import run_bass_kernel_spmd

F32 = mybir.dt.float32
BF16 = mybir.dt.bfloat16
AF = mybir.ActivationFunctionType
ALU = mybir.AluOpType

ENGS = ("pe", "act", "dve", "pool", "sp")

D = 1024
TOK = 2048
GRP = 1024
GT = GRP // 128
ALPHA = float((2.0 * 2) ** 0.25)
EPS = 1e-5
RH = 4
D_FF = 2816
N_EXP = 8
D_EXP = 3584


class Buf:
    __slots__ = ("name", "w", "r", "key")

    def __init__(self, name="", key=None):
        self.name = name
        self.w = None
        self.r = []
        self.key = key


class Op:
    __slots__ = ("eng", "fn", "deps", "kind", "owner", "dval", "ms", "ndma")

    def __init__(self, eng, fn, deps, kind, owner=None, dval=0, ndma=0):
        self.eng = eng
        self.fn = fn
        self.deps = deps
        self.kind = kind
        self.owner = owner
        self.dval = dval
        self.ms = None
        self.ndma = ndma


class Prog:
    def __init__(self):
        self.ops = []
        self.by_eng = {e: [] for e in ENGS}
        self.ndsem = 0
        self.fence = set()
        self.dstate = {}
        self.regions = []
        self.cur_region = None
        self.regs = {}

    def bufs(self, n, name=""):
        return [Buf(f"{name}{i}") for i in range(n)]

    def _deps(self, eng, reads, writes, oid):
        deps = set()
        for b in reads:
            if b.w is not None:
                deps.add(b.w)
        for b in writes:
            if b.w is not None:
                deps.add(b.w)
            deps.update(b.r)
        for b in reads:
            b.r.append(oid)
        for b in writes:
            b.w = oid
            b.r = []
        deps |= self.fence
        deps.discard(oid)
        if eng == "pe":
            deps = {d for d in deps if not (self.ops[d].eng == "pe" and self.ops[d].kind == "c")}
        return deps

    def add(self, eng, fn, reads=(), writes=()):
        oid = len(self.ops)
        deps = self._deps(eng, list(reads), list(writes), oid)
        self.ops.append(Op(eng, fn, deps, "c"))
        self.by_eng[eng].append(oid)
        return oid

    def dma(self, eng, fn, owner, reads=(), writes=(), n=1):
        assert self.cur_region is None, "no DMA inside dynamic regions"
        oid = len(self.ops)
        deps = self._deps(eng, list(reads), list(writes), oid)
        key = owner.key if owner.key is not None else id(owner)
        stt = self.dstate.get(key)
        if stt is None:
            stt = [self.ndsem, 0, None]
            self.ndsem += 1
            self.dstate[key] = stt
            self._keep = getattr(self, "_keep", [])
            self._keep.append(owner)
        if stt[2] is not None:
            deps.add(stt[2])
        stt[1] += n
        stt[2] = oid
        self.ops.append(Op(eng, fn, deps, "d", owner=stt[0], dval=16 * stt[1], ndma=n))
        self.by_eng[eng].append(oid)
        return oid

    def region_begin(self, reg_key, thr):
        rid = len(self.regions)
        self.regions.append((reg_key, thr))
        for e in ENGS:
            oid = len(self.ops)
            self.ops.append(Op(e, rid, set(), "rb"))
            self.by_eng[e].append(oid)
        self.cur_region = rid

    def region_end(self):
        for e in ENGS:
            oid = len(self.ops)
            self.ops.append(Op(e, self.cur_region, set(), "re"))
            self.by_eng[e].append(oid)
        self.cur_region = None

    def reg_load(self, eng, reg_key, ap, reads):
        def fn(h, eng=eng):
            r = self.regs.setdefault((eng, reg_key), None)
            if r is None:
                r = h.alloc_register(f"r_{reg_key}_{eng}")
                self.regs[(eng, reg_key)] = r
            return h.reg_load(r, ap)
        return self.add(eng, fn, reads=reads)

    def barrier(self):
        f = set()
        for e in ENGS:
            for oid in reversed(self.by_eng[e]):
                if self.ops[oid].kind in ("c", "d"):
                    f.add(oid)
                    break
        f.update(v[2] for v in self.dstate.values() if v[2] is not None)
        self.fence = f

    def emit(self, nc):
        ops = self.ops
        needed = set()
        for op in ops:
            for d in op.deps:
                if ops[d].kind == "c":
                    needed.add(d)
        cnt = {e: 0 for e in ENGS}
        rinfo = {}
        for e in ENGS:
            cur = None
            for oid in self.by_eng[e]:
                op = ops[oid]
                if op.kind == "rb":
                    cur = [0, cnt[e], 0]
                    rinfo[(e, op.fn)] = cur
                elif op.kind == "re":
                    cur[2] = cnt[e] - cur[1]
                    cur = None
                else:
                    if cur is not None:
                        cur[0] += 1
                    if op.kind == "c" and oid in needed:
                        cnt[e] += 1
                        op.ms = cnt[e]
        with contextlib.ExitStack() as st:
            esem = {e: st.enter_context(nc.semaphore(f"s_{e}")) for e in ENGS}
            dsem = [st.enter_context(nc.semaphore(f"d_{i}")) for i in range(self.ndsem)]
            block = st.enter_context(nc.Block())

            def token(d):
                o = ops[d]
                if o.kind == "c":
                    return ("e", o.eng), o.ms
                return ("d", o.owner), o.dval

            def run(e, h):
                seen = {}
                snap = None
                guard = None
                for oid in self.by_eng[e]:
                    op = ops[oid]
                    if op.kind == "rb":
                        info = rinfo[(e, op.fn)]
                        if info[0] == 0:
                            continue
                        reg_key, thr = self.regions[op.fn]
                        r = self.regs[(e, reg_key)]
                        g1 = h.If_lt(r, thr + 1)
                        g1.__enter__()
                        if info[2] > 0:
                            if info[1] > 0:
                                h.wait_ge(esem[e], info[1])
                            h.sem_inc(esem[e], info[2])
                        g1.__exit__(None, None, None)
                        guard = h.Else()
                        guard.__enter__()
                        snap = dict(seen)
                        continue
                    if op.kind == "re":
                        if rinfo[(e, op.fn)][0] == 0:
                            continue
                        guard.__exit__(None, None, None)
                        guard = None
                        seen = snap
                        snap = None
                        continue
                    want = {}
                    for d in op.deps:
                        k, v = token(d)
                        if v > want.get(k, 0):
                            want[k] = v
                    for k, v in want.items():
                        if seen.get(k, 0) >= v:
                            continue
                        seen[k] = v
                        s = esem[k[1]] if k[0] == "e" else dsem[k[1]]
                        h.wait_ge(s, v)
                    r = op.fn(h)
                    if op.kind == "c":
                        if op.ms is not None:
                            r.then_inc(esem[e], 1)
                    else:
                        if not isinstance(r, (list, tuple)):
                            r = [r]
                        assert len(r) == op.ndma, (len(r), op.ndma)
                        for ins in r:
                            ins.then_inc(dsem[op.owner], 16)

            block.tensor(lambda h: run("pe", h))
            block.scalar(lambda h: run("act", h))
            block.vector(lambda h: run("dve", h))
            block.gpsimd(lambda h: run("pool", h))
            block.sync(lambda h: run("sp", h))
        return cnt


class Rot:
    def __init__(self, items):
        self.items = items
        self.i = 0

    def next(self):
        it = self.items[self.i % len(self.items)]
        self.i += 1
        return it


def f_mm(out, pairs):
    def fn(h):
        n = len(pairs)
        ins = None
        for i, (l, r) in enumerate(pairs):
            ins = h.matmul(out, lhsT=l, rhs=r, start=(i == 0), stop=(i == n - 1))
        return ins
    return fn


def f_mm_multi(groups):
    def fn(h):
        ins = None
        for out, pairs in groups:
            n = len(pairs)
            for i, (l, r) in enumerate(pairs):
                ins = h.matmul(out, lhsT=l, rhs=r, start=(i == 0), stop=(i == n - 1))
        return ins
    return fn


def f_tr(items, ident):
    def fn(h):
        ins = None
        for o, i in items:
            ins = h.transpose(out=o, in_=i, identity=ident)
        return ins
    return fn


def f_act(out, in_, func, bias=None, scale=None):
    kw = {}
    if bias is not None:
        kw["bias"] = bias
    if scale is not None:
        kw["scale"] = scale
    return lambda h: h.activation(out=out, in_=in_, func=func, **kw)


def f_tt(out, in0, in1, op):
    return lambda h: h.tensor_tensor(out=out, in0=in0, in1=in1, op=op)


def f_ts(out, in0, s1, s2, op0, op1=None):
    if op1 is None:
        return lambda h: h.tensor_scalar(out=out, in0=in0, scalar1=s1, scalar2=None, op0=op0)
    return lambda h: h.tensor_scalar(out=out, in0=in0, scalar1=s1, scalar2=s2, op0=op0, op1=op1)


def f_stt(out, in0, scalar, in1, op0, op1):
    return lambda h: h.scalar_tensor_tensor(out=out, in0=in0, scalar=scalar, in1=in1, op0=op0, op1=op1)


def f_copy(out, in_):
    return lambda h: h.tensor_copy(out=out, in_=in_)


def f_dma(pairs):
    return lambda h: [h.dma_start(out=o, in_=i) for o, i in pairs]


def f_memset(ap, v):
    return lambda h: h.memset(ap, v)


class Ctx:
    def __init__(self, nc):
        self.nc = nc
        self.P = Prog()
        self.root = contextlib.ExitStack()
        self.stack = self.root
        self.uid = 0

    def sb(self, shape, dt, name=None, stack=None):
        self.uid += 1
        name = (name or "t") + f"_{self.uid}"
        return (stack or self.stack).enter_context(self.nc.sbuf_tensor(name, shape, dt))

    def ps(self, shape, dt, name=None):
        self.uid += 1
        name = (name or "p") + f"_{self.uid}"
        return self.root.enter_context(self.nc.psum_tensor(name, shape, dt))

    def dram(self, name, shape, dt, kind):
        return self.nc.dram_tensor(name, shape, dt, kind=kind).ap()

    def rot(self, n, shape, dt, name, stack=None):
        return Rot([(self.sb(shape, dt, name, stack), Buf(name, key=f"{name}{i}")) for i in range(n)])


def setup_common(c, ident_d):
    P = c.P
    c.pm = Rot([(c.ps([128, 512], F32, "pm"), Buf("pm")) for _ in range(6)])
    c.pt = Rot([(c.ps([128, 1024], BF16, "pt"), Buf("pt")) for _ in range(2)])
    c.idf = c.sb([128, 128], F32, "idf")
    c.idb = c.sb([128, 128], BF16, "idb")
    c.b_idf = Buf("idf")
    c.b_idb = Buf("idb")
    P.dma("sp", f_dma([(c.idf[:], ident_d)]), c.b_idf, writes=[c.b_idf])
    P.add("dve", f_copy(c.idb[:], c.idf[:]), reads=[c.b_idf], writes=[c.b_idb])
    c.stage = None
    c.xb = c.rot(2, [128, 1024], BF16, "xb")
    c.stat = c.rot(4, [128, 16], F32, "stat")
    c.eps_t = c.sb([128, 2], F32, "eps")
    c.b_eps = Buf("eps")
    P.add("dve", f_memset(c.eps_t[:], EPS), writes=[c.b_eps])


def load_const(c, dram_ap, shape, dt=F32, name="const", eng="sp"):
    t = c.sb(shape, dt, name)
    b = Buf(name)
    full = t[:] if len(shape) == 2 else t[:]
    c.P.dma(eng, f_dma([(full, dram_ap)]), b, writes=[b])
    return t, b


def tile_to_T(c, src_ap, src_buf, dstT, dst_bufs, t, nchunk=8):
    P = c.P
    xb, b_xb = c.xb.next()
    P.add("act", f_act(xb[:, 0:nchunk * 128], src_ap, AF.Copy), reads=[src_buf], writes=[b_xb])
    pt, b_pt = c.pt.next()
    ptv = pt[:].rearrange("p (c n) -> p c n", n=128)
    P.add("pe", f_tr([(ptv[:, ch, :], xb[:, ch * 128:(ch + 1) * 128]) for ch in range(nchunk)], c.idb[:]),
          reads=[b_xb, c.b_idb], writes=[b_pt])
    P.add("dve", f_copy(dstT[:, 0:nchunk, t * 128:(t + 1) * 128], ptv[:, 0:nchunk, :]),
          reads=[b_pt], writes=[dst_bufs[t]])


def load_w(c, w_ap, slot_view, buf, nsplit=1):
    src = w_ap.rearrange("(c p) n -> p c n", p=128)
    kc = src.shape[1]
    if nsplit == 1 or kc < nsplit:
        c.P.dma("pool", f_dma([(slot_view, src)]), buf, writes=[buf])
    else:
        step = (kc + nsplit - 1) // nsplit
        pairs = [(slot_view[:, i:min(i + step, kc), :], src[:, i:min(i + step, kc), :]) for i in range(0, kc, step)]
        c.P.dma("pool", f_dma(pairs), buf, writes=[buf], n=len(pairs))


def ln_inplace(c, x_ap, x_buf, gam, bet, b_gb, tmp_rot):
    P = c.P
    st, b_st = c.stat.next()
    P.add("dve", lambda h: h.bn_stats(out=st[:, 0:6], in_=x_ap[:, 0:512]), reads=[x_buf], writes=[b_st])
    P.add("dve", lambda h: h.bn_stats(out=st[:, 6:12], in_=x_ap[:, 512:1024]), reads=[x_buf, b_st], writes=[b_st])
    mv, b_mv = c.stat.next()
    P.add("dve", lambda h: h.bn_aggr(out=mv[:, 0:2], in_=st[:, 0:12]), reads=[b_st], writes=[b_mv])
    P.add("act", f_act(mv[:, 2:3], mv[:, 1:2], AF.Sqrt, bias=c.eps_t[:, 0:1]), reads=[b_mv, c.b_eps], writes=[b_mv])
    P.add("dve", lambda h: h.reciprocal(out=mv[:, 2:3], in_=mv[:, 2:3]), reads=[b_mv], writes=[b_mv])
    P.add("dve", f_stt(mv[:, 3:4], mv[:, 0:1], -1.0, mv[:, 2:3], ALU.mult, ALU.mult), reads=[b_mv], writes=[b_mv])
    tmp, b_tmp = tmp_rot.next()
    P.add("act", f_act(tmp[:], x_ap, AF.Identity, bias=mv[:, 3:4], scale=mv[:, 2:3]),
          reads=[x_buf, b_mv], writes=[b_tmp])
    P.add("dve", f_tt(tmp[:], tmp[:], gam[:], ALU.mult), reads=[b_tmp, b_gb], writes=[b_tmp])
    P.add("dve", f_tt(x_ap, tmp[:], bet[:], ALU.add), reads=[b_tmp, b_gb], writes=[x_buf])


def ffn_unit(c, st, xT, xT_bufs, w1, dh, w2, yacc, yacc_bufs, subs, gate=None):
    P = c.P
    maxsub = max(n for _, n in subs)
    actT = c.sb([128, maxsub, GRP], BF16, "actT", st)
    act_bufs = [[Buf("act") for _ in range(GT)] for _ in range(maxsub)]
    w1s = c.rot(2, [128, 8, 2, 256], BF16, "w1s", st)
    w2s = c.rot(2, [128, maxsub, 512], BF16, "w2s", st)
    sg = c.rot(2, [128, 512], BF16, "sg", st)
    for (j0, nj) in subs:
        assert nj % 2 == 0
        for blk in range(nj // 2):
            jj = j0 + blk * 2
            ws, b_ws = w1s.next()
            srcg = w1[:, jj * 128: jj * 128 + 256].rearrange("(c p) n -> p c n", p=128)
            srcu = w1[:, dh + jj * 128: dh + jj * 128 + 256].rearrange("(c p) n -> p c n", p=128)
            P.dma("pool", f_dma([(ws[:, :, 0, :], srcg), (ws[:, :, 1, :], srcu)]), b_ws, writes=[b_ws], n=2)
            for ci in range(2):
                jl = blk * 2 + ci
                for tg in range(GRP // 512):
                    pg, b_pg = c.pm.next()
                    pu, b_pu = c.pm.next()
                    tb = xT_bufs[tg * 4:(tg + 1) * 4]
                    P.add("pe", f_mm_multi([
                        (pg[:], [(ws[:, k, 0, ci * 128:(ci + 1) * 128], xT[:, k, tg * 512:(tg + 1) * 512]) for k in range(8)]),
                        (pu[:], [(ws[:, k, 1, ci * 128:(ci + 1) * 128], xT[:, k, tg * 512:(tg + 1) * 512]) for k in range(8)]),
                    ]), reads=[b_ws] + tb, writes=[b_pg, b_pu])
                    s, b_s = sg.next()
                    P.add("act", f_act(s[:], pg[:], AF.Silu), reads=[b_pg], writes=[b_s])
                    P.add("dve", f_tt(actT[:, jl, tg * 512:(tg + 1) * 512], s[:], pu[:], ALU.mult),
                          reads=[b_s, b_pu], writes=act_bufs[jl][tg * 4:(tg + 1) * 4])
        for half in range(2):
            w2v, b_w2 = w2s.next()
            load_w(c, w2[j0 * 128:(j0 + nj) * 128, half * 512:(half + 1) * 512], w2v[:, 0:nj, :], b_w2, nsplit=2)
            for t in range(GT):
                po, b_po = c.pm.next()
                P.add("pe", f_mm(po[:], [(actT[:, j, t * 128:(t + 1) * 128], w2v[:, j, :]) for j in range(nj)]),
                      reads=[b_w2] + [act_bufs[j][t] for j in range(nj)], writes=[b_po])
                ya = yacc[:, t, half * 512:(half + 1) * 512]
                if gate is None:
                    P.add("dve", f_tt(ya, po[:], ya, ALU.add), reads=[b_po, yacc_bufs[t]], writes=[yacc_bufs[t]])
                else:
                    gfn, b_g = gate
                    P.add("dve", f_stt(ya, po[:], gfn(t), ya, ALU.mult, ALU.add),
                          reads=[b_po, yacc_bufs[t], b_g], writes=[yacc_bufs[t]])


def rope_tables(pos0, n):
    d = 256
    inv = (1.0 / (np.float32(10000.0) ** (np.arange(0, d, 2, dtype=np.float32) / np.float32(d)))).astype(np.float32)
    pos = np.arange(pos0, pos0 + n, dtype=np.float32)
    ang = (pos[:, None] * inv[None, :]).astype(np.float32)
    return np.ascontiguousarray(np.cos(ang).astype(np.float32).T), np.ascontiguousarray(np.sin(ang).astype(np.float32).T)


def retention_tables():
    gam = 1.0 - 2.0 ** (-5.0 - np.arange(RH, dtype=np.float64))
    n = np.arange(128)
    m = np.arange(128)
    dtab = np.zeros((128, RH, 128), np.float64)
    cn = np.zeros((128, RH), np.float64)
    kdec = np.zeros((128, RH), np.float64)
    for h in range(RH):
        g = gam[h]
        M, N = np.meshgrid(m, n, indexing="ij")
        same = (M // 64) == (N // 64)
        cross = (M < 64) & (N >= 64)
        dm = np.where(same, g ** np.abs(N - M), np.where(cross, g ** (N - M).clip(0), 0.0))
        cnh = g ** (n + 1.0)
        dtab[:, h, :] = dm / cnh[None, :] / 16.0
        cn[:, h] = cnh
        kdec[:, h] = g ** (127.0 - m) / 16.0
    cd = [float(g ** 128) for g in gam]
    return dtab.astype(np.float32), cn.astype(np.float32), kdec.astype(np.float32), cd


def load_ln(c, st, gd, bd, name):
    g = c.sb([128, 1024], F32, name + "g", st)
    b = c.sb([128, 1024], F32, name + "b", st)
    buf = Buf(name, key=name)
    c.P.dma("sp", f_dma([(g[:], gd), (b[:], bd)]), buf, writes=[buf], n=2)
    return g, b, buf


def store(c, io, dst_ap, src_ap, src_buf):
    ob = Buf("out")
    io["outs"].append(ob)
    c.P.dma("sp", f_dma([(dst_ap, src_ap)]), src_buf, reads=[src_buf], writes=[ob])


def ret_heads(c, rst, io, mode, g, xsrc, cosd, sind, S32, Sb, b_S32, b_Sb, consts, yT, b_yT, scratch=None):
    P = c.P
    own = mode == "own"
    dtab, b_dtab, cn, b_cn, kdec, b_kdec, cd = consts
    w_in = io["ret_w_in"]
    if scratch is not None:
        sb16 = scratch.bitcast(BF16)
        xT = sb16[:, 0:8192].rearrange("p (c n) -> p c n", c=8)
        qT = sb16[:, 8192:10240].rearrange("p (c n) -> p c n", c=2)
        kT = sb16[:, 10240:12288].rearrange("p (c n) -> p c n", c=2)
        cosT = scratch[:, 6144:7168]
        sinT = scratch[:, 7168:8192]
    else:
        xT = c.sb([128, 8, GRP], BF16, "xT", rst)
        cosT = c.sb([128, GRP], F32, "cosT", rst)
        sinT = c.sb([128, GRP], F32, "sinT", rst)
        qT = c.sb([128, 2, GRP], BF16, "qT", rst)
        kT = c.sb([128, 2, GRP], BF16, "kT", rst)
    b_xT = [Buf("xT") for _ in range(GT)]
    b_cs = Buf("cs", key="cs")
    P.dma("sp", f_dma([(cosT[:], cosd), (sinT[:], sind)]), b_cs, writes=[b_cs], n=2)
    for t in range(GT):
        stg, b_stg = c.stage.next()
        P.dma("sp", f_dma([(stg[:], xsrc[t * 128:(t + 1) * 128, :])]), b_stg, writes=[b_stg])
        tile_to_T(c, stg[:], b_stg, xT, b_xT, t)
    kd = c.sb([128, GT, 256], BF16, "kd", rst)
    Vh = c.sb([128, GT, 512], BF16, "Vh", rst)
    Gh = c.sb([128, GT, 512], BF16, "Gh", rst) if own else None
    wqk = c.rot(2, [128, 8, 256], BF16, "wqk", rst)
    wvg = c.rot(2, [128, 8, 512], BF16, "wvg", rst)
    rtmp = c.rot(4, [128, 512], F32, "rtmp", rst)
    PTr = c.rot(2, [128, 128], BF16, "PT", rst)
    osb = c.rot(3, [128, 512], F32, "osb", rst)
    ysb = c.rot(2, [128, 512], BF16, "ysb", rst)
    for h in range(RH):
        b_qT = [Buf("qT") for _ in range(GT)]
        b_kT = [Buf("kT") for _ in range(GT)]
        b_kd = [Buf("kd") for _ in range(GT)]
        b_V = [Buf("V") for _ in range(GT)]
        b_G = [Buf("G") for _ in range(GT)]
        for which in (("q", "k") if own else ("k",)):
            col0 = (0 if which == "q" else 1024) + h * 256
            ws, b_ws = wqk.next()
            load_w(c, w_in[:, col0:col0 + 256], ws[:], b_ws)
            dst = qT if which == "q" else kT
            b_dst = b_qT if which == "q" else b_kT
            for tg in range(GRP // 512):
                pa, b_pa = c.pm.next()
                pb, b_pb = c.pm.next()
                tb = b_xT[tg * 4:(tg + 1) * 4]
                tok = slice(tg * 512, (tg + 1) * 512)
                P.add("pe", f_mm_multi([
                    (pa[:], [(ws[:, k, 0:128], xT[:, k, tok]) for k in range(8)]),
                    (pb[:], [(ws[:, k, 128:256], xT[:, k, tok]) for k in range(8)]),
                ]), reads=[b_ws] + tb, writes=[b_pa, b_pb])
                t1, b_t1 = rtmp.next()
                t2, b_t2 = rtmp.next()
                P.add("dve", f_tt(t1[:], pa[:], cosT[:, tok], ALU.mult), reads=[b_pa, b_cs], writes=[b_t1])
                P.add("dve", f_tt(t2[:], pb[:], sinT[:, tok], ALU.mult), reads=[b_pb, b_cs], writes=[b_t2])
                P.add("dve", f_tt(dst[:, 0, tok], t1[:], t2[:], ALU.subtract), reads=[b_t1, b_t2],
                      writes=b_dst[tg * 4:(tg + 1) * 4])
                t3, b_t3 = rtmp.next()
                t4, b_t4 = rtmp.next()
                P.add("dve", f_tt(t3[:], pa[:], sinT[:, tok], ALU.mult), reads=[b_pa, b_cs], writes=[b_t3])
                P.add("dve", f_tt(t4[:], pb[:], cosT[:, tok], ALU.mult), reads=[b_pb, b_cs], writes=[b_t4])
                P.add("dve", f_tt(dst[:, 1, tok], t3[:], t4[:], ALU.add), reads=[b_t3, b_t4] + b_dst[tg * 4:(tg + 1) * 4],
                      writes=b_dst[tg * 4:(tg + 1) * 4])
        for t in range(GT):
            pt, b_pt = c.pt.next()
            ptv = pt[:].rearrange("p (c n) -> p c n", n=128)
            P.add("pe", f_tr([(ptv[:, ch, :], kT[:, ch, t * 128:(t + 1) * 128]) for ch in range(2)], c.idb[:]),
                  reads=[b_kT[t], c.b_idb], writes=[b_pt])
            P.add("act", f_act(kd[:, t, :], pt[:, 0:256], AF.Identity, scale=kdec[:, h:h + 1]),
                  reads=[b_pt, b_kdec], writes=[b_kd[t]])
        for which in (("v", "g") if own else ("v",)):
            col0 = (2048 if which == "v" else 4096) + h * 512
            ws, b_ws = wvg.next()
            load_w(c, w_in[:, col0:col0 + 512], ws[:], b_ws, nsplit=2)
            for t in range(GT):
                pv, b_pv = c.pm.next()
                P.add("pe", f_mm(pv[:], [(xT[:, k, t * 128:(t + 1) * 128], ws[:, k, :]) for k in range(8)]),
                      reads=[b_ws, b_xT[t]], writes=[b_pv])
                if which == "v":
                    P.add("act", f_act(Vh[:, t, :], pv[:], AF.Copy), reads=[b_pv], writes=[b_V[t]])
                else:
                    P.add("act", f_act(Gh[:, t, :], pv[:], AF.Silu), reads=[b_pv], writes=[b_G[t]])
        pend = None

        def epilogue(o, b_o, t):
            tk = slice(t * 128, (t + 1) * 128)
            st, b_st = c.stat.next()
            P.add("dve", lambda hh, st=st, o=o: hh.bn_stats(out=st[:, 0:6], in_=o[:]), reads=[b_o], writes=[b_st])
            P.add("dve", lambda hh, st=st: hh.bn_aggr(out=st[:, 8:10], in_=st[:, 0:6]), reads=[b_st], writes=[b_st])
            P.add("act", f_act(st[:, 10:11], st[:, 9:10], AF.Sqrt, bias=c.eps_t[:, 0:1]), reads=[b_st, c.b_eps], writes=[b_st])
            P.add("dve", lambda hh, st=st: hh.reciprocal(out=st[:, 10:11], in_=st[:, 10:11]), reads=[b_st], writes=[b_st])
            P.add("dve", f_stt(st[:, 11:12], st[:, 8:9], -1.0, st[:, 10:11], ALU.mult, ALU.mult), reads=[b_st], writes=[b_st])
            P.add("act", f_act(o[:], o[:], AF.Identity, bias=st[:, 11:12], scale=st[:, 10:11]),
                  reads=[b_o, b_st], writes=[b_o])
            y, b_y = ysb.next()
            P.add("dve", f_tt(y[:], o[:], Gh[:, t, :], ALU.mult), reads=[b_o, b_G[t]], writes=[b_y])
            pt, b_pt = c.pt.next()
            ptv = pt[:].rearrange("p (c n) -> p c n", n=128)
            P.add("pe", f_tr([(ptv[:, ch, :], y[:, ch * 128:(ch + 1) * 128]) for ch in range(4)], c.idb[:]),
                  reads=[b_y, c.b_idb], writes=[b_pt])
            P.add("act", f_act(yT[:, h * 4:(h + 1) * 4, tk], ptv[:, 0:4, :], AF.Copy), reads=[b_pt], writes=[b_yT[h][t]])

        for t in range(GT):
            tk = slice(t * 128, (t + 1) * 128)
            p0, b_p0 = c.pm.next()
            p1, b_p1 = c.pm.next()
            P.add("pe", f_mm_multi([(p0[:], [(kd[:, t, 0:128], Vh[:, t, :])]), (p1[:], [(kd[:, t, 128:256], Vh[:, t, :])])]),
                  reads=[b_kd[t], b_V[t]], writes=[b_p0, b_p1])
            if own:
                pS, b_pS = c.pm.next()
                P.add("pe", f_mm(pS[:, 0:128], [(kT[:, ch, tk], qT[:, ch, tk]) for ch in range(2)]),
                      reads=[b_kT[t], b_qT[t]], writes=[b_pS])
                PT, b_PT = PTr.next()
                P.add("dve", f_tt(PT[:], pS[:, 0:128], dtab[:, h, :], ALU.mult), reads=[b_pS, b_dtab], writes=[b_PT])
                po, b_po = c.pm.next()
                P.add("pe", f_mm(po[:], [(PT[:], Vh[:, t, :]), (qT[:, 0, tk], Sb[:, h, 0, :]), (qT[:, 1, tk], Sb[:, h, 1, :])]),
                      reads=[b_PT, b_V[t], b_qT[t], b_Sb[h]], writes=[b_po])
                o, b_o = osb.next()
                P.add("act", f_act(o[:], po[:], AF.Identity, scale=cn[:, h:h + 1]), reads=[b_po, b_cn], writes=[b_o])
            P.add("dve", f_stt(S32[:, h, 0, :], S32[:, h, 0, :], cd[h], p0[:], ALU.mult, ALU.add),
                  reads=[b_p0, b_S32[h]], writes=[b_S32[h]])
            P.add("dve", f_stt(S32[:, h, 1, :], S32[:, h, 1, :], cd[h], p1[:], ALU.mult, ALU.add),
                  reads=[b_p1, b_S32[h]], writes=[b_S32[h]])
            P.add("act", f_act(Sb[:, h, :, :], S32[:, h, :, :], AF.Copy), reads=[b_S32[h]], writes=[b_Sb[h]])
            if own:
                if pend is not None:
                    epilogue(*pend)
                pend = (o, b_o, t)
        if own and pend is not None:
            epilogue(*pend)


def build_layer0(c, io):
    P = c.P
    _, _, _, cd = retention_tables()
    dtab, b_dtab = load_const(c, io["dtab"], [128, RH, 128], F32, "dtab")
    cn, b_cn = load_const(c, io["cn"], [128, RH], F32, "cn")
    kdec, b_kdec = load_const(c, io["kdec"], [128, RH], F32, "kdec")
    consts = (dtab, b_dtab, cn, b_cn, kdec, b_kdec, cd)
    S32 = c.sb([128, RH, 2, 512], F32, "S32")
    Sb = c.sb([128, RH, 2, 512], BF16, "Sb")
    b_S32 = [Buf("S32") for _ in range(RH)]
    b_Sb = [Buf("Sb") for _ in range(RH)]
    for h in range(RH):
        P.add("dve", f_memset(S32[:, h, :, :], 0.0), writes=[b_S32[h]])
        P.add("dve", f_memset(Sb[:, h, :, :], 0.0), writes=[b_Sb[h]])
    w_out = io["ret_w_out"]

    passes = [("prev", 0), ("prev", 1), ("own", 0), ("own", 1)]
    for mode, g in passes:
        own = mode == "own"
        xsrc = (io["x_own"] if own else io["x_prev"])[g * GRP:(g + 1) * GRP, :]
        cosd = (io["cos_own"] if own else io["cos_prev"])[:, g * GRP:(g + 1) * GRP]
        sind = (io["sin_own"] if own else io["sin_prev"])[:, g * GRP:(g + 1) * GRP]
        P.barrier()
        if not own:
            with contextlib.ExitStack() as rst:
                ret_heads(c, rst, io, mode, g, xsrc, cosd, sind, S32, Sb, b_S32, b_Sb, consts, None, None)
            continue
        with contextlib.ExitStack() as gst:
            x1 = c.sb([128, GT, 1024], F32, "x1", gst)
            b_x1 = [Buf("x1") for _ in range(GT)]
            with contextlib.ExitStack() as yst:
                yT = c.sb([128, 16, GRP], BF16, "yT", yst)
                b_yT = [[Buf("yT") for _ in range(GT)] for _ in range(RH)]
                with contextlib.ExitStack() as rst:
                    ret_heads(c, rst, io, mode, g, xsrc, cosd, sind, S32, Sb, b_S32, b_Sb, consts, yT, b_yT)
                P.barrier()
                with contextlib.ExitStack() as mst:
                    gam, bet, b_gb = load_ln(c, mst, io["g_mix"], io["b_mix"], "lnmix")
                    tmpf = c.rot(2, [128, 1024], F32, "tmpf", mst)
                    wos = c.rot(2, [128, 16, 512], BF16, "wos", mst)
                    xst = c.rot(2, [128, 512], F32, "xst", mst)
                    for half in range(2):
                        wo, b_wo = wos.next()
                        load_w(c, w_out[:, half * 512:(half + 1) * 512], wo[:], b_wo, nsplit=2)
                        for t in range(GT):
                            pm_, b_pm = c.pm.next()
                            P.add("pe", f_mm(pm_[:], [(yT[:, ch, t * 128:(t + 1) * 128], wo[:, ch, :]) for ch in range(16)]),
                                  reads=[b_wo] + [b_yT[h][t] for h in range(RH)], writes=[b_pm])
                            xs, b_xs = xst.next()
                            P.dma("sp", f_dma([(xs[:], xsrc[t * 128:(t + 1) * 128, half * 512:(half + 1) * 512])]), b_xs, writes=[b_xs])
                            P.add("dve", f_stt(x1[:, t, half * 512:(half + 1) * 512], xs[:], ALPHA, pm_[:], ALU.mult, ALU.add),
                                  reads=[b_xs, b_pm], writes=[b_x1[t]])
                    for t in range(GT):
                        ln_inplace(c, x1[:, t, :], b_x1[t], gam, bet, b_gb, tmpf)
            P.barrier()
            with contextlib.ExitStack() as fst:
                gam, bet, b_gb = load_ln(c, fst, io["g_ffn"], io["b_ffn"], "lnffn")
                tmpf = c.rot(2, [128, 1024], F32, "tmpf", fst)
                x1T = c.sb([128, 8, GRP], BF16, "x1T", fst)
                b_x1T = [Buf("x1T") for _ in range(GT)]
                yacc = c.sb([128, GT, 1024], F32, "yacc", fst)
                b_ya = [Buf("yacc", key=f"ya{t}") for t in range(GT)]
                for t in range(GT):
                    tile_to_T(c, x1[:, t, :], b_x1[t], x1T, b_x1T, t)
                    P.add("act", f_act(yacc[:, t, :], x1[:, t, :], AF.Copy, scale=ALPHA), reads=[b_x1[t]], writes=[b_ya[t]])
                with contextlib.ExitStack() as ust:
                    ffn_unit(c, ust, x1T, b_x1T, io["ffn_w_in"], D_FF, io["ffn_w_out"], yacc, b_ya, [(0, 12), (12, 10)])
                for t in range(GT):
                    ln_inplace(c, yacc[:, t, :], b_ya[t], gam, bet, b_gb, tmpf)
                    r0 = g * GRP + t * 128
                    store(c, io, io["x2"][r0:r0 + 128, :], yacc[:, t, :], b_ya[t])
                P.barrier()
                with contextlib.ExitStack() as kst:
                    x2T = x1T
                    b_x2T = [Buf("x2T") for _ in range(GT)]
                    for t in range(GT):
                        tile_to_T(c, yacc[:, t, :], b_ya[t], x2T, b_x2T, t)
                    wkv = c.rot(2, [128, 8, 512], BF16, "wkv", kst)
                    ksb = c.rot(3, [128, 512], BF16, "ksb", kst)
                    for cb in range(2):
                        ws, b_ws = wkv.next()
                        load_w(c, io["sb_w_kv"][:, cb * 512:(cb + 1) * 512], ws[:], b_ws, nsplit=2)
                        for ch in range(4):
                            for tg in range(GRP // 512):
                                pk, b_pk = c.pm.next()
                                tok = slice(tg * 512, (tg + 1) * 512)
                                P.add("pe", f_mm(pk[:], [(ws[:, k, ch * 128:(ch + 1) * 128], x2T[:, k, tok]) for k in range(8)]),
                                      reads=[b_ws] + b_x2T[tg * 4:(tg + 1) * 4], writes=[b_pk])
                                ks, b_ks = ksb.next()
                                P.add("act", f_act(ks[:], pk[:], AF.Copy), reads=[b_pk], writes=[b_ks])
                                f0 = (cb * 4 + ch) * 128
                                t0 = g * GRP + tg * 512
                                store(c, io, io["KT"][f0:f0 + 128, t0:t0 + 512], ks[:], b_ks)
                    for cb in range(2):
                        ws, b_ws = wkv.next()
                        load_w(c, io["sb_w_kv"][:, 1024 + cb * 512:1024 + (cb + 1) * 512], ws[:], b_ws, nsplit=2)
                        for t in range(GT):
                            pv, b_pv = c.pm.next()
                            P.add("pe", f_mm(pv[:], [(x2T[:, k, t * 128:(t + 1) * 128], ws[:, k, :]) for k in range(8)]),
                                  reads=[b_ws, b_x2T[t]], writes=[b_pv])
                            ks, b_ks = ksb.next()
                            P.add("act", f_act(ks[:], pv[:], AF.Copy), reads=[b_pv], writes=[b_ks])
                            r0 = g * GRP + t * 128
                            store(c, io, io["V"][r0:r0 + 128, cb * 512:(cb + 1) * 512], ks[:], b_ks)


def make_nc_layer0():
    nc = bass.Bass("TRN2", target_bir_lowering=False)
    c = Ctx(nc)
    io = {"outs": []}

    def din(name, shape, dt=F32):
        io[name] = c.dram(name, shape, dt, "ExternalInput")
    din("x_own", [TOK, D]); din("x_prev", [TOK, D])
    din("cos_own", [128, TOK]); din("sin_own", [128, TOK]); din("cos_prev", [128, TOK]); din("sin_prev", [128, TOK])
    din("ret_w_in", [D, 6144]); din("ret_w_out", [2048, D])
    din("ffn_w_in", [D, 2 * D_FF]); din("ffn_w_out", [D_FF, D]); din("sb_w_kv", [D, 2048])
    din("g_mix", [128, D]); din("b_mix", [128, D]); din("g_ffn", [128, D]); din("b_ffn", [128, D])
    din("dtab", [128, RH, 128]); din("cn", [128, RH]); din("kdec", [128, RH]); din("ident", [128, 128])
    io["x2"] = c.dram("x2", [TOK, D], F32, "ExternalOutput")
    io["KT"] = c.dram("KT", [D, TOK], BF16, "ExternalOutput")
    io["V"] = c.dram("V", [TOK, D], BF16, "ExternalOutput")
    with c.root:
        setup_common(c, io["ident"])
        build_layer0(c, io)
        c.P.add("sp", lambda h: None, reads=io["outs"])
        cnt = c.P.emit(nc)
    return nc, cnt


def layer0_inputs(inputs, core):
    b, half = core // 2, core % 2
    x = inputs["x"]
    dtab, cn, kdec, _ = retention_tables()
    co, so = rope_tables(half * TOK, TOK)
    cp, sp_ = rope_tables(0, TOK)
    f = np.float32
    bc = lambda v: np.ascontiguousarray(np.broadcast_to(np.asarray(v, f)[None, :], (128, D)))
    return {
        "x_own": np.ascontiguousarray(x[b, half * TOK:(half + 1) * TOK]),
        "x_prev": np.ascontiguousarray(x[b, 0:TOK]) if half == 1 else np.zeros((TOK, D), f),
        "cos_own": co, "sin_own": so, "cos_prev": cp, "sin_prev": sp_,
        "ret_w_in": np.ascontiguousarray(inputs["ret_w_in"][0]),
        "ret_w_out": np.ascontiguousarray(inputs["ret_w_out"][0]),
        "ffn_w_in": np.ascontiguousarray(inputs["ffn_w_in"][0]),
        "ffn_w_out": np.ascontiguousarray(inputs["ffn_w_out"][0]),
        "sb_w_kv": np.ascontiguousarray(inputs["sb_w_kv"]),
        "g_mix": bc(inputs["ln_mix_g"][0]), "b_mix": bc(inputs["ln_mix_b"][0]),
        "g_ffn": bc(inputs["ln_ffn_g"][0]), "b_ffn": bc(inputs["ln_ffn_b"][0]),
        "dtab": dtab, "cn": cn, "kdec": kdec, "ident": np.eye(128, dtype=f),
    }


NEG = -30000.0


def attn_tables():
    s = np.arange(128)[:, None]
    t = np.arange(512)[None, :]
    m01 = np.zeros((128, 4, 512), np.float32)
    for di in range(4):
        m01[:, di, :] = (di * 128 + s < t).astype(np.float32)
    negm = (1.0 - m01) * NEG
    j = np.arange(128)[:, None]
    ss = np.arange(128)[None, :]
    m1 = -(j >= ss).astype(np.float32)
    negones = -np.ones((128, 128), np.float32)
    return m01, negm, m1, negones


def f_mm1(out, l, r, start, stop):
    return lambda h: h.matmul(out, lhsT=l, rhs=r, start=start, stop=stop)


ATT_THR = 130.0
ATT_FREE = 6


def attention_group(c, ast, io, g, oT, b_oT, x2T, b_x2T, consts):
    P = c.P
    m01, b_m01, negm, b_negm, M1, b_M1, NO, b_NO = consts
    nslot_blocks = 16 + g * 8 + 8
    KTr = c.rot(2, [128, 4096], BF16, "KTh", ast)
    Vr = c.rot(2, [128, 32, 128], BF16, "Vhp", ast)
    QTr = c.rot(2, [128, GRP], BF16, "QTh", ast)
    wqr = c.rot(2, [128, 8, 128], BF16, "wq", ast)
    esb = c.rot(4, [128, 512], F32, "esb", ast)
    spb = c.rot(8, [128, 512], BF16, "spb", ast)
    wbf = c.rot(4, [128, 512], BF16, "wbf", ast)
    Rbf = [c.rot(2, [128, 512], BF16, f"Rbf{i}", ast) for i in range(4)]
    rmn = c.rot(2, [128, 512], BF16, "rmn", ast)
    zt = c.sb([128, 128], BF16, "zt", ast)
    b_zt = Buf("zt")
    P.add("dve", f_memset(zt[:], 0.0), writes=[b_zt])
    chk = c.sb([1, 8], F32, "chk", ast)
    chk_i = c.sb([1, 8], I32, "chk_i", ast)
    b_chk = Buf("chk")
    pmi = c.pm.items
    accs = [c.pt.items[0], c.pt.items[1], pmi[4], pmi[5]]
    pma = Rot(pmi[0:4])
    NQ = GRP // 512
    for hp in range(8):
        KT, b_KT = KTr.next()
        V, b_V = Vr.next()
        ns = nslot_blocks * 128
        P.dma("sp", f_dma([(KT[:, 0:ns], io["KTs"][hp * 128:(hp + 1) * 128, 0:ns])]), b_KT, writes=[b_KT])
        P.dma("sp", f_dma([(V[:, 0:nslot_blocks, :],
                            io["Vs"][0:ns, hp * 128:(hp + 1) * 128].rearrange("(k p) n -> p k n", p=128))]),
              b_V, writes=[b_V])
        wq, b_wq = wqr.next()
        load_w(c, io["sb_w_q"][:, hp * 128:(hp + 1) * 128], wq[:], b_wq)
        QT, b_QT = QTr.next()
        for tg in range(NQ):
            pq, b_pq = c.pm.next()
            tok = slice(tg * 512, (tg + 1) * 512)
            P.add("pe", f_mm(pq[:], [(wq[:, k, :], x2T[:, k, tok]) for k in range(8)]),
                  reads=[b_wq] + b_x2T[tg * 4:(tg + 1) * 4], writes=[b_pq])
            P.add("act", f_act(QT[:, tok], pq[:], AF.Copy, scale=0.125), reads=[b_pq], writes=[b_QT])
        streams = [(qt, hd) for qt in range(NQ) for hd in range(2)]
        nkbs = {qt: 16 + g * 8 + qt * 4 + 4 for qt in range(NQ)}
        nmax = max(nkbs.values())
        Rcur = {}

        def block(i):
            live = [(k, qt, hd) for k, (qt, hd) in enumerate(streams) if i < nkbs[qt]]
            sps = {}
            for (k, qt, hd) in live:
                kb = nkbs[qt] - 1 - i
                pb = 64 * hd
                tok = slice(qt * 512, (qt + 1) * 512)
                di = kb - (nkbs[qt] - 4)
                pz, b_pz = pma.next()
                P.add("pe", f_mm(pz[:], [(KT[pb:pb + 64, kb * 128:(kb + 1) * 128], QT[pb:pb + 64, tok])]),
                      reads=[b_KT, b_QT], writes=[b_pz])
                e, b_e = esb.next()
                P.add("act", f_act(e[:], pz[:], AF.Exp), reads=[b_pz], writes=[b_e])
                sp_, b_sp = spb.next()
                P.add("act", f_act(sp_[:], e[:], AF.Ln, bias=1.0), reads=[b_e], writes=[b_sp])
                if di >= 0:
                    P.add("dve", f_tt(sp_[:], sp_[:], m01[:, di, :], ALU.mult), reads=[b_sp, b_m01], writes=[b_sp])
                sps[k] = (sp_, b_sp)
            ws = {}
            for (k, qt, hd) in live:
                kb = nkbs[qt] - 1 - i
                pb = 64 * hd
                tok = slice(qt * 512, (qt + 1) * 512)
                di = kb - (nkbs[qt] - 4)
                sp_, b_sp = sps[k]
                pE, b_pE = pma.next()
                pairs = [(KT[pb:pb + 64, kb * 128:(kb + 1) * 128], QT[pb:pb + 64, tok]), (M1[:], sp_[:])]
                reads = [b_KT, b_QT, b_M1, b_sp]
                if i > 0:
                    rb, b_rb = Rcur[k]
                    pairs.append((NO[:], rb[:]))
                    reads += [b_NO, b_rb]
                if di >= 0:
                    pairs.append((c.idb[:], negm[:, di, :]))
                    reads += [c.b_idb, b_negm]
                P.add("pe", f_mm(pE[:], pairs), reads=reads, writes=[b_pE])
                w, b_w = wbf.next()
                P.add("act", f_act(w[:], pE[:], AF.Exp), reads=[b_pE], writes=[b_w])
                ws[k] = (w, b_w)
                if i == 0:
                    Rcur[k] = (sp_, b_sp)
                else:
                    ro, b_ro = Rcur[k]
                    rb, b_rb = Rbf[k].next()
                    P.add("dve", f_tt(rb[:], ro[:], sp_[:], ALU.add), reads=[b_sp, b_ro], writes=[b_rb])
                    Rcur[k] = (rb, b_rb)
            for (k, qt, hd) in live:
                kb = nkbs[qt] - 1 - i
                w, b_w = ws[k]
                acc_t, b_acc = accs[k]
                acc = acc_t[:].bitcast(F32) if k < 2 else acc_t[:]
                P.add("pe", f_mm1(acc, V[:, kb, :], w[:], i == 0, False), reads=[b_V, b_w], writes=[b_acc])

        def checkpoint():
            r0, b_r0 = Rcur[0]
            cur, b_cur = r0, b_r0
            for k in range(1, len(streams)):
                rk, b_rk = Rcur[k]
                rm, b_rm = rmn.next()
                P.add("dve", f_tt(rm[:], cur[:], rk[:], ALU.min), reads=[b_cur, b_rk], writes=[b_rm])
                cur, b_cur = rm, b_rm
            pc, b_pc = pma.next()
            P.add("pe", f_mm(pc[0:1, :], [(NO[:, 0:1], cur[:])]), reads=[b_NO, b_cur], writes=[b_pc])
            P.add("dve", lambda h: h.reduce_max(out=chk[0:1, 0:1], in_=pc[0:1, :], axis=mybir.AxisListType.X),
                  reads=[b_pc, b_chk], writes=[b_chk])
            P.add("dve", f_ts(chk_i[0:1, 0:1], chk[0:1, 0:1], 1000.0 + ATT_THR, 0.0, ALU.add, ALU.max), reads=[b_chk], writes=[b_chk])
            for eng in ("pe", "act", "dve"):
                P.reg_load(eng, "att", chk_i[0:1, 0:1], reads=[b_chk])

        for i in range(ATT_FREE):
            block(i)
        checkpoint()
        for i in range(ATT_FREE, nmax):
            P.region_begin("att", 1000)
            block(i)
            if i < nmax - 1:
                checkpoint()
            P.region_end()
        for k, (qt, hd) in enumerate(streams):
            pb = 64 * hd
            tok = slice(qt * 512, (qt + 1) * 512)
            acc_t, b_acc = accs[k]
            acc = acc_t[:].bitcast(F32) if k < 2 else acc_t[:]
            P.add("pe", f_mm1(acc, zt[:], m01[:, 0, :], False, True), reads=[b_zt, b_m01], writes=[b_acc])
            P.add("act", f_act(oT[pb:pb + 64, hp, tok], acc[pb:pb + 64, :], AF.Copy), reads=[b_acc],
                  writes=b_oT[qt * 4:(qt + 1) * 4])


def build_layer1(c, io):
    P = c.P
    c.one_t = c.sb([128, 2], F32, "one")
    P.add("dve", f_memset(c.one_t[:], 1.0), writes=[c.b_eps])
    m01, b_m01 = load_const(c, io["m01"], [128, 4, 512], BF16, "m01", eng="pool")
    negm, b_negm = load_const(c, io["negm"], [128, 4, 512], BF16, "negm", eng="pool")
    M1, b_M1 = load_const(c, io["M1"], [128, 128], BF16, "M1", eng="pool")
    NO, b_NO = load_const(c, io["NO"], [128, 128], BF16, "NO", eng="pool")
    aconsts = (m01, b_m01, negm, b_negm, M1, b_M1, NO, b_NO)
    wr, b_wr = load_const(c, io["moe_w_router"].rearrange("(c p) n -> p c n", p=128), [128, 8, N_EXP], F32, "wr")
    for g in range(TOK // GRP):
        xsrc = io["x2"][g * GRP:(g + 1) * GRP, :]
        P.barrier()
        with contextlib.ExitStack() as gst:
            x3 = c.sb([128, GT, 1024], F32, "x3", gst)
            b_x3 = [Buf("x3") for _ in range(GT)]
            with contextlib.ExitStack() as ast:
                x2T = c.sb([128, 8, GRP], BF16, "x2T", ast)
                b_x2T = [Buf("x2T") for _ in range(GT)]
                oT = c.sb([128, 8, GRP], BF16, "oT", ast)
                b_oT = [Buf("oT") for _ in range(GT)]
                for t in range(GT):
                    stg, b_stg = c.stage.next()
                    P.dma("sp", f_dma([(stg[:], xsrc[t * 128:(t + 1) * 128, :])]), b_stg, writes=[b_stg])
                    tile_to_T(c, stg[:], b_stg, x2T, b_x2T, t)
                with contextlib.ExitStack() as hst:
                    attention_group(c, hst, io, g, oT, b_oT, x2T, b_x2T, aconsts)
                P.barrier()
                with contextlib.ExitStack() as mst:
                    gam, bet, b_gb = load_ln(c, mst, io["g_mix"], io["b_mix"], "lnmix")
                    tmpf = c.rot(2, [128, 1024], F32, "tmpf", mst)
                    wos = c.rot(2, [128, 8, 512], BF16, "wos", mst)
                    xst = c.rot(2, [128, 512], F32, "xst", mst)
                    for half in range(2):
                        wo, b_wo = wos.next()
                        load_w(c, io["sb_w_out"][:, half * 512:(half + 1) * 512], wo[:], b_wo, nsplit=2)
                        for t in range(GT):
                            pm_, b_pm = c.pm.next()
                            P.add("pe", f_mm(pm_[:], [(oT[:, ch, t * 128:(t + 1) * 128], wo[:, ch, :]) for ch in range(8)]),
                                  reads=[b_wo, b_oT[t]], writes=[b_pm])
                            xs, b_xs = xst.next()
                            P.dma("sp", f_dma([(xs[:], xsrc[t * 128:(t + 1) * 128, half * 512:(half + 1) * 512])]), b_xs, writes=[b_xs])
                            P.add("dve", f_stt(x3[:, t, half * 512:(half + 1) * 512], xs[:], ALPHA, pm_[:], ALU.mult, ALU.add),
                                  reads=[b_xs, b_pm], writes=[b_x3[t]])
                    for t in range(GT):
                        ln_inplace(c, x3[:, t, :], b_x3[t], gam, bet, b_gb, tmpf)
            P.barrier()
            with contextlib.ExitStack() as fst:
                gam, bet, b_gb = load_ln(c, fst, io["g_ffn"], io["b_ffn"], "lnffn")
                tmpf = c.rot(2, [128, 1024], F32, "tmpf", fst)
                x3T = c.sb([128, 8, GRP], BF16, "x3T", fst)
                b_x3T = [Buf("x3T") for _ in range(GT)]
                yacc = c.sb([128, GT, 1024], F32, "yacc", fst)
                b_ya = [Buf("yacc", key=f"ya{t}") for t in range(GT)]
                gates = c.sb([128, GT, N_EXP], F32, "gates", fst)
                b_gt = Buf("gates")
                xTf = c.rot(2, [128, 8, 128], F32, "xTf", fst)
                lgr = c.rot(2, [128, 32], F32, "lg", fst)
                for t in range(GT):
                    tile_to_T(c, x3[:, t, :], b_x3[t], x3T, b_x3T, t)
                    P.add("act", f_act(yacc[:, t, :], x3[:, t, :], AF.Copy, scale=ALPHA), reads=[b_x3[t]], writes=[b_ya[t]])
                    xf, b_xf = xTf.next()
                    for hf in range(2):
                        pp, b_pp = c.pm.next()
                        ppv = pp[:].rearrange("p (c n) -> p c n", n=128)
                        P.add("pe", f_tr([(ppv[:, ch, :], x3[:, t, (hf * 4 + ch) * 128:(hf * 4 + ch + 1) * 128]) for ch in range(4)], c.idf[:]),
                              reads=[b_x3[t], c.b_idf], writes=[b_pp])
                        P.add("dve", f_copy(xf[:, hf * 4:(hf + 1) * 4, :], ppv[:, 0:4, :]), reads=[b_pp], writes=[b_xf])
                    pl, b_pl = c.pm.next()
                    P.add("pe", f_mm(pl[:, 0:N_EXP], [(xf[:, ch, :], wr[:, ch, :]) for ch in range(8)]),
                          reads=[b_xf, b_wr], writes=[b_pl])
                    lg, b_lg = lgr.next()
                    P.add("dve", f_copy(lg[:, 0:8], pl[:, 0:N_EXP]), reads=[b_pl], writes=[b_lg])
                    P.add("dve", lambda h, lg=lg: h.max(out=lg[:, 8:16], in_=lg[:, 0:8]), reads=[b_lg], writes=[b_lg])
                    P.add("dve", f_ts(lg[:, 16:17], lg[:, 8:9], -1.0, None, ALU.mult), reads=[b_lg], writes=[b_lg])
                    P.add("act", f_act(lg[:, 24:32], lg[:, 0:8], AF.Exp, bias=lg[:, 16:17]), reads=[b_lg], writes=[b_lg])
                    P.add("dve", f_ts(gates[:, t, :], lg[:, 0:8], lg[:, 9:10], None, ALU.is_ge), reads=[b_lg, b_gt], writes=[b_gt])
                    P.add("dve", f_tt(gates[:, t, :], gates[:, t, :], lg[:, 24:32], ALU.mult), reads=[b_lg, b_gt], writes=[b_gt])
                    P.add("dve", lambda h, lg=lg, t=t: h.reduce_sum(out=lg[:, 17:18], in_=gates[:, t, :], axis=mybir.AxisListType.X),
                          reads=[b_gt, b_lg], writes=[b_lg])
                    P.add("dve", lambda h, lg=lg: h.reciprocal(out=lg[:, 17:18], in_=lg[:, 17:18]), reads=[b_lg], writes=[b_lg])
                    P.add("dve", f_ts(gates[:, t, :], gates[:, t, :], lg[:, 17:18], None, ALU.mult), reads=[b_lg, b_gt], writes=[b_gt])
                for e in range(N_EXP):
                    with contextlib.ExitStack() as ust:
                        ffn_unit(c, ust, x3T, b_x3T, io["moe_w_in"][e * D:(e + 1) * D, :], D_EXP,
                                 io["moe_w_out"][e * D_EXP:(e + 1) * D_EXP, :], yacc, b_ya, [(0, 14), (14, 14)],
                                 gate=((lambda t, e=e: gates[:, t, e:e + 1]), b_gt))
                    P.barrier()
                for t in range(GT):
                    ln_inplace(c, yacc[:, t, :], b_ya[t], gam, bet, b_gb, tmpf)
                    r0 = g * GRP + t * 128
                    store(c, io, io["out"][r0:r0 + 128, :], yacc[:, t, :], b_ya[t])


def make_nc_layer1():
    nc = bass.Bass("TRN2", target_bir_lowering=False)
    c = Ctx(nc)
    io = {"outs": []}

    def din(name, shape, dt=F32):
        io[name] = c.dram(name, shape, dt, "ExternalInput")
    din("x2", [TOK, D]); din("KTs", [D, 2 * TOK], BF16); din("Vs", [2 * TOK, D], BF16)
    din("sb_w_q", [D, D]); din("sb_w_out", [D, D])
    din("moe_w_router", [D, N_EXP]); din("moe_w_in", [N_EXP * D, 2 * D_EXP]); din("moe_w_out", [N_EXP * D_EXP, D])
    din("g_mix", [128, D]); din("b_mix", [128, D]); din("g_ffn", [128, D]); din("b_ffn", [128, D])
    din("m01", [128, 4, 512]); din("negm", [128, 4, 512]); din("M1", [128, 128]); din("NO", [128, 128]); din("ident", [128, 128])
    io["out"] = c.dram("out", [TOK, D], F32, "ExternalOutput")
    with c.root:
        setup_common(c, io["ident"])
        build_layer1(c, io)
        c.P.add("sp", lambda h: None, reads=io["outs"])
        cnt = c.P.emit(nc)
    return nc, cnt


def layer1_inputs(inputs, core, x2, KTs, Vs):
    f = np.float32
    m01, negm, m1, negones = attn_tables()
    bc = lambda v: np.ascontiguousarray(np.broadcast_to(np.asarray(v, f)[None, :], (128, D)))
    return {
        "x2": x2, "KTs": KTs, "Vs": Vs,
        "sb_w_q": np.ascontiguousarray(inputs["sb_w_q"][0]), "sb_w_out": np.ascontiguousarray(inputs["sb_w_out"][0]),
        "moe_w_router": np.ascontiguousarray(inputs["moe_w_router"][0]),
        "moe_w_in": np.ascontiguousarray(inputs["moe_w_in"][0]).reshape(N_EXP * D, 2 * D_EXP),
        "moe_w_out": np.ascontiguousarray(inputs["moe_w_out"][0]).reshape(N_EXP * D_EXP, D),
        "g_mix": bc(inputs["ln_mix_g"][1]), "b_mix": bc(inputs["ln_mix_b"][1]),
        "g_ffn": bc(inputs["ln_ffn_g"][1]), "b_ffn": bc(inputs["ln_ffn_b"][1]),
        "m01": m01, "negm": negm, "M1": m1, "NO": negones, "ident": np.eye(128, dtype=f),
    }


def layer0_group(c, io, L0, g_tok, xsrc, cosd, sind, xs, b_xs, slot0, vscale):
    P = c.P
    S32, Sb, b_S32, b_Sb, consts = L0
    scratch = xs.rearrange("p t d -> p (t d)")
    P.barrier()
    with contextlib.ExitStack() as yst:
        yT = c.sb([128, 16, GRP], BF16, "yT", yst)
        b_yT = [[Buf("yT") for _ in range(GT)] for _ in range(RH)]
        with contextlib.ExitStack() as rst:
            ret_heads(c, rst, io, "own", 0, xsrc, cosd, sind, S32, Sb, b_S32, b_Sb, consts, yT, b_yT, scratch=scratch)
        P.barrier()
        with contextlib.ExitStack() as mst:
            gam, bet, b_gb = load_ln(c, mst, io["g_mix0"], io["b_mix0"], "lnmix")
            tmpf = c.rot(2, [128, 1024], F32, "tmpf", mst)
            wos = c.rot(2, [128, 16, 512], BF16, "wos", mst)
            xst = c.rot(2, [128, 512], F32, "xst", mst)
            for half in range(2):
                wo, b_wo = wos.next()
                load_w(c, io["ret_w_out"][:, half * 512:(half + 1) * 512], wo[:], b_wo, nsplit=2)
                for t in range(GT):
                    pm_, b_pm = c.pm.next()
                    P.add("pe", f_mm(pm_[:], [(yT[:, ch, t * 128:(t + 1) * 128], wo[:, ch, :]) for ch in range(16)]),
                          reads=[b_wo] + [b_yT[h][t] for h in range(RH)], writes=[b_pm])
                    xq, b_xq = xst.next()
                    P.dma("sp", f_dma([(xq[:], xsrc[t * 128:(t + 1) * 128, half * 512:(half + 1) * 512])]), b_xq, writes=[b_xq])
                    P.add("dve", f_stt(xs[:, t, half * 512:(half + 1) * 512], xq[:], ALPHA, pm_[:], ALU.mult, ALU.add),
                          reads=[b_xq, b_pm], writes=[b_xs[t]])
            for t in range(GT):
                ln_inplace(c, xs[:, t, :], b_xs[t], gam, bet, b_gb, tmpf)
    P.barrier()
    with contextlib.ExitStack() as fst:
        gam, bet, b_gb = load_ln(c, fst, io["g_ffn0"], io["b_ffn0"], "lnffn")
        tmpf = c.rot(2, [128, 1024], F32, "tmpf", fst)
        x1T = c.sb([128, 8, GRP], BF16, "x1T", fst)
        b_x1T = [Buf("x1T") for _ in range(GT)]
        for t in range(GT):
            tile_to_T(c, xs[:, t, :], b_xs[t], x1T, b_x1T, t)
            P.add("act", f_act(xs[:, t, :], xs[:, t, :], AF.Copy, scale=ALPHA), reads=[b_xs[t]], writes=[b_xs[t]])
        with contextlib.ExitStack() as ust:
            ffn_unit(c, ust, x1T, b_x1T, io["ffn_w_in"], D_FF, io["ffn_w_out"], xs, b_xs, [(0, 12), (12, 10)])
        for t in range(GT):
            ln_inplace(c, xs[:, t, :], b_xs[t], gam, bet, b_gb, tmpf)
        P.barrier()
        with contextlib.ExitStack() as kst:
            x2T = x1T
            b_x2T = [Buf("x2T") for _ in range(GT)]
            for t in range(GT):
                tile_to_T(c, xs[:, t, :], b_xs[t], x2T, b_x2T, t)
            wkv = c.rot(2, [128, 8, 512], BF16, "wkv", kst)
            ksb = c.rot(3, [128, 512], BF16, "ksb", kst)
            for cb in range(2):
                ws, b_ws = wkv.next()
                load_w(c, io["sb_w_kv"][:, cb * 512:(cb + 1) * 512], ws[:], b_ws, nsplit=2)
                for ch in range(4):
                    for tg in range(GRP // 512):
                        pk, b_pk = c.pm.next()
                        tok = slice(tg * 512, (tg + 1) * 512)
                        P.add("pe", f_mm(pk[:], [(ws[:, k, ch * 128:(ch + 1) * 128], x2T[:, k, tok]) for k in range(8)]),
                              reads=[b_ws] + b_x2T[tg * 4:(tg + 1) * 4], writes=[b_pk])
                        ks, b_ks = ksb.next()
                        P.add("act", f_act(ks[:], pk[:], AF.Copy), reads=[b_pk], writes=[b_ks])
                        f0 = (cb * 4 + ch) * 128
                        t0 = slot0 + tg * 512
                        store(c, io, io["KTs"][f0:f0 + 128, t0:t0 + 512], ks[:], b_ks)
            for cb in range(2):
                ws, b_ws = wkv.next()
                load_w(c, io["sb_w_kv"][:, 1024 + cb * 512:1024 + (cb + 1) * 512], ws[:], b_ws, nsplit=2)
                for t in range(GT):
                    pv, b_pv = c.pm.next()
                    P.add("pe", f_mm(pv[:], [(x2T[:, k, t * 128:(t + 1) * 128], ws[:, k, :]) for k in range(8)]),
                          reads=[b_ws, b_x2T[t]], writes=[b_pv])
                    ks, b_ks = ksb.next()
                    if vscale is None:
                        P.add("act", f_act(ks[:], pv[:], AF.Copy), reads=[b_pv], writes=[b_ks])
                    else:
                        P.add("act", f_act(ks[:], pv[:], AF.Identity, scale=vscale[0]), reads=[b_pv, vscale[1]], writes=[b_ks])
                    r0 = slot0 + t * 128
                    store(c, io, io["Vs"][r0:r0 + 128, cb * 512:(cb + 1) * 512], ks[:], b_ks)


def layer1_group(c, io, g, xs, b_xs, aconsts, wr, b_wr, attn_only=False):
    P = c.P
    P.barrier()
    with contextlib.ExitStack() as ast:
        x2T = c.sb([128, 8, GRP], BF16, "x2T", ast)
        b_x2T = [Buf("x2T") for _ in range(GT)]
        oT = c.sb([128, 8, GRP], BF16, "oT", ast)
        b_oT = [Buf("oT") for _ in range(GT)]
        for t in range(GT):
            tile_to_T(c, xs[:, t, :], b_xs[t], x2T, b_x2T, t)
        with contextlib.ExitStack() as hst:
            attention_group(c, hst, io, g, oT, b_oT, x2T, b_x2T, aconsts)
        P.barrier()
        with contextlib.ExitStack() as mst:
            gam, bet, b_gb = load_ln(c, mst, io["g_mix1"], io["b_mix1"], "lnmix")
            tmpf = c.rot(2, [128, 1024], F32, "tmpf", mst)
            wos = c.rot(2, [128, 8, 512], BF16, "wos", mst)
            for half in range(2):
                wo, b_wo = wos.next()
                load_w(c, io["sb_w_out"][:, half * 512:(half + 1) * 512], wo[:], b_wo, nsplit=2)
                for t in range(GT):
                    pm_, b_pm = c.pm.next()
                    P.add("pe", f_mm(pm_[:], [(oT[:, ch, t * 128:(t + 1) * 128], wo[:, ch, :]) for ch in range(8)]),
                          reads=[b_wo, b_oT[t]], writes=[b_pm])
                    xh = xs[:, t, half * 512:(half + 1) * 512]
                    P.add("dve", f_stt(xh, xh, ALPHA, pm_[:], ALU.mult, ALU.add), reads=[b_xs[t], b_pm], writes=[b_xs[t]])
            for t in range(GT):
                ln_inplace(c, xs[:, t, :], b_xs[t], gam, bet, b_gb, tmpf)
    if attn_only:
        return
    P.barrier()
    with contextlib.ExitStack() as fst:
        gam, bet, b_gb = load_ln(c, fst, io["g_ffn1"], io["b_ffn1"], "lnffn")
        tmpf = c.rot(2, [128, 1024], F32, "tmpf", fst)
        x3T = c.sb([128, 8, GRP], BF16, "x3T", fst)
        b_x3T = [Buf("x3T") for _ in range(GT)]
        gates = c.sb([128, GT, N_EXP], F32, "gates", fst)
        b_gt = Buf("gates")
        xTf = c.rot(2, [128, 8, 128], F32, "xTf", fst)
        lgr = c.rot(2, [128, 32], F32, "lg", fst)
        for t in range(GT):
            tile_to_T(c, xs[:, t, :], b_xs[t], x3T, b_x3T, t)
            xf, b_xf = xTf.next()
            for hf in range(2):
                pp, b_pp = c.pm.next()
                ppv = pp[:].rearrange("p (c n) -> p c n", n=128)
                P.add("pe", f_tr([(ppv[:, ch, :], xs[:, t, (hf * 4 + ch) * 128:(hf * 4 + ch + 1) * 128]) for ch in range(4)], c.idf[:]),
                      reads=[b_xs[t], c.b_idf], writes=[b_pp])
                P.add("dve", f_copy(xf[:, hf * 4:(hf + 1) * 4, :], ppv[:, 0:4, :]), reads=[b_pp], writes=[b_xf])
            pl, b_pl = c.pm.next()
            P.add("pe", f_mm(pl[:, 0:N_EXP], [(xf[:, ch, :], wr[:, ch, :]) for ch in range(8)]),
                  reads=[b_xf, b_wr], writes=[b_pl])
            P.add("act", f_act(xs[:, t, :], xs[:, t, :], AF.Copy, scale=ALPHA), reads=[b_xs[t]], writes=[b_xs[t]])
            lg, b_lg = lgr.next()
            P.add("dve", f_copy(lg[:, 0:8], pl[:, 0:N_EXP]), reads=[b_pl], writes=[b_lg])
            P.add("dve", lambda h, lg=lg: h.max(out=lg[:, 8:16], in_=lg[:, 0:8]), reads=[b_lg], writes=[b_lg])
            P.add("dve", f_ts(lg[:, 16:17], lg[:, 8:9], -1.0, None, ALU.mult), reads=[b_lg], writes=[b_lg])
            P.add("act", f_act(lg[:, 24:32], lg[:, 0:8], AF.Exp, bias=lg[:, 16:17]), reads=[b_lg], writes=[b_lg])
            P.add("dve", f_ts(gates[:, t, :], lg[:, 0:8], lg[:, 9:10], None, ALU.is_ge), reads=[b_lg, b_gt], writes=[b_gt])
            P.add("dve", f_tt(gates[:, t, :], gates[:, t, :], lg[:, 24:32], ALU.mult), reads=[b_lg, b_gt], writes=[b_gt])
            P.add("dve", lambda h, lg=lg, t=t: h.reduce_sum(out=lg[:, 17:18], in_=gates[:, t, :], axis=mybir.AxisListType.X),
                  reads=[b_gt, b_lg], writes=[b_lg])
            P.add("dve", lambda h, lg=lg: h.reciprocal(out=lg[:, 17:18], in_=lg[:, 17:18]), reads=[b_lg], writes=[b_lg])
            P.add("dve", f_ts(gates[:, t, :], gates[:, t, :], lg[:, 17:18], None, ALU.mult), reads=[b_lg, b_gt], writes=[b_gt])
        for e in range(N_EXP):
            with contextlib.ExitStack() as ust:
                ffn_unit(c, ust, x3T, b_x3T, io["moe_w_in"][e * D:(e + 1) * D, :], D_EXP,
                         io["moe_w_out"][e * D_EXP:(e + 1) * D_EXP, :], xs, b_xs, [(0, 14), (14, 14)],
                         gate=((lambda t, e=e: gates[:, t, e:e + 1]), b_gt))
            P.barrier()
        for t in range(GT):
            ln_inplace(c, xs[:, t, :], b_xs[t], gam, bet, b_gb, tmpf)
            r0 = g * GRP + t * 128
            store(c, io, io["out"][r0:r0 + 128, :], xs[:, t, :], b_xs[t])


ROUTED = True
I32 = mybir.dt.int32
REGS = [(None, [(0, 384)]), (384, [(384, 128), (512, 512)])]


def moe_tables():
    iota_f = np.broadcast_to(np.arange(128, dtype=np.float32)[None, :], (128, 128)).copy()
    iotap = (np.arange(128, dtype=np.float32)[:, None] + 128.0 * np.arange(8, dtype=np.float32)[None, :]).copy()
    k = np.arange(128)[:, None]
    m = np.arange(128)[None, :]
    ulow = (k < m).astype(np.float32)
    ones = np.ones((128, 128), np.float32)
    oh = np.zeros((8, 8, 128), np.float32)
    for r in range(8):
        oh[r, r, :] = 1.0
    return iota_f, iotap, ulow, ones, oh.reshape(8, 8 * 128)


def moe_routed_group(c, io, g, xs, b_xs, mc):
    P = c.P
    iota_f, b_iota, iotap, b_iotap, Ulow, b_Ul, ones_b, b_ones, OH, b_OH, wr, b_wr = mc
    P.barrier()
    with contextlib.ExitStack() as fst:
        rt_in = c.sb([128, GT, 16], F32, "rt_in", fst)
        b_rt = [Buf("rt_in") for _ in range(GT)]
        maskt = c.sb([128, GT, 8], F32, "maskt", fst)
        maskb = c.sb([128, GT, 8], BF16, "maskb", fst)
        b_mask = [Buf("mask") for _ in range(GT)]
        cnt_i = c.sb([128, 8], I32, "cnt_i", fst)
        b_cnt = Buf("cnt")
        RTp = c.sb([8, GRP], F32, "RTp", fst)
        RTg = c.sb([8, GRP], F32, "RTg", fst)
        b_RT = Buf("RT")
        x3b = c.sb([128, GT, 1024], BF16, "x3b", fst)
        b_x3b = [Buf("x3b") for _ in range(GT)]
        pst = contextlib.ExitStack()
        xTf = c.rot(2, [128, 8, 128], F32, "xTf", pst)
        lgr = c.rot(2, [128, 32], F32, "lg", pst)
        for t in range(GT):
            P.add("act", f_act(x3b[:, t, :], xs[:, t, :], AF.Copy), reads=[b_xs[t]], writes=[b_x3b[t]])
            xf, b_xf = xTf.next()
            for hf in range(2):
                pp, b_pp = c.pm.next()
                ppv = pp[:].rearrange("p (c n) -> p c n", n=128)
                P.add("pe", f_tr([(ppv[:, ch, :], xs[:, t, (hf * 4 + ch) * 128:(hf * 4 + ch + 1) * 128]) for ch in range(4)], c.idf[:]),
                      reads=[b_xs[t], c.b_idf], writes=[b_pp])
                P.add("dve", f_copy(xf[:, hf * 4:(hf + 1) * 4, :], ppv[:, 0:4, :]), reads=[b_pp], writes=[b_xf])
            pl, b_pl = c.pm.next()
            P.add("pe", f_mm(pl[:, 0:N_EXP], [(xf[:, ch, :], wr[:, ch, :]) for ch in range(8)]),
                  reads=[b_xf, b_wr], writes=[b_pl])
            P.add("act", f_act(xs[:, t, :], xs[:, t, :], AF.Copy, scale=ALPHA), reads=[b_xs[t]], writes=[b_xs[t]])
            lg, b_lg = lgr.next()
            gt = rt_in[:, t, 8:16]
            P.add("dve", f_copy(lg[:, 0:8], pl[:, 0:N_EXP]), reads=[b_pl], writes=[b_lg])
            P.add("dve", lambda h, lg=lg: h.max(out=lg[:, 8:16], in_=lg[:, 0:8]), reads=[b_lg], writes=[b_lg])
            P.add("dve", f_ts(lg[:, 16:17], lg[:, 8:9], -1.0, None, ALU.mult), reads=[b_lg], writes=[b_lg])
            P.add("act", f_act(lg[:, 24:32], lg[:, 0:8], AF.Exp, bias=lg[:, 16:17]), reads=[b_lg], writes=[b_lg])
            P.add("dve", f_ts(maskt[:, t, :], lg[:, 0:8], lg[:, 9:10], None, ALU.is_ge), reads=[b_lg], writes=[b_mask[t]])
            P.add("dve", f_copy(maskb[:, t, :], maskt[:, t, :]), reads=[b_mask[t]], writes=[b_mask[t]])
            P.add("dve", f_tt(gt, maskt[:, t, :], lg[:, 24:32], ALU.mult), reads=[b_lg, b_mask[t]], writes=[b_rt[t]])
            P.add("dve", lambda h, lg=lg, gt=gt: h.reduce_sum(out=lg[:, 17:18], in_=gt, axis=mybir.AxisListType.X),
                  reads=[b_rt[t], b_lg], writes=[b_lg])
            P.add("dve", lambda h, lg=lg: h.reciprocal(out=lg[:, 17:18], in_=lg[:, 17:18]), reads=[b_lg], writes=[b_lg])
            P.add("dve", f_ts(gt, gt, lg[:, 17:18], None, ALU.mult), reads=[b_lg, b_rt[t]], writes=[b_rt[t]])
        for t in range(GT):
            pp, b_pp = c.pm.next()
            pairs = [(Ulow[:], maskb[:, t, :])] + [(ones_b[:], maskb[:, t2, :]) for t2 in range(t)]
            P.add("pe", f_mm(pp[:, 0:8], pairs), reads=[b_Ul, b_ones] + b_mask[0:t + 1], writes=[b_pp])
            P.add("dve", f_stt(rt_in[:, t, 0:8], pp[:, 0:8], 1.0, maskt[:, t, :], ALU.add, ALU.mult),
                  reads=[b_pp, b_mask[t], b_rt[t]], writes=[b_rt[t]])
            P.add("dve", f_ts(rt_in[:, t, 0:8], rt_in[:, t, 0:8], -1.0, None, ALU.add), reads=[b_rt[t]], writes=[b_rt[t]])
        pc, b_pc = c.pm.next()
        P.add("pe", f_mm(pc[:, 0:8], [(ones_b[:], maskb[:, t, :]) for t in range(GT)]), reads=[b_ones] + b_mask, writes=[b_pc])
        P.add("dve", f_copy(cnt_i[:], pc[:, 0:8]), reads=[b_pc], writes=[b_cnt])
        for (dst, c0) in ((RTp, 0), (RTg, 8)):
            for hf in range(2):
                pp, b_pp = c.pm.next()
                P.add("pe", f_tr([(pp[0:8, j * 128:(j + 1) * 128], rt_in[:, hf * 4 + j, c0:c0 + 8]) for j in range(4)], c.idf[:]),
                      reads=b_rt[hf * 4:(hf + 1) * 4] + [c.b_idf], writes=[b_pp])
                P.add("act", f_act(dst[:, hf * 512:(hf + 1) * 512], pp[0:8, :], AF.Copy), reads=[b_pp], writes=[b_RT])

        pst.close()
        P.barrier()
        xgT = c.sb([128, 8, GRP], BF16, "xgT", fst)
        b_xg = [Buf("xg") for _ in range(GT)]
        P.add("dve", f_memset(xgT[:], 0.0), writes=b_xg)
        actT = c.sb([128, 14, GRP], BF16, "actT", fst)
        b_act = [[Buf("act") for _ in range(GT)] for _ in range(14)]
        oute = c.sb([128, GT, 512], BF16, "oute", fst)
        b_oute = [Buf("oute") for _ in range(GT)]
        pbc = c.sb([128, GRP], F32, "pbc", fst)
        gbc = c.sb([128, GRP], F32, "gbc", fst)
        b_bc = Buf("bc")
        selr = c.rot(16, [128, 128], BF16, "sel", fst)
        nst0 = REGS[0][1][0][1] // 128
        sTc = c.sb([128, nst0, GT, 128], BF16, "sTc", fst)
        b_sTc = [[Buf("sTc") for _ in range(GT)] for _ in range(nst0)]
        selTr = c.rot(4, [128, 128], BF16, "selT", fst)
        w1s = c.rot(2, [128, 8, 2, 256], BF16, "w1s", fst)
        w2s = c.rot(1, [128, 14, 512], BF16, "w2s", fst)
        sg = c.rot(2, [128, 512], BF16, "sg", fst)

        def regions():
            for thr, pieces in REGS:
                if thr is not None:
                    P.region_begin("cnt", thr)
                yield thr, pieces
                if thr is not None:
                    P.region_end()

        for e in range(N_EXP):
            w1 = io["moe_w_in"][e * D:(e + 1) * D, :]
            w2 = io["moe_w_out"][e * D_EXP:(e + 1) * D_EXP, :]
            for eng in ("pe", "act", "dve"):
                P.reg_load(eng, "cnt", cnt_i[0:1, e:e + 1], reads=[b_cnt])
            for (dst, src) in ((pbc, RTp), (gbc, RTg)):
                for hf in range(2):
                    pb, b_pb = c.pm.next()
                    P.add("pe", f_mm(pb[:], [(OH[:, e * 128:(e + 1) * 128], src[:, hf * 512:(hf + 1) * 512])]),
                          reads=[b_OH, b_RT], writes=[b_pb])
                    P.add("dve", f_copy(dst[:, hf * 512:(hf + 1) * 512], pb[:]), reads=[b_pb], writes=[b_bc])
            for thr, pieces in regions():
                for (s0, w) in pieces:
                    for st in range(w // 128):
                        slot0 = s0 + st * 128
                        sels = []
                        for t in range(GT):
                            sl, b_sl = selr.next()
                            P.add("dve", f_ts(sl[:], iota_f[:], rt_in[:, t, e:e + 1], -float(slot0), ALU.subtract, ALU.is_equal),
                                  reads=[b_iota, b_rt[t]], writes=[b_sl])
                            sels.append((sl, b_sl))
                        for hf in range(2):
                            pg, b_pg = c.pm.next()
                            groups = [(pg[:, j * 128:(j + 1) * 128],
                                       [(x3b[:, t, (hf * 4 + j) * 128:(hf * 4 + j + 1) * 128], sels[t][0][:]) for t in range(GT)])
                                      for j in range(4)]
                            P.add("pe", f_mm_multi(groups), reads=b_x3b + [b for _, b in sels], writes=[b_pg])
                            P.add("dve", f_copy(xgT[:, hf * 4:(hf + 1) * 4, slot0:slot0 + 128],
                                                pg[:].rearrange("p (c n) -> p c n", n=128)),
                                  reads=[b_pg], writes=[b_xg[slot0 // 128]])
            for sti in range(nst0):
                for t in range(GT):
                    tk = slice(t * 128, (t + 1) * 128)
                    P.add("dve", f_stt(sTc[:, sti, t, :], pbc[:, tk], iotap[:, sti:sti + 1], gbc[:, tk], ALU.is_equal, ALU.mult),
                          reads=[b_bc, b_iotap], writes=[b_sTc[sti][t]])
            for (j0, nj) in ((0, 14), (14, 14)):
                for blk in range(nj // 2):
                    jj = j0 + blk * 2
                    ws, b_ws = w1s.next()
                    srcg = w1[:, jj * 128: jj * 128 + 256].rearrange("(c p) n -> p c n", p=128)
                    srcu = w1[:, D_EXP + jj * 128: D_EXP + jj * 128 + 256].rearrange("(c p) n -> p c n", p=128)
                    P.dma("pool", f_dma([(ws[:, :, 0, :], srcg), (ws[:, :, 1, :], srcu)]), b_ws, writes=[b_ws], n=2)
                    for thr, pieces in regions():
                        for (s0, w) in pieces:
                            tiles = list(range(s0 // 128, (s0 + w) // 128))
                            for ci in range(2):
                                jl = blk * 2 + ci
                                pg, b_pg = c.pm.next()
                                pu, b_pu = c.pm.next()
                                P.add("pe", f_mm_multi([
                                    (pg[:, 0:w], [(ws[:, k, 0, ci * 128:(ci + 1) * 128], xgT[:, k, s0:s0 + w]) for k in range(8)]),
                                    (pu[:, 0:w], [(ws[:, k, 1, ci * 128:(ci + 1) * 128], xgT[:, k, s0:s0 + w]) for k in range(8)]),
                                ]), reads=[b_ws] + [b_xg[i] for i in tiles], writes=[b_pg, b_pu])
                                s_, b_s = sg.next()
                                P.add("act", f_act(s_[:, 0:w], pg[:, 0:w], AF.Silu), reads=[b_pg], writes=[b_s])
                                P.add("dve", f_tt(actT[:, jl, s0:s0 + w], s_[:, 0:w], pu[:, 0:w], ALU.mult),
                                      reads=[b_s, b_pu], writes=[b_act[jl][i] for i in tiles])
                for half in range(2):
                    w2v, b_w2 = w2s.next()
                    load_w(c, w2[j0 * 128:(j0 + nj) * 128, half * 512:(half + 1) * 512], w2v[:, 0:nj, :], b_w2, nsplit=2)
                    for thr, pieces in regions():
                        for (s0, w) in pieces:
                            for sti in range(s0 // 128, (s0 + w) // 128):
                                po, b_po = c.pm.next()
                                P.add("pe", f_mm(po[:], [(actT[:, j, sti * 128:(sti + 1) * 128], w2v[:, j, :]) for j in range(nj)]),
                                      reads=[b_w2] + [b_act[j][sti] for j in range(nj)], writes=[b_po])
                                P.add("dve", f_copy(oute[:, sti, :], po[:]), reads=[b_po], writes=[b_oute[sti]])
                    for thr, pieces in regions():
                        stis = [sti for (s0, w) in pieces for sti in range(s0 // 128, (s0 + w) // 128)]
                        for t in range(GT):
                            tk = slice(t * 128, (t + 1) * 128)
                            if thr is None:
                                sTs = [(sTc[:, sti, t, :], b_sTc[sti][t]) for sti in stis]
                                groups = [stis]
                            else:
                                sTs = None
                                groups = [stis[i:i + 4] for i in range(0, len(stis), 4)]
                            for grp in groups:
                                if thr is None:
                                    cur = sTs
                                else:
                                    cur = []
                                    for sti in grp:
                                        sT, b_sT = selTr.next()
                                        P.add("dve", f_stt(sT[:], pbc[:, tk], iotap[:, sti:sti + 1], gbc[:, tk], ALU.is_equal, ALU.mult),
                                              reads=[b_bc, b_iotap], writes=[b_sT])
                                        cur.append((sT[:], b_sT))
                                ps_, b_ps = c.pm.next()
                                P.add("pe", f_mm(ps_[:], [(cur[i][0], oute[:, sti, :]) for i, sti in enumerate(grp)]),
                                      reads=[b for _, b in cur] + [b_oute[sti] for sti in grp], writes=[b_ps])
                                xh = xs[:, t, half * 512:(half + 1) * 512]
                                P.add("dve", f_tt(xh, ps_[:], xh, ALU.add), reads=[b_ps, b_xs[t]], writes=[b_xs[t]])
    P.barrier()
    with contextlib.ExitStack() as lst:
        gam, bet, b_gb = load_ln(c, lst, io["g_ffn1"], io["b_ffn1"], "lnffn")
        tmpf = c.rot(2, [128, 1024], F32, "tmpf", lst)
        for t in range(GT):
            ln_inplace(c, xs[:, t, :], b_xs[t], gam, bet, b_gb, tmpf)
            r0 = g * GRP + t * 128
            store(c, io, io["out"][r0:r0 + 128, :], xs[:, t, :], b_xs[t])


def build_fused(c, io):
    P = c.P
    xres = c.sb([128, TOK // 128, 1024], F32, "xres")
    b_xres = [Buf("xres", key=f"xres{t}") for t in range(TOK // 128)]
    c.one_t = c.sb([128, 2], F32, "one")
    P.add("dve", f_memset(c.one_t[:], 1.0), writes=[c.b_eps])
    flag, b_flag = load_const(c, io["flag"], [128, 1], F32, "flag")
    with contextlib.ExitStack() as l0st:
        c.stack = l0st
        c.stage = c.rot(2, [128, 1024], F32, "stage")
        _, _, _, cd = retention_tables()
        dtab, b_dtab = load_const(c, io["dtab"], [128, RH, 128], F32, "dtab")
        cn, b_cn = load_const(c, io["cn"], [128, RH], F32, "cn")
        kdec, b_kdec = load_const(c, io["kdec"], [128, RH], F32, "kdec")
        consts = (dtab, b_dtab, cn, b_cn, kdec, b_kdec, cd)
        S32 = c.sb([128, RH, 2, 512], F32, "S32")
        Sb = c.sb([128, RH, 2, 512], BF16, "Sb")
        b_S32 = [Buf("S32") for _ in range(RH)]
        b_Sb = [Buf("Sb") for _ in range(RH)]
        for h in range(RH):
            P.add("dve", f_memset(S32[:, h, :, :], 0.0), writes=[b_S32[h]])
            P.add("dve", f_memset(Sb[:, h, :, :], 0.0), writes=[b_Sb[h]])
        L0 = (S32, Sb, b_S32, b_Sb, consts)
        for mode, g in (("prev", 0), ("prev", 1), ("own", 0), ("own", 1)):
            own = mode == "own"
            xsrc = (io["x_own"] if own else io["x_prev"])[g * GRP:(g + 1) * GRP, :]
            cosd = (io["cos_own"] if own else io["cos_prev"])[:, g * GRP:(g + 1) * GRP]
            sind = (io["sin_own"] if own else io["sin_prev"])[:, g * GRP:(g + 1) * GRP]
            gs = g if own else 1
            xs = xres[:, gs * GT:(gs + 1) * GT, :]
            bx = b_xres[gs * GT:(gs + 1) * GT]
            slot0 = (TOK if own else 0) + g * GRP
            layer0_group(c, io, L0, g, xsrc, cosd, sind, xs, bx, slot0, None if own else (flag[:, 0:1], b_flag))
        c.stack = c.root
    P.barrier()
    wr, b_wr = load_const(c, io["moe_w_router"].rearrange("(c p) n -> p c n", p=128), [128, 8, N_EXP], F32, "wr")
    with contextlib.ExitStack() as a1st:
        c.stack = a1st
        m01, b_m01 = load_const(c, io["m01"], [128, 4, 512], BF16, "m01", eng="pool")
        negm, b_negm = load_const(c, io["negm"], [128, 4, 512], BF16, "negm", eng="pool")
        M1, b_M1 = load_const(c, io["M1"], [128, 128], BF16, "M1", eng="pool")
        NO, b_NO = load_const(c, io["NO"], [128, 128], BF16, "NO", eng="pool")
        aconsts = (m01, b_m01, negm, b_negm, M1, b_M1, NO, b_NO)
        c.stack = c.root
        for g in range(TOK // GRP):
            layer1_group(c, io, g, xres[:, g * GT:(g + 1) * GT, :], b_xres[g * GT:(g + 1) * GT], aconsts, wr, b_wr,
                         attn_only=ROUTED)
    if ROUTED:
        P.barrier()
        iota_f, b_iota = load_const(c, io["iota_f"], [128, 128], F32, "iota_f")
        iotap, b_iotap = load_const(c, io["iotap"], [128, 8], F32, "iotap")
        Ulow, b_Ul = load_const(c, io["Ulow"], [128, 128], BF16, "Ulow", eng="pool")
        ones_b, b_ones = load_const(c, io["ones"], [128, 128], BF16, "ones_b", eng="pool")
        OH, b_OH = load_const(c, io["OH"], [8, 8 * 128], F32, "OH")
        mc = (iota_f, b_iota, iotap, b_iotap, Ulow, b_Ul, ones_b, b_ones, OH, b_OH, wr, b_wr)
        for g in range(TOK // GRP):
            moe_routed_group(c, io, g, xres[:, g * GT:(g + 1) * GT, :], b_xres[g * GT:(g + 1) * GT], mc)


def make_nc_fused():
    nc = bass.Bass("TRN2", target_bir_lowering=False)
    c = Ctx(nc)
    io = {"outs": []}

    def din(name, shape, dt=F32):
        io[name] = c.dram(name, shape, dt, "ExternalInput")
    din("x_own", [TOK, D]); din("x_prev", [TOK, D])
    din("cos_own", [128, TOK]); din("sin_own", [128, TOK]); din("cos_prev", [128, TOK]); din("sin_prev", [128, TOK])
    din("ret_w_in", [D, 6144]); din("ret_w_out", [2048, D])
    din("ffn_w_in", [D, 2 * D_FF]); din("ffn_w_out", [D_FF, D]); din("sb_w_kv", [D, 2048])
    din("sb_w_q", [D, D]); din("sb_w_out", [D, D])
    din("moe_w_router", [D, N_EXP]); din("moe_w_in", [N_EXP * D, 2 * D_EXP]); din("moe_w_out", [N_EXP * D_EXP, D])
    for l in (0, 1):
        for nm in ("g_mix", "b_mix", "g_ffn", "b_ffn"):
            din(f"{nm}{l}", [128, D])
    din("dtab", [128, RH, 128]); din("cn", [128, RH]); din("kdec", [128, RH]); din("ident", [128, 128]); din("flag", [128, 1])
    din("m01", [128, 4, 512]); din("negm", [128, 4, 512]); din("M1", [128, 128]); din("NO", [128, 128])
    din("iota_f", [128, 128]); din("iotap", [128, 8]); din("Ulow", [128, 128]); din("ones", [128, 128]); din("OH", [8, 8 * 128])
    io["KTs"] = nc.dram_tensor("KTs_i", [D, 2 * TOK], BF16).ap()
    io["Vs"] = nc.dram_tensor("Vs_i", [2 * TOK, D], BF16).ap()
    io["out"] = c.dram("out", [TOK, D], F32, "ExternalOutput")
    with c.root:
        setup_common(c, io["ident"])
        build_fused(c, io)
        c.P.add("sp", lambda h: None, reads=io["outs"])
        cnt = c.P.emit(nc)
    return nc, cnt


def fused_inputs(inputs, core, shared):
    b, half = core // 2, core % 2
    x = inputs["x"]
    f = np.float32
    co, so = rope_tables(half * TOK, TOK)
    d = dict(shared)
    d.update({
        "x_own": np.ascontiguousarray(x[b, half * TOK:(half + 1) * TOK]),
        "x_prev": np.ascontiguousarray(x[b, 0:TOK]) if half == 1 else np.zeros((TOK, D), f),
        "cos_own": co, "sin_own": so,
        "flag": np.full((128, 1), float(half), f),
    })
    return d


def shared_inputs(inputs):
    f = np.float32
    dtab, cn, kdec, _ = retention_tables()
    cp, sp_ = rope_tables(0, TOK)
    m01, negm, m1, negones = attn_tables()
    bc = lambda v: np.ascontiguousarray(np.broadcast_to(np.asarray(v, f)[None, :], (128, D)))
    d = {
        "cos_prev": cp, "sin_prev": sp_,
        "ret_w_in": np.ascontiguousarray(inputs["ret_w_in"][0]),
        "ret_w_out": np.ascontiguousarray(inputs["ret_w_out"][0]),
        "ffn_w_in": np.ascontiguousarray(inputs["ffn_w_in"][0]),
        "ffn_w_out": np.ascontiguousarray(inputs["ffn_w_out"][0]),
        "sb_w_kv": np.ascontiguousarray(inputs["sb_w_kv"]),
        "sb_w_q": np.ascontiguousarray(inputs["sb_w_q"][0]), "sb_w_out": np.ascontiguousarray(inputs["sb_w_out"][0]),
        "moe_w_router": np.ascontiguousarray(inputs["moe_w_router"][0]),
        "moe_w_in": np.ascontiguousarray(inputs["moe_w_in"][0]).reshape(N_EXP * D, 2 * D_EXP),
        "moe_w_out": np.ascontiguousarray(inputs["moe_w_out"][0]).reshape(N_EXP * D_EXP, D),
        "dtab": dtab, "cn": cn, "kdec": kdec, "ident": np.eye(128, dtype=f),
        "m01": m01, "negm": negm, "M1": m1, "NO": negones,
    }
    d["iota_f"], d["iotap"], d["Ulow"], d["ones"], d["OH"] = moe_tables()
    for l in (0, 1):
        d[f"g_mix{l}"] = bc(inputs["ln_mix_g"][l]); d[f"b_mix{l}"] = bc(inputs["ln_mix_b"][l])
        d[f"g_ffn{l}"] = bc(inputs["ln_ffn_g"][l]); d[f"b_ffn{l}"] = bc(inputs["ln_ffn_b"][l])
    return d


_NC_CACHE = {}


def kernel(**inputs):
    inputs = {k: np.asarray(v) for k, v in inputs.items()}
    cores = list(range(8))
    if "f" not in _NC_CACHE:
        _NC_CACHE["f"] = make_nc_fused()[0]
    shared = shared_inputs(inputs)
    in_maps = [fused_inputs(inputs, cr, shared) for cr in cores]
    res = run_bass_kernel_spmd(_NC_CACHE["f"], in_maps, core_ids=cores).results
    out = np.stack([np.concatenate([res[2 * b]["out"], res[2 * b + 1]["out"]], axis=0) for b in range(4)], axis=0)
    return out.astype(np.float32)
```

```python
import contextlib
import numpy as np
import ml_dtypes
import concourse.bass as bass
import concourse.mybir as mybir
from concourse.bass_utils import run_bass_kernel_spmd

F32 = mybir.dt.float32
BF16 = mybir.dt.bfloat16
AF = mybir.ActivationFunctionType
ALU = mybir.AluOpType

ENGS = ("pe", "act", "dve", "pool", "sp")

D = 1024
TOK = 2048
GRP = 1024
GT = GRP // 128
ALPHA = float((2.0 * 2) ** 0.25)
EPS = 1e-5
RH = 4
D_FF = 2816
N_EXP = 8
D_EXP = 3584


class Buf:
    __slots__ = ("name", "w", "r", "key")

    def __init__(self, name="", key=None):
        self.name = name
        self.w = None
        self.r = []
        self.key = key


class Op:
    __slots__ = ("eng", "fn", "deps", "kind", "owner", "dval", "ms", "ndma")

    def __init__(self, eng, fn, deps, kind, owner=None, dval=0, ndma=0):
        self.eng = eng
        self.fn = fn
        self.deps = deps
        self.kind = kind
        self.owner = owner
        self.dval = dval
        self.ms = None
        self.ndma = ndma


class Prog:
    def __init__(self):
        self.ops = []
        self.by_eng = {e: [] for e in ENGS}
        self.ndsem = 0
        self.fence = set()
        self.dstate = {}
        self.regions = []
        self.cur_region = None
        self.regs = {}

    def bufs(self, n, name=""):
        return [Buf(f"{name}{i}") for i in range(n)]

    def _deps(self, eng, reads, writes, oid):
        deps = set()
        for b in reads:
            if b.w is not None:
                deps.add(b.w)
        for b in writes:
            if b.w is not None:
                deps.add(b.w)
            deps.update(b.r)
        for b in reads:
            b.r.append(oid)
        for b in writes:
            b.w = oid
            b.r = []
        deps |= self.fence
        deps.discard(oid)
        if eng == "pe":
            deps = {d for d in deps if not (self.ops[d].eng == "pe" and self.ops[d].kind == "c")}
        return deps

    def add(self, eng, fn, reads=(), writes=()):
        oid = len(self.ops)
        deps = self._deps(eng, list(reads), list(writes), oid)
        self.ops.append(Op(eng, fn, deps, "c"))
        self.by_eng[eng].append(oid)
        return oid

    def dma(self, eng, fn, owner, reads=(), writes=(), n=1):
        assert self.cur_region is None, "no DMA inside dynamic regions"
        oid = len(self.ops)
        deps = self._deps(eng, list(reads), list(writes), oid)
        key = owner.key if owner.key is not None else id(owner)
        stt = self.dstate.get(key)
        if stt is None:
            stt = [self.ndsem, 0, None]
            self.ndsem += 1
            self.dstate[key] = stt
            self._keep = getattr(self, "_keep", [])
            self._keep.append(owner)
        if stt[2] is not None:
            deps.add(stt[2])
        stt[1] += n
        stt[2] = oid
        self.ops.append(Op(eng, fn, deps, "d", owner=stt[0], dval=16 * stt[1], ndma=n))
        self.by_eng[eng].append(oid)
        return oid

    def region_begin(self, reg_key, thr):
        rid = len(self.regions)
        self.regions.append((reg_key, thr))
        for e in ENGS:
            oid = len(self.ops)
            self.ops.append(Op(e, rid, set(), "rb"))
            self.by_eng[e].append(oid)
        self.cur_region = rid

    def region_end(self):
        for e in ENGS:
            oid = len(self.ops)
            self.ops.append(Op(e, self.cur_region, set(), "re"))
            self.by_eng[e].append(oid)
        self.cur_region = None

    def reg_load(self, eng, reg_key, ap, reads):
        def fn(h, eng=eng):
            r = self.regs.setdefault((eng, reg_key), None)
            if r is None:
                r = h.alloc_register(f"r_{reg_key}_{eng}")
                self.regs[(eng, reg_key)] = r
            return h.reg_load(r, ap)
        return self.add(eng, fn, reads=reads)

    def barrier(self):
        f = set()
        for e in ENGS:
            for oid in reversed(self.by_eng[e]):
                if self.ops[oid].kind in ("c", "d"):
                    f.add(oid)
                    break
        f.update(v[2] for v in self.dstate.values() if v[2] is not None)
        self.fence = f

    def emit(self, nc):
        ops = self.ops
        needed = set()
        for op in ops:
            for d in op.deps:
                if ops[d].kind == "c":
                    needed.add(d)
        cnt = {e: 0 for e in ENGS}
        rinfo = {}
        for e in ENGS:
            cur = None
            for oid in self.by_eng[e]:
                op = ops[oid]
                if op.kind == "rb":
                    cur = [0, cnt[e], 0]
                    rinfo[(e, op.fn)] = cur
                elif op.kind == "re":
                    cur[2] = cnt[e] - cur[1]
                    cur = None
                else:
                    if cur is not None:
                        cur[0] += 1
                    if op.kind == "c" and oid in needed:
                        cnt[e] += 1
                        op.ms = cnt[e]
        with contextlib.ExitStack() as st:
            esem = {e: st.enter_context(nc.semaphore(f"s_{e}")) for e in ENGS}
            dsem = [st.enter_context(nc.semaphore(f"d_{i}")) for i in range(self.ndsem)]
            block = st.enter_context(nc.Block())

            def token(d):
                o = ops[d]
                if o.kind == "c":
                    return ("e", o.eng), o.ms
                return ("d", o.owner), o.dval

            def run(e, h):
                seen = {}
                snap = None
                guard = None
                for oid in self.by_eng[e]:
                    op = ops[oid]
                    if op.kind == "rb":
                        info = rinfo[(e, op.fn)]
                        if info[0] == 0:
                            continue
                        reg_key, thr = self.regions[op.fn]
                        r = self.regs[(e, reg_key)]
                        g1 = h.If_lt(r, thr + 1)
                        g1.__enter__()
                        if info[2] > 0:
                            if info[1] > 0:
                                h.wait_ge(esem[e], info[1])
                            h.sem_inc(esem[e], info[2])
                        g1.__exit__(None, None, None)
                        guard = h.Else()
                        guard.__enter__()
                        snap = dict(seen)
                        continue
                    if op.kind == "re":
                        if rinfo[(e, op.fn)][0] == 0:
                            continue
                        guard.__exit__(None, None, None)
                        guard = None
                        seen = snap
                        snap = None
                        continue
                    want = {}
                    for d in op.deps:
                        k, v = token(d)
                        if v > want.get(k, 0):
                            want[k] = v
                    for k, v in want.items():
                        if seen.get(k, 0) >= v:
                            continue
                        seen[k] = v
                        s = esem[k[1]] if k[0] == "e" else dsem[k[1]]
                        h.wait_ge(s, v)
                    r = op.fn(h)
                    if op.kind == "c":
                        if op.ms is not None:
                            r.then_inc(esem[e], 1)
                    else:
                        if not isinstance(r, (list, tuple)):
                            r = [r]
                        assert len(r) == op.ndma, (len(r), op.ndma)
                        for ins in r:
                            ins.then_inc(dsem[op.owner], 16)

            block.tensor(lambda h: run("pe", h))
            block.scalar(lambda h: run("act", h))
            block.vector(lambda h: run("dve", h))
            block.gpsimd(lambda h: run("pool", h))
            block.sync(lambda h: run("sp", h))
        return cnt


class Rot:
    def __init__(self, items):
        self.items = items
        self.i = 0

    def next(self):
        it = self.items[self.i % len(self.items)]
        self.i += 1
        return it


def f_mm(out, pairs):
    def fn(h):
        n = len(pairs)
        ins = None
        for i, (l, r) in enumerate(pairs):
            ins = h.matmul(out, lhsT=l, rhs=r, start=(i == 0), stop=(i == n - 1))
        return ins
    return fn


def f_mm_multi(groups):
    def fn(h):
        ins = None
        for out, pairs in groups:
            n = len(pairs)
            for i, (l, r) in enumerate(pairs):
                ins = h.matmul(out, lhsT=l, rhs=r, start=(i == 0), stop=(i == n - 1))
        return ins
    return fn


def f_tr(items, ident):
    def fn(h):
        ins = None
        for o, i in items:
            ins = h.transpose(out=o, in_=i, identity=ident)
        return ins
    return fn


def f_act(out, in_, func, bias=None, scale=None):
    kw = {}
    if bias is not None:
        kw["bias"] = bias
    if scale is not None:
        kw["scale"] = scale
    return lambda h: h.activation(out=out, in_=in_, func=func, **kw)


def f_tt(out, in0, in1, op):
    return lambda h: h.tensor_tensor(out=out, in0=in0, in1=in1, op=op)


def f_ts(out, in0, s1, s2, op0, op1=None):
    if op1 is None:
        return lambda h: h.tensor_scalar(out=out, in0=in0, scalar1=s1, scalar2=None, op0=op0)
    return lambda h: h.tensor_scalar(out=out, in0=in0, scalar1=s1, scalar2=s2, op0=op0, op1=op1)


def f_stt(out, in0, scalar, in1, op0, op1):
    return lambda h: h.scalar_tensor_tensor(out=out, in0=in0, scalar=scalar, in1=in1, op0=op0, op1=op1)


def f_copy(out, in_):
    return lambda h: h.tensor_copy(out=out, in_=in_)


def f_dma(pairs):
    return lambda h: [h.dma_start(out=o, in_=i) for o, i in pairs]


def f_memset(ap, v):
    return lambda h: h.memset(ap, v)


class Ctx:
    def __init__(self, nc):
        self.nc = nc
        self.P = Prog()
        self.root = contextlib.ExitStack()
        self.stack = self.root
        self.uid = 0

    def sb(self, shape, dt, name=None, stack=None):
        self.uid += 1
        name = (name or "t") + f"_{self.uid}"
        return (stack or self.stack).enter_context(self.nc.sbuf_tensor(name, shape, dt))

    def ps(self, shape, dt, name=None):
        self.uid += 1
        name = (name or "p") + f"_{self.uid}"
        return self.root.enter_context(self.nc.psum_tensor(name, shape, dt))

    def dram(self, name, shape, dt, kind):
        return self.nc.dram_tensor(name, shape, dt, kind=kind).ap()

    def rot(self, n, shape, dt, name, stack=None):
        return Rot([(self.sb(shape, dt, name, stack), Buf(name, key=f"{name}{i}")) for i in range(n)])


def setup_common(c, ident_d):
    P = c.P
    c.pm = Rot([(c.ps([128, 512], F32, "pm"), Buf("pm")) for _ in range(6)])
    c.pt = Rot([(c.ps([128, 1024], BF16, "pt"), Buf("pt")) for _ in range(2)])
    c.idf = c.sb([128, 128], F32, "idf")
    c.idb = c.sb([128, 128], BF16, "idb")
    c.b_idf = Buf("idf")
    c.b_idb = Buf("idb")
    P.dma("sp", f_dma([(c.idf[:], ident_d)]), c.b_idf, writes=[c.b_idf])
    P.add("dve", f_copy(c.idb[:], c.idf[:]), reads=[c.b_idf], writes=[c.b_idb])
    c.stage = None
    c.xb = c.rot(2, [128, 1024], BF16, "xb")
    c.stat = c.rot(4, [128, 16], F32, "stat")
    c.eps_t = c.sb([128, 2], F32, "eps")
    c.b_eps = Buf("eps")
    P.add("dve", f_memset(c.eps_t[:], EPS), writes=[c.b_eps])


def load_const(c, dram_ap, shape, dt=F32, name="const", eng="sp"):
    t = c.sb(shape, dt, name)
    b = Buf(name)
    full = t[:] if len(shape) == 2 else t[:]
    c.P.dma(eng, f_dma([(full, dram_ap)]), b, writes=[b])
    return t, b


def tile_to_T(c, src_ap, src_buf, dstT, dst_bufs, t, nchunk=8):
    P = c.P
    xb, b_xb = c.xb.next()
    P.add("act", f_act(xb[:, 0:nchunk * 128], src_ap, AF.Copy), reads=[src_buf], writes=[b_xb])
    pt, b_pt = c.pt.next()
    ptv = pt[:].rearrange("p (c n) -> p c n", n=128)
    P.add("pe", f_tr([(ptv[:, ch, :], xb[:, ch * 128:(ch + 1) * 128]) for ch in range(nchunk)], c.idb[:]),
          reads=[b_xb, c.b_idb], writes=[b_pt])
    P.add("dve", f_copy(dstT[:, 0:nchunk, t * 128:(t + 1) * 128], ptv[:, 0:nchunk, :]),
          reads=[b_pt], writes=[dst_bufs[t]])


def load_w(c, w_ap, slot_view, buf, nsplit=1):
    src = w_ap.rearrange("(c p) n -> p c n", p=128)
    kc = src.shape[1]
    if nsplit == 1 or kc < nsplit:
        c.P.dma("pool", f_dma([(slot_view, src)]), buf, writes=[buf])
    else:
        step = (kc + nsplit - 1) // nsplit
        pairs = [(slot_view[:, i:min(i + step, kc), :], src[:, i:min(i + step, kc), :]) for i in range(0, kc, step)]
        c.P.dma("pool", f_dma(pairs), buf, writes=[buf], n=len(pairs))


def ln_inplace(c, x_ap, x_buf, gam, bet, b_gb, tmp_rot):
    P = c.P
    st, b_st = c.stat.next()
    P.add("dve", lambda h: h.bn_stats(out=st[:, 0:6], in_=x_ap[:, 0:512]), reads=[x_buf], writes=[b_st])
    P.add("dve", lambda h: h.bn_stats(out=st[:, 6:12], in_=x_ap[:, 512:1024]), reads=[x_buf, b_st], writes=[b_st])
    mv, b_mv = c.stat.next()
    P.add("dve", lambda h: h.bn_aggr(out=mv[:, 0:2], in_=st[:, 0:12]), reads=[b_st], writes=[b_mv])
    P.add("act", f_act(mv[:, 2:3], mv[:, 1:2], AF.Sqrt, bias=c.eps_t[:, 0:1]), reads=[b_mv, c.b_eps], writes=[b_mv])
    P.add("dve", lambda h: h.reciprocal(out=mv[:, 2:3], in_=mv[:, 2:3]), reads=[b_mv], writes=[b_mv])
    P.add("dve", f_stt(mv[:, 3:4], mv[:, 0:1], -1.0, mv[:, 2:3], ALU.mult, ALU.mult), reads=[b_mv], writes=[b_mv])
    tmp, b_tmp = tmp_rot.next()
    P.add("act", f_act(tmp[:], x_ap, AF.Identity, bias=mv[:, 3:4], scale=mv[:, 2:3]),
          reads=[x_buf, b_mv], writes=[b_tmp])
    P.add("dve", f_tt(tmp[:], tmp[:], gam[:], ALU.mult), reads=[b_tmp, b_gb], writes=[b_tmp])
    P.add("dve", f_tt(x_ap, tmp[:], bet[:], ALU.add), reads=[b_tmp, b_gb], writes=[x_buf])


def ffn_unit(c, st, xT, xT_bufs, w1, dh, w2, yacc, yacc_bufs, subs, gate=None):
    P = c.P
    maxsub = max(n for _, n in subs)
    actT = c.sb([128, maxsub, GRP], BF16, "actT", st)
    act_bufs = [[Buf("act") for _ in range(GT)] for _ in range(maxsub)]
    w1s = c.rot(2, [128, 8, 2, 256], BF16, "w1s", st)
    w2s = c.rot(2, [128, maxsub, 512], BF16, "w2s", st)
    sg = c.rot(2, [128, 512], BF16, "sg", st)
    for (j0, nj) in subs:
        assert nj % 2 == 0
        for blk in range(nj // 2):
            jj = j0 + blk * 2
            ws, b_ws = w1s.next()
            srcg = w1[:, jj * 128: jj * 128 + 256].rearrange("(c p) n -> p c n", p=128)
            srcu = w1[:, dh + jj * 128: dh + jj * 128 + 256].rearrange("(c p) n -> p c n", p=128)
            P.dma("pool", f_dma([(ws[:, :, 0, :], srcg), (ws[:, :, 1, :], srcu)]), b_ws, writes=[b_ws], n=2)
            for ci in range(2):
                jl = blk * 2 + ci
                for tg in range(GRP // 512):
                    pg, b_pg = c.pm.next()
                    pu, b_pu = c.pm.next()
                    tb = xT_bufs[tg * 4:(tg + 1) * 4]
                    P.add("pe", f_mm_multi([
                        (pg[:], [(ws[:, k, 0, ci * 128:(ci + 1) * 128], xT[:, k, tg * 512:(tg + 1) * 512]) for k in range(8)]),
                        (pu[:], [(ws[:, k, 1, ci * 128:(ci + 1) * 128], xT[:, k, tg * 512:(tg + 1) * 512]) for k in range(8)]),
                    ]), reads=[b_ws] + tb, writes=[b_pg, b_pu])
                    s, b_s = sg.next()
                    P.add("act", f_act(s[:], pg[:], AF.Silu), reads=[b_pg], writes=[b_s])
                    P.add("dve", f_tt(actT[:, jl, tg * 512:(tg + 1) * 512], s[:], pu[:], ALU.mult),
                          reads=[b_s, b_pu], writes=act_bufs[jl][tg * 4:(tg + 1) * 4])
        for half in range(2):
            w2v, b_w2 = w2s.next()
            load_w(c, w2[j0 * 128:(j0 + nj) * 128, half * 512:(half + 1) * 512], w2v[:, 0:nj, :], b_w2, nsplit=2)
            for t in range(GT):
                po, b_po = c.pm.next()
                P.add("pe", f_mm(po[:], [(actT[:, j, t * 128:(t + 1) * 128], w2v[:, j, :]) for j in range(nj)]),
                      reads=[b_w2] + [act_bufs[j][t] for j in range(nj)], writes=[b_po])
                ya = yacc[:, t, half * 512:(half + 1) * 512]
                if gate is None:
                    P.add("dve", f_tt(ya, po[:], ya, ALU.add), reads=[b_po, yacc_bufs[t]], writes=[yacc_bufs[t]])
                else:
                    gfn, b_g = gate
                    P.add("dve", f_stt(ya, po[:], gfn(t), ya, ALU.mult, ALU.add),
                          reads=[b_po, yacc_bufs[t], b_g], writes=[yacc_bufs[t]])


def rope_tables(pos0, n):
    d = 256
    inv = (1.0 / (np.float32(10000.0) ** (np.arange(0, d, 2, dtype=np.float32) / np.float32(d)))).astype(np.float32)
    pos = np.arange(pos0, pos0 + n, dtype=np.float32)
    ang = (pos[:, None] * inv[None, :]).astype(np.float32)
    return np.ascontiguousarray(np.cos(ang).astype(np.float32).T), np.ascontiguousarray(np.sin(ang).astype(np.float32).T)


def retention_tables():
    gam = 1.0 - 2.0 ** (-5.0 - np.arange(RH, dtype=np.float64))
    n = np.arange(128)
    m = np.arange(128)
    dtab = np.zeros((128, RH, 128), np.float64)
    cn = np.zeros((128, RH), np.float64)
    kdec = np.zeros((128, RH), np.float64)
    for h in range(RH):
        g = gam[h]
        M, N = np.meshgrid(m, n, indexing="ij")
        same = (M // 64) == (N // 64)
        cross = (M < 64) & (N >= 64)
        dm = np.where(same, g ** np.abs(N - M), np.where(cross, g ** (N - M).clip(0), 0.0))
        cnh = g ** (n + 1.0)
        dtab[:, h, :] = dm / cnh[None, :] / 16.0
        cn[:, h] = cnh
        kdec[:, h] = g ** (127.0 - m) / 16.0
    cd = [float(g ** 128) for g in gam]
    return dtab.astype(np.float32), cn.astype(np.float32), kdec.astype(np.float32), cd


def load_ln(c, st, gd, bd, name):
    g = c.sb([128, 1024], F32, name + "g", st)
    b = c.sb([128, 1024], F32, name + "b", st)
    buf = Buf(name, key=name)
    c.P.dma("sp", f_dma([(g[:], gd), (b[:], bd)]), buf, writes=[buf], n=2)
    return g, b, buf


def store(c, io, dst_ap, src_ap, src_buf):
    ob = Buf("out")
    io["outs"].append(ob)
    c.P.dma("sp", f_dma([(dst_ap, src_ap)]), src_buf, reads=[src_buf], writes=[ob])


def ret_heads(c, rst, io, mode, g, xsrc, cosd, sind, S32, Sb, b_S32, b_Sb, consts, yT, b_yT, scratch=None):
    P = c.P
    own = mode == "own"
    dtab, b_dtab, cn, b_cn, kdec, b_kdec, cd = consts
    w_in = io["ret_w_in"]
    if scratch is not None:
        sb16 = scratch.bitcast(BF16)
        xT = sb16[:, 0:8192].rearrange("p (c n) -> p c n", c=8)
        qT = sb16[:, 8192:10240].rearrange("p (c n) -> p c n", c=2)
        kT = sb16[:, 10240:12288].rearrange("p (c n) -> p c n", c=2)
        cosT = scratch[:, 6144:7168]
        sinT = scratch[:, 7168:8192]
    else:
        xT = c.sb([128, 8, GRP], BF16, "xT", rst)
        cosT = c.sb([128, GRP], F32, "cosT", rst)
        sinT = c.sb([128, GRP], F32, "sinT", rst)
        qT = c.sb([128, 2, GRP], BF16, "qT", rst)
        kT = c.sb([128, 2, GRP], BF16, "kT", rst)
    b_xT = [Buf("xT") for _ in range(GT)]
    b_cs = Buf("cs", key="cs")
    P.dma("sp", f_dma([(cosT[:], cosd), (sinT[:], sind)]), b_cs, writes=[b_cs], n=2)
    for t in range(GT):
        stg, b_stg = c.stage.next()
        P.dma("sp", f_dma([(stg[:], xsrc[t * 128:(t + 1) * 128, :])]), b_stg, writes=[b_stg])
        tile_to_T(c, stg[:], b_stg, xT, b_xT, t)
    kd = c.sb([128, GT, 256], BF16, "kd", rst)
    Vh = c.sb([128, GT, 512], BF16, "Vh", rst)
    Gh = c.sb([128, GT, 512], BF16, "Gh", rst) if own else None
    wqk = c.rot(2, [128, 8, 256], BF16, "wqk", rst)
    wvg = c.rot(2, [128, 8, 512], BF16, "wvg", rst)
    rtmp = c.rot(4, [128, 512], F32, "rtmp", rst)
    PTr = c.rot(2, [128, 128], BF16, "PT", rst)
    osb = c.rot(3, [128, 512], F32, "osb", rst)
    ysb = c.rot(2, [128, 512], BF16, "ysb", rst)
    for h in range(RH):
        b_qT = [Buf("qT") for _ in range(GT)]
        b_kT = [Buf("kT") for _ in range(GT)]
        b_kd = [Buf("kd") for _ in range(GT)]
        b_V = [Buf("V") for _ in range(GT)]
        b_G = [Buf("G") for _ in range(GT)]
        for which in (("q", "k") if own else ("k",)):
            col0 = (0 if which == "q" else 1024) + h * 256
            ws, b_ws = wqk.next()
            load_w(c, w_in[:, col0:col0 + 256], ws[:], b_ws)
            dst = qT if which == "q" else kT
            b_dst = b_qT if which == "q" else b_kT
            for tg in range(GRP // 512):
                pa, b_pa = c.pm.next()
                pb, b_pb = c.pm.next()
                tb = b_xT[tg * 4:(tg + 1) * 4]
                tok = slice(tg * 512, (tg + 1) * 512)
                P.add("pe", f_mm_multi([
                    (pa[:], [(ws[:, k, 0:128], xT[:, k, tok]) for k in range(8)]),
                    (pb[:], [(ws[:, k, 128:256], xT[:, k, tok]) for k in range(8)]),
                ]), reads=[b_ws] + tb, writes=[b_pa, b_pb])
                t1, b_t1 = rtmp.next()
                t2, b_t2 = rtmp.next()
                P.add("dve", f_tt(t1[:], pa[:], cosT[:, tok], ALU.mult), reads=[b_pa, b_cs], writes=[b_t1])
                P.add("dve", f_tt(t2[:], pb[:], sinT[:, tok], ALU.mult), reads=[b_pb, b_cs], writes=[b_t2])
                P.add("dve", f_tt(dst[:, 0, tok], t1[:], t2[:], ALU.subtract), reads=[b_t1, b_t2],
                      writes=b_dst[tg * 4:(tg + 1) * 4])
                t3, b_t3 = rtmp.next()
                t4, b_t4 = rtmp.next()
                P.add("dve", f_tt(t3[:], pa[:], sinT[:, tok], ALU.mult), reads=[b_pa, b_cs], writes=[b_t3])
                P.add("dve", f_tt(t4[:], pb[:], cosT[:, tok], ALU.mult), reads=[b_pb, b_cs], writes=[b_t4])
                P.add("dve", f_tt(dst[:, 1, tok], t3[:], t4[:], ALU.add), reads=[b_t3, b_t4] + b_dst[tg * 4:(tg + 1) * 4],
                      writes=b_dst[tg * 4:(tg + 1) * 4])
        for t in range(GT):
            pt, b_pt = c.pt.next()
            ptv = pt[:].rearrange("p (c n) -> p c n", n=128)
            P.add("pe", f_tr([(ptv[:, ch, :], kT[:, ch, t * 128:(t + 1) * 128]) for ch in range(2)], c.idb[:]),
                  reads=[b_kT[t], c.b_idb], writes=[b_pt])
            P.add("act", f_act(kd[:, t, :], pt[:, 0:256], AF.Identity, scale=kdec[:, h:h + 1]),
                  reads=[b_pt, b_kdec], writes=[b_kd[t]])
        for which in (("v", "g") if own else ("v",)):
            col0 = (2048 if which == "v" else 4096) + h * 512
            ws, b_ws = wvg.next()
            load_w(c, w_in[:, col0:col0 + 512], ws[:], b_ws, nsplit=2)
            for t in range(GT):
                pv, b_pv = c.pm.next()
                P.add("pe", f_mm(pv[:], [(xT[:, k, t * 128:(t + 1) * 128], ws[:, k, :]) for k in range(8)]),
                      reads=[b_ws, b_xT[t]], writes=[b_pv])
                if which == "v":
                    P.add("act", f_act(Vh[:, t, :], pv[:], AF.Copy), reads=[b_pv], writes=[b_V[t]])
                else:
                    P.add("act", f_act(Gh[:, t, :], pv[:], AF.Silu), reads=[b_pv], writes=[b_G[t]])
        pend = None

        def epilogue(o, b_o, t):
            tk = slice(t * 128, (t + 1) * 128)
            st, b_st = c.stat.next()
            P.add("dve", lambda hh, st=st, o=o: hh.bn_stats(out=st[:, 0:6], in_=o[:]), reads=[b_o], writes=[b_st])
            P.add("dve", lambda hh, st=st: hh.bn_aggr(out=st[:, 8:10], in_=st[:, 0:6]), reads=[b_st], writes=[b_st])
            P.add("act", f_act(st[:, 10:11], st[:, 9:10], AF.Sqrt, bias=c.eps_t[:, 0:1]), reads=[b_st, c.b_eps], writes=[b_st])
            P.add("dve", lambda hh, st=st: hh.reciprocal(out=st[:, 10:11], in_=st[:, 10:11]), reads=[b_st], writes=[b_st])
            P.add("dve", f_stt(st[:, 11:12], st[:, 8:9], -1.0, st[:, 10:11], ALU.mult, ALU.mult), reads=[b_st], writes=[b_st])
            P.add("act", f_act(o[:], o[:], AF.Identity, bias=st[:, 11:12], scale=st[:, 10:11]),
                  reads=[b_o, b_st], writes=[b_o])
            y, b_y = ysb.next()
            P.add("dve", f_tt(y[:], o[:], Gh[:, t, :], ALU.mult), reads=[b_o, b_G[t]], writes=[b_y])
            pt, b_pt = c.pt.next()
            ptv = pt[:].rearrange("p (c n) -> p c n", n=128)
            P.add("pe", f_tr([(ptv[:, ch, :], y[:, ch * 128:(ch + 1) * 128]) for ch in range(4)], c.idb[:]),
                  reads=[b_y, c.b_idb], writes=[b_pt])
            P.add("act", f_act(yT[:, h * 4:(h + 1) * 4, tk], ptv[:, 0:4, :], AF.Copy), reads=[b_pt], writes=[b_yT[h][t]])

        for t in range(GT):
            tk = slice(t * 128, (t + 1) * 128)
            p0, b_p0 = c.pm.next()
            p1, b_p1 = c.pm.next()
            P.add("pe", f_mm_multi([(p0[:], [(kd[:, t, 0:128], Vh[:, t, :])]), (p1[:], [(kd[:, t, 128:256], Vh[:, t, :])])]),
                  reads=[b_kd[t], b_V[t]], writes=[b_p0, b_p1])
            if own:
                pS, b_pS = c.pm.next()
                P.add("pe", f_mm(pS[:, 0:128], [(kT[:, ch, tk], qT[:, ch, tk]) for ch in range(2)]),
                      reads=[b_kT[t], b_qT[t]], writes=[b_pS])
                PT, b_PT = PTr.next()
                P.add("dve", f_tt(PT[:], pS[:, 0:128], dtab[:, h, :], ALU.mult), reads=[b_pS, b_dtab], writes=[b_PT])
                po, b_po = c.pm.next()
                P.add("pe", f_mm(po[:], [(PT[:], Vh[:, t, :]), (qT[:, 0, tk], Sb[:, h, 0, :]), (qT[:, 1, tk], Sb[:, h, 1, :])]),
                      reads=[b_PT, b_V[t], b_qT[t], b_Sb[h]], writes=[b_po])
                o, b_o = osb.next()
                P.add("act", f_act(o[:], po[:], AF.Identity, scale=cn[:, h:h + 1]), reads=[b_po, b_cn], writes=[b_o])
            P.add("dve", f_stt(S32[:, h, 0, :], S32[:, h, 0, :], cd[h], p0[:], ALU.mult, ALU.add),
                  reads=[b_p0, b_S32[h]], writes=[b_S32[h]])
            P.add("dve", f_stt(S32[:, h, 1, :], S32[:, h, 1, :], cd[h], p1[:], ALU.mult, ALU.add),
                  reads=[b_p1, b_S32[h]], writes=[b_S32[h]])
            P.add("act", f_act(Sb[:, h, :, :], S32[:, h, :, :], AF.Copy), reads=[b_S32[h]], writes=[b_Sb[h]])
            if own:
                if pend is not None:
                    epilogue(*pend)
                pend = (o, b_o, t)
        if own and pend is not None:
            epilogue(*pend)


def build_layer0(c, io):
    P = c.P
    _, _, _, cd = retention_tables()
    dtab, b_dtab = load_const(c, io["dtab"], [128, RH, 128], F32, "dtab")
    cn, b_cn = load_const(c, io["cn"], [128, RH], F32, "cn")
    kdec, b_kdec = load_const(c, io["kdec"], [128, RH], F32, "kdec")
    consts = (dtab, b_dtab, cn, b_cn, kdec, b_kdec, cd)
    S32 = c.sb([128, RH, 2, 512], F32, "S32")
    Sb = c.sb([128, RH, 2, 512], BF16, "Sb")
    b_S32 = [Buf("S32") for _ in range(RH)]
    b_Sb = [Buf("Sb") for _ in range(RH)]
    for h in range(RH):
        P.add("dve", f_memset(S32[:, h, :, :], 0.0), writes=[b_S32[h]])
        P.add("dve", f_memset(Sb[:, h, :, :], 0.0), writes=[b_Sb[h]])
    w_out = io["ret_w_out"]

    passes = [("prev", 0), ("prev", 1), ("own", 0), ("own", 1)]
    for mode, g in passes:
        own = mode == "own"
        xsrc = (io["x_own"] if own else io["x_prev"])[g * GRP:(g + 1) * GRP, :]
        cosd = (io["cos_own"] if own else io["cos_prev"])[:, g * GRP:(g + 1) * GRP]
        sind = (io["sin_own"] if own else io["sin_prev"])[:, g * GRP:(g + 1) * GRP]
        P.barrier()
        if not own:
            with contextlib.ExitStack() as rst:
                ret_heads(c, rst, io, mode, g, xsrc, cosd, sind, S32, Sb, b_S32, b_Sb, consts, None, None)
            continue
        with contextlib.ExitStack() as gst:
            x1 = c.sb([128, GT, 1024], F32, "x1", gst)
            b_x1 = [Buf("x1") for _ in range(GT)]
            with contextlib.ExitStack() as yst:
                yT = c.sb([128, 16, GRP], BF16, "yT", yst)
                b_yT = [[Buf("yT") for _ in range(GT)] for _ in range(RH)]
                with contextlib.ExitStack() as rst:
                    ret_heads(c, rst, io, mode, g, xsrc, cosd, sind, S32, Sb, b_S32, b_Sb, consts, yT, b_yT)
                P.barrier()
                with contextlib.ExitStack() as mst:
                    gam, bet, b_gb = load_ln(c, mst, io["g_mix"], io["b_mix"], "lnmix")
                    tmpf = c.rot(2, [128, 1024], F32, "tmpf", mst)
                    wos = c.rot(2, [128, 16, 512], BF16, "wos", mst)
                    xst = c.rot(2, [128, 512], F32, "xst", mst)
                    for half in range(2):
                        wo, b_wo = wos.next()
                        load_w(c, w_out[:, half * 512:(half + 1) * 512], wo[:], b_wo, nsplit=2)
                        for t in range(GT):
                            pm_, b_pm = c.pm.next()
                            P.add("pe", f_mm(pm_[:], [(yT[:, ch, t * 128:(t + 1) * 128], wo[:, ch, :]) for ch in range(16)]),
                                  reads=[b_wo] + [b_yT[h][t] for h in range(RH)], writes=[b_pm])
                            xs, b_xs = xst.next()
                            P.dma("sp", f_dma([(xs[:], xsrc[t * 128:(t + 1) * 128, half * 512:(half + 1) * 512])]), b_xs, writes=[b_xs])
                            P.add("dve", f_stt(x1[:, t, half * 512:(half + 1) * 512], xs[:], ALPHA, pm_[:], ALU.mult, ALU.add),
                                  reads=[b_xs, b_pm], writes=[b_x1[t]])
                    for t in range(GT):
                        ln_inplace(c, x1[:, t, :], b_x1[t], gam, bet, b_gb, tmpf)
            P.barrier()
            with contextlib.ExitStack() as fst:
                gam, bet, b_gb = load_ln(c, fst, io["g_ffn"], io["b_ffn"], "lnffn")
                tmpf = c.rot(2, [128, 1024], F32, "tmpf", fst)
                x1T = c.sb([128, 8, GRP], BF16, "x1T", fst)
                b_x1T = [Buf("x1T") for _ in range(GT)]
                yacc = c.sb([128, GT, 1024], F32, "yacc", fst)
                b_ya = [Buf("yacc", key=f"ya{t}") for t in range(GT)]
                for t in range(GT):
                    tile_to_T(c, x1[:, t, :], b_x1[t], x1T, b_x1T, t)
                    P.add("act", f_act(yacc[:, t, :], x1[:, t, :], AF.Copy, scale=ALPHA), reads=[b_x1[t]], writes=[b_ya[t]])
                with contextlib.ExitStack() as ust:
                    ffn_unit(c, ust, x1T, b_x1T, io["ffn_w_in"], D_FF, io["ffn_w_out"], yacc, b_ya, [(0, 12), (12, 10)])
                for t in range(GT):
                    ln_inplace(c, yacc[:, t, :], b_ya[t], gam, bet, b_gb, tmpf)
                    r0 = g * GRP + t * 128
                    store(c, io, io["x2"][r0:r0 + 128, :], yacc[:, t, :], b_ya[t])
                P.barrier()
                with contextlib.ExitStack() as kst:
                    x2T = x1T
                    b_x2T = [Buf("x2T") for _ in range(GT)]
                    for t in range(GT):
                        tile_to_T(c, yacc[:, t, :], b_ya[t], x2T, b_x2T, t)
                    wkv = c.rot(2, [128, 8, 512], BF16, "wkv", kst)
                    ksb = c.rot(3, [128, 512], BF16, "ksb", kst)
                    for cb in range(2):
                        ws, b_ws = wkv.next()
                        load_w(c, io["sb_w_kv"][:, cb * 512:(cb + 1) * 512], ws[:], b_ws, nsplit=2)
                        for ch in range(4):
                            for tg in range(GRP // 512):
                                pk, b_pk = c.pm.next()
                                tok = slice(tg * 512, (tg + 1) * 512)
                                P.add("pe", f_mm(pk[:], [(ws[:, k, ch * 128:(ch + 1) * 128], x2T[:, k, tok]) for k in range(8)]),
                                      reads=[b_ws] + b_x2T[tg * 4:(tg + 1) * 4], writes=[b_pk])
                                ks, b_ks = ksb.next()
                                P.add("act", f_act(ks[:], pk[:], AF.Copy), reads=[b_pk], writes=[b_ks])
                                f0 = (cb * 4 + ch) * 128
                                t0 = g * GRP + tg * 512
                                store(c, io, io["KT"][f0:f0 + 128, t0:t0 + 512], ks[:], b_ks)
                    for cb in range(2):
                        ws, b_ws = wkv.next()
                        load_w(c, io["sb_w_kv"][:, 1024 + cb * 512:1024 + (cb + 1) * 512], ws[:], b_ws, nsplit=2)
                        for t in range(GT):
                            pv, b_pv = c.pm.next()
                            P.add("pe", f_mm(pv[:], [(x2T[:, k, t * 128:(t + 1) * 128], ws[:, k, :]) for k in range(8)]),
                                  reads=[b_ws, b_x2T[t]], writes=[b_pv])
                            ks, b_ks = ksb.next()
                            P.add("act", f_act(ks[:], pv[:], AF.Copy), reads=[b_pv], writes=[b_ks])
                            r0 = g * GRP + t * 128
                            store(c, io, io["V"][r0:r0 + 128, cb * 512:(cb + 1) * 512], ks[:], b_ks)


def make_nc_layer0():
    nc = bass.Bass("TRN2", target_bir_lowering=False)
    c = Ctx(nc)
    io = {"outs": []}

    def din(name, shape, dt=F32):
        io[name] = c.dram(name, shape, dt, "ExternalInput")
    din("x_own", [TOK, D]); din("x_prev", [TOK, D])
    din("cos_own", [128, TOK]); din("sin_own", [128, TOK]); din("cos_prev", [128, TOK]); din("sin_prev", [128, TOK])
    din("ret_w_in", [D, 6144]); din("ret_w_out", [2048, D])
    din("ffn_w_in", [D, 2 * D_FF]); din("ffn_w_out", [D_FF, D]); din("sb_w_kv", [D, 2048])
    din("g_mix", [128, D]); din("b_mix", [128, D]); din("g_ffn", [128, D]); din("b_ffn", [128, D])
    din("dtab", [128, RH, 128]); din("cn", [128, RH]); din("kdec", [128, RH]); din("ident", [128, 128])
    io["x2"] = c.dram("x2", [TOK, D], F32, "ExternalOutput")
    io["KT"] = c.dram("KT", [D, TOK], BF16, "ExternalOutput")
    io["V"] = c.dram("V", [TOK, D], BF16, "ExternalOutput")
    with c.root:
        setup_common(c, io["ident"])
        build_layer0(c, io)
        c.P.add("sp", lambda h: None, reads=io["outs"])
        cnt = c.P.emit(nc)
    return nc, cnt


def layer0_inputs(inputs, core):
    b, half = core // 2, core % 2
    x = inputs["x"]
    dtab, cn, kdec, _ = retention_tables()
    co, so = rope_tables(half * TOK, TOK)
    cp, sp_ = rope_tables(0, TOK)
    f = np.float32
    bc = lambda v: np.ascontiguousarray(np.broadcast_to(np.asarray(v, f)[None, :], (128, D)))
    return {
        "x_own": np.ascontiguousarray(x[b, half * TOK:(half + 1) * TOK]),
        "x_prev": np.ascontiguousarray(x[b, 0:TOK]) if half == 1 else np.zeros((TOK, D), f),
        "cos_own": co, "sin_own": so, "cos_prev": cp, "sin_prev": sp_,
        "ret_w_in": np.ascontiguousarray(inputs["ret_w_in"][0]),
        "ret_w_out": np.ascontiguousarray(inputs["ret_w_out"][0]),
        "ffn_w_in": np.ascontiguousarray(inputs["ffn_w_in"][0]),
        "ffn_w_out": np.ascontiguousarray(inputs["ffn_w_out"][0]),
        "sb_w_kv": np.ascontiguousarray(inputs["sb_w_kv"]),
        "g_mix": bc(inputs["ln_mix_g"][0]), "b_mix": bc(inputs["ln_mix_b"][0]),
        "g_ffn": bc(inputs["ln_ffn_g"][0]), "b_ffn": bc(inputs["ln_ffn_b"][0]),
        "dtab": dtab, "cn": cn, "kdec": kdec, "ident": np.eye(128, dtype=f),
    }


NEG = -30000.0


def attn_tables():
    s = np.arange(128)[:, None]
    t = np.arange(512)[None, :]
    m01 = np.zeros((128, 4, 512), np.float32)
    for di in range(4):
        m01[:, di, :] = (di * 128 + s < t).astype(np.float32)
    negm = (1.0 - m01) * NEG
    j = np.arange(128)[:, None]
    ss = np.arange(128)[None, :]
    m1 = -(j >= ss).astype(np.float32)
    negones = -np.ones((128, 128), np.float32)
    return m01, negm, m1, negones


def f_mm1(out, l, r, start, stop):
    return lambda h: h.matmul(out, lhsT=l, rhs=r, start=start, stop=stop)


DEBUG = False
ATT_THR = 130.0
ATT_FREE = 7


def attention_group(c, ast, io, g, oT, b_oT, x2T, b_x2T, consts):
    P = c.P
    m01, b_m01, negm, b_negm, M1, b_M1, NO, b_NO = consts
    nslot_blocks = 16 + g * 8 + 8
    KTr = c.rot(2, [128, 4096], BF16, "KTh", ast)
    Vr = c.rot(2, [128, 32, 128], BF16, "Vhp", ast)
    QTr = c.rot(2, [128, GRP], BF16, "QTh", ast)
    wqr = c.rot(2, [128, 8, 128], BF16, "wq", ast)
    esb = c.rot(4, [128, 512], F32, "esb", ast)
    spb = c.rot(8, [128, 512], BF16, "spb", ast)
    wbf = c.rot(4, [128, 512], BF16, "wbf", ast)
    Rbf = [c.rot(2, [128, 512], BF16, f"Rbf{i}", ast) for i in range(4)]
    rmn = c.rot(2, [128, 512], BF16, "rmn", ast)
    zt = c.sb([128, 128], BF16, "zt", ast)
    b_zt = Buf("zt")
    P.add("dve", f_memset(zt[:], 0.0), writes=[b_zt])
    chk = c.sb([1, 8], F32, "chk", ast)
    chk_i = c.sb([1, 8], I32, "chk_i", ast)
    b_chk = Buf("chk")
    pmi = c.pm.items
    accs = [c.pt.items[0], c.pt.items[1], pmi[4], pmi[5]]
    pma = Rot(pmi[0:4])
    NQ = GRP // 512
    for hp in range(8):
        KT, b_KT = KTr.next()
        V, b_V = Vr.next()
        ns = nslot_blocks * 128
        P.dma("sp", f_dma([(KT[:, 0:ns], io["KTs"][hp * 128:(hp + 1) * 128, 0:ns])]), b_KT, writes=[b_KT])
        P.dma("sp", f_dma([(V[:, 0:nslot_blocks, :],
                            io["Vs"][0:ns, hp * 128:(hp + 1) * 128].rearrange("(k p) n -> p k n", p=128))]),
              b_V, writes=[b_V])
        wq, b_wq = wqr.next()
        load_w(c, io["sb_w_q"][:, hp * 128:(hp + 1) * 128], wq[:], b_wq)
        QT, b_QT = QTr.next()
        for tg in range(NQ):
            pq, b_pq = c.pm.next()
            tok = slice(tg * 512, (tg + 1) * 512)
            P.add("pe", f_mm(pq[:], [(wq[:, k, :], x2T[:, k, tok]) for k in range(8)]),
                  reads=[b_wq] + b_x2T[tg * 4:(tg + 1) * 4], writes=[b_pq])
            P.add("act", f_act(QT[:, tok], pq[:], AF.Copy, scale=0.125), reads=[b_pq], writes=[b_QT])
        streams = [(qt, hd) for qt in range(NQ) for hd in range(2)]
        nkbs = {qt: 16 + g * 8 + qt * 4 + 4 for qt in range(NQ)}
        nmax = max(nkbs.values())
        Rcur = {}

        def block(i):
            live = [(k, qt, hd) for k, (qt, hd) in enumerate(streams) if i < nkbs[qt]]
            sps = {}
            for (k, qt, hd) in live:
                kb = nkbs[qt] - 1 - i
                pb = 64 * hd
                tok = slice(qt * 512, (qt + 1) * 512)
                di = kb - (nkbs[qt] - 4)
                pz, b_pz = pma.next()
                P.add("pe", f_mm(pz[:], [(KT[pb:pb + 64, kb * 128:(kb + 1) * 128], QT[pb:pb + 64, tok])]),
                      reads=[b_KT, b_QT], writes=[b_pz])
                e, b_e = esb.next()
                P.add("act", f_act(e[:], pz[:], AF.Exp), reads=[b_pz], writes=[b_e])
                sp_, b_sp = spb.next()
                P.add("act", f_act(sp_[:], e[:], AF.Ln, bias=1.0), reads=[b_e], writes=[b_sp])
                if di >= 0:
                    P.add("dve", f_tt(sp_[:], sp_[:], m01[:, di, :], ALU.mult), reads=[b_sp, b_m01], writes=[b_sp])
                sps[k] = (sp_, b_sp)
            ws = {}
            for (k, qt, hd) in live:
                kb = nkbs[qt] - 1 - i
                pb = 64 * hd
                tok = slice(qt * 512, (qt + 1) * 512)
                di = kb - (nkbs[qt] - 4)
                sp_, b_sp = sps[k]
                pE, b_pE = pma.next()
                pairs = [(KT[pb:pb + 64, kb * 128:(kb + 1) * 128], QT[pb:pb + 64, tok]), (M1[:], sp_[:])]
                reads = [b_KT, b_QT, b_M1, b_sp]
                if i > 0:
                    rb, b_rb = Rcur[k]
                    pairs.append((NO[:], rb[:]))
                    reads += [b_NO, b_rb]
                if di >= 0:
                    pairs.append((c.idb[:], negm[:, di, :]))
                    reads += [c.b_idb, b_negm]
                P.add("pe", f_mm(pE[:], pairs), reads=reads, writes=[b_pE])
                w, b_w = wbf.next()
                P.add("act", f_act(w[:], pE[:], AF.Exp), reads=[b_pE], writes=[b_w])
                ws[k] = (w, b_w)
                if i == 0:
                    Rcur[k] = (sp_, b_sp)
                else:
                    ro, b_ro = Rcur[k]
                    rb, b_rb = Rbf[k].next()
                    P.add("dve", f_tt(rb[:], ro[:], sp_[:], ALU.add), reads=[b_sp, b_ro], writes=[b_rb])
                    Rcur[k] = (rb, b_rb)
            for (k, qt, hd) in live:
                kb = nkbs[qt] - 1 - i
                w, b_w = ws[k]
                acc_t, b_acc = accs[k]
                acc = acc_t[:].bitcast(F32) if k < 2 else acc_t[:]
                P.add("pe", f_mm1(acc, V[:, kb, :], w[:], i == 0, False), reads=[b_V, b_w], writes=[b_acc])

        def checkpoint():
            r0, b_r0 = Rcur[0]
            cur, b_cur = r0, b_r0
            for k in range(1, len(streams)):
                rk, b_rk = Rcur[k]
                rm, b_rm = rmn.next()
                P.add("dve", f_tt(rm[:], cur[:], rk[:], ALU.min), reads=[b_cur, b_rk], writes=[b_rm])
                cur, b_cur = rm, b_rm
            pc, b_pc = pma.next()
            P.add("pe", f_mm(pc[0:1, :], [(NO[:, 0:1], cur[:])]), reads=[b_NO, b_cur], writes=[b_pc])
            P.add("dve", lambda h: h.reduce_max(out=chk[0:1, 0:1], in_=pc[0:1, :], axis=mybir.AxisListType.X),
                  reads=[b_pc, b_chk], writes=[b_chk])
            P.add("dve", f_ts(chk[0:1, 1:2], chk[0:1, 0:1], 1000.0 + ATT_THR, 0.0, ALU.add, ALU.max), reads=[b_chk], writes=[b_chk])
            P.add("dve", f_copy(chk_i[0:1, 0:1], chk[0:1, 1:2]), reads=[b_chk], writes=[b_chk])
            if "dbg" in io and not io.get("dbg_done"):
                io["dbg_done"] = True
                ob = Buf("dbgout")
                io["outs"].append(ob)
                P.dma("sp", f_dma([(io["dbg"], chk[0:1, 0:8])]), b_chk, reads=[b_chk], writes=[ob])
                for k2 in range(4):
                    ob2 = Buf("dbgout2")
                    io["outs"].append(ob2)
                    P.dma("sp", f_dma([(io["dbg2"][k2], Rcur[k2][0][:])]), Rcur[k2][1], reads=[Rcur[k2][1]], writes=[ob2])
                ob2 = Buf("dbgout2")
                io["outs"].append(ob2)
                P.dma("sp", f_dma([(io["dbg2"][4], cur[:])]), b_cur, reads=[b_cur], writes=[ob2])
            for eng in ("pe", "act", "dve"):
                P.reg_load(eng, "att", chk_i[0:1, 0:1], reads=[b_chk])

        for i in range(ATT_FREE):
            block(i)
        checkpoint()
        for i in range(ATT_FREE, nmax):
            P.region_begin("att", 1000)
            block(i)
            if i < nmax - 1:
                checkpoint()
            P.region_end()
        for k, (qt, hd) in enumerate(streams):
            pb = 64 * hd
            tok = slice(qt * 512, (qt + 1) * 512)
            acc_t, b_acc = accs[k]
            acc = acc_t[:].bitcast(F32) if k < 2 else acc_t[:]
            P.add("pe", f_mm1(acc, zt[:], m01[:, 0, :], False, True), reads=[b_zt, b_m01], writes=[b_acc])
            P.add("act", f_act(oT[pb:pb + 64, hp, tok], acc[pb:pb + 64, :], AF.Copy), reads=[b_acc],
                  writes=b_oT[qt * 4:(qt + 1) * 4])


def build_layer1(c, io):
    P = c.P
    c.one_t = c.sb([128, 2], F32, "one")
    P.add("dve", f_memset(c.one_t[:], 1.0), writes=[c.b_eps])
    m01, b_m01 = load_const(c, io["m01"], [128, 4, 512], BF16, "m01", eng="pool")
    negm, b_negm = load_const(c, io["negm"], [128, 4, 512], BF16, "negm", eng="pool")
    M1, b_M1 = load_const(c, io["M1"], [128, 128], BF16, "M1", eng="pool")
    NO, b_NO = load_const(c, io["NO"], [128, 128], BF16, "NO", eng="pool")
    aconsts = (m01, b_m01, negm, b_negm, M1, b_M1, NO, b_NO)
    wr, b_wr = load_const(c, io["moe_w_router"].rearrange("(c p) n -> p c n", p=128), [128, 8, N_EXP], F32, "wr")
    for g in range(TOK // GRP):
        xsrc = io["x2"][g * GRP:(g + 1) * GRP, :]
        P.barrier()
        with contextlib.ExitStack() as gst:
            x3 = c.sb([128, GT, 1024], F32, "x3", gst)
            b_x3 = [Buf("x3") for _ in range(GT)]
            with contextlib.ExitStack() as ast:
                x2T = c.sb([128, 8, GRP], BF16, "x2T", ast)
                b_x2T = [Buf("x2T") for _ in range(GT)]
                oT = c.sb([128, 8, GRP], BF16, "oT", ast)
                b_oT = [Buf("oT") for _ in range(GT)]
                for t in range(GT):
                    stg, b_stg = c.stage.next()
                    P.dma("sp", f_dma([(stg[:], xsrc[t * 128:(t + 1) * 128, :])]), b_stg, writes=[b_stg])
                    tile_to_T(c, stg[:], b_stg, x2T, b_x2T, t)
                with contextlib.ExitStack() as hst:
                    attention_group(c, hst, io, g, oT, b_oT, x2T, b_x2T, aconsts)
                P.barrier()
                with contextlib.ExitStack() as mst:
                    gam, bet, b_gb = load_ln(c, mst, io["g_mix"], io["b_mix"], "lnmix")
                    tmpf = c.rot(2, [128, 1024], F32, "tmpf", mst)
                    wos = c.rot(2, [128, 8, 512], BF16, "wos", mst)
                    xst = c.rot(2, [128, 512], F32, "xst", mst)
                    for half in range(2):
                        wo, b_wo = wos.next()
                        load_w(c, io["sb_w_out"][:, half * 512:(half + 1) * 512], wo[:], b_wo, nsplit=2)
                        for t in range(GT):
                            pm_, b_pm = c.pm.next()
                            P.add("pe", f_mm(pm_[:], [(oT[:, ch, t * 128:(t + 1) * 128], wo[:, ch, :]) for ch in range(8)]),
                                  reads=[b_wo, b_oT[t]], writes=[b_pm])
                            xs, b_xs = xst.next()
                            P.dma("sp", f_dma([(xs[:], xsrc[t * 128:(t + 1) * 128, half * 512:(half + 1) * 512])]), b_xs, writes=[b_xs])
                            P.add("dve", f_stt(x3[:, t, half * 512:(half + 1) * 512], xs[:], ALPHA, pm_[:], ALU.mult, ALU.add),
                                  reads=[b_xs, b_pm], writes=[b_x3[t]])
                    for t in range(GT):
                        ln_inplace(c, x3[:, t, :], b_x3[t], gam, bet, b_gb, tmpf)
            P.barrier()
            with contextlib.ExitStack() as fst:
                gam, bet, b_gb = load_ln(c, fst, io["g_ffn"], io["b_ffn"], "lnffn")
                tmpf = c.rot(2, [128, 1024], F32, "tmpf", fst)
                x3T = c.sb([128, 8, GRP], BF16, "x3T", fst)
                b_x3T = [Buf("x3T") for _ in range(GT)]
                yacc = c.sb([128, GT, 1024], F32, "yacc", fst)
                b_ya = [Buf("yacc", key=f"ya{t}") for t in range(GT)]
                gates = c.sb([128, GT, N_EXP], F32, "gates", fst)
                b_gt = Buf("gates")
                xTf = c.rot(2, [128, 8, 128], F32, "xTf", fst)
                lgr = c.rot(2, [128, 32], F32, "lg", fst)
                for t in range(GT):
                    tile_to_T(c, x3[:, t, :], b_x3[t], x3T, b_x3T, t)
                    P.add("act", f_act(yacc[:, t, :], x3[:, t, :], AF.Copy, scale=ALPHA), reads=[b_x3[t]], writes=[b_ya[t]])
                    xf, b_xf = xTf.next()
                    for hf in range(2):
                        pp, b_pp = c.pm.next()
                        ppv = pp[:].rearrange("p (c n) -> p c n", n=128)
                        P.add("pe", f_tr([(ppv[:, ch, :], x3[:, t, (hf * 4 + ch) * 128:(hf * 4 + ch + 1) * 128]) for ch in range(4)], c.idf[:]),
                              reads=[b_x3[t], c.b_idf], writes=[b_pp])
                        P.add("dve", f_copy(xf[:, hf * 4:(hf + 1) * 4, :], ppv[:, 0:4, :]), reads=[b_pp], writes=[b_xf])
                    pl, b_pl = c.pm.next()
                    P.add("pe", f_mm(pl[:, 0:N_EXP], [(xf[:, ch, :], wr[:, ch, :]) for ch in range(8)]),
                          reads=[b_xf, b_wr], writes=[b_pl])
                    lg, b_lg = lgr.next()
                    P.add("dve", f_copy(lg[:, 0:8], pl[:, 0:N_EXP]), reads=[b_pl], writes=[b_lg])
                    P.add("dve", lambda h, lg=lg: h.max(out=lg[:, 8:16], in_=lg[:, 0:8]), reads=[b_lg], writes=[b_lg])
                    P.add("dve", f_ts(lg[:, 16:17], lg[:, 8:9], -1.0, None, ALU.mult), reads=[b_lg], writes=[b_lg])
                    P.add("act", f_act(lg[:, 24:32], lg[:, 0:8], AF.Exp, bias=lg[:, 16:17]), reads=[b_lg], writes=[b_lg])
                    P.add("dve", f_ts(gates[:, t, :], lg[:, 0:8], lg[:, 9:10], None, ALU.is_ge), reads=[b_lg, b_gt], writes=[b_gt])
                    P.add("dve", f_tt(gates[:, t, :], gates[:, t, :], lg[:, 24:32], ALU.mult), reads=[b_lg, b_gt], writes=[b_gt])
                    P.add("dve", lambda h, lg=lg, t=t: h.reduce_sum(out=lg[:, 17:18], in_=gates[:, t, :], axis=mybir.AxisListType.X),
                          reads=[b_gt, b_lg], writes=[b_lg])
                    P.add("dve", lambda h, lg=lg: h.reciprocal(out=lg[:, 17:18], in_=lg[:, 17:18]), reads=[b_lg], writes=[b_lg])
                    P.add("dve", f_ts(gates[:, t, :], gates[:, t, :], lg[:, 17:18], None, ALU.mult), reads=[b_lg, b_gt], writes=[b_gt])
                for e in range(N_EXP):
                    with contextlib.ExitStack() as ust:
                        ffn_unit(c, ust, x3T, b_x3T, io["moe_w_in"][e * D:(e + 1) * D, :], D_EXP,
                                 io["moe_w_out"][e * D_EXP:(e + 1) * D_EXP, :], yacc, b_ya, [(0, 14), (14, 14)],
                                 gate=((lambda t, e=e: gates[:, t, e:e + 1]), b_gt))
                    P.barrier()
                for t in range(GT):
                    ln_inplace(c, yacc[:, t, :], b_ya[t], gam, bet, b_gb, tmpf)
                    r0 = g * GRP + t * 128
                    store(c, io, io["out"][r0:r0 + 128, :], yacc[:, t, :], b_ya[t])


def make_nc_layer1():
    nc = bass.Bass("TRN2", target_bir_lowering=False)
    c = Ctx(nc)
    io = {"outs": []}

    def din(name, shape, dt=F32):
        io[name] = c.dram(name, shape, dt, "ExternalInput")
    din("x2", [TOK, D]); din("KTs", [D, 2 * TOK], BF16); din("Vs", [2 * TOK, D], BF16)
    din("sb_w_q", [D, D]); din("sb_w_out", [D, D])
    din("moe_w_router", [D, N_EXP]); din("moe_w_in", [N_EXP * D, 2 * D_EXP]); din("moe_w_out", [N_EXP * D_EXP, D])
    din("g_mix", [128, D]); din("b_mix", [128, D]); din("g_ffn", [128, D]); din("b_ffn", [128, D])
    din("m01", [128, 4, 512]); din("negm", [128, 4, 512]); din("M1", [128, 128]); din("NO", [128, 128]); din("ident", [128, 128])
    io["out"] = c.dram("out", [TOK, D], F32, "ExternalOutput")
    with c.root:
        setup_common(c, io["ident"])
        build_layer1(c, io)
        c.P.add("sp", lambda h: None, reads=io["outs"])
        cnt = c.P.emit(nc)
    return nc, cnt


def layer1_inputs(inputs, core, x2, KTs, Vs):
    f = np.float32
    m01, negm, m1, negones = attn_tables()
    bc = lambda v: np.ascontiguousarray(np.broadcast_to(np.asarray(v, f)[None, :], (128, D)))
    return {
        "x2": x2, "KTs": KTs, "Vs": Vs,
        "sb_w_q": np.ascontiguousarray(inputs["sb_w_q"][0]), "sb_w_out": np.ascontiguousarray(inputs["sb_w_out"][0]),
        "moe_w_router": np.ascontiguousarray(inputs["moe_w_router"][0]),
        "moe_w_in": np.ascontiguousarray(inputs["moe_w_in"][0]).reshape(N_EXP * D, 2 * D_EXP),
        "moe_w_out": np.ascontiguousarray(inputs["moe_w_out"][0]).reshape(N_EXP * D_EXP, D),
        "g_mix": bc(inputs["ln_mix_g"][1]), "b_mix": bc(inputs["ln_mix_b"][1]),
        "g_ffn": bc(inputs["ln_ffn_g"][1]), "b_ffn": bc(inputs["ln_ffn_b"][1]),
        "m01": m01, "negm": negm, "M1": m1, "NO": negones, "ident": np.eye(128, dtype=f),
    }


def layer0_group(c, io, L0, g_tok, xsrc, cosd, sind, xs, b_xs, slot0, vscale):
    P = c.P
    S32, Sb, b_S32, b_Sb, consts = L0
    scratch = xs.rearrange("p t d -> p (t d)")
    P.barrier()
    with contextlib.ExitStack() as yst:
        yT = c.sb([128, 16, GRP], BF16, "yT", yst)
        b_yT = [[Buf("yT") for _ in range(GT)] for _ in range(RH)]
        with contextlib.ExitStack() as rst:
            ret_heads(c, rst, io, "own", 0, xsrc, cosd, sind, S32, Sb, b_S32, b_Sb, consts, yT, b_yT, scratch=scratch)
        P.barrier()
        with contextlib.ExitStack() as mst:
            gam, bet, b_gb = load_ln(c, mst, io["g_mix0"], io["b_mix0"], "lnmix")
            tmpf = c.rot(2, [128, 1024], F32, "tmpf", mst)
            wos = c.rot(2, [128, 16, 512], BF16, "wos", mst)
            xst = c.rot(2, [128, 512], F32, "xst", mst)
            for half in range(2):
                wo, b_wo = wos.next()
                load_w(c, io["ret_w_out"][:, half * 512:(half + 1) * 512], wo[:], b_wo, nsplit=2)
                for t in range(GT):
                    pm_, b_pm = c.pm.next()
                    P.add("pe", f_mm(pm_[:], [(yT[:, ch, t * 128:(t + 1) * 128], wo[:, ch, :]) for ch in range(16)]),
                          reads=[b_wo] + [b_yT[h][t] for h in range(RH)], writes=[b_pm])
                    xq, b_xq = xst.next()
                    P.dma("sp", f_dma([(xq[:], xsrc[t * 128:(t + 1) * 128, half * 512:(half + 1) * 512])]), b_xq, writes=[b_xq])
                    P.add("dve", f_stt(xs[:, t, half * 512:(half + 1) * 512], xq[:], ALPHA, pm_[:], ALU.mult, ALU.add),
                          reads=[b_xq, b_pm], writes=[b_xs[t]])
            for t in range(GT):
                ln_inplace(c, xs[:, t, :], b_xs[t], gam, bet, b_gb, tmpf)
    P.barrier()
    with contextlib.ExitStack() as fst:
        gam, bet, b_gb = load_ln(c, fst, io["g_ffn0"], io["b_ffn0"], "lnffn")
        tmpf = c.rot(2, [128, 1024], F32, "tmpf", fst)
        x1T = c.sb([128, 8, GRP], BF16, "x1T", fst)
        b_x1T = [Buf("x1T") for _ in range(GT)]
        for t in range(GT):
            tile_to_T(c, xs[:, t, :], b_xs[t], x1T, b_x1T, t)
            P.add("act", f_act(xs[:, t, :], xs[:, t, :], AF.Copy, scale=ALPHA), reads=[b_xs[t]], writes=[b_xs[t]])
        with contextlib.ExitStack() as ust:
            ffn_unit(c, ust, x1T, b_x1T, io["ffn_w_in"], D_FF, io["ffn_w_out"], xs, b_xs, [(0, 12), (12, 10)])
        for t in range(GT):
            ln_inplace(c, xs[:, t, :], b_xs[t], gam, bet, b_gb, tmpf)
        P.barrier()
        with contextlib.ExitStack() as kst:
            x2T = x1T
            b_x2T = [Buf("x2T") for _ in range(GT)]
            for t in range(GT):
                tile_to_T(c, xs[:, t, :], b_xs[t], x2T, b_x2T, t)
            wkv = c.rot(2, [128, 8, 512], BF16, "wkv", kst)
            ksb = c.rot(3, [128, 512], BF16, "ksb", kst)
            for cb in range(2):
                ws, b_ws = wkv.next()
                load_w(c, io["sb_w_kv"][:, cb * 512:(cb + 1) * 512], ws[:], b_ws, nsplit=2)
                for ch in range(4):
                    for tg in range(GRP // 512):
                        pk, b_pk = c.pm.next()
                        tok = slice(tg * 512, (tg + 1) * 512)
                        P.add("pe", f_mm(pk[:], [(ws[:, k, ch * 128:(ch + 1) * 128], x2T[:, k, tok]) for k in range(8)]),
                              reads=[b_ws] + b_x2T[tg * 4:(tg + 1) * 4], writes=[b_pk])
                        ks, b_ks = ksb.next()
                        if vscale is None:
                            P.add("act", f_act(ks[:], pk[:], AF.Copy), reads=[b_pk], writes=[b_ks])
                        else:
                            P.add("act", f_act(ks[:], pk[:], AF.Identity, scale=vscale[0]), reads=[b_pk, vscale[1]], writes=[b_ks])
                        f0 = (cb * 4 + ch) * 128
                        t0 = slot0 + tg * 512
                        store(c, io, io["KTs"][f0:f0 + 128, t0:t0 + 512], ks[:], b_ks)
            for cb in range(2):
                ws, b_ws = wkv.next()
                load_w(c, io["sb_w_kv"][:, 1024 + cb * 512:1024 + (cb + 1) * 512], ws[:], b_ws, nsplit=2)
                for t in range(GT):
                    pv, b_pv = c.pm.next()
                    P.add("pe", f_mm(pv[:], [(x2T[:, k, t * 128:(t + 1) * 128], ws[:, k, :]) for k in range(8)]),
                          reads=[b_ws, b_x2T[t]], writes=[b_pv])
                    ks, b_ks = ksb.next()
                    if vscale is None:
                        P.add("act", f_act(ks[:], pv[:], AF.Copy), reads=[b_pv], writes=[b_ks])
                    else:
                        P.add("act", f_act(ks[:], pv[:], AF.Identity, scale=vscale[0]), reads=[b_pv, vscale[1]], writes=[b_ks])
                    r0 = slot0 + t * 128
                    store(c, io, io["Vs"][r0:r0 + 128, cb * 512:(cb + 1) * 512], ks[:], b_ks)


def layer1_group(c, io, g, xs, b_xs, aconsts, wr, b_wr, attn_only=False):
    P = c.P
    P.barrier()
    with contextlib.ExitStack() as ast:
        x2T = c.sb([128, 8, GRP], BF16, "x2T", ast)
        b_x2T = [Buf("x2T") for _ in range(GT)]
        oT = c.sb([128, 8, GRP], BF16, "oT", ast)
        b_oT = [Buf("oT") for _ in range(GT)]
        for t in range(GT):
            tile_to_T(c, xs[:, t, :], b_xs[t], x2T, b_x2T, t)
        with contextlib.ExitStack() as hst:
            attention_group(c, hst, io, g, oT, b_oT, x2T, b_x2T, aconsts)
        P.barrier()
        with contextlib.ExitStack() as mst:
            gam, bet, b_gb = load_ln(c, mst, io["g_mix1"], io["b_mix1"], "lnmix")
            tmpf = c.rot(2, [128, 1024], F32, "tmpf", mst)
            wos = c.rot(2, [128, 8, 512], BF16, "wos", mst)
            for half in range(2):
                wo, b_wo = wos.next()
                load_w(c, io["sb_w_out"][:, half * 512:(half + 1) * 512], wo[:], b_wo, nsplit=2)
                for t in range(GT):
                    pm_, b_pm = c.pm.next()
                    P.add("pe", f_mm(pm_[:], [(oT[:, ch, t * 128:(t + 1) * 128], wo[:, ch, :]) for ch in range(8)]),
                          reads=[b_wo, b_oT[t]], writes=[b_pm])
                    xh = xs[:, t, half * 512:(half + 1) * 512]
                    P.add("dve", f_stt(xh, xh, ALPHA, pm_[:], ALU.mult, ALU.add), reads=[b_xs[t], b_pm], writes=[b_xs[t]])
            for t in range(GT):
                ln_inplace(c, xs[:, t, :], b_xs[t], gam, bet, b_gb, tmpf)
    if attn_only:
        return
    P.barrier()
    with contextlib.ExitStack() as fst:
        gam, bet, b_gb = load_ln(c, fst, io["g_ffn1"], io["b_ffn1"], "lnffn")
        tmpf = c.rot(2, [128, 1024], F32, "tmpf", fst)
        x3T = c.sb([128, 8, GRP], BF16, "x3T", fst)
        b_x3T = [Buf("x3T") for _ in range(GT)]
        gates = c.sb([128, GT, N_EXP], F32, "gates", fst)
        b_gt = Buf("gates")
        xTf = c.rot(2, [128, 8, 128], F32, "xTf", fst)
        lgr = c.rot(2, [128, 32], F32, "lg", fst)
        for t in range(GT):
            tile_to_T(c, xs[:, t, :], b_xs[t], x3T, b_x3T, t)
            xf, b_xf = xTf.next()
            for hf in range(2):
                pp, b_pp = c.pm.next()
                ppv = pp[:].rearrange("p (c n) -> p c n", n=128)
                P.add("pe", f_tr([(ppv[:, ch, :], xs[:, t, (hf * 4 + ch) * 128:(hf * 4 + ch + 1) * 128]) for ch in range(4)], c.idf[:]),
                      reads=[b_xs[t], c.b_idf], writes=[b_pp])
                P.add("dve", f_copy(xf[:, hf * 4:(hf + 1) * 4, :], ppv[:, 0:4, :]), reads=[b_pp], writes=[b_xf])
            pl, b_pl = c.pm.next()
            P.add("pe", f_mm(pl[:, 0:N_EXP], [(xf[:, ch, :], wr[:, ch, :]) for ch in range(8)]),
                  reads=[b_xf, b_wr], writes=[b_pl])
            P.add("act", f_act(xs[:, t, :], xs[:, t, :], AF.Copy, scale=ALPHA), reads=[b_xs[t]], writes=[b_xs[t]])
            lg, b_lg = lgr.next()
            P.add("dve", f_copy(lg[:, 0:8], pl[:, 0:N_EXP]), reads=[b_pl], writes=[b_lg])
            P.add("dve", lambda h, lg=lg: h.max(out=lg[:, 8:16], in_=lg[:, 0:8]), reads=[b_lg], writes=[b_lg])
            P.add("dve", f_ts(lg[:, 16:17], lg[:, 8:9], -1.0, None, ALU.mult), reads=[b_lg], writes=[b_lg])
            P.add("act", f_act(lg[:, 24:32], lg[:, 0:8], AF.Exp, bias=lg[:, 16:17]), reads=[b_lg], writes=[b_lg])
            P.add("dve", f_ts(gates[:, t, :], lg[:, 0:8], lg[:, 9:10], None, ALU.is_ge), reads=[b_lg, b_gt], writes=[b_gt])
            P.add("dve", f_tt(gates[:, t, :], gates[:, t, :], lg[:, 24:32], ALU.mult), reads=[b_lg, b_gt], writes=[b_gt])
            P.add("dve", lambda h, lg=lg, t=t: h.reduce_sum(out=lg[:, 17:18], in_=gates[:, t, :], axis=mybir.AxisListType.X),
                  reads=[b_gt, b_lg], writes=[b_lg])
            P.add("dve", lambda h, lg=lg: h.reciprocal(out=lg[:, 17:18], in_=lg[:, 17:18]), reads=[b_lg], writes=[b_lg])
            P.add("dve", f_ts(gates[:, t, :], gates[:, t, :], lg[:, 17:18], None, ALU.mult), reads=[b_lg, b_gt], writes=[b_gt])
        for e in range(N_EXP):
            with contextlib.ExitStack() as ust:
                ffn_unit(c, ust, x3T, b_x3T, io["moe_w_in"][e * D:(e + 1) * D, :], D_EXP,
                         io["moe_w_out"][e * D_EXP:(e + 1) * D_EXP, :], xs, b_xs, [(0, 14), (14, 14)],
                         gate=((lambda t, e=e: gates[:, t, e:e + 1]), b_gt))
            P.barrier()
        for t in range(GT):
            ln_inplace(c, xs[:, t, :], b_xs[t], gam, bet, b_gb, tmpf)
            r0 = g * GRP + t * 128
            store(c, io, io["out"][r0:r0 + 128, :], xs[:, t, :], b_xs[t])


ROUTED = True
I32 = mybir.dt.int32
REGS = [(None, [(0, 384)]), (384, [(384, 128), (512, 512)])]


def moe_tables():
    iota_f = np.broadcast_to(np.arange(128, dtype=np.float32)[None, :], (128, 128)).copy()
    iotap = (np.arange(128, dtype=np.float32)[:, None] + 128.0 * np.arange(8, dtype=np.float32)[None, :]).copy()
    k = np.arange(128)[:, None]
    m = np.arange(128)[None, :]
    ulow = (k < m).astype(np.float32)
    ones = np.ones((128, 128), np.float32)
    oh = np.zeros((8, 8, 128), np.float32)
    for r in range(8):
        oh[r, r, :] = 1.0
    return iota_f, iotap, ulow, ones, oh.reshape(8, 8 * 128)


def moe_routed_group(c, io, g, xs, b_xs, mc):
    P = c.P
    iota_f, b_iota, iotap, b_iotap, Ulow, b_Ul, ones_b, b_ones, OH, b_OH, wr, b_wr = mc
    P.barrier()
    with contextlib.ExitStack() as fst:
        rt_in = c.sb([128, GT, 16], F32, "rt_in", fst)
        b_rt = [Buf("rt_in") for _ in range(GT)]
        maskt = c.sb([128, GT, 8], F32, "maskt", fst)
        maskb = c.sb([128, GT, 8], BF16, "maskb", fst)
        b_mask = [Buf("mask") for _ in range(GT)]
        cnt_i = c.sb([128, 8], I32, "cnt_i", fst)
        b_cnt = Buf("cnt")
        RTp = c.sb([8, GRP], F32, "RTp", fst)
        RTg = c.sb([8, GRP], F32, "RTg", fst)
        b_RT = Buf("RT")
        x3b = c.sb([128, GT, 1024], BF16, "x3b", fst)
        b_x3b = [Buf("x3b") for _ in range(GT)]
        pst = contextlib.ExitStack()
        xTf = c.rot(2, [128, 8, 128], F32, "xTf", pst)
        lgr = c.rot(2, [128, 32], F32, "lg", pst)
        for t in range(GT):
            P.add("act", f_act(x3b[:, t, :], xs[:, t, :], AF.Copy), reads=[b_xs[t]], writes=[b_x3b[t]])
            xf, b_xf = xTf.next()
            for hf in range(2):
                pp, b_pp = c.pm.next()
                ppv = pp[:].rearrange("p (c n) -> p c n", n=128)
                P.add("pe", f_tr([(ppv[:, ch, :], xs[:, t, (hf * 4 + ch) * 128:(hf * 4 + ch + 1) * 128]) for ch in range(4)], c.idf[:]),
                      reads=[b_xs[t], c.b_idf], writes=[b_pp])
                P.add("dve", f_copy(xf[:, hf * 4:(hf + 1) * 4, :], ppv[:, 0:4, :]), reads=[b_pp], writes=[b_xf])
            pl, b_pl = c.pm.next()
            P.add("pe", f_mm(pl[:, 0:N_EXP], [(xf[:, ch, :], wr[:, ch, :]) for ch in range(8)]),
                  reads=[b_xf, b_wr], writes=[b_pl])
            P.add("act", f_act(xs[:, t, :], xs[:, t, :], AF.Copy, scale=ALPHA), reads=[b_xs[t]], writes=[b_xs[t]])
            lg, b_lg = lgr.next()
            gt = rt_in[:, t, 8:16]
            P.add("dve", f_copy(lg[:, 0:8], pl[:, 0:N_EXP]), reads=[b_pl], writes=[b_lg])
            P.add("dve", lambda h, lg=lg: h.max(out=lg[:, 8:16], in_=lg[:, 0:8]), reads=[b_lg], writes=[b_lg])
            P.add("dve", f_ts(lg[:, 16:17], lg[:, 8:9], -1.0, None, ALU.mult), reads=[b_lg], writes=[b_lg])
            P.add("act", f_act(lg[:, 24:32], lg[:, 0:8], AF.Exp, bias=lg[:, 16:17]), reads=[b_lg], writes=[b_lg])
            P.add("dve", f_ts(maskt[:, t, :], lg[:, 0:8], lg[:, 9:10], None, ALU.is_ge), reads=[b_lg], writes=[b_mask[t]])
            P.add("dve", f_copy(maskb[:, t, :], maskt[:, t, :]), reads=[b_mask[t]], writes=[b_mask[t]])
            P.add("dve", f_tt(gt, maskt[:, t, :], lg[:, 24:32], ALU.mult), reads=[b_lg, b_mask[t]], writes=[b_rt[t]])
            P.add("dve", lambda h, lg=lg, gt=gt: h.reduce_sum(out=lg[:, 17:18], in_=gt, axis=mybir.AxisListType.X),
                  reads=[b_rt[t], b_lg], writes=[b_lg])
            P.add("dve", lambda h, lg=lg: h.reciprocal(out=lg[:, 17:18], in_=lg[:, 17:18]), reads=[b_lg], writes=[b_lg])
            P.add("dve", f_ts(gt, gt, lg[:, 17:18], None, ALU.mult), reads=[b_lg, b_rt[t]], writes=[b_rt[t]])
        for t in range(GT):
            pp, b_pp = c.pm.next()
            pairs = [(Ulow[:], maskb[:, t, :])] + [(ones_b[:], maskb[:, t2, :]) for t2 in range(t)]
            P.add("pe", f_mm(pp[:, 0:8], pairs), reads=[b_Ul, b_ones] + b_mask[0:t + 1], writes=[b_pp])
            P.add("dve", f_stt(rt_in[:, t, 0:8], pp[:, 0:8], 1.0, maskt[:, t, :], ALU.add, ALU.mult),
                  reads=[b_pp, b_mask[t], b_rt[t]], writes=[b_rt[t]])
            P.add("dve", f_ts(rt_in[:, t, 0:8], rt_in[:, t, 0:8], -1.0, None, ALU.add), reads=[b_rt[t]], writes=[b_rt[t]])
        pc, b_pc = c.pm.next()
        P.add("pe", f_mm(pc[:, 0:8], [(ones_b[:], maskb[:, t, :]) for t in range(GT)]), reads=[b_ones] + b_mask, writes=[b_pc])
        P.add("dve", f_copy(cnt_i[:], pc[:, 0:8]), reads=[b_pc], writes=[b_cnt])
        for (dst, c0) in ((RTp, 0), (RTg, 8)):
            for hf in range(2):
                pp, b_pp = c.pm.next()
                P.add("pe", f_tr([(pp[0:8, j * 128:(j + 1) * 128], rt_in[:, hf * 4 + j, c0:c0 + 8]) for j in range(4)], c.idf[:]),
                      reads=b_rt[hf * 4:(hf + 1) * 4] + [c.b_idf], writes=[b_pp])
                P.add("act", f_act(dst[:, hf * 512:(hf + 1) * 512], pp[0:8, :], AF.Copy), reads=[b_pp], writes=[b_RT])

        pst.close()
        P.barrier()
        xgT = c.sb([128, 8, GRP], BF16, "xgT", fst)
        b_xg = [Buf("xg") for _ in range(GT)]
        P.add("dve", f_memset(xgT[:], 0.0), writes=b_xg)
        actT = c.sb([128, 14, GRP], BF16, "actT", fst)
        b_act = [[Buf("act") for _ in range(GT)] for _ in range(14)]
        oute = c.sb([128, GT, 512], BF16, "oute", fst)
        b_oute = [Buf("oute") for _ in range(GT)]
        pbc = c.sb([128, GRP], F32, "pbc", fst)
        gbc = c.sb([128, GRP], F32, "gbc", fst)
        b_bc = Buf("bc")
        selr = c.rot(16, [128, 128], BF16, "sel", fst)
        nst0 = REGS[0][1][0][1] // 128
        sTc = c.sb([128, nst0, GT, 128], BF16, "sTc", fst)
        b_sTc = [[Buf("sTc") for _ in range(GT)] for _ in range(nst0)]
        selTr = c.rot(4, [128, 128], BF16, "selT", fst)
        w1s = c.rot(2, [128, 8, 2, 256], BF16, "w1s", fst)
        w2s = c.rot(1, [128, 14, 512], BF16, "w2s", fst)
        sg = c.rot(2, [128, 512], BF16, "sg", fst)

        def regions():
            for thr, pieces in REGS:
                if thr is not None:
                    P.region_begin("cnt", thr)
                yield thr, pieces
                if thr is not None:
                    P.region_end()

        for e in range(N_EXP):
            w1 = io["moe_w_in"][e * D:(e + 1) * D, :]
            w2 = io["moe_w_out"][e * D_EXP:(e + 1) * D_EXP, :]
            for eng in ("pe", "act", "dve"):
                P.reg_load(eng, "cnt", cnt_i[0:1, e:e + 1], reads=[b_cnt])
            for (dst, src) in ((pbc, RTp), (gbc, RTg)):
                for hf in range(2):
                    pb, b_pb = c.pm.next()
                    P.add("pe", f_mm(pb[:], [(OH[:, e * 128:(e + 1) * 128], src[:, hf * 512:(hf + 1) * 512])]),
                          reads=[b_OH, b_RT], writes=[b_pb])
                    P.add("dve", f_copy(dst[:, hf * 512:(hf + 1) * 512], pb[:]), reads=[b_pb], writes=[b_bc])
            for thr, pieces in regions():
                for (s0, w) in pieces:
                    for st in range(w // 128):
                        slot0 = s0 + st * 128
                        sels = []
                        for t in range(GT):
                            sl, b_sl = selr.next()
                            P.add("dve", f_ts(sl[:], iota_f[:], rt_in[:, t, e:e + 1], -float(slot0), ALU.subtract, ALU.is_equal),
                                  reads=[b_iota, b_rt[t]], writes=[b_sl])
                            sels.append((sl, b_sl))
                        for hf in range(2):
                            pg, b_pg = c.pm.next()
                            groups = [(pg[:, j * 128:(j + 1) * 128],
                                       [(x3b[:, t, (hf * 4 + j) * 128:(hf * 4 + j + 1) * 128], sels[t][0][:]) for t in range(GT)])
                                      for j in range(4)]
                            P.add("pe", f_mm_multi(groups), reads=b_x3b + [b for _, b in sels], writes=[b_pg])
                            P.add("dve", f_copy(xgT[:, hf * 4:(hf + 1) * 4, slot0:slot0 + 128],
                                                pg[:].rearrange("p (c n) -> p c n", n=128)),
                                  reads=[b_pg], writes=[b_xg[slot0 // 128]])
            for sti in range(nst0):
                for t in range(GT):
                    tk = slice(t * 128, (t + 1) * 128)
                    P.add("dve", f_stt(sTc[:, sti, t, :], pbc[:, tk], iotap[:, sti:sti + 1], gbc[:, tk], ALU.is_equal, ALU.mult),
                          reads=[b_bc, b_iotap], writes=[b_sTc[sti][t]])
            for (j0, nj) in ((0, 14), (14, 14)):
                for blk in range(nj // 2):
                    jj = j0 + blk * 2
                    ws, b_ws = w1s.next()
                    srcg = w1[:, jj * 128: jj * 128 + 256].rearrange("(c p) n -> p c n", p=128)
                    srcu = w1[:, D_EXP + jj * 128: D_EXP + jj * 128 + 256].rearrange("(c p) n -> p c n", p=128)
                    P.dma("pool", f_dma([(ws[:, :, 0, :], srcg), (ws[:, :, 1, :], srcu)]), b_ws, writes=[b_ws], n=2)
                    for thr, pieces in regions():
                        for (s0, w) in pieces:
                            tiles = list(range(s0 // 128, (s0 + w) // 128))
                            for ci in range(2):
                                jl = blk * 2 + ci
                                pg, b_pg = c.pm.next()
                                pu, b_pu = c.pm.next()
                                P.add("pe", f_mm_multi([
                                    (pg[:, 0:w], [(ws[:, k, 0, ci * 128:(ci + 1) * 128], xgT[:, k, s0:s0 + w]) for k in range(8)]),
                                    (pu[:, 0:w], [(ws[:, k, 1, ci * 128:(ci + 1) * 128], xgT[:, k, s0:s0 + w]) for k in range(8)]),
                                ]), reads=[b_ws] + [b_xg[i] for i in tiles], writes=[b_pg, b_pu])
                                s_, b_s = sg.next()
                                P.add("act", f_act(s_[:, 0:w], pg[:, 0:w], AF.Silu), reads=[b_pg], writes=[b_s])
                                P.add("dve", f_tt(actT[:, jl, s0:s0 + w], s_[:, 0:w], pu[:, 0:w], ALU.mult),
                                      reads=[b_s, b_pu], writes=[b_act[jl][i] for i in tiles])
                for half in range(2):
                    w2v, b_w2 = w2s.next()
                    load_w(c, w2[j0 * 128:(j0 + nj) * 128, half * 512:(half + 1) * 512], w2v[:, 0:nj, :], b_w2, nsplit=2)
                    for thr, pieces in regions():
                        for (s0, w) in pieces:
                            for sti in range(s0 // 128, (s0 + w) // 128):
                                po, b_po = c.pm.next()
                                P.add("pe", f_mm(po[:], [(actT[:, j, sti * 128:(sti + 1) * 128], w2v[:, j, :]) for j in range(nj)]),
                                      reads=[b_w2] + [b_act[j][sti] for j in range(nj)], writes=[b_po])
                                P.add("dve", f_copy(oute[:, sti, :], po[:]), reads=[b_po], writes=[b_oute[sti]])
                    for thr, pieces in regions():
                        stis = [sti for (s0, w) in pieces for sti in range(s0 // 128, (s0 + w) // 128)]
                        for t in range(GT):
                            tk = slice(t * 128, (t + 1) * 128)
                            if thr is None:
                                sTs = [(sTc[:, sti, t, :], b_sTc[sti][t]) for sti in stis]
                                groups = [stis]
                            else:
                                sTs = None
                                groups = [stis[i:i + 4] for i in range(0, len(stis), 4)]
                            for grp in groups:
                                if thr is None:
                                    cur = sTs
                                else:
                                    cur = []
                                    for sti in grp:
                                        sT, b_sT = selTr.next()
                                        P.add("dve", f_stt(sT[:], pbc[:, tk], iotap[:, sti:sti + 1], gbc[:, tk], ALU.is_equal, ALU.mult),
                                              reads=[b_bc, b_iotap], writes=[b_sT])
                                        cur.append((sT[:], b_sT))
                                ps_, b_ps = c.pm.next()
                                P.add("pe", f_mm(ps_[:], [(cur[i][0], oute[:, sti, :]) for i, sti in enumerate(grp)]),
                                      reads=[b for _, b in cur] + [b_oute[sti] for sti in grp], writes=[b_ps])
                                xh = xs[:, t, half * 512:(half + 1) * 512]
                                P.add("dve", f_tt(xh, ps_[:], xh, ALU.add), reads=[b_ps, b_xs[t]], writes=[b_xs[t]])
    P.barrier()
    with contextlib.ExitStack() as lst:
        gam, bet, b_gb = load_ln(c, lst, io["g_ffn1"], io["b_ffn1"], "lnffn")
        tmpf = c.rot(2, [128, 1024], F32, "tmpf", lst)
        for t in range(GT):
            ln_inplace(c, xs[:, t, :], b_xs[t], gam, bet, b_gb, tmpf)
            r0 = g * GRP + t * 128
            store(c, io, io["out"][r0:r0 + 128, :], xs[:, t, :], b_xs[t])


def build_fused(c, io):
    P = c.P
    xres = c.sb([128, TOK // 128, 1024], F32, "xres")
    b_xres = [Buf("xres", key=f"xres{t}") for t in range(TOK // 128)]
    c.one_t = c.sb([128, 2], F32, "one")
    P.add("dve", f_memset(c.one_t[:], 1.0), writes=[c.b_eps])
    flag, b_flag = load_const(c, io["flag"], [128, 1], F32, "flag")
    with contextlib.ExitStack() as l0st:
        c.stack = l0st
        c.stage = c.rot(2, [128, 1024], F32, "stage")
        _, _, _, cd = retention_tables()
        dtab, b_dtab = load_const(c, io["dtab"], [128, RH, 128], F32, "dtab")
        cn, b_cn = load_const(c, io["cn"], [128, RH], F32, "cn")
        kdec, b_kdec = load_const(c, io["kdec"], [128, RH], F32, "kdec")
        consts = (dtab, b_dtab, cn, b_cn, kdec, b_kdec, cd)
        S32 = c.sb([128, RH, 2, 512], F32, "S32")
        Sb = c.sb([128, RH, 2, 512], BF16, "Sb")
        b_S32 = [Buf("S32") for _ in range(RH)]
        b_Sb = [Buf("Sb") for _ in range(RH)]
        for h in range(RH):
            P.add("dve", f_memset(S32[:, h, :, :], 0.0), writes=[b_S32[h]])
            P.add("dve", f_memset(Sb[:, h, :, :], 0.0), writes=[b_Sb[h]])
        L0 = (S32, Sb, b_S32, b_Sb, consts)
        for mode, g in (("prev", 0), ("prev", 1), ("own", 0), ("own", 1)):
            own = mode == "own"
            xsrc = (io["x_own"] if own else io["x_prev"])[g * GRP:(g + 1) * GRP, :]
            cosd = (io["cos_own"] if own else io["cos_prev"])[:, g * GRP:(g + 1) * GRP]
            sind = (io["sin_own"] if own else io["sin_prev"])[:, g * GRP:(g + 1) * GRP]
            gs = g if own else 1
            xs = xres[:, gs * GT:(gs + 1) * GT, :]
            bx = b_xres[gs * GT:(gs + 1) * GT]
            slot0 = (TOK if own else 0) + g * GRP
            layer0_group(c, io, L0, g, xsrc, cosd, sind, xs, bx, slot0, None if own else (flag[:, 0:1], b_flag))
        c.stack = c.root
    P.barrier()
    wr, b_wr = load_const(c, io["moe_w_router"].rearrange("(c p) n -> p c n", p=128), [128, 8, N_EXP], F32, "wr")
    with contextlib.ExitStack() as a1st:
        c.stack = a1st
        m01, b_m01 = load_const(c, io["m01"], [128, 4, 512], BF16, "m01", eng="pool")
        negm, b_negm = load_const(c, io["negm"], [128, 4, 512], BF16, "negm", eng="pool")
        M1, b_M1 = load_const(c, io["M1"], [128, 128], BF16, "M1", eng="pool")
        NO, b_NO = load_const(c, io["NO"], [128, 128], BF16, "NO", eng="pool")
        aconsts = (m01, b_m01, negm, b_negm, M1, b_M1, NO, b_NO)
        c.stack = c.root
        for g in range(TOK // GRP):
            layer1_group(c, io, g, xres[:, g * GT:(g + 1) * GT, :], b_xres[g * GT:(g + 1) * GT], aconsts, wr, b_wr,
                         attn_only=ROUTED)
    if ROUTED:
        P.barrier()
        iota_f, b_iota = load_const(c, io["iota_f"], [128, 128], F32, "iota_f")
        iotap, b_iotap = load_const(c, io["iotap"], [128, 8], F32, "iotap")
        Ulow, b_Ul = load_const(c, io["Ulow"], [128, 128], BF16, "Ulow", eng="pool")
        ones_b, b_ones = load_const(c, io["ones"], [128, 128], BF16, "ones_b", eng="pool")
        OH, b_OH = load_const(c, io["OH"], [8, 8 * 128], F32, "OH")
        mc = (iota_f, b_iota, iotap, b_iotap, Ulow, b_Ul, ones_b, b_ones, OH, b_OH, wr, b_wr)
        for g in range(TOK // GRP):
            moe_routed_group(c, io, g, xres[:, g * GT:(g + 1) * GT, :], b_xres[g * GT:(g + 1) * GT], mc)


def make_nc_fused():
    nc = bass.Bass("TRN2", target_bir_lowering=False)
    c = Ctx(nc)
    io = {"outs": []}

    def din(name, shape, dt=F32):
        io[name] = c.dram(name, shape, dt, "ExternalInput")
    din("x_own", [TOK, D]); din("x_prev", [TOK, D])
    din("cos_own", [128, TOK]); din("sin_own", [128, TOK]); din("cos_prev", [128, TOK]); din("sin_prev", [128, TOK])
    din("ret_w_in", [D, 6144]); din("ret_w_out", [2048, D])
    din("ffn_w_in", [D, 2 * D_FF]); din("ffn_w_out", [D_FF, D]); din("sb_w_kv", [D, 2048])
    din("sb_w_q", [D, D]); din("sb_w_out", [D, D])
    din("moe_w_router", [D, N_EXP]); din("moe_w_in", [N_EXP * D, 2 * D_EXP]); din("moe_w_out", [N_EXP * D_EXP, D])
    for l in (0, 1):
        for nm in ("g_mix", "b_mix", "g_ffn", "b_ffn"):
            din(f"{nm}{l}", [128, D])
    din("dtab", [128, RH, 128]); din("cn", [128, RH]); din("kdec", [128, RH]); din("ident", [128, 128]); din("flag", [128, 1])
    din("m01", [128, 4, 512]); din("negm", [128, 4, 512]); din("M1", [128, 128]); din("NO", [128, 128])
    din("iota_f", [128, 128]); din("iotap", [128, 8]); din("Ulow", [128, 128]); din("ones", [128, 128]); din("OH", [8, 8 * 128])
    io["KTs"] = nc.dram_tensor("KTs_i", [D, 2 * TOK], BF16).ap()
    io["Vs"] = nc.dram_tensor("Vs_i", [2 * TOK, D], BF16).ap()
    io["out"] = c.dram("out", [TOK, D], F32, "ExternalOutput")
    if DEBUG:
        io["dbg"] = c.dram("dbg", [1, 8], F32, "ExternalOutput")
        io["dbg2"] = c.dram("dbg2", [5, 128, 512], BF16, "ExternalOutput")
    with c.root:
        setup_common(c, io["ident"])
        build_fused(c, io)
        c.P.add("sp", lambda h: None, reads=io["outs"])
        cnt = c.P.emit(nc)
    return nc, cnt


def fused_inputs(inputs, core, shared):
    b, half = core // 2, core % 2
    x = inputs["x"]
    f = np.float32
    co, so = rope_tables(half * TOK, TOK)
    d = dict(shared)
    d.update({
        "x_own": np.ascontiguousarray(x[b, half * TOK:(half + 1) * TOK]),
        "x_prev": np.ascontiguousarray(x[b, 0:TOK]) if half == 1 else np.zeros((TOK, D), f),
        "cos_own": co, "sin_own": so,
        "flag": np.full((128, 1), float(half), f),
    })
    return d


def shared_inputs(inputs):
    f = np.float32
    dtab, cn, kdec, _ = retention_tables()
    cp, sp_ = rope_tables(0, TOK)
    m01, negm, m1, negones = attn_tables()
    bc = lambda v: np.ascontiguousarray(np.broadcast_to(np.asarray(v, f)[None, :], (128, D)))
    d = {
        "cos_prev": cp, "sin_prev": sp_,
        "ret_w_in": np.ascontiguousarray(inputs["ret_w_in"][0]),
        "ret_w_out": np.ascontiguousarray(inputs["ret_w_out"][0]),
        "ffn_w_in": np.ascontiguousarray(inputs["ffn_w_in"][0]),
        "ffn_w_out": np.ascontiguousarray(inputs["ffn_w_out"][0]),
        "sb_w_kv": np.ascontiguousarray(inputs["sb_w_kv"]),
        "sb_w_q": np.ascontiguousarray(inputs["sb_w_q"][0]), "sb_w_out": np.ascontiguousarray(inputs["sb_w_out"][0]),
        "moe_w_router": np.ascontiguousarray(inputs["moe_w_router"][0]),
        "moe_w_in": np.ascontiguousarray(inputs["moe_w_in"][0]).reshape(N_EXP * D, 2 * D_EXP),
        "moe_w_out": np.ascontiguousarray(inputs["moe_w_out"][0]).reshape(N_EXP * D_EXP, D),
        "dtab": dtab, "cn": cn, "kdec": kdec, "ident": np.eye(128, dtype=f),
        "m01": m01, "negm": negm, "M1": m1, "NO": negones,
    }
    d["iota_f"], d["iotap"], d["Ulow"], d["ones"], d["OH"] = moe_tables()
    for l in (0, 1):
        d[f"g_mix{l}"] = bc(inputs["ln_mix_g"][l]); d[f"b_mix{l}"] = bc(inputs["ln_mix_b"][l])
        d[f"g_ffn{l}"] = bc(inputs["ln_ffn_g"][l]); d[f"b_ffn{l}"] = bc(inputs["ln_ffn_b"][l])
    return d


_NC_CACHE = {}


def kernel(**inputs):
    inputs = {k: np.asarray(v) for k, v in inputs.items()}
    cores = list(range(8))
    if "f" not in _NC_CACHE:
        _NC_CACHE["f"] = make_nc_fused()[0]
    shared = shared_inputs(inputs)
    in_maps = [fused_inputs(inputs, cr, shared) for cr in cores]
    res = run_bass_kernel_spmd(_NC_CACHE["f"], in_maps, core_ids=cores).results
    out = np.stack([np.concatenate([res[2 * b]["out"], res[2 * b + 1]["out"]], axis=0) for b in range(4)], axis=0)
    return out.astype(np.float32)
```

```python
import contextlib
import numpy as np
import ml_dtypes
import concourse.bass as bass
import concourse.mybir as mybir
from concourse.bass_utils import run_bass_kernel_spmd

F32 = mybir.dt.float32
BF16 = mybir.dt.bfloat16
AF = mybir.ActivationFunctionType
ALU = mybir.AluOpType

ENGS = ("pe", "act", "dve", "pool", "sp")

D = 1024
TOK = 2048
GRP = 1024
GT = GRP // 128
ALPHA = float((2.0 * 2) ** 0.25)
EPS = 1e-5
RH = 4
D_FF = 2816
N_EXP = 8
D_EXP = 3584


class Buf:
    __slots__ = ("name", "w", "r", "key")

    def __init__(self, name="", key=None):
        self.name = name
        self.w = None
        self.r = []
        self.key = key


class Op:
    __slots__ = ("eng", "fn", "deps", "kind", "owner", "dval", "ms", "ndma")

    def __init__(self, eng, fn, deps, kind, owner=None, dval=0, ndma=0):
        self.eng = eng
        self.fn = fn
        self.deps = deps
        self.kind = kind
        self.owner = owner
        self.dval = dval
        self.ms = None
        self.ndma = ndma


class Prog:
    def __init__(self):
        self.ops = []
        self.by_eng = {e: [] for e in ENGS}
        self.ndsem = 0
        self.fence = set()
        self.dstate = {}
        self.regions = []
        self.cur_region = None
        self.regs = {}

    def bufs(self, n, name=""):
        return [Buf(f"{name}{i}") for i in range(n)]

    def _deps(self, eng, reads, writes, oid):
        deps = set()
        for b in reads:
            if b.w is not None:
                deps.add(b.w)
        for b in writes:
            if b.w is not None:
                deps.add(b.w)
            deps.update(b.r)
        for b in reads:
            b.r.append(oid)
        for b in writes:
            b.w = oid
            b.r = []
        deps |= self.fence
        deps.discard(oid)
        if eng == "pe":
            deps = {d for d in deps if not (self.ops[d].eng == "pe" and self.ops[d].kind == "c")}
        return deps

    def add(self, eng, fn, reads=(), writes=()):
        oid = len(self.ops)
        deps = self._deps(eng, list(reads), list(writes), oid)
        self.ops.append(Op(eng, fn, deps, "c"))
        self.by_eng[eng].append(oid)
        return oid

    def dma(self, eng, fn, owner, reads=(), writes=(), n=1):
        assert self.cur_region is None, "no DMA inside dynamic regions"
        oid = len(self.ops)
        deps = self._deps(eng, list(reads), list(writes), oid)
        key = owner.key if owner.key is not None else id(owner)
        stt = self.dstate.get(key)
        if stt is None:
            stt = [self.ndsem, 0, None]
            self.ndsem += 1
            self.dstate[key] = stt
            self._keep = getattr(self, "_keep", [])
            self._keep.append(owner)
        if stt[2] is not None:
            deps.add(stt[2])
        stt[1] += n
        stt[2] = oid
        self.ops.append(Op(eng, fn, deps, "d", owner=stt[0], dval=16 * stt[1], ndma=n))
        self.by_eng[eng].append(oid)
        return oid

    def region_begin(self, reg_key, thr):
        rid = len(self.regions)
        self.regions.append((reg_key, thr))
        for e in ENGS:
            oid = len(self.ops)
            self.ops.append(Op(e, rid, set(), "rb"))
            self.by_eng[e].append(oid)
        self.cur_region = rid

    def region_end(self):
        for e in ENGS:
            oid = len(self.ops)
            self.ops.append(Op(e, self.cur_region, set(), "re"))
            self.by_eng[e].append(oid)
        self.cur_region = None

    def reg_load(self, eng, reg_key, ap, reads):
        def fn(h, eng=eng):
            r = self.regs.setdefault((eng, reg_key), None)
            if r is None:
                r = h.alloc_register(f"r_{reg_key}_{eng}")
                self.regs[(eng, reg_key)] = r
            return h.reg_load(r, ap)
        return self.add(eng, fn, reads=reads)

    def barrier(self):
        f = set()
        for e in ENGS:
            for oid in reversed(self.by_eng[e]):
                if self.ops[oid].kind in ("c", "d"):
                    f.add(oid)
                    break
        f.update(v[2] for v in self.dstate.values() if v[2] is not None)
        self.fence = f

    def emit(self, nc):
        ops = self.ops
        needed = set()
        for op in ops:
            for d in op.deps:
                if ops[d].kind == "c":
                    needed.add(d)
        cnt = {e: 0 for e in ENGS}
        rinfo = {}
        for e in ENGS:
            cur = None
            for oid in self.by_eng[e]:
                op = ops[oid]
                if op.kind == "rb":
                    cur = [0, cnt[e], 0]
                    rinfo[(e, op.fn)] = cur
                elif op.kind == "re":
                    cur[2] = cnt[e] - cur[1]
                    cur = None
                else:
                    if cur is not None:
                        cur[0] += 1
                    if op.kind == "c" and oid in needed:
                        cnt[e] += 1
                        op.ms = cnt[e]
        with contextlib.ExitStack() as st:
            esem = {e: st.enter_context(nc.semaphore(f"s_{e}")) for e in ENGS}
            dsem = [st.enter_context(nc.semaphore(f"d_{i}")) for i in range(self.ndsem)]
            block = st.enter_context(nc.Block())

            def token(d):
                o = ops[d]
                if o.kind == "c":
                    return ("e", o.eng), o.ms
                return ("d", o.owner), o.dval

            def run(e, h):
                seen = {}
                snap = None
                guard = None
                for oid in self.by_eng[e]:
                    op = ops[oid]
                    if op.kind == "rb":
                        info = rinfo[(e, op.fn)]
                        if info[0] == 0:
                            continue
                        reg_key, thr = self.regions[op.fn]
                        r = self.regs[(e, reg_key)]
                        g1 = h.If_lt(r, thr + 1)
                        g1.__enter__()
                        if info[2] > 0:
                            if info[1] > 0:
                                h.wait_ge(esem[e], info[1])
                            h.sem_inc(esem[e], info[2])
                        g1.__exit__(None, None, None)
                        guard = h.Else()
                        guard.__enter__()
                        snap = dict(seen)
                        continue
                    if op.kind == "re":
                        if rinfo[(e, op.fn)][0] == 0:
                            continue
                        guard.__exit__(None, None, None)
                        guard = None
                        seen = snap
                        snap = None
                        continue
                    want = {}
                    for d in op.deps:
                        k, v = token(d)
                        if v > want.get(k, 0):
                            want[k] = v
                    for k, v in want.items():
                        if seen.get(k, 0) >= v:
                            continue
                        seen[k] = v
                        s = esem[k[1]] if k[0] == "e" else dsem[k[1]]
                        h.wait_ge(s, v)
                    r = op.fn(h)
                    if op.kind == "c":
                        if op.ms is not None:
                            r.then_inc(esem[e], 1)
                    else:
                        if not isinstance(r, (list, tuple)):
                            r = [r]
                        assert len(r) == op.ndma, (len(r), op.ndma)
                        for ins in r:
                            ins.then_inc(dsem[op.owner], 16)

            block.tensor(lambda h: run("pe", h))
            block.scalar(lambda h: run("act", h))
            block.vector(lambda h: run("dve", h))
            block.gpsimd(lambda h: run("pool", h))
            block.sync(lambda h: run("sp", h))
        return cnt


class Rot:
    def __init__(self, items):
        self.items = items
        self.i = 0

    def next(self):
        it = self.items[self.i % len(self.items)]
        self.i += 1
        return it


def f_mm(out, pairs):
    def fn(h):
        n = len(pairs)
        ins = None
        for i, (l, r) in enumerate(pairs):
            ins = h.matmul(out, lhsT=l, rhs=r, start=(i == 0), stop=(i == n - 1))
        return ins
    return fn


def f_mm_multi(groups):
    def fn(h):
        ins = None
        for out, pairs in groups:
            n = len(pairs)
            for i, (l, r) in enumerate(pairs):
                ins = h.matmul(out, lhsT=l, rhs=r, start=(i == 0), stop=(i == n - 1))
        return ins
    return fn


def f_tr(items, ident):
    def fn(h):
        ins = None
        for o, i in items:
            ins = h.transpose(out=o, in_=i, identity=ident)
        return ins
    return fn


def f_act(out, in_, func, bias=None, scale=None):
    kw = {}
    if bias is not None:
        kw["bias"] = bias
    if scale is not None:
        kw["scale"] = scale
    return lambda h: h.activation(out=out, in_=in_, func=func, **kw)


def f_tt(out, in0, in1, op):
    return lambda h: h.tensor_tensor(out=out, in0=in0, in1=in1, op=op)


def f_ts(out, in0, s1, s2, op0, op1=None):
    if op1 is None:
        return lambda h: h.tensor_scalar(out=out, in0=in0, scalar1=s1, scalar2=None, op0=op0)
    return lambda h: h.tensor_scalar(out=out, in0=in0, scalar1=s1, scalar2=s2, op0=op0, op1=op1)


def f_stt(out, in0, scalar, in1, op0, op1):
    return lambda h: h.scalar_tensor_tensor(out=out, in0=in0, scalar=scalar, in1=in1, op0=op0, op1=op1)


def f_copy(out, in_):
    return lambda h: h.tensor_copy(out=out, in_=in_)


def f_dma(pairs):
    return lambda h: [h.dma_start(out=o, in_=i) for o, i in pairs]


def f_memset(ap, v):
    return lambda h: h.memset(ap, v)


class Ctx:
    def __init__(self, nc):
        self.nc = nc
        self.P = Prog()
        self.root = contextlib.ExitStack()
        self.stack = self.root
        self.uid = 0

    def sb(self, shape, dt, name=None, stack=None):
        self.uid += 1
        name = (name or "t") + f"_{self.uid}"
        return (stack or self.stack).enter_context(self.nc.sbuf_tensor(name, shape, dt))

    def ps(self, shape, dt, name=None):
        self.uid += 1
        name = (name or "p") + f"_{self.uid}"
        return self.root.enter_context(self.nc.psum_tensor(name, shape, dt))

    def dram(self, name, shape, dt, kind):
        return self.nc.dram_tensor(name, shape, dt, kind=kind).ap()

    def rot(self, n, shape, dt, name, stack=None):
        return Rot([(self.sb(shape, dt, name, stack), Buf(name, key=f"{name}{i}")) for i in range(n)])


def setup_common(c, ident_d):
    P = c.P
    c.pm = Rot([(c.ps([128, 512], F32, "pm"), Buf("pm")) for _ in range(6)])
    c.pt = Rot([(c.ps([128, 1024], BF16, "pt"), Buf("pt")) for _ in range(2)])
    c.idf = c.sb([128, 128], F32, "idf")
    c.idb = c.sb([128, 128], BF16, "idb")
    c.b_idf = Buf("idf")
    c.b_idb = Buf("idb")
    P.dma("sp", f_dma([(c.idf[:], ident_d)]), c.b_idf, writes=[c.b_idf])
    P.add("dve", f_copy(c.idb[:], c.idf[:]), reads=[c.b_idf], writes=[c.b_idb])
    c.stage = None
    c.xb = c.rot(2, [128, 1024], BF16, "xb")
    c.stat = c.rot(4, [128, 16], F32, "stat")
    c.eps_t = c.sb([128, 2], F32, "eps")
    c.b_eps = Buf("eps")
    P.add("dve", f_memset(c.eps_t[:], EPS), writes=[c.b_eps])


def load_const(c, dram_ap, shape, dt=F32, name="const", eng="sp"):
    t = c.sb(shape, dt, name)
    b = Buf(name)
    full = t[:] if len(shape) == 2 else t[:]
    c.P.dma(eng, f_dma([(full, dram_ap)]), b, writes=[b])
    return t, b


def tile_to_T(c, src_ap, src_buf, dstT, dst_bufs, t, nchunk=8):
    P = c.P
    xb, b_xb = c.xb.next()
    P.add("act", f_act(xb[:, 0:nchunk * 128], src_ap, AF.Copy), reads=[src_buf], writes=[b_xb])
    pt, b_pt = c.pt.next()
    ptv = pt[:].rearrange("p (c n) -> p c n", n=128)
    P.add("pe", f_tr([(ptv[:, ch, :], xb[:, ch * 128:(ch + 1) * 128]) for ch in range(nchunk)], c.idb[:]),
          reads=[b_xb, c.b_idb], writes=[b_pt])
    P.add("dve", f_copy(dstT[:, 0:nchunk, t * 128:(t + 1) * 128], ptv[:, 0:nchunk, :]),
          reads=[b_pt], writes=[dst_bufs[t]])


def load_w(c, w_ap, slot_view, buf, nsplit=1):
    src = w_ap.rearrange("(c p) n -> p c n", p=128)
    kc = src.shape[1]
    if nsplit == 1 or kc < nsplit:
        c.P.dma("pool", f_dma([(slot_view, src)]), buf, writes=[buf])
    else:
        step = (kc + nsplit - 1) // nsplit
        pairs = [(slot_view[:, i:min(i + step, kc), :], src[:, i:min(i + step, kc), :]) for i in range(0, kc, step)]
        c.P.dma("pool", f_dma(pairs), buf, writes=[buf], n=len(pairs))


def ln_inplace(c, x_ap, x_buf, gam, bet, b_gb, tmp_rot):
    P = c.P
    st, b_st = c.stat.next()
    P.add("dve", lambda h: h.bn_stats(out=st[:, 0:6], in_=x_ap[:, 0:512]), reads=[x_buf], writes=[b_st])
    P.add("dve", lambda h: h.bn_stats(out=st[:, 6:12], in_=x_ap[:, 512:1024]), reads=[x_buf, b_st], writes=[b_st])
    mv, b_mv = c.stat.next()
    P.add("dve", lambda h: h.bn_aggr(out=mv[:, 0:2], in_=st[:, 0:12]), reads=[b_st], writes=[b_mv])
    P.add("act", f_act(mv[:, 2:3], mv[:, 1:2], AF.Sqrt, bias=c.eps_t[:, 0:1]), reads=[b_mv, c.b_eps], writes=[b_mv])
    P.add("dve", lambda h: h.reciprocal(out=mv[:, 2:3], in_=mv[:, 2:3]), reads=[b_mv], writes=[b_mv])
    P.add("dve", f_stt(mv[:, 3:4], mv[:, 0:1], -1.0, mv[:, 2:3], ALU.mult, ALU.mult), reads=[b_mv], writes=[b_mv])
    tmp, b_tmp = tmp_rot.next()
    P.add("act", f_act(tmp[:], x_ap, AF.Identity, bias=mv[:, 3:4], scale=mv[:, 2:3]),
          reads=[x_buf, b_mv], writes=[b_tmp])
    P.add("dve", f_tt(tmp[:], tmp[:], gam[:], ALU.mult), reads=[b_tmp, b_gb], writes=[b_tmp])
    P.add("dve", f_tt(x_ap, tmp[:], bet[:], ALU.add), reads=[b_tmp, b_gb], writes=[x_buf])


def ffn_unit(c, st, xT, xT_bufs, w1, dh, w2, yacc, yacc_bufs, subs, gate=None):
    P = c.P
    maxsub = max(n for _, n in subs)
    actT = c.sb([128, maxsub, GRP], BF16, "actT", st)
    act_bufs = [[Buf("act") for _ in range(GT)] for _ in range(maxsub)]
    w1s = c.rot(2, [128, 8, 2, 256], BF16, "w1s", st)
    w2s = c.rot(2, [128, maxsub, 512], BF16, "w2s", st)
    sg = c.rot(2, [128, 512], BF16, "sg", st)
    for (j0, nj) in subs:
        assert nj % 2 == 0
        for blk in range(nj // 2):
            jj = j0 + blk * 2
            ws, b_ws = w1s.next()
            srcg = w1[:, jj * 128: jj * 128 + 256].rearrange("(c p) n -> p c n", p=128)
            srcu = w1[:, dh + jj * 128: dh + jj * 128 + 256].rearrange("(c p) n -> p c n", p=128)
            P.dma("pool", f_dma([(ws[:, :, 0, :], srcg), (ws[:, :, 1, :], srcu)]), b_ws, writes=[b_ws], n=2)
            for ci in range(2):
                jl = blk * 2 + ci
                for tg in range(GRP // 512):
                    pg, b_pg = c.pm.next()
                    pu, b_pu = c.pm.next()
                    tb = xT_bufs[tg * 4:(tg + 1) * 4]
                    P.add("pe", f_mm_multi([
                        (pg[:], [(ws[:, k, 0, ci * 128:(ci + 1) * 128], xT[:, k, tg * 512:(tg + 1) * 512]) for k in range(8)]),
                        (pu[:], [(ws[:, k, 1, ci * 128:(ci + 1) * 128], xT[:, k, tg * 512:(tg + 1) * 512]) for k in range(8)]),
                    ]), reads=[b_ws] + tb, writes=[b_pg, b_pu])
                    s, b_s = sg.next()
                    P.add("act", f_act(s[:], pg[:], AF.Silu), reads=[b_pg], writes=[b_s])
                    P.add("dve", f_tt(actT[:, jl, tg * 512:(tg + 1) * 512], s[:], pu[:], ALU.mult),
                          reads=[b_s, b_pu], writes=act_bufs[jl][tg * 4:(tg + 1) * 4])
        for half in range(2):
            w2v, b_w2 = w2s.next()
            load_w(c, w2[j0 * 128:(j0 + nj) * 128, half * 512:(half + 1) * 512], w2v[:, 0:nj, :], b_w2, nsplit=2)
            for t in range(GT):
                po, b_po = c.pm.next()
                P.add("pe", f_mm(po[:], [(actT[:, j, t * 128:(t + 1) * 128], w2v[:, j, :]) for j in range(nj)]),
                      reads=[b_w2] + [act_bufs[j][t] for j in range(nj)], writes=[b_po])
                ya = yacc[:, t, half * 512:(half + 1) * 512]
                if gate is None:
                    P.add("dve", f_tt(ya, po[:], ya, ALU.add), reads=[b_po, yacc_bufs[t]], writes=[yacc_bufs[t]])
                else:
                    gfn, b_g = gate
                    P.add("dve", f_stt(ya, po[:], gfn(t), ya, ALU.mult, ALU.add),
                          reads=[b_po, yacc_bufs[t], b_g], writes=[yacc_bufs[t]])


def rope_tables(pos0, n):
    d = 256
    inv = (1.0 / (np.float32(10000.0) ** (np.arange(0, d, 2, dtype=np.float32) / np.float32(d)))).astype(np.float32)
    pos = np.arange(pos0, pos0 + n, dtype=np.float32)
    ang = (pos[:, None] * inv[None, :]).astype(np.float32)
    return np.ascontiguousarray(np.cos(ang).astype(np.float32).T), np.ascontiguousarray(np.sin(ang).astype(np.float32).T)


def retention_tables():
    gam = 1.0 - 2.0 ** (-5.0 - np.arange(RH, dtype=np.float64))
    n = np.arange(128)
    m = np.arange(128)
    dtab = np.zeros((128, RH, 128), np.float64)
    cn = np.zeros((128, RH), np.float64)
    kdec = np.zeros((128, RH), np.float64)
    for h in range(RH):
        g = gam[h]
        M, N = np.meshgrid(m, n, indexing="ij")
        same = (M // 64) == (N // 64)
        cross = (M < 64) & (N >= 64)
        dm = np.where(same, g ** np.abs(N - M), np.where(cross, g ** (N - M).clip(0), 0.0))
        cnh = g ** (n + 1.0)
        dtab[:, h, :] = dm / cnh[None, :] / 16.0
        cn[:, h] = cnh
        kdec[:, h] = g ** (127.0 - m) / 16.0
    cd = [float(g ** 128) for g in gam]
    return dtab.astype(np.float32), cn.astype(np.float32), kdec.astype(np.float32), cd


def load_ln(c, st, gd, bd, name):
    g = c.sb([128, 1024], F32, name + "g", st)
    b = c.sb([128, 1024], F32, name + "b", st)
    buf = Buf(name, key=name)
    c.P.dma("sp", f_dma([(g[:], gd), (b[:], bd)]), buf, writes=[buf], n=2)
    return g, b, buf


def store(c, io, dst_ap, src_ap, src_buf):
    ob = Buf("out")
    io["outs"].append(ob)
    c.P.dma("sp", f_dma([(dst_ap, src_ap)]), src_buf, reads=[src_buf], writes=[ob])


def ret_heads(c, rst, io, mode, g, xsrc, cosd, sind, S32, Sb, b_S32, b_Sb, consts, yT, b_yT, scratch=None):
    P = c.P
    own = mode == "own"
    dtab, b_dtab, cn, b_cn, kdec, b_kdec, cd = consts
    w_in = io["ret_w_in"]
    if scratch is not None:
        sb16 = scratch.bitcast(BF16)
        xT = sb16[:, 0:8192].rearrange("p (c n) -> p c n", c=8)
        qT = sb16[:, 8192:10240].rearrange("p (c n) -> p c n", c=2)
        kT = sb16[:, 10240:12288].rearrange("p (c n) -> p c n", c=2)
        cosT = scratch[:, 6144:7168]
        sinT = scratch[:, 7168:8192]
    else:
        xT = c.sb([128, 8, GRP], BF16, "xT", rst)
        cosT = c.sb([128, GRP], F32, "cosT", rst)
        sinT = c.sb([128, GRP], F32, "sinT", rst)
        qT = c.sb([128, 2, GRP], BF16, "qT", rst)
        kT = c.sb([128, 2, GRP], BF16, "kT", rst)
    b_xT = [Buf("xT") for _ in range(GT)]
    b_cs = Buf("cs", key="cs")
    P.dma("sp", f_dma([(cosT[:], cosd), (sinT[:], sind)]), b_cs, writes=[b_cs], n=2)
    for t in range(GT):
        stg, b_stg = c.stage.next()
        P.dma("sp", f_dma([(stg[:], xsrc[t * 128:(t + 1) * 128, :])]), b_stg, writes=[b_stg])
        tile_to_T(c, stg[:], b_stg, xT, b_xT, t)
    kd = c.sb([128, GT, 256], BF16, "kd", rst)
    Vh = c.sb([128, GT, 512], BF16, "Vh", rst)
    Gh = c.sb([128, GT, 512], BF16, "Gh", rst) if own else None
    wqk = c.rot(2, [128, 8, 256], BF16, "wqk", rst)
    wvg = c.rot(2, [128, 8, 512], BF16, "wvg", rst)
    rtmp = c.rot(4, [128, 512], F32, "rtmp", rst)
    PTr = c.rot(2, [128, 128], BF16, "PT", rst)
    osb = c.rot(3, [128, 512], F32, "osb", rst)
    ysb = c.rot(2, [128, 512], BF16, "ysb", rst)
    for h in range(RH):
        b_qT = [Buf("qT") for _ in range(GT)]
        b_kT = [Buf("kT") for _ in range(GT)]
        b_kd = [Buf("kd") for _ in range(GT)]
        b_V = [Buf("V") for _ in range(GT)]
        b_G = [Buf("G") for _ in range(GT)]
        for which in (("q", "k") if own else ("k",)):
            col0 = (0 if which == "q" else 1024) + h * 256
            ws, b_ws = wqk.next()
            load_w(c, w_in[:, col0:col0 + 256], ws[:], b_ws)
            dst = qT if which == "q" else kT
            b_dst = b_qT if which == "q" else b_kT
            for tg in range(GRP // 512):
                pa, b_pa = c.pm.next()
                pb, b_pb = c.pm.next()
                tb = b_xT[tg * 4:(tg + 1) * 4]
                tok = slice(tg * 512, (tg + 1) * 512)
                P.add("pe", f_mm_multi([
                    (pa[:], [(ws[:, k, 0:128], xT[:, k, tok]) for k in range(8)]),
                    (pb[:], [(ws[:, k, 128:256], xT[:, k, tok]) for k in range(8)]),
                ]), reads=[b_ws] + tb, writes=[b_pa, b_pb])
                t1, b_t1 = rtmp.next()
                t2, b_t2 = rtmp.next()
                P.add("dve", f_tt(t1[:], pa[:], cosT[:, tok], ALU.mult), reads=[b_pa, b_cs], writes=[b_t1])
                P.add("dve", f_tt(t2[:], pb[:], sinT[:, tok], ALU.mult), reads=[b_pb, b_cs], writes=[b_t2])
                P.add("dve", f_tt(dst[:, 0, tok], t1[:], t2[:], ALU.subtract), reads=[b_t1, b_t2],
                      writes=b_dst[tg * 4:(tg + 1) * 4])
                t3, b_t3 = rtmp.next()
                t4, b_t4 = rtmp.next()
                P.add("dve", f_tt(t3[:], pa[:], sinT[:, tok], ALU.mult), reads=[b_pa, b_cs], writes=[b_t3])
                P.add("dve", f_tt(t4[:], pb[:], cosT[:, tok], ALU.mult), reads=[b_pb, b_cs], writes=[b_t4])
                P.add("dve", f_tt(dst[:, 1, tok], t3[:], t4[:], ALU.add), reads=[b_t3, b_t4] + b_dst[tg * 4:(tg + 1) * 4],
                      writes=b_dst[tg * 4:(tg + 1) * 4])
        for t in range(GT):
            pt, b_pt = c.pt.next()
            ptv = pt[:].rearrange("p (c n) -> p c n", n=128)
            P.add("pe", f_tr([(ptv[:, ch, :], kT[:, ch, t * 128:(t + 1) * 128]) for ch in range(2)], c.idb[:]),
                  reads=[b_kT[t], c.b_idb], writes=[b_pt])
            P.add("act", f_act(kd[:, t, :], pt[:, 0:256], AF.Identity, scale=kdec[:, h:h + 1]),
                  reads=[b_pt, b_kdec], writes=[b_kd[t]])
        for which in (("v", "g") if own else ("v",)):
            col0 = (2048 if which == "v" else 4096) + h * 512
            ws, b_ws = wvg.next()
            load_w(c, w_in[:, col0:col0 + 512], ws[:], b_ws, nsplit=2)
            for t in range(GT):
                pv, b_pv = c.pm.next()
                P.add("pe", f_mm(pv[:], [(xT[:, k, t * 128:(t + 1) * 128], ws[:, k, :]) for k in range(8)]),
                      reads=[b_ws, b_xT[t]], writes=[b_pv])
                if which == "v":
                    P.add("act", f_act(Vh[:, t, :], pv[:], AF.Copy), reads=[b_pv], writes=[b_V[t]])
                else:
                    P.add("act", f_act(Gh[:, t, :], pv[:], AF.Silu), reads=[b_pv], writes=[b_G[t]])
        pend = None

        def epilogue(o, b_o, t):
            tk = slice(t * 128, (t + 1) * 128)
            st, b_st = c.stat.next()
            P.add("dve", lambda hh, st=st, o=o: hh.bn_stats(out=st[:, 0:6], in_=o[:]), reads=[b_o], writes=[b_st])
            P.add("dve", lambda hh, st=st: hh.bn_aggr(out=st[:, 8:10], in_=st[:, 0:6]), reads=[b_st], writes=[b_st])
            P.add("act", f_act(st[:, 10:11], st[:, 9:10], AF.Sqrt, bias=c.eps_t[:, 0:1]), reads=[b_st, c.b_eps], writes=[b_st])
            P.add("dve", lambda hh, st=st: hh.reciprocal(out=st[:, 10:11], in_=st[:, 10:11]), reads=[b_st], writes=[b_st])
            P.add("dve", f_stt(st[:, 11:12], st[:, 8:9], -1.0, st[:, 10:11], ALU.mult, ALU.mult), reads=[b_st], writes=[b_st])
            P.add("act", f_act(o[:], o[:], AF.Identity, bias=st[:, 11:12], scale=st[:, 10:11]),
                  reads=[b_o, b_st], writes=[b_o])
            y, b_y = ysb.next()
            P.add("dve", f_tt(y[:], o[:], Gh[:, t, :], ALU.mult), reads=[b_o, b_G[t]], writes=[b_y])
            pt, b_pt = c.pt.next()
            ptv = pt[:].rearrange("p (c n) -> p c n", n=128)
            P.add("pe", f_tr([(ptv[:, ch, :], y[:, ch * 128:(ch + 1) * 128]) for ch in range(4)], c.idb[:]),
                  reads=[b_y, c.b_idb], writes=[b_pt])
            P.add("act", f_act(yT[:, h * 4:(h + 1) * 4, tk], ptv[:, 0:4, :], AF.Copy), reads=[b_pt], writes=[b_yT[h][t]])

        for t in range(GT):
            tk = slice(t * 128, (t + 1) * 128)
            p0, b_p0 = c.pm.next()
            p1, b_p1 = c.pm.next()
            P.add("pe", f_mm_multi([(p0[:], [(kd[:, t, 0:128], Vh[:, t, :])]), (p1[:], [(kd[:, t, 128:256], Vh[:, t, :])])]),
                  reads=[b_kd[t], b_V[t]], writes=[b_p0, b_p1])
            if own:
                pS, b_pS = c.pm.next()
                P.add("pe", f_mm(pS[:, 0:128], [(kT[:, ch, tk], qT[:, ch, tk]) for ch in range(2)]),
                      reads=[b_kT[t], b_qT[t]], writes=[b_pS])
                PT, b_PT = PTr.next()
                P.add("dve", f_tt(PT[:], pS[:, 0:128], dtab[:, h, :], ALU.mult), reads=[b_pS, b_dtab], writes=[b_PT])
                po, b_po = c.pm.next()
                P.add("pe", f_mm(po[:], [(PT[:], Vh[:, t, :]), (qT[:, 0, tk], Sb[:, h, 0, :]), (qT[:, 1, tk], Sb[:, h, 1, :])]),
                      reads=[b_PT, b_V[t], b_qT[t], b_Sb[h]], writes=[b_po])
                o, b_o = osb.next()
                P.add("act", f_act(o[:], po[:], AF.Identity, scale=cn[:, h:h + 1]), reads=[b_po, b_cn], writes=[b_o])
            P.add("dve", f_stt(S32[:, h, 0, :], S32[:, h, 0, :], cd[h], p0[:], ALU.mult, ALU.add),
                  reads=[b_p0, b_S32[h]], writes=[b_S32[h]])
            P.add("dve", f_stt(S32[:, h, 1, :], S32[:, h, 1, :], cd[h], p1[:], ALU.mult, ALU.add),
                  reads=[b_p1, b_S32[h]], writes=[b_S32[h]])
            P.add("act", f_act(Sb[:, h, :, :], S32[:, h, :, :], AF.Copy), reads=[b_S32[h]], writes=[b_Sb[h]])
            if own:
                if pend is not None:
                    epilogue(*pend)
                pend = (o, b_o, t)
        if own and pend is not None:
            epilogue(*pend)


def build_layer0(c, io):
    P = c.P
    _, _, _, cd = retention_tables()
    dtab, b_dtab = load_const(c, io["dtab"], [128, RH, 128], F32, "dtab")
    cn, b_cn = load_const(c, io["cn"], [128, RH], F32, "cn")
    kdec, b_kdec = load_const(c, io["kdec"], [128, RH], F32, "kdec")
    consts = (dtab, b_dtab, cn, b_cn, kdec, b_kdec, cd)
    S32 = c.sb([128, RH, 2, 512], F32, "S32")
    Sb = c.sb([128, RH, 2, 512], BF16, "Sb")
    b_S32 = [Buf("S32") for _ in range(RH)]
    b_Sb = [Buf("Sb") for _ in range(RH)]
    for h in range(RH):
        P.add("dve", f_memset(S32[:, h, :, :], 0.0), writes=[b_S32[h]])
        P.add("dve", f_memset(Sb[:, h, :, :], 0.0), writes=[b_Sb[h]])
    w_out = io["ret_w_out"]

    passes = [("prev", 0), ("prev", 1), ("own", 0), ("own", 1)]
    for mode, g in passes:
        own = mode == "own"
        xsrc = (io["x_own"] if own else io["x_prev"])[g * GRP:(g + 1) * GRP, :]
        cosd = (io["cos_own"] if own else io["cos_prev"])[:, g * GRP:(g + 1) * GRP]
        sind = (io["sin_own"] if own else io["sin_prev"])[:, g * GRP:(g + 1) * GRP]
        P.barrier()
        if not own:
            with contextlib.ExitStack() as rst:
                ret_heads(c, rst, io, mode, g, xsrc, cosd, sind, S32, Sb, b_S32, b_Sb, consts, None, None)
            continue
        with contextlib.ExitStack() as gst:
            x1 = c.sb([128, GT, 1024], F32, "x1", gst)
            b_x1 = [Buf("x1") for _ in range(GT)]
            with contextlib.ExitStack() as yst:
                yT = c.sb([128, 16, GRP], BF16, "yT", yst)
                b_yT = [[Buf("yT") for _ in range(GT)] for _ in range(RH)]
                with contextlib.ExitStack() as rst:
                    ret_heads(c, rst, io, mode, g, xsrc, cosd, sind, S32, Sb, b_S32, b_Sb, consts, yT, b_yT)
                P.barrier()
                with contextlib.ExitStack() as mst:
                    gam, bet, b_gb = load_ln(c, mst, io["g_mix"], io["b_mix"], "lnmix")
                    tmpf = c.rot(2, [128, 1024], F32, "tmpf", mst)
                    wos = c.rot(2, [128, 16, 512], BF16, "wos", mst)
                    xst = c.rot(2, [128, 512], F32, "xst", mst)
                    for half in range(2):
                        wo, b_wo = wos.next()
                        load_w(c, w_out[:, half * 512:(half + 1) * 512], wo[:], b_wo, nsplit=2)
                        for t in range(GT):
                            pm_, b_pm = c.pm.next()
                            P.add("pe", f_mm(pm_[:], [(yT[:, ch, t * 128:(t + 1) * 128], wo[:, ch, :]) for ch in range(16)]),
                                  reads=[b_wo] + [b_yT[h][t] for h in range(RH)], writes=[b_pm])
                            xs, b_xs = xst.next()
                            P.dma("sp", f_dma([(xs[:], xsrc[t * 128:(t + 1) * 128, half * 512:(half + 1) * 512])]), b_xs, writes=[b_xs])
                            P.add("dve", f_stt(x1[:, t, half * 512:(half + 1) * 512], xs[:], ALPHA, pm_[:], ALU.mult, ALU.add),
                                  reads=[b_xs, b_pm], writes=[b_x1[t]])
                    for t in range(GT):
                        ln_inplace(c, x1[:, t, :], b_x1[t], gam, bet, b_gb, tmpf)
            P.barrier()
            with contextlib.ExitStack() as fst:
                gam, bet, b_gb = load_ln(c, fst, io["g_ffn"], io["b_ffn"], "lnffn")
                tmpf = c.rot(2, [128, 1024], F32, "tmpf", fst)
                x1T = c.sb([128, 8, GRP], BF16, "x1T", fst)
                b_x1T = [Buf("x1T") for _ in range(GT)]
                yacc = c.sb([128, GT, 1024], F32, "yacc", fst)
                b_ya = [Buf("yacc", key=f"ya{t}") for t in range(GT)]
                for t in range(GT):
                    tile_to_T(c, x1[:, t, :], b_x1[t], x1T, b_x1T, t)
                    P.add("act", f_act(yacc[:, t, :], x1[:, t, :], AF.Copy, scale=ALPHA), reads=[b_x1[t]], writes=[b_ya[t]])
                with contextlib.ExitStack() as ust:
                    ffn_unit(c, ust, x1T, b_x1T, io["ffn_w_in"], D_FF, io["ffn_w_out"], yacc, b_ya, [(0, 12), (12, 10)])
                for t in range(GT):
                    ln_inplace(c, yacc[:, t, :], b_ya[t], gam, bet, b_gb, tmpf)
                    r0 = g * GRP + t * 128
                    store(c, io, io["x2"][r0:r0 + 128, :], yacc[:, t, :], b_ya[t])
                P.barrier()
                with contextlib.ExitStack() as kst:
                    x2T = x1T
                    b_x2T = [Buf("x2T") for _ in range(GT)]
                    for t in range(GT):
                        tile_to_T(c, yacc[:, t, :], b_ya[t], x2T, b_x2T, t)
                    wkv = c.rot(2, [128, 8, 512], BF16, "wkv", kst)
                    ksb = c.rot(3, [128, 512], BF16, "ksb", kst)
                    for cb in range(2):
                        ws, b_ws = wkv.next()
                        load_w(c, io["sb_w_kv"][:, cb * 512:(cb + 1) * 512], ws[:], b_ws, nsplit=2)
                        for ch in range(4):
                            for tg in range(GRP // 512):
                                pk, b_pk = c.pm.next()
                                tok = slice(tg * 512, (tg + 1) * 512)
                                P.add("pe", f_mm(pk[:], [(ws[:, k, ch * 128:(ch + 1) * 128], x2T[:, k, tok]) for k in range(8)]),
                                      reads=[b_ws] + b_x2T[tg * 4:(tg + 1) * 4], writes=[b_pk])
                                ks, b_ks = ksb.next()
                                P.add("act", f_act(ks[:], pk[:], AF.Copy), reads=[b_pk], writes=[b_ks])
                                f0 = (cb * 4 + ch) * 128
                                t0 = g * GRP + tg * 512
                                store(c, io, io["KT"][f0:f0 + 128, t0:t0 + 512], ks[:], b_ks)
                    for cb in range(2):
                        ws, b_ws = wkv.next()
                        load_w(c, io["sb_w_kv"][:, 1024 + cb * 512:1024 + (cb + 1) * 512], ws[:], b_ws, nsplit=2)
                        for t in range(GT):
                            pv, b_pv = c.pm.next()
                            P.add("pe", f_mm(pv[:], [(x2T[:, k, t * 128:(t + 1) * 128], ws[:, k, :]) for k in range(8)]),
                                  reads=[b_ws, b_x2T[t]], writes=[b_pv])
                            ks, b_ks = ksb.next()
                            P.add("act", f_act(ks[:], pv[:], AF.Copy), reads=[b_pv], writes=[b_ks])
                            r0 = g * GRP + t * 128
                            store(c, io, io["V"][r0:r0 + 128, cb * 512:(cb + 1) * 512], ks[:], b_ks)


def make_nc_layer0():
    nc = bass.Bass("TRN2", target_bir_lowering=False)
    c = Ctx(nc)
    io = {"outs": []}

    def din(name, shape, dt=F32):
        io[name] = c.dram(name, shape, dt, "ExternalInput")
    din("x_own", [TOK, D]); din("x_prev", [TOK, D])
    din("cos_own", [128, TOK]); din("sin_own", [128, TOK]); din("cos_prev", [128, TOK]); din("sin_prev", [128, TOK])
    din("ret_w_in", [D, 6144]); din("ret_w_out", [2048, D])
    din("ffn_w_in", [D, 2 * D_FF]); din("ffn_w_out", [D_FF, D]); din("sb_w_kv", [D, 2048])
    din("g_mix", [128, D]); din("b_mix", [128, D]); din("g_ffn", [128, D]); din("b_ffn", [128, D])
    din("dtab", [128, RH, 128]); din("cn", [128, RH]); din("kdec", [128, RH]); din("ident", [128, 128])
    io["x2"] = c.dram("x2", [TOK, D], F32, "ExternalOutput")
    io["KT"] = c.dram("KT", [D, TOK], BF16, "ExternalOutput")
    io["V"] = c.dram("V", [TOK, D], BF16, "ExternalOutput")
    with c.root:
        setup_common(c, io["ident"])
        build_layer0(c, io)
        c.P.add("sp", lambda h: None, reads=io["outs"])
        cnt = c.P.emit(nc)
    return nc, cnt


def layer0_inputs(inputs, core):
    b, half = core // 2, core % 2
    x = inputs["x"]
    dtab, cn, kdec, _ = retention_tables()
    co, so = rope_tables(half * TOK, TOK)
    cp, sp_ = rope_tables(0, TOK)
    f = np.float32
    bc = lambda v: np.ascontiguousarray(np.broadcast_to(np.asarray(v, f)[None, :], (128, D)))
    return {
        "x_own": np.ascontiguousarray(x[b, half * TOK:(half + 1) * TOK]),
        "x_prev": np.ascontiguousarray(x[b, 0:TOK]) if half == 1 else np.zeros((TOK, D), f),
        "cos_own": co, "sin_own": so, "cos_prev": cp, "sin_prev": sp_,
        "ret_w_in": np.ascontiguousarray(inputs["ret_w_in"][0]),
        "ret_w_out": np.ascontiguousarray(inputs["ret_w_out"][0]),
        "ffn_w_in": np.ascontiguousarray(inputs["ffn_w_in"][0]),
        "ffn_w_out": np.ascontiguousarray(inputs["ffn_w_out"][0]),
        "sb_w_kv": np.ascontiguousarray(inputs["sb_w_kv"]),
        "g_mix": bc(inputs["ln_mix_g"][0]), "b_mix": bc(inputs["ln_mix_b"][0]),
        "g_ffn": bc(inputs["ln_ffn_g"][0]), "b_ffn": bc(inputs["ln_ffn_b"][0]),
        "dtab": dtab, "cn": cn, "kdec": kdec, "ident": np.eye(128, dtype=f),
    }


NEG = -30000.0


def attn_tables():
    s = np.arange(128)[:, None]
    t = np.arange(512)[None, :]
    m01 = np.zeros((128, 4, 512), np.float32)
    for di in range(4):
        m01[:, di, :] = (di * 128 + s < t).astype(np.float32)
    negm = (1.0 - m01) * NEG
    j = np.arange(128)[:, None]
    ss = np.arange(128)[None, :]
    m1 = -(j >= ss).astype(np.float32)
    negones = -np.ones((128, 128), np.float32)
    return m01, negm, m1, negones


def f_mm1(out, l, r, start, stop):
    return lambda h: h.matmul(out, lhsT=l, rhs=r, start=start, stop=stop)


DEBUG = False
ATT_THR = 130.0
ATT_FREE = 7


def attention_group(c, ast, io, g, oT, b_oT, x2T, b_x2T, consts):
    P = c.P
    m01, b_m01, negm, b_negm, M1, b_M1, NO, b_NO = consts
    nslot_blocks = 16 + g * 8 + 8
    KTr = c.rot(2, [128, 4096], BF16, "KTh", ast)
    Vr = c.rot(2, [128, 32, 128], BF16, "Vhp", ast)
    QTr = c.rot(2, [128, GRP], BF16, "QTh", ast)
    wqr = c.rot(2, [128, 8, 128], BF16, "wq", ast)
    esb = c.rot(4, [128, 512], F32, "esb", ast)
    spb = c.rot(16, [128, 512], BF16, "spb", ast)
    wbf = c.rot(8, [128, 512], BF16, "wbf", ast)
    Rbf = [c.rot(2, [128, 512], BF16, f"Rbf{i}", ast) for i in range(4)]
    rmn = c.rot(2, [128, 512], BF16, "rmn", ast)
    zt = c.sb([128, 128], BF16, "zt", ast)
    b_zt = Buf("zt")
    P.add("dve", f_memset(zt[:], 0.0), writes=[b_zt])
    chk = c.sb([1, 8], F32, "chk", ast)
    chk_i = c.sb([1, 8], I32, "chk_i", ast)
    b_chk = Buf("chk")
    pmi = c.pm.items
    accs = [c.pt.items[0], c.pt.items[1], pmi[4], pmi[5]]
    pma = Rot(pmi[0:4])
    NQ = GRP // 512
    for hp in range(8):
        KT, b_KT = KTr.next()
        V, b_V = Vr.next()
        ns = nslot_blocks * 128
        P.dma("sp", f_dma([(KT[:, 0:ns], io["KTs"][hp * 128:(hp + 1) * 128, 0:ns])]), b_KT, writes=[b_KT])
        P.dma("sp", f_dma([(V[:, 0:nslot_blocks, :],
                            io["Vs"][0:ns, hp * 128:(hp + 1) * 128].rearrange("(k p) n -> p k n", p=128))]),
              b_V, writes=[b_V])
        wq, b_wq = wqr.next()
        load_w(c, io["sb_w_q"][:, hp * 128:(hp + 1) * 128], wq[:], b_wq)
        QT, b_QT = QTr.next()
        for tg in range(NQ):
            pq, b_pq = c.pm.next()
            tok = slice(tg * 512, (tg + 1) * 512)
            P.add("pe", f_mm(pq[:], [(wq[:, k, :], x2T[:, k, tok]) for k in range(8)]),
                  reads=[b_wq] + b_x2T[tg * 4:(tg + 1) * 4], writes=[b_pq])
            P.add("act", f_act(QT[:, tok], pq[:], AF.Copy, scale=0.125), reads=[b_pq], writes=[b_QT])
        streams = [(qt, hd) for qt in range(NQ) for hd in range(2)]
        nkbs = {qt: 16 + g * 8 + qt * 4 + 4 for qt in range(NQ)}
        nmax = max(nkbs.values())
        Rcur = {}

        SP = {}
        WS = {}

        def live_of(i):
            return [(k, qt, hd) for k, (qt, hd) in enumerate(streams) if i < nkbs[qt]]

        def s1(i):
            sps = {}
            for (k, qt, hd) in live_of(i):
                kb = nkbs[qt] - 1 - i
                pb = 64 * hd
                tok = slice(qt * 512, (qt + 1) * 512)
                di = kb - (nkbs[qt] - 4)
                pz, b_pz = pma.next()
                P.add("pe", f_mm(pz[:], [(KT[pb:pb + 64, kb * 128:(kb + 1) * 128], QT[pb:pb + 64, tok])]),
                      reads=[b_KT, b_QT], writes=[b_pz])
                e, b_e = esb.next()
                P.add("act", f_act(e[:], pz[:], AF.Exp), reads=[b_pz], writes=[b_e])
                sp_, b_sp = spb.next()
                P.add("act", f_act(sp_[:], e[:], AF.Ln, bias=1.0), reads=[b_e], writes=[b_sp])
                if di >= 0:
                    P.add("dve", f_tt(sp_[:], sp_[:], m01[:, di, :], ALU.mult), reads=[b_sp, b_m01], writes=[b_sp])
                sps[k] = (sp_, b_sp)
            SP[i] = sps

        def s2(i):
            sps = SP.pop(i)
            ws = {}
            for (k, qt, hd) in live_of(i):
                kb = nkbs[qt] - 1 - i
                pb = 64 * hd
                tok = slice(qt * 512, (qt + 1) * 512)
                di = kb - (nkbs[qt] - 4)
                sp_, b_sp = sps[k]
                pE, b_pE = pma.next()
                pairs = [(KT[pb:pb + 64, kb * 128:(kb + 1) * 128], QT[pb:pb + 64, tok]), (M1[:], sp_[:])]
                reads = [b_KT, b_QT, b_M1, b_sp]
                if i > 0:
                    rb, b_rb = Rcur[k]
                    pairs.append((NO[:], rb[:]))
                    reads += [b_NO, b_rb]
                if di >= 0:
                    pairs.append((c.idb[:], negm[:, di, :]))
                    reads += [c.b_idb, b_negm]
                P.add("pe", f_mm(pE[:], pairs), reads=reads, writes=[b_pE])
                w, b_w = wbf.next()
                P.add("act", f_act(w[:], pE[:], AF.Exp), reads=[b_pE], writes=[b_w])
                ws[k] = (w, b_w)
                if i == 0:
                    Rcur[k] = (sp_, b_sp)
                else:
                    ro, b_ro = Rcur[k]
                    rb, b_rb = Rbf[k].next()
                    P.add("dve", f_tt(rb[:], ro[:], sp_[:], ALU.add), reads=[b_sp, b_ro], writes=[b_rb])
                    Rcur[k] = (rb, b_rb)
            WS[i] = ws

        def s3(i):
            ws = WS.pop(i)
            for (k, qt, hd) in live_of(i):
                kb = nkbs[qt] - 1 - i
                w, b_w = ws[k]
                acc_t, b_acc = accs[k]
                acc = acc_t[:].bitcast(F32) if k < 2 else acc_t[:]
                P.add("pe", f_mm1(acc, V[:, kb, :], w[:], i == 0, False), reads=[b_V, b_w], writes=[b_acc])

        def block(i):
            s1(i)
            s2(i)
            s3(i)

        def checkpoint():
            r0, b_r0 = Rcur[0]
            cur, b_cur = r0, b_r0
            for k in range(1, len(streams)):
                rk, b_rk = Rcur[k]
                rm, b_rm = rmn.next()
                P.add("dve", f_tt(rm[:], cur[:], rk[:], ALU.min), reads=[b_cur, b_rk], writes=[b_rm])
                cur, b_cur = rm, b_rm
            pc, b_pc = pma.next()
            P.add("pe", f_mm(pc[0:1, :], [(NO[:, 0:1], cur[:])]), reads=[b_NO, b_cur], writes=[b_pc])
            P.add("dve", lambda h: h.reduce_max(out=chk[0:1, 0:1], in_=pc[0:1, :], axis=mybir.AxisListType.X),
                  reads=[b_pc, b_chk], writes=[b_chk])
            P.add("dve", f_ts(chk[0:1, 1:2], chk[0:1, 0:1], 1000.0 + ATT_THR, 0.0, ALU.add, ALU.max), reads=[b_chk], writes=[b_chk])
            P.add("dve", f_copy(chk_i[0:1, 0:1], chk[0:1, 1:2]), reads=[b_chk], writes=[b_chk])
            if "dbg" in io and not io.get("dbg_done"):
                io["dbg_done"] = True
                ob = Buf("dbgout")
                io["outs"].append(ob)
                P.dma("sp", f_dma([(io["dbg"], chk[0:1, 0:8])]), b_chk, reads=[b_chk], writes=[ob])
                for k2 in range(4):
                    ob2 = Buf("dbgout2")
                    io["outs"].append(ob2)
                    P.dma("sp", f_dma([(io["dbg2"][k2], Rcur[k2][0][:])]), Rcur[k2][1], reads=[Rcur[k2][1]], writes=[ob2])
                ob2 = Buf("dbgout2")
                io["outs"].append(ob2)
                P.dma("sp", f_dma([(io["dbg2"][4], cur[:])]), b_cur, reads=[b_cur], writes=[ob2])
            for eng in ("pe", "act", "dve"):
                P.reg_load(eng, "att", chk_i[0:1, 0:1], reads=[b_chk])

        for step in range(ATT_FREE + 2):
            if step < ATT_FREE:
                s1(step)
            if 0 <= step - 1 < ATT_FREE:
                s2(step - 1)
            if 0 <= step - 2 < ATT_FREE:
                s3(step - 2)
        checkpoint()
        for i in range(ATT_FREE, nmax):
            P.region_begin("att", 1000)
            block(i)
            if i < nmax - 1:
                checkpoint()
            P.region_end()
        for k, (qt, hd) in enumerate(streams):
            pb = 64 * hd
            tok = slice(qt * 512, (qt + 1) * 512)
            acc_t, b_acc = accs[k]
            acc = acc_t[:].bitcast(F32) if k < 2 else acc_t[:]
            P.add("pe", f_mm1(acc, zt[:], m01[:, 0, :], False, True), reads=[b_zt, b_m01], writes=[b_acc])
            P.add("act", f_act(oT[pb:pb + 64, hp, tok], acc[pb:pb + 64, :], AF.Copy), reads=[b_acc],
                  writes=b_oT[qt * 4:(qt + 1) * 4])


def build_layer1(c, io):
    P = c.P
    c.one_t = c.sb([128, 2], F32, "one")
    P.add("dve", f_memset(c.one_t[:], 1.0), writes=[c.b_eps])
    m01, b_m01 = load_const(c, io["m01"], [128, 4, 512], BF16, "m01", eng="pool")
    negm, b_negm = load_const(c, io["negm"], [128, 4, 512], BF16, "negm", eng="pool")
    M1, b_M1 = load_const(c, io["M1"], [128, 128], BF16, "M1", eng="pool")
    NO, b_NO = load_const(c, io["NO"], [128, 128], BF16, "NO", eng="pool")
    aconsts = (m01, b_m01, negm, b_negm, M1, b_M1, NO, b_NO)
    wr, b_wr = load_const(c, io["moe_w_router"].rearrange("(c p) n -> p c n", p=128), [128, 8, N_EXP], F32, "wr")
    for g in range(TOK // GRP):
        xsrc = io["x2"][g * GRP:(g + 1) * GRP, :]
        P.barrier()
        with contextlib.ExitStack() as gst:
            x3 = c.sb([128, GT, 1024], F32, "x3", gst)
            b_x3 = [Buf("x3") for _ in range(GT)]
            with contextlib.ExitStack() as ast:
                x2T = c.sb([128, 8, GRP], BF16, "x2T", ast)
                b_x2T = [Buf("x2T") for _ in range(GT)]
                oT = c.sb([128, 8, GRP], BF16, "oT", ast)
                b_oT = [Buf("oT") for _ in range(GT)]
                for t in range(GT):
                    stg, b_stg = c.stage.next()
                    P.dma("sp", f_dma([(stg[:], xsrc[t * 128:(t + 1) * 128, :])]), b_stg, writes=[b_stg])
                    tile_to_T(c, stg[:], b_stg, x2T, b_x2T, t)
                with contextlib.ExitStack() as hst:
                    attention_group(c, hst, io, g, oT, b_oT, x2T, b_x2T, aconsts)
                P.barrier()
                with contextlib.ExitStack() as mst:
                    gam, bet, b_gb = load_ln(c, mst, io["g_mix"], io["b_mix"], "lnmix")
                    tmpf = c.rot(2, [128, 1024], F32, "tmpf", mst)
                    wos = c.rot(2, [128, 8, 512], BF16, "wos", mst)
                    xst = c.rot(2, [128, 512], F32, "xst", mst)
                    for half in range(2):
                        wo, b_wo = wos.next()
                        load_w(c, io["sb_w_out"][:, half * 512:(half + 1) * 512], wo[:], b_wo, nsplit=2)
                        for t in range(GT):
                            pm_, b_pm = c.pm.next()
                            P.add("pe", f_mm(pm_[:], [(oT[:, ch, t * 128:(t + 1) * 128], wo[:, ch, :]) for ch in range(8)]),
                                  reads=[b_wo, b_oT[t]], writes=[b_pm])
                            xs, b_xs = xst.next()
                            P.dma("sp", f_dma([(xs[:], xsrc[t * 128:(t + 1) * 128, half * 512:(half + 1) * 512])]), b_xs, writes=[b_xs])
                            P.add("dve", f_stt(x3[:, t, half * 512:(half + 1) * 512], xs[:], ALPHA, pm_[:], ALU.mult, ALU.add),
                                  reads=[b_xs, b_pm], writes=[b_x3[t]])
                    for t in range(GT):
                        ln_inplace(c, x3[:, t, :], b_x3[t], gam, bet, b_gb, tmpf)
            P.barrier()
            with contextlib.ExitStack() as fst:
                gam, bet, b_gb = load_ln(c, fst, io["g_ffn"], io["b_ffn"], "lnffn")
                tmpf = c.rot(2, [128, 1024], F32, "tmpf", fst)
                x3T = c.sb([128, 8, GRP], BF16, "x3T", fst)
                b_x3T = [Buf("x3T") for _ in range(GT)]
                yacc = c.sb([128, GT, 1024], F32, "yacc", fst)
                b_ya = [Buf("yacc", key=f"ya{t}") for t in range(GT)]
                gates = c.sb([128, GT, N_EXP], F32, "gates", fst)
                b_gt = Buf("gates")
                xTf = c.rot(2, [128, 8, 128], F32, "xTf", fst)
                lgr = c.rot(2, [128, 32], F32, "lg", fst)
                for t in range(GT):
                    tile_to_T(c, x3[:, t, :], b_x3[t], x3T, b_x3T, t)
                    P.add("act", f_act(yacc[:, t, :], x3[:, t, :], AF.Copy, scale=ALPHA), reads=[b_x3[t]], writes=[b_ya[t]])
                    xf, b_xf = xTf.next()
                    for hf in range(2):
                        pp, b_pp = c.pm.next()
                        ppv = pp[:].rearrange("p (c n) -> p c n", n=128)
                        P.add("pe", f_tr([(ppv[:, ch, :], x3[:, t, (hf * 4 + ch) * 128:(hf * 4 + ch + 1) * 128]) for ch in range(4)], c.idf[:]),
                              reads=[b_x3[t], c.b_idf], writes=[b_pp])
                        P.add("dve", f_copy(xf[:, hf * 4:(hf + 1) * 4, :], ppv[:, 0:4, :]), reads=[b_pp], writes=[b_xf])
                    pl, b_pl = c.pm.next()
                    P.add("pe", f_mm(pl[:, 0:N_EXP], [(xf[:, ch, :], wr[:, ch, :]) for ch in range(8)]),
                          reads=[b_xf, b_wr], writes=[b_pl])
                    lg, b_lg = lgr.next()
                    P.add("dve", f_copy(lg[:, 0:8], pl[:, 0:N_EXP]), reads=[b_pl], writes=[b_lg])
                    P.add("dve", lambda h, lg=lg: h.max(out=lg[:, 8:16], in_=lg[:, 0:8]), reads=[b_lg], writes=[b_lg])
                    P.add("dve", f_ts(lg[:, 16:17], lg[:, 8:9], -1.0, None, ALU.mult), reads=[b_lg], writes=[b_lg])
                    P.add("act", f_act(lg[:, 24:32], lg[:, 0:8], AF.Exp, bias=lg[:, 16:17]), reads=[b_lg], writes=[b_lg])
                    P.add("dve", f_ts(gates[:, t, :], lg[:, 0:8], lg[:, 9:10], None, ALU.is_ge), reads=[b_lg, b_gt], writes=[b_gt])
                    P.add("dve", f_tt(gates[:, t, :], gates[:, t, :], lg[:, 24:32], ALU.mult), reads=[b_lg, b_gt], writes=[b_gt])
                    P.add("dve", lambda h, lg=lg, t=t: h.reduce_sum(out=lg[:, 17:18], in_=gates[:, t, :], axis=mybir.AxisListType.X),
                          reads=[b_gt, b_lg], writes=[b_lg])
                    P.add("dve", lambda h, lg=lg: h.reciprocal(out=lg[:, 17:18], in_=lg[:, 17:18]), reads=[b_lg], writes=[b_lg])
                    P.add("dve", f_ts(gates[:, t, :], gates[:, t, :], lg[:, 17:18], None, ALU.mult), reads=[b_lg, b_gt], writes=[b_gt])
                for e in range(N_EXP):
                    with contextlib.ExitStack() as ust:
                        ffn_unit(c, ust, x3T, b_x3T, io["moe_w_in"][e * D:(e + 1) * D, :], D_EXP,
                                 io["moe_w_out"][e * D_EXP:(e + 1) * D_EXP, :], yacc, b_ya, [(0, 14), (14, 14)],
                                 gate=((lambda t, e=e: gates[:, t, e:e + 1]), b_gt))
                    P.barrier()
                for t in range(GT):
                    ln_inplace(c, yacc[:, t, :], b_ya[t], gam, bet, b_gb, tmpf)
                    r0 = g * GRP + t * 128
                    store(c, io, io["out"][r0:r0 + 128, :], yacc[:, t, :], b_ya[t])


def make_nc_layer1():
    nc = bass.Bass("TRN2", target_bir_lowering=False)
    c = Ctx(nc)
    io = {"outs": []}

    def din(name, shape, dt=F32):
        io[name] = c.dram(name, shape, dt, "ExternalInput")
    din("x2", [TOK, D]); din("KTs", [D, 2 * TOK], BF16); din("Vs", [2 * TOK, D], BF16)
    din("sb_w_q", [D, D]); din("sb_w_out", [D, D])
    din("moe_w_router", [D, N_EXP]); din("moe_w_in", [N_EXP * D, 2 * D_EXP]); din("moe_w_out", [N_EXP * D_EXP, D])
    din("g_mix", [128, D]); din("b_mix", [128, D]); din("g_ffn", [128, D]); din("b_ffn", [128, D])
    din("m01", [128, 4, 512]); din("negm", [128, 4, 512]); din("M1", [128, 128]); din("NO", [128, 128]); din("ident", [128, 128])
    io["out"] = c.dram("out", [TOK, D], F32, "ExternalOutput")
    with c.root:
        setup_common(c, io["ident"])
        build_layer1(c, io)
        c.P.add("sp", lambda h: None, reads=io["outs"])
        cnt = c.P.emit(nc)
    return nc, cnt


def layer1_inputs(inputs, core, x2, KTs, Vs):
    f = np.float32
    m01, negm, m1, negones = attn_tables()
    bc = lambda v: np.ascontiguousarray(np.broadcast_to(np.asarray(v, f)[None, :], (128, D)))
    return {
        "x2": x2, "KTs": KTs, "Vs": Vs,
        "sb_w_q": np.ascontiguousarray(inputs["sb_w_q"][0]), "sb_w_out": np.ascontiguousarray(inputs["sb_w_out"][0]),
        "moe_w_router": np.ascontiguousarray(inputs["moe_w_router"][0]),
        "moe_w_in": np.ascontiguousarray(inputs["moe_w_in"][0]).reshape(N_EXP * D, 2 * D_EXP),
        "moe_w_out": np.ascontiguousarray(inputs["moe_w_out"][0]).reshape(N_EXP * D_EXP, D),
        "g_mix": bc(inputs["ln_mix_g"][1]), "b_mix": bc(inputs["ln_mix_b"][1]),
        "g_ffn": bc(inputs["ln_ffn_g"][1]), "b_ffn": bc(inputs["ln_ffn_b"][1]),
        "m01": m01, "negm": negm, "M1": m1, "NO": negones, "ident": np.eye(128, dtype=f),
    }


def layer0_group(c, io, L0, g_tok, xsrc, cosd, sind, xs, b_xs, slot0, vscale):
    P = c.P
    S32, Sb, b_S32, b_Sb, consts = L0
    scratch = xs.rearrange("p t d -> p (t d)")
    P.barrier()
    with contextlib.ExitStack() as yst:
        yT = c.sb([128, 16, GRP], BF16, "yT", yst)
        b_yT = [[Buf("yT") for _ in range(GT)] for _ in range(RH)]
        with contextlib.ExitStack() as rst:
            ret_heads(c, rst, io, "own", 0, xsrc, cosd, sind, S32, Sb, b_S32, b_Sb, consts, yT, b_yT, scratch=scratch)
        P.barrier()
        with contextlib.ExitStack() as mst:
            gam, bet, b_gb = load_ln(c, mst, io["g_mix0"], io["b_mix0"], "lnmix")
            tmpf = c.rot(2, [128, 1024], F32, "tmpf", mst)
            wos = c.rot(2, [128, 16, 512], BF16, "wos", mst)
            xst = c.rot(2, [128, 512], F32, "xst", mst)
            for half in range(2):
                wo, b_wo = wos.next()
                load_w(c, io["ret_w_out"][:, half * 512:(half + 1) * 512], wo[:], b_wo, nsplit=2)
                for t in range(GT):
                    pm_, b_pm = c.pm.next()
                    P.add("pe", f_mm(pm_[:], [(yT[:, ch, t * 128:(t + 1) * 128], wo[:, ch, :]) for ch in range(16)]),
                          reads=[b_wo] + [b_yT[h][t] for h in range(RH)], writes=[b_pm])
                    xq, b_xq = xst.next()
                    P.dma("sp", f_dma([(xq[:], xsrc[t * 128:(t + 1) * 128, half * 512:(half + 1) * 512])]), b_xq, writes=[b_xq])
                    P.add("dve", f_stt(xs[:, t, half * 512:(half + 1) * 512], xq[:], ALPHA, pm_[:], ALU.mult, ALU.add),
                          reads=[b_xq, b_pm], writes=[b_xs[t]])
            for t in range(GT):
                ln_inplace(c, xs[:, t, :], b_xs[t], gam, bet, b_gb, tmpf)
    P.barrier()
    with contextlib.ExitStack() as fst:
        gam, bet, b_gb = load_ln(c, fst, io["g_ffn0"], io["b_ffn0"], "lnffn")
        tmpf = c.rot(2, [128, 1024], F32, "tmpf", fst)
        x1T = c.sb([128, 8, GRP], BF16, "x1T", fst)
        b_x1T = [Buf("x1T") for _ in range(GT)]
        for t in range(GT):
            tile_to_T(c, xs[:, t, :], b_xs[t], x1T, b_x1T, t)
            P.add("act", f_act(xs[:, t, :], xs[:, t, :], AF.Copy, scale=ALPHA), reads=[b_xs[t]], writes=[b_xs[t]])
        with contextlib.ExitStack() as ust:
            ffn_unit(c, ust, x1T, b_x1T, io["ffn_w_in"], D_FF, io["ffn_w_out"], xs, b_xs, [(0, 12), (12, 10)])
        for t in range(GT):
            ln_inplace(c, xs[:, t, :], b_xs[t], gam, bet, b_gb, tmpf)
        P.barrier()
        with contextlib.ExitStack() as kst:
            x2T = x1T
            b_x2T = [Buf("x2T") for _ in range(GT)]
            for t in range(GT):
                tile_to_T(c, xs[:, t, :], b_xs[t], x2T, b_x2T, t)
            wkv = c.rot(2, [128, 8, 512], BF16, "wkv", kst)
            ksb = c.rot(3, [128, 512], BF16, "ksb", kst)
            for cb in range(2):
                ws, b_ws = wkv.next()
                load_w(c, io["sb_w_kv"][:, cb * 512:(cb + 1) * 512], ws[:], b_ws, nsplit=2)
                for ch in range(4):
                    for tg in range(GRP // 512):
                        pk, b_pk = c.pm.next()
                        tok = slice(tg * 512, (tg + 1) * 512)
                        P.add("pe", f_mm(pk[:], [(ws[:, k, ch * 128:(ch + 1) * 128], x2T[:, k, tok]) for k in range(8)]),
                              reads=[b_ws] + b_x2T[tg * 4:(tg + 1) * 4], writes=[b_pk])
                        ks, b_ks = ksb.next()
                        if vscale is None:
                            P.add("act", f_act(ks[:], pk[:], AF.Copy), reads=[b_pk], writes=[b_ks])
                        else:
                            P.add("act", f_act(ks[:], pk[:], AF.Identity, scale=vscale[0]), reads=[b_pk, vscale[1]], writes=[b_ks])
                        f0 = (cb * 4 + ch) * 128
                        t0 = slot0 + tg * 512
                        store(c, io, io["KTs"][f0:f0 + 128, t0:t0 + 512], ks[:], b_ks)
            for cb in range(2):
                ws, b_ws = wkv.next()
                load_w(c, io["sb_w_kv"][:, 1024 + cb * 512:1024 + (cb + 1) * 512], ws[:], b_ws, nsplit=2)
                for t in range(GT):
                    pv, b_pv = c.pm.next()
                    P.add("pe", f_mm(pv[:], [(x2T[:, k, t * 128:(t + 1) * 128], ws[:, k, :]) for k in range(8)]),
                          reads=[b_ws, b_x2T[t]], writes=[b_pv])
                    ks, b_ks = ksb.next()
                    if vscale is None:
                        P.add("act", f_act(ks[:], pv[:], AF.Copy), reads=[b_pv], writes=[b_ks])
                    else:
                        P.add("act", f_act(ks[:], pv[:], AF.Identity, scale=vscale[0]), reads=[b_pv, vscale[1]], writes=[b_ks])
                    r0 = slot0 + t * 128
                    store(c, io, io["Vs"][r0:r0 + 128, cb * 512:(cb + 1) * 512], ks[:], b_ks)


def layer1_group(c, io, g, xs, b_xs, aconsts, wr, b_wr, attn_only=False):
    P = c.P
    P.barrier()
    with contextlib.ExitStack() as ast:
        x2T = c.sb([128, 8, GRP], BF16, "x2T", ast)
        b_x2T = [Buf("x2T") for _ in range(GT)]
        oT = c.sb([128, 8, GRP], BF16, "oT", ast)
        b_oT = [Buf("oT") for _ in range(GT)]
        for t in range(GT):
            tile_to_T(c, xs[:, t, :], b_xs[t], x2T, b_x2T, t)
        with contextlib.ExitStack() as hst:
            attention_group(c, hst, io, g, oT, b_oT, x2T, b_x2T, aconsts)
        P.barrier()
        with contextlib.ExitStack() as mst:
            gam, bet, b_gb = load_ln(c, mst, io["g_mix1"], io["b_mix1"], "lnmix")
            tmpf = c.rot(2, [128, 1024], F32, "tmpf", mst)
            wos = c.rot(2, [128, 8, 512], BF16, "wos", mst)
            for half in range(2):
                wo, b_wo = wos.next()
                load_w(c, io["sb_w_out"][:, half * 512:(half + 1) * 512], wo[:], b_wo, nsplit=2)
                for t in range(GT):
                    pm_, b_pm = c.pm.next()
                    P.add("pe", f_mm(pm_[:], [(oT[:, ch, t * 128:(t + 1) * 128], wo[:, ch, :]) for ch in range(8)]),
                          reads=[b_wo, b_oT[t]], writes=[b_pm])
                    xh = xs[:, t, half * 512:(half + 1) * 512]
                    P.add("dve", f_stt(xh, xh, ALPHA, pm_[:], ALU.mult, ALU.add), reads=[b_xs[t], b_pm], writes=[b_xs[t]])
            for t in range(GT):
                ln_inplace(c, xs[:, t, :], b_xs[t], gam, bet, b_gb, tmpf)
    if attn_only:
        return
    P.barrier()
    with contextlib.ExitStack() as fst:
        gam, bet, b_gb = load_ln(c, fst, io["g_ffn1"], io["b_ffn1"], "lnffn")
        tmpf = c.rot(2, [128, 1024], F32, "tmpf", fst)
        x3T = c.sb([128, 8, GRP], BF16, "x3T", fst)
        b_x3T = [Buf("x3T") for _ in range(GT)]
        gates = c.sb([128, GT, N_EXP], F32, "gates", fst)
        b_gt = Buf("gates")
        xTf = c.rot(2, [128, 8, 128], F32, "xTf", fst)
        lgr = c.rot(2, [128, 32], F32, "lg", fst)
        for t in range(GT):
            tile_to_T(c, xs[:, t, :], b_xs[t], x3T, b_x3T, t)
            xf, b_xf = xTf.next()
            for hf in range(2):
                pp, b_pp = c.pm.next()
                ppv = pp[:].rearrange("p (c n) -> p c n", n=128)
                P.add("pe", f_tr([(ppv[:, ch, :], xs[:, t, (hf * 4 + ch) * 128:(hf * 4 + ch + 1) * 128]) for ch in range(4)], c.idf[:]),
                      reads=[b_xs[t], c.b_idf], writes=[b_pp])
                P.add("dve", f_copy(xf[:, hf * 4:(hf + 1) * 4, :], ppv[:, 0:4, :]), reads=[b_pp], writes=[b_xf])
            pl, b_pl = c.pm.next()
            P.add("pe", f_mm(pl[:, 0:N_EXP], [(xf[:, ch, :], wr[:, ch, :]) for ch in range(8)]),
                  reads=[b_xf, b_wr], writes=[b_pl])
            P.add("act", f_act(xs[:, t, :], xs[:, t, :], AF.Copy, scale=ALPHA), reads=[b_xs[t]], writes=[b_xs[t]])
            lg, b_lg = lgr.next()
            P.add("dve", f_copy(lg[:, 0:8], pl[:, 0:N_EXP]), reads=[b_pl], writes=[b_lg])
            P.add("dve", lambda h, lg=lg: h.max(out=lg[:, 8:16], in_=lg[:, 0:8]), reads=[b_lg], writes=[b_lg])
            P.add("dve", f_ts(lg[:, 16:17], lg[:, 8:9], -1.0, None, ALU.mult), reads=[b_lg], writes=[b_lg])
            P.add("act", f_act(lg[:, 24:32], lg[:, 0:8], AF.Exp, bias=lg[:, 16:17]), reads=[b_lg], writes=[b_lg])
            P.add("dve", f_ts(gates[:, t, :], lg[:, 0:8], lg[:, 9:10], None, ALU.is_ge), reads=[b_lg, b_gt], writes=[b_gt])
            P.add("dve", f_tt(gates[:, t, :], gates[:, t, :], lg[:, 24:32], ALU.mult), reads=[b_lg, b_gt], writes=[b_gt])
            P.add("dve", lambda h, lg=lg, t=t: h.reduce_sum(out=lg[:, 17:18], in_=gates[:, t, :], axis=mybir.AxisListType.X),
                  reads=[b_gt, b_lg], writes=[b_lg])
            P.add("dve", lambda h, lg=lg: h.reciprocal(out=lg[:, 17:18], in_=lg[:, 17:18]), reads=[b_lg], writes=[b_lg])
            P.add("dve", f_ts(gates[:, t, :], gates[:, t, :], lg[:, 17:18], None, ALU.mult), reads=[b_lg, b_gt], writes=[b_gt])
        for e in range(N_EXP):
            with contextlib.ExitStack() as ust:
                ffn_unit(c, ust, x3T, b_x3T, io["moe_w_in"][e * D:(e + 1) * D, :], D_EXP,
                         io["moe_w_out"][e * D_EXP:(e + 1) * D_EXP, :], xs, b_xs, [(0, 14), (14, 14)],
                         gate=((lambda t, e=e: gates[:, t, e:e + 1]), b_gt))
            P.barrier()
        for t in range(GT):
            ln_inplace(c, xs[:, t, :], b_xs[t], gam, bet, b_gb, tmpf)
            r0 = g * GRP + t * 128
            store(c, io, io["out"][r0:r0 + 128, :], xs[:, t, :], b_xs[t])


ROUTED = True
I32 = mybir.dt.int32
REGS = [(None, [(0, 384)]), (384, [(384, 128), (512, 512)])]


def moe_tables():
    iota_f = np.broadcast_to(np.arange(128, dtype=np.float32)[None, :], (128, 128)).copy()
    iotap = (np.arange(128, dtype=np.float32)[:, None] + 128.0 * np.arange(8, dtype=np.float32)[None, :]).copy()
    k = np.arange(128)[:, None]
    m = np.arange(128)[None, :]
    ulow = (k < m).astype(np.float32)
    ones = np.ones((128, 128), np.float32)
    oh = np.zeros((8, 8, 128), np.float32)
    for r in range(8):
        oh[r, r, :] = 1.0
    return iota_f, iotap, ulow, ones, oh.reshape(8, 8 * 128)


def moe_routed_group(c, io, g, xs, b_xs, mc):
    P = c.P
    iota_f, b_iota, iotap, b_iotap, Ulow, b_Ul, ones_b, b_ones, OH, b_OH, wr, b_wr = mc
    P.barrier()
    with contextlib.ExitStack() as fst:
        rt_in = c.sb([128, GT, 16], F32, "rt_in", fst)
        b_rt = [Buf("rt_in") for _ in range(GT)]
        maskt = c.sb([128, GT, 8], F32, "maskt", fst)
        maskb = c.sb([128, GT, 8], BF16, "maskb", fst)
        b_mask = [Buf("mask") for _ in range(GT)]
        cnt_i = c.sb([128, 8], I32, "cnt_i", fst)
        b_cnt = Buf("cnt")
        RTp = c.sb([8, GRP], F32, "RTp", fst)
        RTg = c.sb([8, GRP], F32, "RTg", fst)
        b_RT = Buf("RT")
        x3b = c.sb([128, GT, 1024], BF16, "x3b", fst)
        b_x3b = [Buf("x3b") for _ in range(GT)]
        pst = contextlib.ExitStack()
        xTf = c.rot(2, [128, 8, 128], F32, "xTf", pst)
        lgr = c.rot(2, [128, 32], F32, "lg", pst)
        for t in range(GT):
            P.add("act", f_act(x3b[:, t, :], xs[:, t, :], AF.Copy), reads=[b_xs[t]], writes=[b_x3b[t]])
            xf, b_xf = xTf.next()
            for hf in range(2):
                pp, b_pp = c.pm.next()
                ppv = pp[:].rearrange("p (c n) -> p c n", n=128)
                P.add("pe", f_tr([(ppv[:, ch, :], xs[:, t, (hf * 4 + ch) * 128:(hf * 4 + ch + 1) * 128]) for ch in range(4)], c.idf[:]),
                      reads=[b_xs[t], c.b_idf], writes=[b_pp])
                P.add("dve", f_copy(xf[:, hf * 4:(hf + 1) * 4, :], ppv[:, 0:4, :]), reads=[b_pp], writes=[b_xf])
            pl, b_pl = c.pm.next()
            P.add("pe", f_mm(pl[:, 0:N_EXP], [(xf[:, ch, :], wr[:, ch, :]) for ch in range(8)]),
                  reads=[b_xf, b_wr], writes=[b_pl])
            P.add("act", f_act(xs[:, t, :], xs[:, t, :], AF.Copy, scale=ALPHA), reads=[b_xs[t]], writes=[b_xs[t]])
            lg, b_lg = lgr.next()
            gt = rt_in[:, t, 8:16]
            P.add("dve", f_copy(lg[:, 0:8], pl[:, 0:N_EXP]), reads=[b_pl], writes=[b_lg])
            P.add("dve", lambda h, lg=lg: h.max(out=lg[:, 8:16], in_=lg[:, 0:8]), reads=[b_lg], writes=[b_lg])
            P.add("dve", f_ts(lg[:, 16:17], lg[:, 8:9], -1.0, None, ALU.mult), reads=[b_lg], writes=[b_lg])
            P.add("act", f_act(lg[:, 24:32], lg[:, 0:8], AF.Exp, bias=lg[:, 16:17]), reads=[b_lg], writes=[b_lg])
            P.add("dve", f_ts(maskt[:, t, :], lg[:, 0:8], lg[:, 9:10], None, ALU.is_ge), reads=[b_lg], writes=[b_mask[t]])
            P.add("dve", f_copy(maskb[:, t, :], maskt[:, t, :]), reads=[b_mask[t]], writes=[b_mask[t]])
            P.add("dve", f_tt(gt, maskt[:, t, :], lg[:, 24:32], ALU.mult), reads=[b_lg, b_mask[t]], writes=[b_rt[t]])
            P.add("dve", lambda h, lg=lg, gt=gt: h.reduce_sum(out=lg[:, 17:18], in_=gt, axis=mybir.AxisListType.X),
                  reads=[b_rt[t], b_lg], writes=[b_lg])
            P.add("dve", lambda h, lg=lg: h.reciprocal(out=lg[:, 17:18], in_=lg[:, 17:18]), reads=[b_lg], writes=[b_lg])
            P.add("dve", f_ts(gt, gt, lg[:, 17:18], None, ALU.mult), reads=[b_lg, b_rt[t]], writes=[b_rt[t]])
        for t in range(GT):
            pp, b_pp = c.pm.next()
            pairs = [(Ulow[:], maskb[:, t, :])] + [(ones_b[:], maskb[:, t2, :]) for t2 in range(t)]
            P.add("pe", f_mm(pp[:, 0:8], pairs), reads=[b_Ul, b_ones] + b_mask[0:t + 1], writes=[b_pp])
            P.add("dve", f_stt(rt_in[:, t, 0:8], pp[:, 0:8], 1.0, maskt[:, t, :], ALU.add, ALU.mult),
                  reads=[b_pp, b_mask[t], b_rt[t]], writes=[b_rt[t]])
            P.add("dve", f_ts(rt_in[:, t, 0:8], rt_in[:, t, 0:8], -1.0, None, ALU.add), reads=[b_rt[t]], writes=[b_rt[t]])
        pc, b_pc = c.pm.next()
        P.add("pe", f_mm(pc[:, 0:8], [(ones_b[:], maskb[:, t, :]) for t in range(GT)]), reads=[b_ones] + b_mask, writes=[b_pc])
        P.add("dve", f_copy(cnt_i[:], pc[:, 0:8]), reads=[b_pc], writes=[b_cnt])
        for (dst, c0) in ((RTp, 0), (RTg, 8)):
            for hf in range(2):
                pp, b_pp = c.pm.next()
                P.add("pe", f_tr([(pp[0:8, j * 128:(j + 1) * 128], rt_in[:, hf * 4 + j, c0:c0 + 8]) for j in range(4)], c.idf[:]),
                      reads=b_rt[hf * 4:(hf + 1) * 4] + [c.b_idf], writes=[b_pp])
                P.add("act", f_act(dst[:, hf * 512:(hf + 1) * 512], pp[0:8, :], AF.Copy), reads=[b_pp], writes=[b_RT])

        pst.close()
        P.barrier()
        xgT = c.sb([128, 8, GRP], BF16, "xgT", fst)
        b_xg = [Buf("xg") for _ in range(GT)]
        P.add("dve", f_memset(xgT[:], 0.0), writes=b_xg)
        actT = c.sb([128, 14, GRP], BF16, "actT", fst)
        b_act = [[Buf("act") for _ in range(GT)] for _ in range(14)]
        oute = c.sb([128, GT, 512], BF16, "oute", fst)
        b_oute = [Buf("oute") for _ in range(GT)]
        pbc = c.sb([128, GRP], F32, "pbc", fst)
        gbc = c.sb([128, GRP], F32, "gbc", fst)
        b_bc = Buf("bc")
        selr = c.rot(16, [128, 128], BF16, "sel", fst)
        nst0 = REGS[0][1][0][1] // 128
        sTc = c.sb([128, nst0, GT, 128], BF16, "sTc", fst)
        b_sTc = [[Buf("sTc") for _ in range(GT)] for _ in range(nst0)]
        selTr = c.rot(4, [128, 128], BF16, "selT", fst)
        w1s = c.rot(2, [128, 8, 2, 256], BF16, "w1s", fst)
        w2s = c.rot(1, [128, 14, 512], BF16, "w2s", fst)
        sg = c.rot(2, [128, 512], BF16, "sg", fst)

        def regions():
            for thr, pieces in REGS:
                if thr is not None:
                    P.region_begin("cnt", thr)
                yield thr, pieces
                if thr is not None:
                    P.region_end()

        for e in range(N_EXP):
            w1 = io["moe_w_in"][e * D:(e + 1) * D, :]
            w2 = io["moe_w_out"][e * D_EXP:(e + 1) * D_EXP, :]
            for eng in ("pe", "act", "dve"):
                P.reg_load(eng, "cnt", cnt_i[0:1, e:e + 1], reads=[b_cnt])
            for (dst, src) in ((pbc, RTp), (gbc, RTg)):
                for hf in range(2):
                    pb, b_pb = c.pm.next()
                    P.add("pe", f_mm(pb[:], [(OH[:, e * 128:(e + 1) * 128], src[:, hf * 512:(hf + 1) * 512])]),
                          reads=[b_OH, b_RT], writes=[b_pb])
                    P.add("dve", f_copy(dst[:, hf * 512:(hf + 1) * 512], pb[:]), reads=[b_pb], writes=[b_bc])
            for thr, pieces in regions():
                for (s0, w) in pieces:
                    for st in range(w // 128):
                        slot0 = s0 + st * 128
                        sels = []
                        for t in range(GT):
                            sl, b_sl = selr.next()
                            P.add("dve", f_ts(sl[:], iota_f[:], rt_in[:, t, e:e + 1], -float(slot0), ALU.subtract, ALU.is_equal),
                                  reads=[b_iota, b_rt[t]], writes=[b_sl])
                            sels.append((sl, b_sl))
                        for hf in range(2):
                            pg, b_pg = c.pm.next()
                            groups = [(pg[:, j * 128:(j + 1) * 128],
                                       [(x3b[:, t, (hf * 4 + j) * 128:(hf * 4 + j + 1) * 128], sels[t][0][:]) for t in range(GT)])
                                      for j in range(4)]
                            P.add("pe", f_mm_multi(groups), reads=b_x3b + [b for _, b in sels], writes=[b_pg])
                            P.add("dve", f_copy(xgT[:, hf * 4:(hf + 1) * 4, slot0:slot0 + 128],
                                                pg[:].rearrange("p (c n) -> p c n", n=128)),
                                  reads=[b_pg], writes=[b_xg[slot0 // 128]])
            for sti in range(nst0):
                for t in range(GT):
                    tk = slice(t * 128, (t + 1) * 128)
                    P.add("dve", f_stt(sTc[:, sti, t, :], pbc[:, tk], iotap[:, sti:sti + 1], gbc[:, tk], ALU.is_equal, ALU.mult),
                          reads=[b_bc, b_iotap], writes=[b_sTc[sti][t]])
            for (j0, nj) in ((0, 14), (14, 14)):
                for blk in range(nj // 2):
                    jj = j0 + blk * 2
                    ws, b_ws = w1s.next()
                    srcg = w1[:, jj * 128: jj * 128 + 256].rearrange("(c p) n -> p c n", p=128)
                    srcu = w1[:, D_EXP + jj * 128: D_EXP + jj * 128 + 256].rearrange("(c p) n -> p c n", p=128)
                    P.dma("pool", f_dma([(ws[:, :, 0, :], srcg), (ws[:, :, 1, :], srcu)]), b_ws, writes=[b_ws], n=2)
                    for thr, pieces in regions():
                        for (s0, w) in pieces:
                            tiles = list(range(s0 // 128, (s0 + w) // 128))
                            for ci in range(2):
                                jl = blk * 2 + ci
                                pg, b_pg = c.pm.next()
                                pu, b_pu = c.pm.next()
                                P.add("pe", f_mm_multi([
                                    (pg[:, 0:w], [(ws[:, k, 0, ci * 128:(ci + 1) * 128], xgT[:, k, s0:s0 + w]) for k in range(8)]),
                                    (pu[:, 0:w], [(ws[:, k, 1, ci * 128:(ci + 1) * 128], xgT[:, k, s0:s0 + w]) for k in range(8)]),
                                ]), reads=[b_ws] + [b_xg[i] for i in tiles], writes=[b_pg, b_pu])
                                s_, b_s = sg.next()
                                P.add("act", f_act(s_[:, 0:w], pg[:, 0:w], AF.Silu), reads=[b_pg], writes=[b_s])
                                P.add("dve", f_tt(actT[:, jl, s0:s0 + w], s_[:, 0:w], pu[:, 0:w], ALU.mult),
                                      reads=[b_s, b_pu], writes=[b_act[jl][i] for i in tiles])
                for half in range(2):
                    w2v, b_w2 = w2s.next()
                    load_w(c, w2[j0 * 128:(j0 + nj) * 128, half * 512:(half + 1) * 512], w2v[:, 0:nj, :], b_w2, nsplit=2)
                    for thr, pieces in regions():
                        for (s0, w) in pieces:
                            for sti in range(s0 // 128, (s0 + w) // 128):
                                po, b_po = c.pm.next()
                                P.add("pe", f_mm(po[:], [(actT[:, j, sti * 128:(sti + 1) * 128], w2v[:, j, :]) for j in range(nj)]),
                                      reads=[b_w2] + [b_act[j][sti] for j in range(nj)], writes=[b_po])
                                P.add("dve", f_copy(oute[:, sti, :], po[:]), reads=[b_po], writes=[b_oute[sti]])
                    for thr, pieces in regions():
                        stis = [sti for (s0, w) in pieces for sti in range(s0 // 128, (s0 + w) // 128)]
                        for t in range(GT):
                            tk = slice(t * 128, (t + 1) * 128)
                            if thr is None:
                                sTs = [(sTc[:, sti, t, :], b_sTc[sti][t]) for sti in stis]
                                groups = [stis]
                            else:
                                sTs = None
                                groups = [stis[i:i + 4] for i in range(0, len(stis), 4)]
                            for grp in groups:
                                if thr is None:
                                    cur = sTs
                                else:
                                    cur = []
                                    for sti in grp:
                                        sT, b_sT = selTr.next()
                                        P.add("dve", f_stt(sT[:], pbc[:, tk], iotap[:, sti:sti + 1], gbc[:, tk], ALU.is_equal, ALU.mult),
                                              reads=[b_bc, b_iotap], writes=[b_sT])
                                        cur.append((sT[:], b_sT))
                                ps_, b_ps = c.pm.next()
                                P.add("pe", f_mm(ps_[:], [(cur[i][0], oute[:, sti, :]) for i, sti in enumerate(grp)]),
                                      reads=[b for _, b in cur] + [b_oute[sti] for sti in grp], writes=[b_ps])
                                xh = xs[:, t, half * 512:(half + 1) * 512]
                                P.add("dve", f_tt(xh, ps_[:], xh, ALU.add), reads=[b_ps, b_xs[t]], writes=[b_xs[t]])
    P.barrier()
    with contextlib.ExitStack() as lst:
        gam, bet, b_gb = load_ln(c, lst, io["g_ffn1"], io["b_ffn1"], "lnffn")
        tmpf = c.rot(2, [128, 1024], F32, "tmpf", lst)
        for t in range(GT):
            ln_inplace(c, xs[:, t, :], b_xs[t], gam, bet, b_gb, tmpf)
            r0 = g * GRP + t * 128
            store(c, io, io["out"][r0:r0 + 128, :], xs[:, t, :], b_xs[t])


def build_fused(c, io):
    P = c.P
    xres = c.sb([128, TOK // 128, 1024], F32, "xres")
    b_xres = [Buf("xres", key=f"xres{t}") for t in range(TOK // 128)]
    c.one_t = c.sb([128, 2], F32, "one")
    P.add("dve", f_memset(c.one_t[:], 1.0), writes=[c.b_eps])
    flag, b_flag = load_const(c, io["flag"], [128, 1], F32, "flag")
    with contextlib.ExitStack() as l0st:
        c.stack = l0st
        c.stage = c.rot(2, [128, 1024], F32, "stage")
        _, _, _, cd = retention_tables()
        dtab, b_dtab = load_const(c, io["dtab"], [128, RH, 128], F32, "dtab")
        cn, b_cn = load_const(c, io["cn"], [128, RH], F32, "cn")
        kdec, b_kdec = load_const(c, io["kdec"], [128, RH], F32, "kdec")
        consts = (dtab, b_dtab, cn, b_cn, kdec, b_kdec, cd)
        S32 = c.sb([128, RH, 2, 512], F32, "S32")
        Sb = c.sb([128, RH, 2, 512], BF16, "Sb")
        b_S32 = [Buf("S32") for _ in range(RH)]
        b_Sb = [Buf("Sb") for _ in range(RH)]
        for h in range(RH):
            P.add("dve", f_memset(S32[:, h, :, :], 0.0), writes=[b_S32[h]])
            P.add("dve", f_memset(Sb[:, h, :, :], 0.0), writes=[b_Sb[h]])
        L0 = (S32, Sb, b_S32, b_Sb, consts)
        for mode, g in (("prev", 0), ("prev", 1), ("own", 0), ("own", 1)):
            own = mode == "own"
            xsrc = (io["x_own"] if own else io["x_prev"])[g * GRP:(g + 1) * GRP, :]
            cosd = (io["cos_own"] if own else io["cos_prev"])[:, g * GRP:(g + 1) * GRP]
            sind = (io["sin_own"] if own else io["sin_prev"])[:, g * GRP:(g + 1) * GRP]
            gs = g if own else 1
            xs = xres[:, gs * GT:(gs + 1) * GT, :]
            bx = b_xres[gs * GT:(gs + 1) * GT]
            slot0 = (TOK if own else 0) + g * GRP
            layer0_group(c, io, L0, g, xsrc, cosd, sind, xs, bx, slot0, None if own else (flag[:, 0:1], b_flag))
        c.stack = c.root
    P.barrier()
    wr, b_wr = load_const(c, io["moe_w_router"].rearrange("(c p) n -> p c n", p=128), [128, 8, N_EXP], F32, "wr")
    with contextlib.ExitStack() as a1st:
        c.stack = a1st
        m01, b_m01 = load_const(c, io["m01"], [128, 4, 512], BF16, "m01", eng="pool")
        negm, b_negm = load_const(c, io["negm"], [128, 4, 512], BF16, "negm", eng="pool")
        M1, b_M1 = load_const(c, io["M1"], [128, 128], BF16, "M1", eng="pool")
        NO, b_NO = load_const(c, io["NO"], [128, 128], BF16, "NO", eng="pool")
        aconsts = (m01, b_m01, negm, b_negm, M1, b_M1, NO, b_NO)
        c.stack = c.root
        for g in range(TOK // GRP):
            layer1_group(c, io, g, xres[:, g * GT:(g + 1) * GT, :], b_xres[g * GT:(g + 1) * GT], aconsts, wr, b_wr,
                         attn_only=ROUTED)
    if ROUTED:
        P.barrier()
        iota_f, b_iota = load_const(c, io["iota_f"], [128, 128], F32, "iota_f")
        iotap, b_iotap = load_const(c, io["iotap"], [128, 8], F32, "iotap")
        Ulow, b_Ul = load_const(c, io["Ulow"], [128, 128], BF16, "Ulow", eng="pool")
        ones_b, b_ones = load_const(c, io["ones"], [128, 128], BF16, "ones_b", eng="pool")
        OH, b_OH = load_const(c, io["OH"], [8, 8 * 128], F32, "OH")
        mc = (iota_f, b_iota, iotap, b_iotap, Ulow, b_Ul, ones_b, b_ones, OH, b_OH, wr, b_wr)
        for g in range(TOK // GRP):
            moe_routed_group(c, io, g, xres[:, g * GT:(g + 1) * GT, :], b_xres[g * GT:(g + 1) * GT], mc)


def make_nc_fused():
    nc = bass.Bass("TRN2", target_bir_lowering=False)
    c = Ctx(nc)
    io = {"outs": []}

    def din(name, shape, dt=F32):
        io[name] = c.dram(name, shape, dt, "ExternalInput")
    din("x_own", [TOK, D]); din("x_prev", [TOK, D])
    din("cos_own", [128, TOK]); din("sin_own", [128, TOK]); din("cos_prev", [128, TOK]); din("sin_prev", [128, TOK])
    din("ret_w_in", [D, 6144]); din("ret_w_out", [2048, D])
    din("ffn_w_in", [D, 2 * D_FF]); din("ffn_w_out", [D_FF, D]); din("sb_w_kv", [D, 2048])
    din("sb_w_q", [D, D]); din("sb_w_out", [D, D])
    din("moe_w_router", [D, N_EXP]); din("moe_w_in", [N_EXP * D, 2 * D_EXP]); din("moe_w_out", [N_EXP * D_EXP, D])
    for l in (0, 1):
        for nm in ("g_mix", "b_mix", "g_ffn", "b_ffn"):
            din(f"{nm}{l}", [128, D])
    din("dtab", [128, RH, 128]); din("cn", [128, RH]); din("kdec", [128, RH]); din("ident", [128, 128]); din("flag", [128, 1])
    din("m01", [128, 4, 512]); din("negm", [128, 4, 512]); din("M1", [128, 128]); din("NO", [128, 128])
    din("iota_f", [128, 128]); din("iotap", [128, 8]); din("Ulow", [128, 128]); din("ones", [128, 128]); din("OH", [8, 8 * 128])
    io["KTs"] = nc.dram_tensor("KTs_i", [D, 2 * TOK], BF16).ap()
    io["Vs"] = nc.dram_tensor("Vs_i", [2 * TOK, D], BF16).ap()
    io["out"] = c.dram("out", [TOK, D], F32, "ExternalOutput")
    if DEBUG:
        io["dbg"] = c.dram("dbg", [1, 8], F32, "ExternalOutput")
        io["dbg2"] = c.dram("dbg2", [5, 128, 512], BF16, "ExternalOutput")
    with c.root:
        setup_common(c, io["ident"])
        build_fused(c, io)
        c.P.add("sp", lambda h: None, reads=io["outs"])
        cnt = c.P.emit(nc)
    return nc, cnt


def fused_inputs(inputs, core, shared):
    b, half = core // 2, core % 2
    x = inputs["x"]
    f = np.float32
    co, so = rope_tables(half * TOK, TOK)
    d = dict(shared)
    d.update({
        "x_own": np.ascontiguousarray(x[b, half * TOK:(half + 1) * TOK]),
        "x_prev": np.ascontiguousarray(x[b, 0:TOK]) if half == 1 else np.zeros((TOK, D), f),
        "cos_own": co, "sin_own": so,
        "flag": np.full((128, 1), float(half), f),
    })
    return d


def shared_inputs(inputs):
    f = np.float32
    dtab, cn, kdec, _ = retention_tables()
    cp, sp_ = rope_tables(0, TOK)
    m01, negm, m1, negones = attn_tables()
    bc = lambda v: np.ascontiguousarray(np.broadcast_to(np.asarray(v, f)[None, :], (128, D)))
    d = {
        "cos_prev": cp, "sin_prev": sp_,
        "ret_w_in": np.ascontiguousarray(inputs["ret_w_in"][0]),
        "ret_w_out": np.ascontiguousarray(inputs["ret_w_out"][0]),
        "ffn_w_in": np.ascontiguousarray(inputs["ffn_w_in"][0]),
        "ffn_w_out": np.ascontiguousarray(inputs["ffn_w_out"][0]),
        "sb_w_kv": np.ascontiguousarray(inputs["sb_w_kv"]),
        "sb_w_q": np.ascontiguousarray(inputs["sb_w_q"][0]), "sb_w_out": np.ascontiguousarray(inputs["sb_w_out"][0]),
        "moe_w_router": np.ascontiguousarray(inputs["moe_w_router"][0]),
        "moe_w_in": np.ascontiguousarray(inputs["moe_w_in"][0]).reshape(N_EXP * D, 2 * D_EXP),
        "moe_w_out": np.ascontiguousarray(inputs["moe_w_out"][0]).reshape(N_EXP * D_EXP, D),
        "dtab": dtab, "cn": cn, "kdec": kdec, "ident": np.eye(128, dtype=f),
        "m01": m01, "negm": negm, "M1": m1, "NO": negones,
    }
    d["iota_f"], d["iotap"], d["Ulow"], d["ones"], d["OH"] = moe_tables()
    for l in (0, 1):
        d[f"g_mix{l}"] = bc(inputs["ln_mix_g"][l]); d[f"b_mix{l}"] = bc(inputs["ln_mix_b"][l])
        d[f"g_ffn{l}"] = bc(inputs["ln_ffn_g"][l]); d[f"b_ffn{l}"] = bc(inputs["ln_ffn_b"][l])
    return d


_NC_CACHE = {}


def kernel(**inputs):
    inputs = {k: np.asarray(v) for k, v in inputs.items()}
    cores = list(range(8))
    if "f" not in _NC_CACHE:
        _NC_CACHE["f"] = make_nc_fused()[0]
    shared = shared_inputs(inputs)
    in_maps = [fused_inputs(inputs, cr, shared) for cr in cores]
    res = run_bass_kernel_spmd(_NC_CACHE["f"], in_maps, core_ids=cores).results
    out = np.stack([np.concatenate([res[2 * b]["out"], res[2 * b + 1]["out"]], axis=0) for b in range(4)], axis=0)
    return out.astype(np.float32)
```

```python
import contextlib
import numpy as np
import ml_dtypes
import concourse.bass as bass
import concourse.mybir as mybir
from concourse.bass_utils import run_bass_kernel_spmd

F32 = mybir.dt.float32
BF16 = mybir.dt.bfloat16
AF = mybir.ActivationFunctionType
ALU = mybir.AluOpType

ENGS = ("pe", "act", "dve", "pool", "sp")

D = 1024
TOK = 2048
GRP = 1024
GT = GRP // 128
ALPHA = float((2.0 * 2) ** 0.25)
EPS = 1e-5
RH = 4
D_FF = 2816
N_EXP = 8
D_EXP = 3584


class Buf:
    __slots__ = ("name", "w", "r", "key")

    def __init__(self, name="", key=None):
        self.name = name
        self.w = None
        self.r = []
        self.key = key


class Op:
    __slots__ = ("eng", "fn", "deps", "kind", "owner", "dval", "ms", "ndma")

    def __init__(self, eng, fn, deps, kind, owner=None, dval=0, ndma=0):
        self.eng = eng
        self.fn = fn
        self.deps = deps
        self.kind = kind
        self.owner = owner
        self.dval = dval
        self.ms = None
        self.ndma = ndma


class Prog:
    def __init__(self):
        self.ops = []
        self.by_eng = {e: [] for e in ENGS}
        self.ndsem = 0
        self.fence = set()
        self.dstate = {}
        self.regions = []
        self.cur_region = None
        self.regs = {}

    def bufs(self, n, name=""):
        return [Buf(f"{name}{i}") for i in range(n)]

    def _deps(self, eng, reads, writes, oid):
        deps = set()
        for b in reads:
            if b.w is not None:
                deps.add(b.w)
        for b in writes:
            if b.w is not None:
                deps.add(b.w)
            deps.update(b.r)
        for b in reads:
            b.r.append(oid)
        for b in writes:
            b.w = oid
            b.r = []
        deps |= self.fence
        deps.discard(oid)
        if eng == "pe":
            deps = {d for d in deps if not (self.ops[d].eng == "pe" and self.ops[d].kind == "c")}
        return deps

    def add(self, eng, fn, reads=(), writes=()):
        oid = len(self.ops)
        deps = self._deps(eng, list(reads), list(writes), oid)
        self.ops.append(Op(eng, fn, deps, "c"))
        self.by_eng[eng].append(oid)
        return oid

    def dma(self, eng, fn, owner, reads=(), writes=(), n=1):
        assert self.cur_region is None, "no DMA inside dynamic regions"
        oid = len(self.ops)
        deps = self._deps(eng, list(reads), list(writes), oid)
        key = owner.key if owner.key is not None else id(owner)
        stt = self.dstate.get(key)
        if stt is None:
            stt = [self.ndsem, 0, None]
            self.ndsem += 1
            self.dstate[key] = stt
            self._keep = getattr(self, "_keep", [])
            self._keep.append(owner)
        if stt[2] is not None:
            deps.add(stt[2])
        stt[1] += n
        stt[2] = oid
        self.ops.append(Op(eng, fn, deps, "d", owner=stt[0], dval=16 * stt[1], ndma=n))
        self.by_eng[eng].append(oid)
        return oid

    def region_begin(self, reg_key, thr):
        rid = len(self.regions)
        self.regions.append((reg_key, thr))
        for e in ENGS:
            oid = len(self.ops)
            self.ops.append(Op(e, rid, set(), "rb"))
            self.by_eng[e].append(oid)
        self.cur_region = rid

    def region_end(self):
        for e in ENGS:
            oid = len(self.ops)
            self.ops.append(Op(e, self.cur_region, set(), "re"))
            self.by_eng[e].append(oid)
        self.cur_region = None

    def reg_load(self, eng, reg_key, ap, reads):
        def fn(h, eng=eng):
            r = self.regs.setdefault((eng, reg_key), None)
            if r is None:
                r = h.alloc_register(f"r_{reg_key}_{eng}")
                self.regs[(eng, reg_key)] = r
            return h.reg_load(r, ap)
        return self.add(eng, fn, reads=reads)

    def barrier(self):
        f = set()
        for e in ENGS:
            for oid in reversed(self.by_eng[e]):
                if self.ops[oid].kind in ("c", "d"):
                    f.add(oid)
                    break
        f.update(v[2] for v in self.dstate.values() if v[2] is not None)
        self.fence = f

    def emit(self, nc):
        ops = self.ops
        needed = set()
        for op in ops:
            for d in op.deps:
                if ops[d].kind == "c":
                    needed.add(d)
        cnt = {e: 0 for e in ENGS}
        rinfo = {}
        for e in ENGS:
            cur = None
            for oid in self.by_eng[e]:
                op = ops[oid]
                if op.kind == "rb":
                    cur = [0, cnt[e], 0]
                    rinfo[(e, op.fn)] = cur
                elif op.kind == "re":
                    cur[2] = cnt[e] - cur[1]
                    cur = None
                else:
                    if cur is not None:
                        cur[0] += 1
                    if op.kind == "c" and oid in needed:
                        cnt[e] += 1
                        op.ms = cnt[e]
        with contextlib.ExitStack() as st:
            esem = {e: st.enter_context(nc.semaphore(f"s_{e}")) for e in ENGS}
            dsem = [st.enter_context(nc.semaphore(f"d_{i}")) for i in range(self.ndsem)]
            block = st.enter_context(nc.Block())

            def token(d):
                o = ops[d]
                if o.kind == "c":
                    return ("e", o.eng), o.ms
                return ("d", o.owner), o.dval

            def run(e, h):
                seen = {}
                snap = None
                guard = None
                for oid in self.by_eng[e]:
                    op = ops[oid]
                    if op.kind == "rb":
                        info = rinfo[(e, op.fn)]
                        if info[0] == 0:
                            continue
                        reg_key, thr = self.regions[op.fn]
                        r = self.regs[(e, reg_key)]
                        g1 = h.If_lt(r, thr + 1)
                        g1.__enter__()
                        if info[2] > 0:
                            if info[1] > 0:
                                h.wait_ge(esem[e], info[1])
                            h.sem_inc(esem[e], info[2])
                        g1.__exit__(None, None, None)
                        guard = h.Else()
                        guard.__enter__()
                        snap = dict(seen)
                        continue
                    if op.kind == "re":
                        if rinfo[(e, op.fn)][0] == 0:
                            continue
                        guard.__exit__(None, None, None)
                        guard = None
                        seen = snap
                        snap = None
                        continue
                    want = {}
                    for d in op.deps:
                        k, v = token(d)
                        if v > want.get(k, 0):
                            want[k] = v
                    for k, v in want.items():
                        if seen.get(k, 0) >= v:
                            continue
                        seen[k] = v
                        s = esem[k[1]] if k[0] == "e" else dsem[k[1]]
                        h.wait_ge(s, v)
                    r = op.fn(h)
                    if op.kind == "c":
                        if op.ms is not None:
                            r.then_inc(esem[e], 1)
                    else:
                        if not isinstance(r, (list, tuple)):
                            r = [r]
                        assert len(r) == op.ndma, (len(r), op.ndma)
                        for ins in r:
                            ins.then_inc(dsem[op.owner], 16)

            block.tensor(lambda h: run("pe", h))
            block.scalar(lambda h: run("act", h))
            block.vector(lambda h: run("dve", h))
            block.gpsimd(lambda h: run("pool", h))
            block.sync(lambda h: run("sp", h))
        return cnt


class Rot:
    def __init__(self, items):
        self.items = items
        self.i = 0

    def next(self):
        it = self.items[self.i % len(self.items)]
        self.i += 1
        return it


def f_mm(out, pairs):
    def fn(h):
        n = len(pairs)
        ins = None
        for i, (l, r) in enumerate(pairs):
            ins = h.matmul(out, lhsT=l, rhs=r, start=(i == 0), stop=(i == n - 1))
        return ins
    return fn


def f_mm_multi(groups):
    def fn(h):
        ins = None
        for out, pairs in groups:
            n = len(pairs)
            for i, (l, r) in enumerate(pairs):
                ins = h.matmul(out, lhsT=l, rhs=r, start=(i == 0), stop=(i == n - 1))
        return ins
    return fn


def f_tr(items, ident):
    def fn(h):
        ins = None
        for o, i in items:
            ins = h.transpose(out=o, in_=i, identity=ident)
        return ins
    return fn


def f_act(out, in_, func, bias=None, scale=None):
    kw = {}
    if bias is not None:
        kw["bias"] = bias
    if scale is not None:
        kw["scale"] = scale
    return lambda h: h.activation(out=out, in_=in_, func=func, **kw)


def f_tt(out, in0, in1, op):
    return lambda h: h.tensor_tensor(out=out, in0=in0, in1=in1, op=op)


def f_ts(out, in0, s1, s2, op0, op1=None):
    if op1 is None:
        return lambda h: h.tensor_scalar(out=out, in0=in0, scalar1=s1, scalar2=None, op0=op0)
    return lambda h: h.tensor_scalar(out=out, in0=in0, scalar1=s1, scalar2=s2, op0=op0, op1=op1)


def f_stt(out, in0, scalar, in1, op0, op1):
    return lambda h: h.scalar_tensor_tensor(out=out, in0=in0, scalar=scalar, in1=in1, op0=op0, op1=op1)


def f_copy(out, in_):
    return lambda h: h.tensor_copy(out=out, in_=in_)


def f_dma(pairs):
    return lambda h: [h.dma_start(out=o, in_=i) for o, i in pairs]


def f_memset(ap, v):
    return lambda h: h.memset(ap, v)


class Ctx:
    def __init__(self, nc):
        self.nc = nc
        self.P = Prog()
        self.root = contextlib.ExitStack()
        self.stack = self.root
        self.uid = 0

    def sb(self, shape, dt, name=None, stack=None):
        self.uid += 1
        name = (name or "t") + f"_{self.uid}"
        return (stack or self.stack).enter_context(self.nc.sbuf_tensor(name, shape, dt))

    def ps(self, shape, dt, name=None):
        self.uid += 1
        name = (name or "p") + f"_{self.uid}"
        return self.root.enter_context(self.nc.psum_tensor(name, shape, dt))

    def dram(self, name, shape, dt, kind):
        return self.nc.dram_tensor(name, shape, dt, kind=kind).ap()

    def rot(self, n, shape, dt, name, stack=None):
        return Rot([(self.sb(shape, dt, name, stack), Buf(name, key=f"{name}{i}")) for i in range(n)])


def setup_common(c, ident_d):
    P = c.P
    c.pm = Rot([(c.ps([128, 512], F32, "pm"), Buf("pm")) for _ in range(6)])
    c.pt = Rot([(c.ps([128, 1024], BF16, "pt"), Buf("pt")) for _ in range(2)])
    c.idf = c.sb([128, 128], F32, "idf")
    c.idb = c.sb([128, 128], BF16, "idb")
    c.b_idf = Buf("idf")
    c.b_idb = Buf("idb")
    P.dma("sp", f_dma([(c.idf[:], ident_d)]), c.b_idf, writes=[c.b_idf])
    P.add("dve", f_copy(c.idb[:], c.idf[:]), reads=[c.b_idf], writes=[c.b_idb])
    c.stage = None
    c.xb = c.rot(2, [128, 1024], BF16, "xb")
    c.stat = c.rot(8, [128, 16], F32, "stat")
    c.eps_t = c.sb([128, 2], F32, "eps")
    c.b_eps = Buf("eps")
    P.add("dve", f_memset(c.eps_t[:], EPS), writes=[c.b_eps])


def load_const(c, dram_ap, shape, dt=F32, name="const", eng="sp"):
    t = c.sb(shape, dt, name)
    b = Buf(name)
    full = t[:] if len(shape) == 2 else t[:]
    c.P.dma(eng, f_dma([(full, dram_ap)]), b, writes=[b])
    return t, b


def tile_to_T(c, src_ap, src_buf, dstT, dst_bufs, t, nchunk=8):
    P = c.P
    xb, b_xb = c.xb.next()
    P.add("act", f_act(xb[:, 0:nchunk * 128], src_ap, AF.Copy), reads=[src_buf], writes=[b_xb])
    pt, b_pt = c.pt.next()
    ptv = pt[:].rearrange("p (c n) -> p c n", n=128)
    P.add("pe", f_tr([(ptv[:, ch, :], xb[:, ch * 128:(ch + 1) * 128]) for ch in range(nchunk)], c.idb[:]),
          reads=[b_xb, c.b_idb], writes=[b_pt])
    P.add("dve", f_copy(dstT[:, 0:nchunk, t * 128:(t + 1) * 128], ptv[:, 0:nchunk, :]),
          reads=[b_pt], writes=[dst_bufs[t]])


def load_w(c, w_ap, slot_view, buf, nsplit=1):
    src = w_ap.rearrange("(c p) n -> p c n", p=128)
    kc = src.shape[1]
    if nsplit == 1 or kc < nsplit:
        c.P.dma("pool", f_dma([(slot_view, src)]), buf, writes=[buf])
    else:
        step = (kc + nsplit - 1) // nsplit
        pairs = [(slot_view[:, i:min(i + step, kc), :], src[:, i:min(i + step, kc), :]) for i in range(0, kc, step)]
        c.P.dma("pool", f_dma(pairs), buf, writes=[buf], n=len(pairs))


def ln_inplace(c, x_ap, x_buf, gam, bet, b_gb, tmp_rot):
    P = c.P
    st, b_st = c.stat.next()
    P.add("dve", lambda h: h.bn_stats(out=st[:, 0:6], in_=x_ap[:, 0:512]), reads=[x_buf], writes=[b_st])
    P.add("dve", lambda h: h.bn_stats(out=st[:, 6:12], in_=x_ap[:, 512:1024]), reads=[x_buf, b_st], writes=[b_st])
    mv, b_mv = c.stat.next()
    P.add("dve", lambda h: h.bn_aggr(out=mv[:, 0:2], in_=st[:, 0:12]), reads=[b_st], writes=[b_mv])
    P.add("act", f_act(mv[:, 2:3], mv[:, 1:2], AF.Sqrt, bias=c.eps_t[:, 0:1]), reads=[b_mv, c.b_eps], writes=[b_mv])
    P.add("dve", lambda h: h.reciprocal(out=mv[:, 2:3], in_=mv[:, 2:3]), reads=[b_mv], writes=[b_mv])
    P.add("dve", f_stt(mv[:, 3:4], mv[:, 0:1], -1.0, mv[:, 2:3], ALU.mult, ALU.mult), reads=[b_mv], writes=[b_mv])
    tmp, b_tmp = tmp_rot.next()
    P.add("act", f_act(tmp[:], x_ap, AF.Identity, bias=mv[:, 3:4], scale=mv[:, 2:3]),
          reads=[x_buf, b_mv], writes=[b_tmp])
    P.add("dve", f_tt(tmp[:], tmp[:], gam[:], ALU.mult), reads=[b_tmp, b_gb], writes=[b_tmp])
    P.add("dve", f_tt(x_ap, tmp[:], bet[:], ALU.add), reads=[b_tmp, b_gb], writes=[x_buf])


def ffn_unit(c, st, xT, xT_bufs, w1, dh, w2, yacc, yacc_bufs, subs, gate=None):
    P = c.P
    maxsub = max(n for _, n in subs)
    actT = c.sb([128, maxsub, GRP], BF16, "actT", st)
    act_bufs = [[Buf("act") for _ in range(GT)] for _ in range(maxsub)]
    w1s = c.rot(2, [128, 8, 2, 256], BF16, "w1s", st)
    w2s = c.rot(2, [128, maxsub, 512], BF16, "w2s", st)
    sg = c.rot(2, [128, 512], BF16, "sg", st)
    for (j0, nj) in subs:
        assert nj % 2 == 0
        for blk in range(nj // 2):
            jj = j0 + blk * 2
            ws, b_ws = w1s.next()
            srcg = w1[:, jj * 128: jj * 128 + 256].rearrange("(c p) n -> p c n", p=128)
            srcu = w1[:, dh + jj * 128: dh + jj * 128 + 256].rearrange("(c p) n -> p c n", p=128)
            P.dma("pool", f_dma([(ws[:, :, 0, :], srcg), (ws[:, :, 1, :], srcu)]), b_ws, writes=[b_ws], n=2)
            for ci in range(2):
                jl = blk * 2 + ci
                for tg in range(GRP // 512):
                    pg, b_pg = c.pm.next()
                    pu, b_pu = c.pm.next()
                    tb = xT_bufs[tg * 4:(tg + 1) * 4]
                    P.add("pe", f_mm_multi([
                        (pg[:], [(ws[:, k, 0, ci * 128:(ci + 1) * 128], xT[:, k, tg * 512:(tg + 1) * 512]) for k in range(8)]),
                        (pu[:], [(ws[:, k, 1, ci * 128:(ci + 1) * 128], xT[:, k, tg * 512:(tg + 1) * 512]) for k in range(8)]),
                    ]), reads=[b_ws] + tb, writes=[b_pg, b_pu])
                    s, b_s = sg.next()
                    P.add("act", f_act(s[:], pg[:], AF.Silu), reads=[b_pg], writes=[b_s])
                    P.add("dve", f_tt(actT[:, jl, tg * 512:(tg + 1) * 512], s[:], pu[:], ALU.mult),
                          reads=[b_s, b_pu], writes=act_bufs[jl][tg * 4:(tg + 1) * 4])
        for half in range(2):
            w2v, b_w2 = w2s.next()
            load_w(c, w2[j0 * 128:(j0 + nj) * 128, half * 512:(half + 1) * 512], w2v[:, 0:nj, :], b_w2, nsplit=2)
            for t in range(GT):
                po, b_po = c.pm.next()
                P.add("pe", f_mm(po[:], [(actT[:, j, t * 128:(t + 1) * 128], w2v[:, j, :]) for j in range(nj)]),
                      reads=[b_w2] + [act_bufs[j][t] for j in range(nj)], writes=[b_po])
                ya = yacc[:, t, half * 512:(half + 1) * 512]
                if gate is None:
                    P.add("dve", f_tt(ya, po[:], ya, ALU.add), reads=[b_po, yacc_bufs[t]], writes=[yacc_bufs[t]])
                else:
                    gfn, b_g = gate
                    P.add("dve", f_stt(ya, po[:], gfn(t), ya, ALU.mult, ALU.add),
                          reads=[b_po, yacc_bufs[t], b_g], writes=[yacc_bufs[t]])


def rope_tables(pos0, n):
    d = 256
    inv = (1.0 / (np.float32(10000.0) ** (np.arange(0, d, 2, dtype=np.float32) / np.float32(d)))).astype(np.float32)
    pos = np.arange(pos0, pos0 + n, dtype=np.float32)
    ang = (pos[:, None] * inv[None, :]).astype(np.float32)
    return np.ascontiguousarray(np.cos(ang).astype(np.float32).T), np.ascontiguousarray(np.sin(ang).astype(np.float32).T)


def retention_tables():
    gam = 1.0 - 2.0 ** (-5.0 - np.arange(RH, dtype=np.float64))
    n = np.arange(128)
    m = np.arange(128)
    dtab = np.zeros((128, RH, 128), np.float64)
    cn = np.zeros((128, RH), np.float64)
    kdec = np.zeros((128, RH), np.float64)
    for h in range(RH):
        g = gam[h]
        M, N = np.meshgrid(m, n, indexing="ij")
        same = (M // 64) == (N // 64)
        cross = (M < 64) & (N >= 64)
        dm = np.where(same, g ** np.abs(N - M), np.where(cross, g ** (N - M).clip(0), 0.0))
        cnh = g ** (n + 1.0)
        dtab[:, h, :] = dm / cnh[None, :] / 16.0
        cn[:, h] = cnh
        kdec[:, h] = g ** (127.0 - m) / 16.0
    cd = [float(g ** 128) for g in gam]
    return dtab.astype(np.float32), cn.astype(np.float32), kdec.astype(np.float32), cd


def load_ln(c, st, gd, bd, name):
    g = c.sb([128, 1024], F32, name + "g", st)
    b = c.sb([128, 1024], F32, name + "b", st)
    buf = Buf(name, key=name)
    c.P.dma("sp", f_dma([(g[:], gd), (b[:], bd)]), buf, writes=[buf], n=2)
    return g, b, buf


def store(c, io, dst_ap, src_ap, src_buf):
    ob = Buf("out")
    io["outs"].append(ob)
    c.P.dma("sp", f_dma([(dst_ap, src_ap)]), src_buf, reads=[src_buf], writes=[ob])


def ret_heads(c, rst, io, mode, g, xsrc, cosd, sind, S32, Sb, b_S32, b_Sb, consts, yT, b_yT, scratch=None):
    P = c.P
    own = mode == "own"
    dtab, b_dtab, cn, b_cn, kdec, b_kdec, cd = consts
    w_in = io["ret_w_in"]
    if scratch is not None:
        sb16 = scratch.bitcast(BF16)
        xT = sb16[:, 0:8192].rearrange("p (c n) -> p c n", c=8)
        qT = sb16[:, 8192:10240].rearrange("p (c n) -> p c n", c=2)
        kT = sb16[:, 10240:12288].rearrange("p (c n) -> p c n", c=2)
        cosT = scratch[:, 6144:7168]
        sinT = scratch[:, 7168:8192]
    else:
        xT = c.sb([128, 8, GRP], BF16, "xT", rst)
        cosT = c.sb([128, GRP], F32, "cosT", rst)
        sinT = c.sb([128, GRP], F32, "sinT", rst)
        qT = c.sb([128, 2, GRP], BF16, "qT", rst)
        kT = c.sb([128, 2, GRP], BF16, "kT", rst)
    b_xT = [Buf("xT") for _ in range(GT)]
    b_cs = Buf("cs", key="cs")
    P.dma("sp", f_dma([(cosT[:], cosd), (sinT[:], sind)]), b_cs, writes=[b_cs], n=2)
    for t in range(GT):
        stg, b_stg = c.stage.next()
        P.dma("sp", f_dma([(stg[:], xsrc[t * 128:(t + 1) * 128, :])]), b_stg, writes=[b_stg])
        tile_to_T(c, stg[:], b_stg, xT, b_xT, t)
    kd = c.sb([128, GT, 256], BF16, "kd", rst)
    Vh = c.sb([128, GT, 512], BF16, "Vh", rst)
    Gh = c.sb([128, GT, 512], BF16, "Gh", rst) if own else None
    wqk = c.rot(2, [128, 8, 256], BF16, "wqk", rst)
    wvg = c.rot(2, [128, 8, 512], BF16, "wvg", rst)
    rtmp = c.rot(4, [128, 512], F32, "rtmp", rst)
    PTr = c.rot(2, [128, 128], BF16, "PT", rst)
    osb = c.rot(3, [128, 512], F32, "osb", rst)
    ysb = c.rot(2, [128, 512], BF16, "ysb", rst)
    for h in range(RH):
        b_qT = [Buf("qT") for _ in range(GT)]
        b_kT = [Buf("kT") for _ in range(GT)]
        b_kd = [Buf("kd") for _ in range(GT)]
        b_V = [Buf("V") for _ in range(GT)]
        b_G = [Buf("G") for _ in range(GT)]
        for which in (("q", "k") if own else ("k",)):
            col0 = (0 if which == "q" else 1024) + h * 256
            ws, b_ws = wqk.next()
            load_w(c, w_in[:, col0:col0 + 256], ws[:], b_ws)
            dst = qT if which == "q" else kT
            b_dst = b_qT if which == "q" else b_kT
            for tg in range(GRP // 512):
                pa, b_pa = c.pm.next()
                pb, b_pb = c.pm.next()
                tb = b_xT[tg * 4:(tg + 1) * 4]
                tok = slice(tg * 512, (tg + 1) * 512)
                P.add("pe", f_mm_multi([
                    (pa[:], [(ws[:, k, 0:128], xT[:, k, tok]) for k in range(8)]),
                    (pb[:], [(ws[:, k, 128:256], xT[:, k, tok]) for k in range(8)]),
                ]), reads=[b_ws] + tb, writes=[b_pa, b_pb])
                t1, b_t1 = rtmp.next()
                t2, b_t2 = rtmp.next()
                P.add("dve", f_tt(t1[:], pa[:], cosT[:, tok], ALU.mult), reads=[b_pa, b_cs], writes=[b_t1])
                P.add("dve", f_tt(t2[:], pb[:], sinT[:, tok], ALU.mult), reads=[b_pb, b_cs], writes=[b_t2])
                P.add("dve", f_tt(dst[:, 0, tok], t1[:], t2[:], ALU.subtract), reads=[b_t1, b_t2],
                      writes=b_dst[tg * 4:(tg + 1) * 4])
                t3, b_t3 = rtmp.next()
                t4, b_t4 = rtmp.next()
                P.add("dve", f_tt(t3[:], pa[:], sinT[:, tok], ALU.mult), reads=[b_pa, b_cs], writes=[b_t3])
                P.add("dve", f_tt(t4[:], pb[:], cosT[:, tok], ALU.mult), reads=[b_pb, b_cs], writes=[b_t4])
                P.add("dve", f_tt(dst[:, 1, tok], t3[:], t4[:], ALU.add), reads=[b_t3, b_t4] + b_dst[tg * 4:(tg + 1) * 4],
                      writes=b_dst[tg * 4:(tg + 1) * 4])
        for t in range(GT):
            pt, b_pt = c.pt.next()
            ptv = pt[:].rearrange("p (c n) -> p c n", n=128)
            P.add("pe", f_tr([(ptv[:, ch, :], kT[:, ch, t * 128:(t + 1) * 128]) for ch in range(2)], c.idb[:]),
                  reads=[b_kT[t], c.b_idb], writes=[b_pt])
            P.add("act", f_act(kd[:, t, :], pt[:, 0:256], AF.Identity, scale=kdec[:, h:h + 1]),
                  reads=[b_pt, b_kdec], writes=[b_kd[t]])
        for which in (("v", "g") if own else ("v",)):
            col0 = (2048 if which == "v" else 4096) + h * 512
            ws, b_ws = wvg.next()
            load_w(c, w_in[:, col0:col0 + 512], ws[:], b_ws, nsplit=2)
            for t in range(GT):
                pv, b_pv = c.pm.next()
                P.add("pe", f_mm(pv[:], [(xT[:, k, t * 128:(t + 1) * 128], ws[:, k, :]) for k in range(8)]),
                      reads=[b_ws, b_xT[t]], writes=[b_pv])
                if which == "v":
                    P.add("act", f_act(Vh[:, t, :], pv[:], AF.Copy), reads=[b_pv], writes=[b_V[t]])
                else:
                    P.add("act", f_act(Gh[:, t, :], pv[:], AF.Silu), reads=[b_pv], writes=[b_G[t]])
        pend = None

        def epilogue(o, b_o, t):
            tk = slice(t * 128, (t + 1) * 128)
            st, b_st = c.stat.next()
            P.add("dve", lambda hh, st=st, o=o: hh.bn_stats(out=st[:, 0:6], in_=o[:]), reads=[b_o], writes=[b_st])
            P.add("dve", lambda hh, st=st: hh.bn_aggr(out=st[:, 8:10], in_=st[:, 0:6]), reads=[b_st], writes=[b_st])
            P.add("act", f_act(st[:, 10:11], st[:, 9:10], AF.Sqrt, bias=c.eps_t[:, 0:1]), reads=[b_st, c.b_eps], writes=[b_st])
            P.add("dve", lambda hh, st=st: hh.reciprocal(out=st[:, 10:11], in_=st[:, 10:11]), reads=[b_st], writes=[b_st])
            P.add("dve", f_stt(st[:, 11:12], st[:, 8:9], -1.0, st[:, 10:11], ALU.mult, ALU.mult), reads=[b_st], writes=[b_st])
            P.add("act", f_act(o[:], o[:], AF.Identity, bias=st[:, 11:12], scale=st[:, 10:11]),
                  reads=[b_o, b_st], writes=[b_o])
            y, b_y = ysb.next()
            P.add("dve", f_tt(y[:], o[:], Gh[:, t, :], ALU.mult), reads=[b_o, b_G[t]], writes=[b_y])
            pt, b_pt = c.pt.next()
            ptv = pt[:].rearrange("p (c n) -> p c n", n=128)
            P.add("pe", f_tr([(ptv[:, ch, :], y[:, ch * 128:(ch + 1) * 128]) for ch in range(4)], c.idb[:]),
                  reads=[b_y, c.b_idb], writes=[b_pt])
            P.add("act", f_act(yT[:, h * 4:(h + 1) * 4, tk], ptv[:, 0:4, :], AF.Copy), reads=[b_pt], writes=[b_yT[h][t]])

        for t in range(GT):
            tk = slice(t * 128, (t + 1) * 128)
            p0, b_p0 = c.pm.next()
            p1, b_p1 = c.pm.next()
            P.add("pe", f_mm_multi([(p0[:], [(kd[:, t, 0:128], Vh[:, t, :])]), (p1[:], [(kd[:, t, 128:256], Vh[:, t, :])])]),
                  reads=[b_kd[t], b_V[t]], writes=[b_p0, b_p1])
            if own:
                pS, b_pS = c.pm.next()
                P.add("pe", f_mm(pS[:, 0:128], [(kT[:, ch, tk], qT[:, ch, tk]) for ch in range(2)]),
                      reads=[b_kT[t], b_qT[t]], writes=[b_pS])
                PT, b_PT = PTr.next()
                P.add("dve", f_tt(PT[:], pS[:, 0:128], dtab[:, h, :], ALU.mult), reads=[b_pS, b_dtab], writes=[b_PT])
                po, b_po = c.pm.next()
                P.add("pe", f_mm(po[:], [(PT[:], Vh[:, t, :]), (qT[:, 0, tk], Sb[:, h, 0, :]), (qT[:, 1, tk], Sb[:, h, 1, :])]),
                      reads=[b_PT, b_V[t], b_qT[t], b_Sb[h]], writes=[b_po])
                o, b_o = osb.next()
                P.add("act", f_act(o[:], po[:], AF.Identity, scale=cn[:, h:h + 1]), reads=[b_po, b_cn], writes=[b_o])
            P.add("dve", f_stt(S32[:, h, 0, :], S32[:, h, 0, :], cd[h], p0[:], ALU.mult, ALU.add),
                  reads=[b_p0, b_S32[h]], writes=[b_S32[h]])
            P.add("dve", f_stt(S32[:, h, 1, :], S32[:, h, 1, :], cd[h], p1[:], ALU.mult, ALU.add),
                  reads=[b_p1, b_S32[h]], writes=[b_S32[h]])
            P.add("act", f_act(Sb[:, h, :, :], S32[:, h, :, :], AF.Copy), reads=[b_S32[h]], writes=[b_Sb[h]])
            if own:
                if pend is not None:
                    epilogue(*pend)
                pend = (o, b_o, t)
        if own and pend is not None:
            epilogue(*pend)


def build_layer0(c, io):
    P = c.P
    _, _, _, cd = retention_tables()
    dtab, b_dtab = load_const(c, io["dtab"], [128, RH, 128], F32, "dtab")
    cn, b_cn = load_const(c, io["cn"], [128, RH], F32, "cn")
    kdec, b_kdec = load_const(c, io["kdec"], [128, RH], F32, "kdec")
    consts = (dtab, b_dtab, cn, b_cn, kdec, b_kdec, cd)
    S32 = c.sb([128, RH, 2, 512], F32, "S32")
    Sb = c.sb([128, RH, 2, 512], BF16, "Sb")
    b_S32 = [Buf("S32") for _ in range(RH)]
    b_Sb = [Buf("Sb") for _ in range(RH)]
    for h in range(RH):
        P.add("dve", f_memset(S32[:, h, :, :], 0.0), writes=[b_S32[h]])
        P.add("dve", f_memset(Sb[:, h, :, :], 0.0), writes=[b_Sb[h]])
    w_out = io["ret_w_out"]

    passes = [("prev", 0), ("prev", 1), ("own", 0), ("own", 1)]
    for mode, g in passes:
        own = mode == "own"
        xsrc = (io["x_own"] if own else io["x_prev"])[g * GRP:(g + 1) * GRP, :]
        cosd = (io["cos_own"] if own else io["cos_prev"])[:, g * GRP:(g + 1) * GRP]
        sind = (io["sin_own"] if own else io["sin_prev"])[:, g * GRP:(g + 1) * GRP]
        P.barrier()
        if not own:
            with contextlib.ExitStack() as rst:
                ret_heads(c, rst, io, mode, g, xsrc, cosd, sind, S32, Sb, b_S32, b_Sb, consts, None, None)
            continue
        with contextlib.ExitStack() as gst:
            x1 = c.sb([128, GT, 1024], F32, "x1", gst)
            b_x1 = [Buf("x1") for _ in range(GT)]
            with contextlib.ExitStack() as yst:
                yT = c.sb([128, 16, GRP], BF16, "yT", yst)
                b_yT = [[Buf("yT") for _ in range(GT)] for _ in range(RH)]
                with contextlib.ExitStack() as rst:
                    ret_heads(c, rst, io, mode, g, xsrc, cosd, sind, S32, Sb, b_S32, b_Sb, consts, yT, b_yT)
                P.barrier()
                with contextlib.ExitStack() as mst:
                    gam, bet, b_gb = load_ln(c, mst, io["g_mix"], io["b_mix"], "lnmix")
                    tmpf = c.rot(2, [128, 1024], F32, "tmpf", mst)
                    wos = c.rot(2, [128, 16, 512], BF16, "wos", mst)
                    xst = c.rot(2, [128, 512], F32, "xst", mst)
                    for half in range(2):
                        wo, b_wo = wos.next()
                        load_w(c, w_out[:, half * 512:(half + 1) * 512], wo[:], b_wo, nsplit=2)
                        for t in range(GT):
                            pm_, b_pm = c.pm.next()
                            P.add("pe", f_mm(pm_[:], [(yT[:, ch, t * 128:(t + 1) * 128], wo[:, ch, :]) for ch in range(16)]),
                                  reads=[b_wo] + [b_yT[h][t] for h in range(RH)], writes=[b_pm])
                            xs, b_xs = xst.next()
                            P.dma("sp", f_dma([(xs[:], xsrc[t * 128:(t + 1) * 128, half * 512:(half + 1) * 512])]), b_xs, writes=[b_xs])
                            P.add("dve", f_stt(x1[:, t, half * 512:(half + 1) * 512], xs[:], ALPHA, pm_[:], ALU.mult, ALU.add),
                                  reads=[b_xs, b_pm], writes=[b_x1[t]])
                    for t in range(GT):
                        ln_inplace(c, x1[:, t, :], b_x1[t], gam, bet, b_gb, tmpf)
            P.barrier()
            with contextlib.ExitStack() as fst:
                gam, bet, b_gb = load_ln(c, fst, io["g_ffn"], io["b_ffn"], "lnffn")
                tmpf = c.rot(2, [128, 1024], F32, "tmpf", fst)
                x1T = c.sb([128, 8, GRP], BF16, "x1T", fst)
                b_x1T = [Buf("x1T") for _ in range(GT)]
                yacc = c.sb([128, GT, 1024], F32, "yacc", fst)
                b_ya = [Buf("yacc", key=f"ya{t}") for t in range(GT)]
                for t in range(GT):
                    tile_to_T(c, x1[:, t, :], b_x1[t], x1T, b_x1T, t)
                    P.add("act", f_act(yacc[:, t, :], x1[:, t, :], AF.Copy, scale=ALPHA), reads=[b_x1[t]], writes=[b_ya[t]])
                with contextlib.ExitStack() as ust:
                    ffn_unit(c, ust, x1T, b_x1T, io["ffn_w_in"], D_FF, io["ffn_w_out"], yacc, b_ya, [(0, 12), (12, 10)])
                for t in range(GT):
                    ln_inplace(c, yacc[:, t, :], b_ya[t], gam, bet, b_gb, tmpf)
                    r0 = g * GRP + t * 128
                    store(c, io, io["x2"][r0:r0 + 128, :], yacc[:, t, :], b_ya[t])
                P.barrier()
                with contextlib.ExitStack() as kst:
                    x2T = x1T
                    b_x2T = [Buf("x2T") for _ in range(GT)]
                    for t in range(GT):
                        tile_to_T(c, yacc[:, t, :], b_ya[t], x2T, b_x2T, t)
                    wkv = c.rot(2, [128, 8, 512], BF16, "wkv", kst)
                    ksb = c.rot(3, [128, 512], BF16, "ksb", kst)
                    for cb in range(2):
                        ws, b_ws = wkv.next()
                        load_w(c, io["sb_w_kv"][:, cb * 512:(cb + 1) * 512], ws[:], b_ws, nsplit=2)
                        for ch in range(4):
                            for tg in range(GRP // 512):
                                pk, b_pk = c.pm.next()
                                tok = slice(tg * 512, (tg + 1) * 512)
                                P.add("pe", f_mm(pk[:], [(ws[:, k, ch * 128:(ch + 1) * 128], x2T[:, k, tok]) for k in range(8)]),
                                      reads=[b_ws] + b_x2T[tg * 4:(tg + 1) * 4], writes=[b_pk])
                                ks, b_ks = ksb.next()
                                P.add("act", f_act(ks[:], pk[:], AF.Copy), reads=[b_pk], writes=[b_ks])
                                f0 = (cb * 4 + ch) * 128
                                t0 = g * GRP + tg * 512
                                store(c, io, io["KT"][f0:f0 + 128, t0:t0 + 512], ks[:], b_ks)
                    for cb in range(2):
                        ws, b_ws = wkv.next()
                        load_w(c, io["sb_w_kv"][:, 1024 + cb * 512:1024 + (cb + 1) * 512], ws[:], b_ws, nsplit=2)
                        for t in range(GT):
                            pv, b_pv = c.pm.next()
                            P.add("pe", f_mm(pv[:], [(x2T[:, k, t * 128:(t + 1) * 128], ws[:, k, :]) for k in range(8)]),
                                  reads=[b_ws, b_x2T[t]], writes=[b_pv])
                            ks, b_ks = ksb.next()
                            P.add("act", f_act(ks[:], pv[:], AF.Copy), reads=[b_pv], writes=[b_ks])
                            r0 = g * GRP + t * 128
                            store(c, io, io["V"][r0:r0 + 128, cb * 512:(cb + 1) * 512], ks[:], b_ks)


def make_nc_layer0():
    nc = bass.Bass("TRN2", target_bir_lowering=False)
    c = Ctx(nc)
    io = {"outs": []}

    def din(name, shape, dt=F32):
        io[name] = c.dram(name, shape, dt, "ExternalInput")
    din("x_own", [TOK, D]); din("x_prev", [TOK, D])
    din("cos_own", [128, TOK]); din("sin_own", [128, TOK]); din("cos_prev", [128, TOK]); din("sin_prev", [128, TOK])
    din("ret_w_in", [D, 6144]); din("ret_w_out", [2048, D])
    din("ffn_w_in", [D, 2 * D_FF]); din("ffn_w_out", [D_FF, D]); din("sb_w_kv", [D, 2048])
    din("g_mix", [128, D]); din("b_mix", [128, D]); din("g_ffn", [128, D]); din("b_ffn", [128, D])
    din("dtab", [128, RH, 128]); din("cn", [128, RH]); din("kdec", [128, RH]); din("ident", [128, 128])
    io["x2"] = c.dram("x2", [TOK, D], F32, "ExternalOutput")
    io["KT"] = c.dram("KT", [D, TOK], BF16, "ExternalOutput")
    io["V"] = c.dram("V", [TOK, D], BF16, "ExternalOutput")
    with c.root:
        setup_common(c, io["ident"])
        build_layer0(c, io)
        c.P.add("sp", lambda h: None, reads=io["outs"])
        cnt = c.P.emit(nc)
    return nc, cnt


def layer0_inputs(inputs, core):
    b, half = core // 2, core % 2
    x = inputs["x"]
    dtab, cn, kdec, _ = retention_tables()
    co, so = rope_tables(half * TOK, TOK)
    cp, sp_ = rope_tables(0, TOK)
    f = np.float32
    bc = lambda v: np.ascontiguousarray(np.broadcast_to(np.asarray(v, f)[None, :], (128, D)))
    return {
        "x_own": np.ascontiguousarray(x[b, half * TOK:(half + 1) * TOK]),
        "x_prev": np.ascontiguousarray(x[b, 0:TOK]) if half == 1 else np.zeros((TOK, D), f),
        "cos_own": co, "sin_own": so, "cos_prev": cp, "sin_prev": sp_,
        "ret_w_in": np.ascontiguousarray(inputs["ret_w_in"][0]),
        "ret_w_out": np.ascontiguousarray(inputs["ret_w_out"][0]),
        "ffn_w_in": np.ascontiguousarray(inputs["ffn_w_in"][0]),
        "ffn_w_out": np.ascontiguousarray(inputs["ffn_w_out"][0]),
        "sb_w_kv": np.ascontiguousarray(inputs["sb_w_kv"]),
        "g_mix": bc(inputs["ln_mix_g"][0]), "b_mix": bc(inputs["ln_mix_b"][0]),
        "g_ffn": bc(inputs["ln_ffn_g"][0]), "b_ffn": bc(inputs["ln_ffn_b"][0]),
        "dtab": dtab, "cn": cn, "kdec": kdec, "ident": np.eye(128, dtype=f),
    }


NEG = -30000.0


def attn_tables():
    s = np.arange(128)[:, None]
    t = np.arange(512)[None, :]
    m01 = np.zeros((128, 4, 512), np.float32)
    for di in range(4):
        m01[:, di, :] = (di * 128 + s < t).astype(np.float32)
    negm = (1.0 - m01) * NEG
    j = np.arange(128)[:, None]
    ss = np.arange(128)[None, :]
    m1 = -(j >= ss).astype(np.float32)
    negones = -np.ones((128, 128), np.float32)
    return m01, negm, m1, negones


def f_mm1(out, l, r, start, stop):
    return lambda h: h.matmul(out, lhsT=l, rhs=r, start=start, stop=stop)


DEBUG = False
ATT_THR = 112.0
ATT_FREE = 6


def attention_group(c, ast, io, g, oT, b_oT, x2T, b_x2T, consts):
    P = c.P
    m01, b_m01, negm, b_negm, M1, b_M1, NO, b_NO = consts
    nslot_blocks = 16 + g * 8 + 8
    KTr = c.rot(2, [128, 4096], BF16, "KTh", ast)
    Vr = c.rot(2, [128, 32, 128], BF16, "Vhp", ast)
    QTr = c.rot(2, [128, GRP], BF16, "QTh", ast)
    wqr = c.rot(2, [128, 8, 128], BF16, "wq", ast)
    esb = c.rot(4, [128, 512], F32, "esb", ast)
    spb = c.rot(16, [128, 512], BF16, "spb", ast)
    wbf = c.rot(8, [128, 512], BF16, "wbf", ast)
    Rbf = [c.rot(2, [128, 512], BF16, f"Rbf{i}", ast) for i in range(4)]
    rmn = c.rot(2, [128, 512], BF16, "rmn", ast)
    zt = c.sb([128, 128], BF16, "zt", ast)
    b_zt = Buf("zt")
    P.add("dve", f_memset(zt[:], 0.0), writes=[b_zt])
    chk = c.sb([1, 8], F32, "chk", ast)
    chk_i = c.sb([1, 8], I32, "chk_i", ast)
    b_chk = Buf("chk")
    pmi = c.pm.items
    accs = [c.pt.items[0], c.pt.items[1], pmi[4], pmi[5]]
    pma = Rot(pmi[0:4])
    NQ = GRP // 512
    for hp in range(8):
        KT, b_KT = KTr.next()
        V, b_V = Vr.next()
        ns = nslot_blocks * 128
        P.dma("sp", f_dma([(KT[:, 0:ns], io["KTs"][hp * 128:(hp + 1) * 128, 0:ns])]), b_KT, writes=[b_KT])
        P.dma("sp", f_dma([(V[:, 0:nslot_blocks, :],
                            io["Vs"][0:ns, hp * 128:(hp + 1) * 128].rearrange("(k p) n -> p k n", p=128))]),
              b_V, writes=[b_V])
        wq, b_wq = wqr.next()
        load_w(c, io["sb_w_q"][:, hp * 128:(hp + 1) * 128], wq[:], b_wq)
        QT, b_QT = QTr.next()
        for tg in range(NQ):
            pq, b_pq = c.pm.next()
            tok = slice(tg * 512, (tg + 1) * 512)
            P.add("pe", f_mm(pq[:], [(wq[:, k, :], x2T[:, k, tok]) for k in range(8)]),
                  reads=[b_wq] + b_x2T[tg * 4:(tg + 1) * 4], writes=[b_pq])
            P.add("act", f_act(QT[:, tok], pq[:], AF.Copy, scale=0.125), reads=[b_pq], writes=[b_QT])
        streams = [(qt, hd) for qt in range(NQ) for hd in range(2)]
        nkbs = {qt: 16 + g * 8 + qt * 4 + 4 for qt in range(NQ)}
        nmax = max(nkbs.values())
        Rcur = {}

        SP = {}
        WS = {}

        def live_of(i):
            return [(k, qt, hd) for k, (qt, hd) in enumerate(streams) if i < nkbs[qt]]

        def s1(i):
            sps = {}
            for (k, qt, hd) in live_of(i):
                kb = nkbs[qt] - 1 - i
                pb = 64 * hd
                tok = slice(qt * 512, (qt + 1) * 512)
                di = kb - (nkbs[qt] - 4)
                pz, b_pz = pma.next()
                P.add("pe", f_mm(pz[:], [(KT[pb:pb + 64, kb * 128:(kb + 1) * 128], QT[pb:pb + 64, tok])]),
                      reads=[b_KT, b_QT], writes=[b_pz])
                e, b_e = esb.next()
                P.add("act", f_act(e[:], pz[:], AF.Exp), reads=[b_pz], writes=[b_e])
                sp_, b_sp = spb.next()
                P.add("act", f_act(sp_[:], e[:], AF.Ln, bias=1.0), reads=[b_e], writes=[b_sp])
                if di >= 0:
                    P.add("dve", f_tt(sp_[:], sp_[:], m01[:, di, :], ALU.mult), reads=[b_sp, b_m01], writes=[b_sp])
                sps[k] = (sp_, b_sp)
            SP[i] = sps

        def s2(i):
            sps = SP.pop(i)
            ws = {}
            for (k, qt, hd) in live_of(i):
                kb = nkbs[qt] - 1 - i
                pb = 64 * hd
                tok = slice(qt * 512, (qt + 1) * 512)
                di = kb - (nkbs[qt] - 4)
                sp_, b_sp = sps[k]
                pE, b_pE = pma.next()
                pairs = [(KT[pb:pb + 64, kb * 128:(kb + 1) * 128], QT[pb:pb + 64, tok]), (M1[:], sp_[:])]
                reads = [b_KT, b_QT, b_M1, b_sp]
                if i > 0:
                    rb, b_rb = Rcur[k]
                    pairs.append((NO[:], rb[:]))
                    reads += [b_NO, b_rb]
                if di >= 0:
                    pairs.append((c.idb[:], negm[:, di, :]))
                    reads += [c.b_idb, b_negm]
                P.add("pe", f_mm(pE[:], pairs), reads=reads, writes=[b_pE])
                w, b_w = wbf.next()
                P.add("act", f_act(w[:], pE[:], AF.Exp), reads=[b_pE], writes=[b_w])
                ws[k] = (w, b_w)
                if i == 0:
                    Rcur[k] = (sp_, b_sp)
                else:
                    ro, b_ro = Rcur[k]
                    rb, b_rb = Rbf[k].next()
                    P.add("dve", f_tt(rb[:], ro[:], sp_[:], ALU.add), reads=[b_sp, b_ro], writes=[b_rb])
                    Rcur[k] = (rb, b_rb)
            WS[i] = ws

        def s3(i):
            ws = WS.pop(i)
            for (k, qt, hd) in live_of(i):
                kb = nkbs[qt] - 1 - i
                w, b_w = ws[k]
                acc_t, b_acc = accs[k]
                acc = acc_t[:].bitcast(F32) if k < 2 else acc_t[:]
                P.add("pe", f_mm1(acc, V[:, kb, :], w[:], i == 0, False), reads=[b_V, b_w], writes=[b_acc])

        def block(i):
            s1(i)
            s2(i)
            s3(i)

        def checkpoint():
            r0, b_r0 = Rcur[0]
            cur, b_cur = r0, b_r0
            for k in range(1, len(streams)):
                rk, b_rk = Rcur[k]
                rm, b_rm = rmn.next()
                P.add("dve", f_tt(rm[:], cur[:], rk[:], ALU.min), reads=[b_cur, b_rk], writes=[b_rm])
                cur, b_cur = rm, b_rm
            pc, b_pc = pma.next()
            P.add("pe", f_mm(pc[0:1, :], [(NO[:, 0:1], cur[:])]), reads=[b_NO, b_cur], writes=[b_pc])
            P.add("dve", lambda h: h.reduce_max(out=chk[0:1, 0:1], in_=pc[0:1, :], axis=mybir.AxisListType.X),
                  reads=[b_pc, b_chk], writes=[b_chk])
            P.add("dve", f_ts(chk[0:1, 1:2], chk[0:1, 0:1], 1000.0 + ATT_THR, 0.0, ALU.add, ALU.max), reads=[b_chk], writes=[b_chk])
            P.add("dve", f_copy(chk_i[0:1, 0:1], chk[0:1, 1:2]), reads=[b_chk], writes=[b_chk])
            if "dbg" in io and not io.get("dbg_done"):
                io["dbg_done"] = True
                ob = Buf("dbgout")
                io["outs"].append(ob)
                P.dma("sp", f_dma([(io["dbg"], chk[0:1, 0:8])]), b_chk, reads=[b_chk], writes=[ob])
                for k2 in range(4):
                    ob2 = Buf("dbgout2")
                    io["outs"].append(ob2)
                    P.dma("sp", f_dma([(io["dbg2"][k2], Rcur[k2][0][:])]), Rcur[k2][1], reads=[Rcur[k2][1]], writes=[ob2])
                ob2 = Buf("dbgout2")
                io["outs"].append(ob2)
                P.dma("sp", f_dma([(io["dbg2"][4], cur[:])]), b_cur, reads=[b_cur], writes=[ob2])
            for eng in ("pe", "act", "dve"):
                P.reg_load(eng, "att", chk_i[0:1, 0:1], reads=[b_chk])

        for step in range(ATT_FREE + 2):
            if step < ATT_FREE:
                s1(step)
            if 0 <= step - 1 < ATT_FREE:
                s2(step - 1)
            if 0 <= step - 2 < ATT_FREE:
                s3(step - 2)
        checkpoint()
        for i in range(ATT_FREE, nmax):
            P.region_begin("att", 1000)
            block(i)
            if i < nmax - 1:
                checkpoint()
            P.region_end()
        for k, (qt, hd) in enumerate(streams):
            pb = 64 * hd
            tok = slice(qt * 512, (qt + 1) * 512)
            acc_t, b_acc = accs[k]
            acc = acc_t[:].bitcast(F32) if k < 2 else acc_t[:]
            P.add("pe", f_mm1(acc, zt[:], m01[:, 0, :], False, True), reads=[b_zt, b_m01], writes=[b_acc])
            P.add("act", f_act(oT[pb:pb + 64, hp, tok], acc[pb:pb + 64, :], AF.Copy), reads=[b_acc],
                  writes=b_oT[qt * 4:(qt + 1) * 4])


def build_layer1(c, io):
    P = c.P
    c.one_t = c.sb([128, 2], F32, "one")
    P.add("dve", f_memset(c.one_t[:], 1.0), writes=[c.b_eps])
    m01, b_m01 = load_const(c, io["m01"], [128, 4, 512], BF16, "m01", eng="pool")
    negm, b_negm = load_const(c, io["negm"], [128, 4, 512], BF16, "negm", eng="pool")
    M1, b_M1 = load_const(c, io["M1"], [128, 128], BF16, "M1", eng="pool")
    NO, b_NO = load_const(c, io["NO"], [128, 128], BF16, "NO", eng="pool")
    aconsts = (m01, b_m01, negm, b_negm, M1, b_M1, NO, b_NO)
    wr, b_wr = load_const(c, io["moe_w_router"].rearrange("(c p) n -> p c n", p=128), [128, 8, N_EXP], F32, "wr")
    for g in range(TOK // GRP):
        xsrc = io["x2"][g * GRP:(g + 1) * GRP, :]
        P.barrier()
        with contextlib.ExitStack() as gst:
            x3 = c.sb([128, GT, 1024], F32, "x3", gst)
            b_x3 = [Buf("x3") for _ in range(GT)]
            with contextlib.ExitStack() as ast:
                x2T = c.sb([128, 8, GRP], BF16, "x2T", ast)
                b_x2T = [Buf("x2T") for _ in range(GT)]
                oT = c.sb([128, 8, GRP], BF16, "oT", ast)
                b_oT = [Buf("oT") for _ in range(GT)]
                for t in range(GT):
                    stg, b_stg = c.stage.next()
                    P.dma("sp", f_dma([(stg[:], xsrc[t * 128:(t + 1) * 128, :])]), b_stg, writes=[b_stg])
                    tile_to_T(c, stg[:], b_stg, x2T, b_x2T, t)
                with contextlib.ExitStack() as hst:
                    attention_group(c, hst, io, g, oT, b_oT, x2T, b_x2T, aconsts)
                P.barrier()
                with contextlib.ExitStack() as mst:
                    gam, bet, b_gb = load_ln(c, mst, io["g_mix"], io["b_mix"], "lnmix")
                    tmpf = c.rot(2, [128, 1024], F32, "tmpf", mst)
                    wos = c.rot(2, [128, 8, 512], BF16, "wos", mst)
                    xst = c.rot(2, [128, 512], F32, "xst", mst)
                    for half in range(2):
                        wo, b_wo = wos.next()
                        load_w(c, io["sb_w_out"][:, half * 512:(half + 1) * 512], wo[:], b_wo, nsplit=2)
                        for t in range(GT):
                            pm_, b_pm = c.pm.next()
                            P.add("pe", f_mm(pm_[:], [(oT[:, ch, t * 128:(t + 1) * 128], wo[:, ch, :]) for ch in range(8)]),
                                  reads=[b_wo, b_oT[t]], writes=[b_pm])
                            xs, b_xs = xst.next()
                            P.dma("sp", f_dma([(xs[:], xsrc[t * 128:(t + 1) * 128, half * 512:(half + 1) * 512])]), b_xs, writes=[b_xs])
                            P.add("dve", f_stt(x3[:, t, half * 512:(half + 1) * 512], xs[:], ALPHA, pm_[:], ALU.mult, ALU.add),
                                  reads=[b_xs, b_pm], writes=[b_x3[t]])
                    for t in range(GT):
                        ln_inplace(c, x3[:, t, :], b_x3[t], gam, bet, b_gb, tmpf)
            P.barrier()
            with contextlib.ExitStack() as fst:
                gam, bet, b_gb = load_ln(c, fst, io["g_ffn"], io["b_ffn"], "lnffn")
                tmpf = c.rot(2, [128, 1024], F32, "tmpf", fst)
                x3T = c.sb([128, 8, GRP], BF16, "x3T", fst)
                b_x3T = [Buf("x3T") for _ in range(GT)]
                yacc = c.sb([128, GT, 1024], F32, "yacc", fst)
                b_ya = [Buf("yacc", key=f"ya{t}") for t in range(GT)]
                gates = c.sb([128, GT, N_EXP], F32, "gates", fst)
                b_gt = Buf("gates")
                xTf = c.rot(2, [128, 8, 128], F32, "xTf", fst)
                lgr = c.rot(2, [128, 32], F32, "lg", fst)
                for t in range(GT):
                    tile_to_T(c, x3[:, t, :], b_x3[t], x3T, b_x3T, t)
                    P.add("act", f_act(yacc[:, t, :], x3[:, t, :], AF.Copy, scale=ALPHA), reads=[b_x3[t]], writes=[b_ya[t]])
                    xf, b_xf = xTf.next()
                    for hf in range(2):
                        pp, b_pp = c.pm.next()
                        ppv = pp[:].rearrange("p (c n) -> p c n", n=128)
                        P.add("pe", f_tr([(ppv[:, ch, :], x3[:, t, (hf * 4 + ch) * 128:(hf * 4 + ch + 1) * 128]) for ch in range(4)], c.idf[:]),
                              reads=[b_x3[t], c.b_idf], writes=[b_pp])
                        P.add("dve", f_copy(xf[:, hf * 4:(hf + 1) * 4, :], ppv[:, 0:4, :]), reads=[b_pp], writes=[b_xf])
                    pl, b_pl = c.pm.next()
                    P.add("pe", f_mm(pl[:, 0:N_EXP], [(xf[:, ch, :], wr[:, ch, :]) for ch in range(8)]),
                          reads=[b_xf, b_wr], writes=[b_pl])
                    lg, b_lg = lgr.next()
                    P.add("dve", f_copy(lg[:, 0:8], pl[:, 0:N_EXP]), reads=[b_pl], writes=[b_lg])
                    P.add("dve", lambda h, lg=lg: h.max(out=lg[:, 8:16], in_=lg[:, 0:8]), reads=[b_lg], writes=[b_lg])
                    P.add("dve", f_ts(lg[:, 16:17], lg[:, 8:9], -1.0, None, ALU.mult), reads=[b_lg], writes=[b_lg])
                    P.add("act", f_act(lg[:, 24:32], lg[:, 0:8], AF.Exp, bias=lg[:, 16:17]), reads=[b_lg], writes=[b_lg])
                    P.add("dve", f_ts(gates[:, t, :], lg[:, 0:8], lg[:, 9:10], None, ALU.is_ge), reads=[b_lg, b_gt], writes=[b_gt])
                    P.add("dve", f_tt(gates[:, t, :], gates[:, t, :], lg[:, 24:32], ALU.mult), reads=[b_lg, b_gt], writes=[b_gt])
                    P.add("dve", lambda h, lg=lg, t=t: h.reduce_sum(out=lg[:, 17:18], in_=gates[:, t, :], axis=mybir.AxisListType.X),
                          reads=[b_gt, b_lg], writes=[b_lg])
                    P.add("dve", lambda h, lg=lg: h.reciprocal(out=lg[:, 17:18], in_=lg[:, 17:18]), reads=[b_lg], writes=[b_lg])
                    P.add("dve", f_ts(gates[:, t, :], gates[:, t, :], lg[:, 17:18], None, ALU.mult), reads=[b_lg, b_gt], writes=[b_gt])
                for e in range(N_EXP):
                    with contextlib.ExitStack() as ust:
                        ffn_unit(c, ust, x3T, b_x3T, io["moe_w_in"][e * D:(e + 1) * D, :], D_EXP,
                                 io["moe_w_out"][e * D_EXP:(e + 1) * D_EXP, :], yacc, b_ya, [(0, 14), (14, 14)],
                                 gate=((lambda t, e=e: gates[:, t, e:e + 1]), b_gt))
                    P.barrier()
                for t in range(GT):
                    ln_inplace(c, yacc[:, t, :], b_ya[t], gam, bet, b_gb, tmpf)
                    r0 = g * GRP + t * 128
                    store(c, io, io["out"][r0:r0 + 128, :], yacc[:, t, :], b_ya[t])


def make_nc_layer1():
    nc = bass.Bass("TRN2", target_bir_lowering=False)
    c = Ctx(nc)
    io = {"outs": []}

    def din(name, shape, dt=F32):
        io[name] = c.dram(name, shape, dt, "ExternalInput")
    din("x2", [TOK, D]); din("KTs", [D, 2 * TOK], BF16); din("Vs", [2 * TOK, D], BF16)
    din("sb_w_q", [D, D]); din("sb_w_out", [D, D])
    din("moe_w_router", [D, N_EXP]); din("moe_w_in", [N_EXP * D, 2 * D_EXP]); din("moe_w_out", [N_EXP * D_EXP, D])
    din("g_mix", [128, D]); din("b_mix", [128, D]); din("g_ffn", [128, D]); din("b_ffn", [128, D])
    din("m01", [128, 4, 512]); din("negm", [128, 4, 512]); din("M1", [128, 128]); din("NO", [128, 128]); din("ident", [128, 128])
    io["out"] = c.dram("out", [TOK, D], F32, "ExternalOutput")
    with c.root:
        setup_common(c, io["ident"])
        build_layer1(c, io)
        c.P.add("sp", lambda h: None, reads=io["outs"])
        cnt = c.P.emit(nc)
    return nc, cnt


def layer1_inputs(inputs, core, x2, KTs, Vs):
    f = np.float32
    m01, negm, m1, negones = attn_tables()
    bc = lambda v: np.ascontiguousarray(np.broadcast_to(np.asarray(v, f)[None, :], (128, D)))
    return {
        "x2": x2, "KTs": KTs, "Vs": Vs,
        "sb_w_q": np.ascontiguousarray(inputs["sb_w_q"][0]), "sb_w_out": np.ascontiguousarray(inputs["sb_w_out"][0]),
        "moe_w_router": np.ascontiguousarray(inputs["moe_w_router"][0]),
        "moe_w_in": np.ascontiguousarray(inputs["moe_w_in"][0]).reshape(N_EXP * D, 2 * D_EXP),
        "moe_w_out": np.ascontiguousarray(inputs["moe_w_out"][0]).reshape(N_EXP * D_EXP, D),
        "g_mix": bc(inputs["ln_mix_g"][1]), "b_mix": bc(inputs["ln_mix_b"][1]),
        "g_ffn": bc(inputs["ln_ffn_g"][1]), "b_ffn": bc(inputs["ln_ffn_b"][1]),
        "m01": m01, "negm": negm, "M1": m1, "NO": negones, "ident": np.eye(128, dtype=f),
    }


def layer0_group(c, io, L0, g_tok, xsrc, cosd, sind, xs, b_xs, slot0, vscale):
    P = c.P
    S32, Sb, b_S32, b_Sb, consts = L0
    scratch = xs.rearrange("p t d -> p (t d)")
    P.barrier()
    with contextlib.ExitStack() as yst:
        yT = c.sb([128, 16, GRP], BF16, "yT", yst)
        b_yT = [[Buf("yT") for _ in range(GT)] for _ in range(RH)]
        with contextlib.ExitStack() as rst:
            ret_heads(c, rst, io, "own", 0, xsrc, cosd, sind, S32, Sb, b_S32, b_Sb, consts, yT, b_yT, scratch=scratch)
        P.barrier()
        with contextlib.ExitStack() as mst:
            gam, bet, b_gb = load_ln(c, mst, io["g_mix0"], io["b_mix0"], "lnmix")
            tmpf = c.rot(2, [128, 1024], F32, "tmpf", mst)
            wos = c.rot(2, [128, 16, 512], BF16, "wos", mst)
            xst = c.rot(2, [128, 512], F32, "xst", mst)
            for half in range(2):
                wo, b_wo = wos.next()
                load_w(c, io["ret_w_out"][:, half * 512:(half + 1) * 512], wo[:], b_wo, nsplit=2)
                for t in range(GT):
                    pm_, b_pm = c.pm.next()
                    P.add("pe", f_mm(pm_[:], [(yT[:, ch, t * 128:(t + 1) * 128], wo[:, ch, :]) for ch in range(16)]),
                          reads=[b_wo] + [b_yT[h][t] for h in range(RH)], writes=[b_pm])
                    xq, b_xq = xst.next()
                    P.dma("sp", f_dma([(xq[:], xsrc[t * 128:(t + 1) * 128, half * 512:(half + 1) * 512])]), b_xq, writes=[b_xq])
                    P.add("dve", f_stt(xs[:, t, half * 512:(half + 1) * 512], xq[:], ALPHA, pm_[:], ALU.mult, ALU.add),
                          reads=[b_xq, b_pm], writes=[b_xs[t]])
            for t in range(GT):
                ln_inplace(c, xs[:, t, :], b_xs[t], gam, bet, b_gb, tmpf)
    P.barrier()
    with contextlib.ExitStack() as fst:
        gam, bet, b_gb = load_ln(c, fst, io["g_ffn0"], io["b_ffn0"], "lnffn")
        tmpf = c.rot(2, [128, 1024], F32, "tmpf", fst)
        x1T = c.sb([128, 8, GRP], BF16, "x1T", fst)
        b_x1T = [Buf("x1T") for _ in range(GT)]
        for t in range(GT):
            tile_to_T(c, xs[:, t, :], b_xs[t], x1T, b_x1T, t)
            P.add("act", f_act(xs[:, t, :], xs[:, t, :], AF.Copy, scale=ALPHA), reads=[b_xs[t]], writes=[b_xs[t]])
        with contextlib.ExitStack() as ust:
            ffn_unit(c, ust, x1T, b_x1T, io["ffn_w_in"], D_FF, io["ffn_w_out"], xs, b_xs, [(0, 12), (12, 10)])
        for t in range(GT):
            ln_inplace(c, xs[:, t, :], b_xs[t], gam, bet, b_gb, tmpf)
        P.barrier()
        with contextlib.ExitStack() as kst:
            x2T = x1T
            b_x2T = [Buf("x2T") for _ in range(GT)]
            for t in range(GT):
                tile_to_T(c, xs[:, t, :], b_xs[t], x2T, b_x2T, t)
            wkv = c.rot(2, [128, 8, 512], BF16, "wkv", kst)
            ksb = c.rot(3, [128, 512], BF16, "ksb", kst)
            for cb in range(2):
                ws, b_ws = wkv.next()
                load_w(c, io["sb_w_kv"][:, cb * 512:(cb + 1) * 512], ws[:], b_ws, nsplit=2)
                for ch in range(4):
                    for tg in range(GRP // 512):
                        pk, b_pk = c.pm.next()
                        tok = slice(tg * 512, (tg + 1) * 512)
                        P.add("pe", f_mm(pk[:], [(ws[:, k, ch * 128:(ch + 1) * 128], x2T[:, k, tok]) for k in range(8)]),
                              reads=[b_ws] + b_x2T[tg * 4:(tg + 1) * 4], writes=[b_pk])
                        ks, b_ks = ksb.next()
                        if vscale is None:
                            P.add("act", f_act(ks[:], pk[:], AF.Copy), reads=[b_pk], writes=[b_ks])
                        else:
                            P.add("act", f_act(ks[:], pk[:], AF.Identity, scale=vscale[0]), reads=[b_pk, vscale[1]], writes=[b_ks])
                        f0 = (cb * 4 + ch) * 128
                        t0 = slot0 + tg * 512
                        store(c, io, io["KTs"][f0:f0 + 128, t0:t0 + 512], ks[:], b_ks)
            for cb in range(2):
                ws, b_ws = wkv.next()
                load_w(c, io["sb_w_kv"][:, 1024 + cb * 512:1024 + (cb + 1) * 512], ws[:], b_ws, nsplit=2)
                for t in range(GT):
                    pv, b_pv = c.pm.next()
                    P.add("pe", f_mm(pv[:], [(x2T[:, k, t * 128:(t + 1) * 128], ws[:, k, :]) for k in range(8)]),
                          reads=[b_ws, b_x2T[t]], writes=[b_pv])
                    ks, b_ks = ksb.next()
                    if vscale is None:
                        P.add("act", f_act(ks[:], pv[:], AF.Copy), reads=[b_pv], writes=[b_ks])
                    else:
                        P.add("act", f_act(ks[:], pv[:], AF.Identity, scale=vscale[0]), reads=[b_pv, vscale[1]], writes=[b_ks])
                    r0 = slot0 + t * 128
                    store(c, io, io["Vs"][r0:r0 + 128, cb * 512:(cb + 1) * 512], ks[:], b_ks)


def layer1_group(c, io, g, xs, b_xs, aconsts, wr, b_wr, attn_only=False):
    P = c.P
    P.barrier()
    with contextlib.ExitStack() as ast:
        x2T = c.sb([128, 8, GRP], BF16, "x2T", ast)
        b_x2T = [Buf("x2T") for _ in range(GT)]
        oT = c.sb([128, 8, GRP], BF16, "oT", ast)
        b_oT = [Buf("oT") for _ in range(GT)]
        for t in range(GT):
            tile_to_T(c, xs[:, t, :], b_xs[t], x2T, b_x2T, t)
        with contextlib.ExitStack() as hst:
            attention_group(c, hst, io, g, oT, b_oT, x2T, b_x2T, aconsts)
        P.barrier()
        with contextlib.ExitStack() as mst:
            gam, bet, b_gb = load_ln(c, mst, io["g_mix1"], io["b_mix1"], "lnmix")
            tmpf = c.rot(2, [128, 1024], F32, "tmpf", mst)
            wos = c.rot(2, [128, 8, 512], BF16, "wos", mst)
            for half in range(2):
                wo, b_wo = wos.next()
                load_w(c, io["sb_w_out"][:, half * 512:(half + 1) * 512], wo[:], b_wo, nsplit=2)
                for t in range(GT):
                    pm_, b_pm = c.pm.next()
                    P.add("pe", f_mm(pm_[:], [(oT[:, ch, t * 128:(t + 1) * 128], wo[:, ch, :]) for ch in range(8)]),
                          reads=[b_wo, b_oT[t]], writes=[b_pm])
                    xh = xs[:, t, half * 512:(half + 1) * 512]
                    P.add("dve", f_stt(xh, xh, ALPHA, pm_[:], ALU.mult, ALU.add), reads=[b_xs[t], b_pm], writes=[b_xs[t]])
            for t in range(GT):
                ln_inplace(c, xs[:, t, :], b_xs[t], gam, bet, b_gb, tmpf)
    if attn_only:
        return
    P.barrier()
    with contextlib.ExitStack() as fst:
        gam, bet, b_gb = load_ln(c, fst, io["g_ffn1"], io["b_ffn1"], "lnffn")
        tmpf = c.rot(2, [128, 1024], F32, "tmpf", fst)
        x3T = c.sb([128, 8, GRP], BF16, "x3T", fst)
        b_x3T = [Buf("x3T") for _ in range(GT)]
        gates = c.sb([128, GT, N_EXP], F32, "gates", fst)
        b_gt = Buf("gates")
        xTf = c.rot(2, [128, 8, 128], F32, "xTf", fst)
        lgr = c.rot(2, [128, 32], F32, "lg", fst)
        for t in range(GT):
            tile_to_T(c, xs[:, t, :], b_xs[t], x3T, b_x3T, t)
            xf, b_xf = xTf.next()
            for hf in range(2):
                pp, b_pp = c.pm.next()
                ppv = pp[:].rearrange("p (c n) -> p c n", n=128)
                P.add("pe", f_tr([(ppv[:, ch, :], xs[:, t, (hf * 4 + ch) * 128:(hf * 4 + ch + 1) * 128]) for ch in range(4)], c.idf[:]),
                      reads=[b_xs[t], c.b_idf], writes=[b_pp])
                P.add("dve", f_copy(xf[:, hf * 4:(hf + 1) * 4, :], ppv[:, 0:4, :]), reads=[b_pp], writes=[b_xf])
            pl, b_pl = c.pm.next()
            P.add("pe", f_mm(pl[:, 0:N_EXP], [(xf[:, ch, :], wr[:, ch, :]) for ch in range(8)]),
                  reads=[b_xf, b_wr], writes=[b_pl])
            P.add("act", f_act(xs[:, t, :], xs[:, t, :], AF.Copy, scale=ALPHA), reads=[b_xs[t]], writes=[b_xs[t]])
            lg, b_lg = lgr.next()
            P.add("dve", f_copy(lg[:, 0:8], pl[:, 0:N_EXP]), reads=[b_pl], writes=[b_lg])
            P.add("dve", lambda h, lg=lg: h.max(out=lg[:, 8:16], in_=lg[:, 0:8]), reads=[b_lg], writes=[b_lg])
            P.add("dve", f_ts(lg[:, 16:17], lg[:, 8:9], -1.0, None, ALU.mult), reads=[b_lg], writes=[b_lg])
            P.add("act", f_act(lg[:, 24:32], lg[:, 0:8], AF.Exp, bias=lg[:, 16:17]), reads=[b_lg], writes=[b_lg])
            P.add("dve", f_ts(gates[:, t, :], lg[:, 0:8], lg[:, 9:10], None, ALU.is_ge), reads=[b_lg, b_gt], writes=[b_gt])
            P.add("dve", f_tt(gates[:, t, :], gates[:, t, :], lg[:, 24:32], ALU.mult), reads=[b_lg, b_gt], writes=[b_gt])
            P.add("dve", lambda h, lg=lg, t=t: h.reduce_sum(out=lg[:, 17:18], in_=gates[:, t, :], axis=mybir.AxisListType.X),
                  reads=[b_gt, b_lg], writes=[b_lg])
            P.add("dve", lambda h, lg=lg: h.reciprocal(out=lg[:, 17:18], in_=lg[:, 17:18]), reads=[b_lg], writes=[b_lg])
            P.add("dve", f_ts(gates[:, t, :], gates[:, t, :], lg[:, 17:18], None, ALU.mult), reads=[b_lg, b_gt], writes=[b_gt])
        for e in range(N_EXP):
            with contextlib.ExitStack() as ust:
                ffn_unit(c, ust, x3T, b_x3T, io["moe_w_in"][e * D:(e + 1) * D, :], D_EXP,
                         io["moe_w_out"][e * D_EXP:(e + 1) * D_EXP, :], xs, b_xs, [(0, 14), (14, 14)],
                         gate=((lambda t, e=e: gates[:, t, e:e + 1]), b_gt))
            P.barrier()
        for t in range(GT):
            ln_inplace(c, xs[:, t, :], b_xs[t], gam, bet, b_gb, tmpf)
            r0 = g * GRP + t * 128
            store(c, io, io["out"][r0:r0 + 128, :], xs[:, t, :], b_xs[t])


ROUTED = True
I32 = mybir.dt.int32
REGS = [(None, [(0, 384)]), (384, [(384, 128), (512, 512)])]


def moe_tables():
    iota_f = np.broadcast_to(np.arange(128, dtype=np.float32)[None, :], (128, 128)).copy()
    iotap = (np.arange(128, dtype=np.float32)[:, None] + 128.0 * np.arange(8, dtype=np.float32)[None, :]).copy()
    k = np.arange(128)[:, None]
    m = np.arange(128)[None, :]
    ulow = (k < m).astype(np.float32)
    ones = np.ones((128, 128), np.float32)
    oh = np.zeros((8, 8, 128), np.float32)
    for r in range(8):
        oh[r, r, :] = 1.0
    return iota_f, iotap, ulow, ones, oh.reshape(8, 8 * 128)


def moe_routed_group(c, io, g, xs, b_xs, mc):
    P = c.P
    iota_f, b_iota, iotap, b_iotap, Ulow, b_Ul, ones_b, b_ones, OH, b_OH, wr, b_wr = mc
    P.barrier()
    with contextlib.ExitStack() as fst:
        rt_in = c.sb([128, GT, 16], F32, "rt_in", fst)
        b_rt = [Buf("rt_in") for _ in range(GT)]
        maskt = c.sb([128, GT, 8], F32, "maskt", fst)
        maskb = c.sb([128, GT, 8], BF16, "maskb", fst)
        b_mask = [Buf("mask") for _ in range(GT)]
        cnt_i = c.sb([128, 8], I32, "cnt_i", fst)
        b_cnt = Buf("cnt")
        RTp = c.sb([8, GRP], F32, "RTp", fst)
        RTg = c.sb([8, GRP], F32, "RTg", fst)
        b_RT = Buf("RT")
        x3b = c.sb([128, GT, 1024], BF16, "x3b", fst)
        b_x3b = [Buf("x3b") for _ in range(GT)]
        pst = contextlib.ExitStack()
        xTf = c.rot(2, [128, 8, 128], F32, "xTf", pst)
        lgr = c.rot(2, [128, 32], F32, "lg", pst)
        for t in range(GT):
            P.add("act", f_act(x3b[:, t, :], xs[:, t, :], AF.Copy), reads=[b_xs[t]], writes=[b_x3b[t]])
            xf, b_xf = xTf.next()
            for hf in range(2):
                pp, b_pp = c.pm.next()
                ppv = pp[:].rearrange("p (c n) -> p c n", n=128)
                P.add("pe", f_tr([(ppv[:, ch, :], xs[:, t, (hf * 4 + ch) * 128:(hf * 4 + ch + 1) * 128]) for ch in range(4)], c.idf[:]),
                      reads=[b_xs[t], c.b_idf], writes=[b_pp])
                P.add("dve", f_copy(xf[:, hf * 4:(hf + 1) * 4, :], ppv[:, 0:4, :]), reads=[b_pp], writes=[b_xf])
            pl, b_pl = c.pm.next()
            P.add("pe", f_mm(pl[:, 0:N_EXP], [(xf[:, ch, :], wr[:, ch, :]) for ch in range(8)]),
                  reads=[b_xf, b_wr], writes=[b_pl])
            P.add("act", f_act(xs[:, t, :], xs[:, t, :], AF.Copy, scale=ALPHA), reads=[b_xs[t]], writes=[b_xs[t]])
            lg, b_lg = lgr.next()
            gt = rt_in[:, t, 8:16]
            P.add("dve", f_copy(lg[:, 0:8], pl[:, 0:N_EXP]), reads=[b_pl], writes=[b_lg])
            P.add("dve", lambda h, lg=lg: h.max(out=lg[:, 8:16], in_=lg[:, 0:8]), reads=[b_lg], writes=[b_lg])
            P.add("dve", f_ts(lg[:, 16:17], lg[:, 8:9], -1.0, None, ALU.mult), reads=[b_lg], writes=[b_lg])
            P.add("act", f_act(lg[:, 24:32], lg[:, 0:8], AF.Exp, bias=lg[:, 16:17]), reads=[b_lg], writes=[b_lg])
            P.add("dve", f_ts(maskt[:, t, :], lg[:, 0:8], lg[:, 9:10], None, ALU.is_ge), reads=[b_lg], writes=[b_mask[t]])
            P.add("dve", f_copy(maskb[:, t, :], maskt[:, t, :]), reads=[b_mask[t]], writes=[b_mask[t]])
            P.add("dve", f_tt(gt, maskt[:, t, :], lg[:, 24:32], ALU.mult), reads=[b_lg, b_mask[t]], writes=[b_rt[t]])
            P.add("dve", lambda h, lg=lg, gt=gt: h.reduce_sum(out=lg[:, 17:18], in_=gt, axis=mybir.AxisListType.X),
                  reads=[b_rt[t], b_lg], writes=[b_lg])
            P.add("dve", lambda h, lg=lg: h.reciprocal(out=lg[:, 17:18], in_=lg[:, 17:18]), reads=[b_lg], writes=[b_lg])
            P.add("dve", f_ts(gt, gt, lg[:, 17:18], None, ALU.mult), reads=[b_lg, b_rt[t]], writes=[b_rt[t]])
        for t in range(GT):
            pp, b_pp = c.pm.next()
            pairs = [(Ulow[:], maskb[:, t, :])] + [(ones_b[:], maskb[:, t2, :]) for t2 in range(t)]
            P.add("pe", f_mm(pp[:, 0:8], pairs), reads=[b_Ul, b_ones] + b_mask[0:t + 1], writes=[b_pp])
            P.add("dve", f_stt(rt_in[:, t, 0:8], pp[:, 0:8], 1.0, maskt[:, t, :], ALU.add, ALU.mult),
                  reads=[b_pp, b_mask[t], b_rt[t]], writes=[b_rt[t]])
            P.add("dve", f_ts(rt_in[:, t, 0:8], rt_in[:, t, 0:8], -1.0, None, ALU.add), reads=[b_rt[t]], writes=[b_rt[t]])
        pc, b_pc = c.pm.next()
        P.add("pe", f_mm(pc[:, 0:8], [(ones_b[:], maskb[:, t, :]) for t in range(GT)]), reads=[b_ones] + b_mask, writes=[b_pc])
        P.add("dve", f_copy(cnt_i[:], pc[:, 0:8]), reads=[b_pc], writes=[b_cnt])
        for (dst, c0) in ((RTp, 0), (RTg, 8)):
            for hf in range(2):
                pp, b_pp = c.pm.next()
                P.add("pe", f_tr([(pp[0:8, j * 128:(j + 1) * 128], rt_in[:, hf * 4 + j, c0:c0 + 8]) for j in range(4)], c.idf[:]),
                      reads=b_rt[hf * 4:(hf + 1) * 4] + [c.b_idf], writes=[b_pp])
                P.add("act", f_act(dst[:, hf * 512:(hf + 1) * 512], pp[0:8, :], AF.Copy), reads=[b_pp], writes=[b_RT])

        pst.close()
        P.barrier()
        xgT = c.sb([128, 8, GRP], BF16, "xgT", fst)
        b_xg = [Buf("xg") for _ in range(GT)]
        P.add("dve", f_memset(xgT[:], 0.0), writes=b_xg)
        actT = c.sb([128, 14, GRP], BF16, "actT", fst)
        b_act = [[Buf("act") for _ in range(GT)] for _ in range(14)]
        oute = c.sb([128, GT, 512], BF16, "oute", fst)
        b_oute = [Buf("oute") for _ in range(GT)]
        pbc = c.sb([128, GRP], F32, "pbc", fst)
        gbc = c.sb([128, GRP], F32, "gbc", fst)
        b_bc = Buf("bc")
        selr = c.rot(16, [128, 128], BF16, "sel", fst)
        nst0 = REGS[0][1][0][1] // 128
        sTc = c.sb([128, nst0, GT, 128], BF16, "sTc", fst)
        b_sTc = [[Buf("sTc") for _ in range(GT)] for _ in range(nst0)]
        selTr = c.rot(4, [128, 128], BF16, "selT", fst)
        w1s = c.rot(2, [128, 8, 2, 256], BF16, "w1s", fst)
        w2s = c.rot(1, [128, 14, 512], BF16, "w2s", fst)
        sg = c.rot(2, [128, 512], BF16, "sg", fst)

        def regions():
            for thr, pieces in REGS:
                if thr is not None:
                    P.region_begin("cnt", thr)
                yield thr, pieces
                if thr is not None:
                    P.region_end()

        for e in range(N_EXP):
            w1 = io["moe_w_in"][e * D:(e + 1) * D, :]
            w2 = io["moe_w_out"][e * D_EXP:(e + 1) * D_EXP, :]
            for eng in ("pe", "act", "dve"):
                P.reg_load(eng, "cnt", cnt_i[0:1, e:e + 1], reads=[b_cnt])
            for (dst, src) in ((pbc, RTp), (gbc, RTg)):
                for hf in range(2):
                    pb, b_pb = c.pm.next()
                    P.add("pe", f_mm(pb[:], [(OH[:, e * 128:(e + 1) * 128], src[:, hf * 512:(hf + 1) * 512])]),
                          reads=[b_OH, b_RT], writes=[b_pb])
                    P.add("dve", f_copy(dst[:, hf * 512:(hf + 1) * 512], pb[:]), reads=[b_pb], writes=[b_bc])
            for thr, pieces in regions():
                for (s0, w) in pieces:
                    for st in range(w // 128):
                        slot0 = s0 + st * 128
                        sels = []
                        for t in range(GT):
                            sl, b_sl = selr.next()
                            P.add("dve", f_ts(sl[:], iota_f[:], rt_in[:, t, e:e + 1], -float(slot0), ALU.subtract, ALU.is_equal),
                                  reads=[b_iota, b_rt[t]], writes=[b_sl])
                            sels.append((sl, b_sl))
                        for hf in range(2):
                            pg, b_pg = c.pm.next()
                            groups = [(pg[:, j * 128:(j + 1) * 128],
                                       [(x3b[:, t, (hf * 4 + j) * 128:(hf * 4 + j + 1) * 128], sels[t][0][:]) for t in range(GT)])
                                      for j in range(4)]
                            P.add("pe", f_mm_multi(groups), reads=b_x3b + [b for _, b in sels], writes=[b_pg])
                            P.add("dve", f_copy(xgT[:, hf * 4:(hf + 1) * 4, slot0:slot0 + 128],
                                                pg[:].rearrange("p (c n) -> p c n", n=128)),
                                  reads=[b_pg], writes=[b_xg[slot0 // 128]])
            for sti in range(nst0):
                for t in range(GT):
                    tk = slice(t * 128, (t + 1) * 128)
                    P.add("dve", f_stt(sTc[:, sti, t, :], pbc[:, tk], iotap[:, sti:sti + 1], gbc[:, tk], ALU.is_equal, ALU.mult),
                          reads=[b_bc, b_iotap], writes=[b_sTc[sti][t]])
            for (j0, nj) in ((0, 14), (14, 14)):
                for blk in range(nj // 2):
                    jj = j0 + blk * 2
                    ws, b_ws = w1s.next()
                    srcg = w1[:, jj * 128: jj * 128 + 256].rearrange("(c p) n -> p c n", p=128)
                    srcu = w1[:, D_EXP + jj * 128: D_EXP + jj * 128 + 256].rearrange("(c p) n -> p c n", p=128)
                    P.dma("pool", f_dma([(ws[:, :, 0, :], srcg), (ws[:, :, 1, :], srcu)]), b_ws, writes=[b_ws], n=2)
                    for thr, pieces in regions():
                        for (s0, w) in pieces:
                            tiles = list(range(s0 // 128, (s0 + w) // 128))
                            for ci in range(2):
                                jl = blk * 2 + ci
                                pg, b_pg = c.pm.next()
                                pu, b_pu = c.pm.next()
                                P.add("pe", f_mm_multi([
                                    (pg[:, 0:w], [(ws[:, k, 0, ci * 128:(ci + 1) * 128], xgT[:, k, s0:s0 + w]) for k in range(8)]),
                                    (pu[:, 0:w], [(ws[:, k, 1, ci * 128:(ci + 1) * 128], xgT[:, k, s0:s0 + w]) for k in range(8)]),
                                ]), reads=[b_ws] + [b_xg[i] for i in tiles], writes=[b_pg, b_pu])
                                s_, b_s = sg.next()
                                P.add("act", f_act(s_[:, 0:w], pg[:, 0:w], AF.Silu), reads=[b_pg], writes=[b_s])
                                P.add("dve", f_tt(actT[:, jl, s0:s0 + w], s_[:, 0:w], pu[:, 0:w], ALU.mult),
                                      reads=[b_s, b_pu], writes=[b_act[jl][i] for i in tiles])
                for half in range(2):
                    w2v, b_w2 = w2s.next()
                    load_w(c, w2[j0 * 128:(j0 + nj) * 128, half * 512:(half + 1) * 512], w2v[:, 0:nj, :], b_w2, nsplit=2)
                    for thr, pieces in regions():
                        for (s0, w) in pieces:
                            for sti in range(s0 // 128, (s0 + w) // 128):
                                po, b_po = c.pm.next()
                                P.add("pe", f_mm(po[:], [(actT[:, j, sti * 128:(sti + 1) * 128], w2v[:, j, :]) for j in range(nj)]),
                                      reads=[b_w2] + [b_act[j][sti] for j in range(nj)], writes=[b_po])
                                P.add("dve", f_copy(oute[:, sti, :], po[:]), reads=[b_po], writes=[b_oute[sti]])
                    for thr, pieces in regions():
                        stis = [sti for (s0, w) in pieces for sti in range(s0 // 128, (s0 + w) // 128)]
                        for t in range(GT):
                            tk = slice(t * 128, (t + 1) * 128)
                            if thr is None:
                                sTs = [(sTc[:, sti, t, :], b_sTc[sti][t]) for sti in stis]
                                groups = [stis]
                            else:
                                sTs = None
                                groups = [stis[i:i + 4] for i in range(0, len(stis), 4)]
                            for grp in groups:
                                if thr is None:
                                    cur = sTs
                                else:
                                    cur = []
                                    for sti in grp:
                                        sT, b_sT = selTr.next()
                                        P.add("dve", f_stt(sT[:], pbc[:, tk], iotap[:, sti:sti + 1], gbc[:, tk], ALU.is_equal, ALU.mult),
                                              reads=[b_bc, b_iotap], writes=[b_sT])
                                        cur.append((sT[:], b_sT))
                                ps_, b_ps = c.pm.next()
                                P.add("pe", f_mm(ps_[:], [(cur[i][0], oute[:, sti, :]) for i, sti in enumerate(grp)]),
                                      reads=[b for _, b in cur] + [b_oute[sti] for sti in grp], writes=[b_ps])
                                xh = xs[:, t, half * 512:(half + 1) * 512]
                                P.add("dve", f_tt(xh, ps_[:], xh, ALU.add), reads=[b_ps, b_xs[t]], writes=[b_xs[t]])
    P.barrier()
    with contextlib.ExitStack() as lst:
        gam, bet, b_gb = load_ln(c, lst, io["g_ffn1"], io["b_ffn1"], "lnffn")
        tmpf = c.rot(2, [128, 1024], F32, "tmpf", lst)
        for t in range(GT):
            ln_inplace(c, xs[:, t, :], b_xs[t], gam, bet, b_gb, tmpf)
            r0 = g * GRP + t * 128
            store(c, io, io["out"][r0:r0 + 128, :], xs[:, t, :], b_xs[t])


def build_fused(c, io):
    P = c.P
    xres = c.sb([128, TOK // 128, 1024], F32, "xres")
    b_xres = [Buf("xres", key=f"xres{t}") for t in range(TOK // 128)]
    c.one_t = c.sb([128, 2], F32, "one")
    P.add("dve", f_memset(c.one_t[:], 1.0), writes=[c.b_eps])
    flag, b_flag = load_const(c, io["flag"], [128, 1], F32, "flag")
    with contextlib.ExitStack() as l0st:
        c.stack = l0st
        c.stage = c.rot(2, [128, 1024], F32, "stage")
        _, _, _, cd = retention_tables()
        dtab, b_dtab = load_const(c, io["dtab"], [128, RH, 128], F32, "dtab")
        cn, b_cn = load_const(c, io["cn"], [128, RH], F32, "cn")
        kdec, b_kdec = load_const(c, io["kdec"], [128, RH], F32, "kdec")
        consts = (dtab, b_dtab, cn, b_cn, kdec, b_kdec, cd)
        S32 = c.sb([128, RH, 2, 512], F32, "S32")
        Sb = c.sb([128, RH, 2, 512], BF16, "Sb")
        b_S32 = [Buf("S32") for _ in range(RH)]
        b_Sb = [Buf("Sb") for _ in range(RH)]
        for h in range(RH):
            P.add("dve", f_memset(S32[:, h, :, :], 0.0), writes=[b_S32[h]])
            P.add("dve", f_memset(Sb[:, h, :, :], 0.0), writes=[b_Sb[h]])
        L0 = (S32, Sb, b_S32, b_Sb, consts)
        for mode, g in (("prev", 0), ("prev", 1), ("own", 0), ("own", 1)):
            own = mode == "own"
            xsrc = (io["x_own"] if own else io["x_prev"])[g * GRP:(g + 1) * GRP, :]
            cosd = (io["cos_own"] if own else io["cos_prev"])[:, g * GRP:(g + 1) * GRP]
            sind = (io["sin_own"] if own else io["sin_prev"])[:, g * GRP:(g + 1) * GRP]
            gs = g if own else 1
            xs = xres[:, gs * GT:(gs + 1) * GT, :]
            bx = b_xres[gs * GT:(gs + 1) * GT]
            slot0 = (TOK if own else 0) + g * GRP
            layer0_group(c, io, L0, g, xsrc, cosd, sind, xs, bx, slot0, None if own else (flag[:, 0:1], b_flag))
        c.stack = c.root
    P.barrier()
    wr, b_wr = load_const(c, io["moe_w_router"].rearrange("(c p) n -> p c n", p=128), [128, 8, N_EXP], F32, "wr")
    with contextlib.ExitStack() as a1st:
        c.stack = a1st
        m01, b_m01 = load_const(c, io["m01"], [128, 4, 512], BF16, "m01", eng="pool")
        negm, b_negm = load_const(c, io["negm"], [128, 4, 512], BF16, "negm", eng="pool")
        M1, b_M1 = load_const(c, io["M1"], [128, 128], BF16, "M1", eng="pool")
        NO, b_NO = load_const(c, io["NO"], [128, 128], BF16, "NO", eng="pool")
        aconsts = (m01, b_m01, negm, b_negm, M1, b_M1, NO, b_NO)
        c.stack = c.root
        for g in range(TOK // GRP):
            layer1_group(c, io, g, xres[:, g * GT:(g + 1) * GT, :], b_xres[g * GT:(g + 1) * GT], aconsts, wr, b_wr,
                         attn_only=ROUTED)
    if ROUTED:
        P.barrier()
        iota_f, b_iota = load_const(c, io["iota_f"], [128, 128], F32, "iota_f")
        iotap, b_iotap = load_const(c, io["iotap"], [128, 8], F32, "iotap")
        Ulow, b_Ul = load_const(c, io["Ulow"], [128, 128], BF16, "Ulow", eng="pool")
        ones_b, b_ones = load_const(c, io["ones"], [128, 128], BF16, "ones_b", eng="pool")
        OH, b_OH = load_const(c, io["OH"], [8, 8 * 128], F32, "OH")
        mc = (iota_f, b_iota, iotap, b_iotap, Ulow, b_Ul, ones_b, b_ones, OH, b_OH, wr, b_wr)
        for g in range(TOK // GRP):
            moe_routed_group(c, io, g, xres[:, g * GT:(g + 1) * GT, :], b_xres[g * GT:(g + 1) * GT], mc)


def make_nc_fused():
    nc = bass.Bass("TRN2", target_bir_lowering=False)
    c = Ctx(nc)
    io = {"outs": []}

    def din(name, shape, dt=F32):
        io[name] = c.dram(name, shape, dt, "ExternalInput")
    din("x_own", [TOK, D]); din("x_prev", [TOK, D])
    din("cos_own", [128, TOK]); din("sin_own", [128, TOK]); din("cos_prev", [128, TOK]); din("sin_prev", [128, TOK])
    din("ret_w_in", [D, 6144]); din("ret_w_out", [2048, D])
    din("ffn_w_in", [D, 2 * D_FF]); din("ffn_w_out", [D_FF, D]); din("sb_w_kv", [D, 2048])
    din("sb_w_q", [D, D]); din("sb_w_out", [D, D])
    din("moe_w_router", [D, N_EXP]); din("moe_w_in", [N_EXP * D, 2 * D_EXP]); din("moe_w_out", [N_EXP * D_EXP, D])
    for l in (0, 1):
        for nm in ("g_mix", "b_mix", "g_ffn", "b_ffn"):
            din(f"{nm}{l}", [128, D])
    din("dtab", [128, RH, 128]); din("cn", [128, RH]); din("kdec", [128, RH]); din("ident", [128, 128]); din("flag", [128, 1])
    din("m01", [128, 4, 512]); din("negm", [128, 4, 512]); din("M1", [128, 128]); din("NO", [128, 128])
    din("iota_f", [128, 128]); din("iotap", [128, 8]); din("Ulow", [128, 128]); din("ones", [128, 128]); din("OH", [8, 8 * 128])
    io["KTs"] = nc.dram_tensor("KTs_i", [D, 2 * TOK], BF16).ap()
    io["Vs"] = nc.dram_tensor("Vs_i", [2 * TOK, D], BF16).ap()
    io["out"] = c.dram("out", [TOK, D], F32, "ExternalOutput")
    if DEBUG:
        io["dbg"] = c.dram("dbg", [1, 8], F32, "ExternalOutput")
        io["dbg2"] = c.dram("dbg2", [5, 128, 512], BF16, "ExternalOutput")
    with c.root:
        setup_common(c, io["ident"])
        build_fused(c, io)
        c.P.add("sp", lambda h: None, reads=io["outs"])
        cnt = c.P.emit(nc)
    return nc, cnt


def fused_inputs(inputs, core, shared):
    b, half = core // 2, core % 2
    x = inputs["x"]
    f = np.float32
    co, so = rope_tables(half * TOK, TOK)
    d = dict(shared)
    d.update({
        "x_own": np.ascontiguousarray(x[b, half * TOK:(half + 1) * TOK]),
        "x_prev": np.ascontiguousarray(x[b, 0:TOK]) if half == 1 else np.zeros((TOK, D), f),
        "cos_own": co, "sin_own": so,
        "flag": np.full((128, 1), float(half), f),
    })
    return d


def shared_inputs(inputs):
    f = np.float32
    dtab, cn, kdec, _ = retention_tables()
    cp, sp_ = rope_tables(0, TOK)
    m01, negm, m1, negones = attn_tables()
    bc = lambda v: np.ascontiguousarray(np.broadcast_to(np.asarray(v, f)[None, :], (128, D)))
    d = {
        "cos_prev": cp, "sin_prev": sp_,
        "ret_w_in": np.ascontiguousarray(inputs["ret_w_in"][0]),
        "ret_w_out": np.ascontiguousarray(inputs["ret_w_out"][0]),
        "ffn_w_in": np.ascontiguousarray(inputs["ffn_w_in"][0]),
        "ffn_w_out": np.ascontiguousarray(inputs["ffn_w_out"][0]),
        "sb_w_kv": np.ascontiguousarray(inputs["sb_w_kv"]),
        "sb_w_q": np.ascontiguousarray(inputs["sb_w_q"][0]), "sb_w_out": np.ascontiguousarray(inputs["sb_w_out"][0]),
        "moe_w_router": np.ascontiguousarray(inputs["moe_w_router"][0]),
        "moe_w_in": np.ascontiguousarray(inputs["moe_w_in"][0]).reshape(N_EXP * D, 2 * D_EXP),
        "moe_w_out": np.ascontiguousarray(inputs["moe_w_out"][0]).reshape(N_EXP * D_EXP, D),
        "dtab": dtab, "cn": cn, "kdec": kdec, "ident": np.eye(128, dtype=f),
        "m01": m01, "negm": negm, "M1": m1, "NO": negones,
    }
    d["iota_f"], d["iotap"], d["Ulow"], d["ones"], d["OH"] = moe_tables()
    for l in (0, 1):
        d[f"g_mix{l}"] = bc(inputs["ln_mix_g"][l]); d[f"b_mix{l}"] = bc(inputs["ln_mix_b"][l])
        d[f"g_ffn{l}"] = bc(inputs["ln_ffn_g"][l]); d[f"b_ffn{l}"] = bc(inputs["ln_ffn_b"][l])
    return d


_NC_CACHE = {}


def kernel(**inputs):
    inputs = {k: np.asarray(v) for k, v in inputs.items()}
    cores = list(range(8))
    if "f" not in _NC_CACHE:
        _NC_CACHE["f"] = make_nc_fused()[0]
    shared = shared_inputs(inputs)
    in_maps = [fused_inputs(inputs, cr, shared) for cr in cores]
    res = run_bass_kernel_spmd(_NC_CACHE["f"], in_maps, core_ids=cores).results
    out = np.stack([np.concatenate([res[2 * b]["out"], res[2 * b + 1]["out"]], axis=0) for b in range(4)], axis=0)
    return out.astype(np.float32)
```
